# Optimizing a Trainium2 kernel written in Bass

```python
import math
import jax, jax.numpy as jnp
from jax import lax
import numpy as np

D_MODEL = 1024
BATCH = 8
SEQ = 2048
DEPTH = 4

HEAD_DIM = 64
D_MIX = D_MODEL
N_MIX_HEADS = D_MIX // HEAD_DIM
C_HEADS = N_MIX_HEADS // 4
A_HEADS = (N_MIX_HEADS - C_HEADS) // 2
B_HEADS = N_MIX_HEADS - C_HEADS - A_HEADS
A_QK_DIM = HEAD_DIM // 2
A_WIDTH = A_HEADS * HEAD_DIM
B_WIDTH = B_HEADS * HEAD_DIM
C_WIDTH = C_HEADS * HEAD_DIM
DILATED_PATTERNS = ((128, 1), (512, 4), (2048, 16))
BLOCK_Q = 128
ROPE_THETA = 10000.0
D_FF = ((8 * D_MODEL // 3 + 255) // 256) * 256
C_W_RANK = 64
C_A_RANK = 64
C_V_RANK = 32
C_G_RANK = 128
DEEPNORM_ALPHA = (2 * DEPTH) ** 0.25
DEEPNORM_BETA = (8 * DEPTH) ** -0.25
LN_EPS = 1e-5
RMS_EPS = 1e-5
C_GN_EPS = 64e-5
NEG_INF = -1e30
A_Q_COLS = 2 * A_HEADS * A_QK_DIM
B_COLS = B_WIDTH
C_COLS = 3 * C_WIDTH + C_W_RANK + C_A_RANK + C_G_RANK
IN_SIZES = (A_Q_COLS, A_Q_COLS, A_WIDTH, B_COLS, B_COLS, B_COLS, C_COLS)
IN_SPLIT_IDX = [int(v) for v in np.cumsum(IN_SIZES)[:-1]]
N_IN = int(sum(IN_SIZES))
C_SPLIT_IDX = [int(v) for v in np.cumsum((C_WIDTH, C_WIDTH, C_WIDTH, C_W_RANK, C_A_RANK))]

kernel_name = "hybrid_diffattn_dilated_rwkv7_macaron_deepnorm"


def layer_norm(x, g, b):
    xf = x.astype(jnp.float32)
    mu = jnp.mean(xf, -1, keepdims=True)
    var = jnp.mean(jnp.square(xf - mu), -1, keepdims=True)
    return ((xf - mu) * lax.rsqrt(var + LN_EPS) * g + b).astype(x.dtype)


def rms_norm(x, g):
    xf = x.astype(jnp.float32)
    return (xf * lax.rsqrt(jnp.mean(jnp.square(xf), -1, keepdims=True) + RMS_EPS) * g).astype(x.dtype)


def swiglu(x, w_gate, w_up, w_down):
    return (jax.nn.silu(x @ w_gate) * (x @ w_up)) @ w_down


def rope(x, positions):
    d = x.shape[-1]
    inv = ROPE_THETA ** (-jnp.arange(0, d, 2, dtype=jnp.float32) / d)
    ang = positions.astype(jnp.float32)[:, None] * inv[None, :]
    ang = ang.reshape((ang.shape[0],) + (1,) * (x.ndim - 3) + (d // 2,))
    cos, sin = jnp.cos(ang).astype(x.dtype), jnp.sin(ang).astype(x.dtype)
    x1, x2 = x[..., : d // 2], x[..., d // 2:]
    return jnp.concatenate([x1 * cos - x2 * sin, x2 * cos + x1 * sin], axis=-1)


def token_shift(x):
    return jnp.pad(x[:, :-1], ((0, 0), (1, 0), (0, 0)))


def diff_attention(q, k, v, lam):
    bsz, s_len, h, _, dq = q.shape
    nb = s_len // BLOCK_Q
    scale = dq ** -0.5
    qb = jnp.moveaxis(q.reshape(bsz, nb, BLOCK_Q, h, 2, dq), 1, 0)
    vf = v.astype(jnp.float32)
    key_pos = jnp.arange(s_len)

    def one_block(args):
        q_blk, n = args
        q_pos = n * BLOCK_Q + jnp.arange(BLOCK_Q)
        s = jnp.einsum('bqhmd,bkhmd->bhmqk', q_blk, k).astype(jnp.float32) * scale
        causal = key_pos[None, :] <= q_pos[:, None]
        p = jax.nn.softmax(jnp.where(causal, s, NEG_INF), axis=-1)
        attn = p[:, :, 0] - lam * p[:, :, 1]
        return jnp.einsum('bhqk,bkhd->bqhd', attn, vf)

    out = lax.map(one_block, (qb, jnp.arange(nb)))
    return jnp.moveaxis(out, 0, 1).reshape(bsz, s_len, h, v.shape[-1])


def dilated_branch(q, k, v, window, dilation):
    bsz, s_len, h, d = q.shape
    span = window // dilation
    group = dilation * span
    s_pad = -(-s_len // group) * group
    nb = s_pad // group
    pad = ((0, 0), (0, s_pad - s_len), (0, 0), (0, 0))

    def strided_blocks(t):
        return jnp.pad(t, pad).reshape(bsz, nb, span, dilation, h, d)

    def with_prev(t):
        prev = jnp.pad(t[:, :-1], ((0, 0), (1, 0), (0, 0), (0, 0), (0, 0), (0, 0)))
        return jnp.concatenate([prev, t], axis=2)

    qb = strided_blocks(q)
    kc = with_prev(strided_blocks(k))
    vc = with_prev(strided_blocks(v)).astype(jnp.float32)
    s = jnp.einsum('bnqrhd,bnkrhd->bnrhqk', qb, kc).astype(jnp.float32) * (d ** -0.5)
    i = jnp.arange(span)[:, None]
    c = jnp.arange(2 * span)[None, :]
    n = jnp.arange(nb)[:, None, None]
    valid = (c >= i) & (c <= i + span) & ((n > 0) | (c >= span))
    s = jnp.where(valid[None, :, None, None], s, NEG_INF)
    m = jnp.max(s, axis=-1, keepdims=True)
    e = jnp.exp(s - m)
    denom = jnp.sum(e, axis=-1)
    o = jnp.einsum('bnrhqk,bnkrhd->bnqrhd', e, vc)
    o = o / jnp.moveaxis(denom, 4, 2)[..., None]
    lse = jnp.moveaxis(m[..., 0] + jnp.log(denom), 4, 2)
    o = o.reshape(bsz, s_pad, h, d)[:, :s_len]
    lse = lse.reshape(bsz, s_pad, h)[:, :s_len]
    return o, lse


def dilated_attention(q, k, v):
    outs, lses = [], []
    for window, dilation in DILATED_PATTERNS:
        o, lse = dilated_branch(q, k, v, window, dilation)
        outs.append(o)
        lses.append(lse)
    wts = jax.nn.softmax(jnp.stack(lses), axis=0)
    return jnp.sum(wts[..., None] * jnp.stack(outs), axis=0).astype(q.dtype)


def rwkv7_scan(r, decay, k, v, kk, a):
    bsz, _, h, d = r.shape

    def step(state, inp):
        r_t, w_t, k_t, v_t, kk_t, a_t = inp
        sa = jnp.einsum('bhvk,bhk->bhv', state, -kk_t)
        state = (state * w_t[:, :, None, :] + sa[..., None] * (kk_t * a_t)[:, :, None, :]
                 + v_t[..., None] * k_t[:, :, None, :])
        return state, jnp.einsum('bhvk,bhk->bhv', state, r_t)

    xs = tuple(jnp.moveaxis(t, 1, 0) for t in (r, decay, k, v, kk, a))
    _, ys = lax.scan(step, jnp.zeros((bsz, h, d, d), jnp.float32), xs)
    return jnp.moveaxis(ys, 0, 1)


def rwkv7_mixer(c, c_mu, w0, w2, a0, a2, g2, k_k, k_a, r_k, gn_g, gn_b, v_first, v_res):
    bsz, s_len, _ = c.shape
    c = c + (token_shift(c) - c) * c_mu
    r, k, v, xw, xa, xg = jnp.split(c, C_SPLIT_IDX, axis=-1)
    if v_first is None:
        v_first = v
    else:
        v0, v1, v2 = v_res
        v = v + (v_first - v) * jax.nn.sigmoid(v0 + (v @ v1) @ v2)
    f32 = lambda t: t.astype(jnp.float32)
    decay = jnp.exp(-math.exp(-0.5) * jax.nn.sigmoid(f32(w0 + jnp.tanh(xw) @ w2)))
    a = jax.nn.sigmoid(f32(a0 + xa @ a2))
    g = jax.nn.sigmoid(xg) @ g2
    heads = lambda t: t.reshape(bsz, s_len, C_HEADS, HEAD_DIM)
    rh, kh, vh, ah, dh = heads(f32(r)), heads(f32(k)), heads(f32(v)), heads(a), heads(decay)
    kk = kh * k_k.reshape(C_HEADS, HEAD_DIM)
    kk = kk / jnp.maximum(jnp.sqrt(jnp.sum(jnp.square(kk), -1, keepdims=True)), 1e-12)
    kh = kh * (1.0 + (ah - 1.0) * k_a.reshape(C_HEADS, HEAD_DIM))
    y = rwkv7_scan(rh, dh, kh, vh, kk, ah)
    mu = jnp.mean(y, -1, keepdims=True)
    var = jnp.mean(jnp.square(y - mu), -1, keepdims=True)
    y = ((y - mu) * lax.rsqrt(var + C_GN_EPS) * gn_g.reshape(C_HEADS, HEAD_DIM)
         + gn_b.reshape(C_HEADS, HEAD_DIM))
    y = y + jnp.sum(rh * kh * r_k, -1, keepdims=True) * vh
    y = y.reshape(bsz, s_len, C_WIDTH) * g
    return y.astype(c.dtype), v_first


def setup_inputs(seed: int = 0) -> dict:
    key = jax.random.key(seed)
    ks = iter(jax.random.split(key, 40))
    nrm = lambda shape, scale: jax.random.normal(next(ks), shape, jnp.float32) * scale
    L = DEPTH
    return {
        "x": nrm((BATCH, SEQ, D_MODEL), 1.0),
        "ffn_a_gate": nrm((L, D_MODEL, D_FF), D_MODEL ** -0.5),
        "ffn_a_up": nrm((L, D_MODEL, D_FF), D_MODEL ** -0.5),
        "ffn_a_down": nrm((L, D_FF, D_MODEL), D_FF ** -0.5 * DEEPNORM_BETA),
        "ffn_b_gate": nrm((L, D_MODEL, D_FF), D_MODEL ** -0.5),
        "ffn_b_up": nrm((L, D_MODEL, D_FF), D_MODEL ** -0.5),
        "ffn_b_down": nrm((L, D_FF, D_MODEL), D_FF ** -0.5 * DEEPNORM_BETA),
        "ln_g": 1.0 + nrm((L, 3, D_MODEL), 0.02),
        "ln_b": nrm((L, 3, D_MODEL), 0.02),
        "w_in": nrm((L, D_MODEL, N_IN), D_MODEL ** -0.5),
        "w_out": nrm((L, D_MIX, D_MODEL), D_MIX ** -0.5 * DEEPNORM_BETA),
        "a_lam_q1": nrm((L, A_QK_DIM), 0.1),
        "a_lam_k1": nrm((L, A_QK_DIM), 0.1),
        "a_lam_q2": nrm((L, A_QK_DIM), 0.1),
        "a_lam_k2": nrm((L, A_QK_DIM), 0.1),
        "a_norm_g": 1.0 + nrm((L, HEAD_DIM), 0.02),
        "b_norm_g": 1.0 + nrm((L, HEAD_DIM), 0.02),
        "c_mu": jax.random.uniform(next(ks), (L, C_COLS), jnp.float32),
        "c_w0": nrm((L, C_WIDTH), 0.5),
        "c_w2": nrm((L, C_W_RANK, C_WIDTH), C_W_RANK ** -0.5),
        "c_a0": nrm((L, C_WIDTH), 0.5),
        "c_a2": nrm((L, C_A_RANK, C_WIDTH), C_A_RANK ** -0.5),
        "c_g2": nrm((L, C_G_RANK, C_WIDTH), C_G_RANK ** -0.5),
        "c_k_k": 0.85 + nrm((L, C_WIDTH), 0.05),
        "c_k_a": 1.0 + nrm((L, C_WIDTH), 0.05),
        "c_r_k": nrm((L, C_HEADS, HEAD_DIM), 0.1),
        "c_gn_g": 1.0 + nrm((L, C_WIDTH), 0.02),
        "c_gn_b": nrm((L, C_WIDTH), 0.02),
        "c_v0": nrm((L - 1, C_WIDTH), 0.5),
        "c_v1": nrm((L - 1, C_WIDTH, C_V_RANK), C_WIDTH ** -0.5),
        "c_v2": nrm((L - 1, C_V_RANK, C_WIDTH), C_V_RANK ** -0.5),
    }


def reference(x, ffn_a_gate, ffn_a_up, ffn_a_down, ffn_b_gate, ffn_b_up, ffn_b_down,
              ln_g, ln_b, w_in, w_out, a_lam_q1, a_lam_k1, a_lam_q2, a_lam_k2,
              a_norm_g, b_norm_g, c_mu, c_w0, c_w2, c_a0, c_a2, c_g2, c_k_k, c_k_a,
              c_r_k, c_gn_g, c_gn_b, c_v0, c_v1, c_v2):
    bsz, s_len, _ = x.shape
    positions = jnp.arange(s_len)
    v_first = None
    for l in range(DEPTH):
        x = layer_norm(DEEPNORM_ALPHA * x + 0.5 * swiglu(x, ffn_a_gate[l], ffn_a_up[l], ffn_a_down[l]),
                       ln_g[l, 0], ln_b[l, 0])

        aq, ak, av, bq, bk, bv, cc = jnp.split(x @ w_in[l], IN_SPLIT_IDX, axis=-1)

        lambda_init = 0.8 - 0.6 * math.exp(-0.3 * l)
        lam = (jnp.exp(jnp.sum(a_lam_q1[l] * a_lam_k1[l]).astype(jnp.float32))
               - jnp.exp(jnp.sum(a_lam_q2[l] * a_lam_k2[l]).astype(jnp.float32)) + lambda_init)
        aq = rope(aq.reshape(bsz, s_len, A_HEADS, 2, A_QK_DIM), positions)
        ak = rope(ak.reshape(bsz, s_len, A_HEADS, 2, A_QK_DIM), positions)
        o_a = diff_attention(aq, ak, av.reshape(bsz, s_len, A_HEADS, HEAD_DIM), lam)
        o_a = (rms_norm(o_a, a_norm_g[l]) * (1.0 - lambda_init)).astype(x.dtype)
        o_a = o_a.reshape(bsz, s_len, A_WIDTH)

        bq = rope(bq.reshape(bsz, s_len, B_HEADS, HEAD_DIM), positions)
        bk = rope(bk.reshape(bsz, s_len, B_HEADS, HEAD_DIM), positions)
        o_b = dilated_attention(bq, bk, bv.reshape(bsz, s_len, B_HEADS, HEAD_DIM))
        o_b = rms_norm(o_b, b_norm_g[l]).reshape(bsz, s_len, B_WIDTH)

        v_res = None if l == 0 else (c_v0[l - 1], c_v1[l - 1], c_v2[l - 1])
        o_c, v_first = rwkv7_mixer(cc, c_mu[l], c_w0[l], c_w2[l], c_a0[l], c_a2[l], c_g2[l],
                                   c_k_k[l], c_k_a[l], c_r_k[l], c_gn_g[l], c_gn_b[l],
                                   v_first, v_res)

        mix = jnp.concatenate([o_a, o_b, o_c], axis=-1) @ w_out[l]
        x = layer_norm(DEEPNORM_ALPHA * x + mix, ln_g[l, 1], ln_b[l, 1])

        x = layer_norm(DEEPNORM_ALPHA * x + 0.5 * swiglu(x, ffn_b_gate[l], ffn_b_up[l], ffn_b_down[l]),
                       ln_g[l, 2], ln_b[l, 2])
    return x
```

```python
import contextlib
import math
import numpy as np
import concourse.bass as bass
import concourse.mybir as mybir
from concourse.bass_utils import run_bass_kernel_spmd

F32 = mybir.dt.float32
AF = mybir.ActivationFunctionType
ALU = mybir.AluOpType
AX = mybir.AxisListType

DEPTH = 4
D = 1024
S = 2048
FF = 2816
NFC = 22
ALPHA = (2 * DEPTH) ** 0.25
LN_EPS = 1e-5
RMS_EPS = 1e-5
GN_EPS = 64e-5
THETA = 10000.0
NCHUNK_W = 38 + 8
ARC = 31300


class TT:
    __slots__ = ("w", "r")

    def __init__(self):
        self.w = None
        self.r = []


class Prog:
    ENG = ("pe", "act", "dve", "pool", "sp")

    def __init__(self, nc, same_sync=True, n_dma_sems=40):
        self.nc = nc
        self.same_sync = same_sync
        self.q = {e: [] for e in self.ENG}
        self.cnt = {e: 0 for e in self.ENG}
        self.known = {e: {} for e in self.ENG}
        self.stack = contextlib.ExitStack()
        self.sem = {}
        for e in self.ENG:
            self.sem[e] = self.stack.enter_context(nc.semaphore("s_" + e))
        self.dsem = []
        self.dval = []
        for i in range(n_dma_sems):
            self.dsem.append(self.stack.enter_context(nc.semaphore("d%d" % i)))
            self.dval.append(0)
        self.ndma = 0

    def sb(self, name, shape, dt=F32):
        return self.stack.enter_context(self.nc.sbuf_tensor(name, shape, dt))

    def ps(self, name, shape, dt=F32):
        return self.stack.enter_context(self.nc.psum_tensor(name, shape, dt))

    def _waits(self, eng, reads, writes):
        deps = {}
        for t in reads:
            if t.w is not None:
                k, v = t.w
                if deps.get(k, 0) < v:
                    deps[k] = v
        for t in writes:
            if t.w is not None:
                k, v = t.w
                if deps.get(k, 0) < v:
                    deps[k] = v
            for (k, v) in t.r:
                if deps.get(k, 0) < v:
                    deps[k] = v
        out = []
        kn = self.known[eng]
        for k, v in deps.items():
            if k == eng and (eng == "pe" or not self.same_sync):
                continue
            if kn.get(k, 0) >= v:
                continue
            kn[k] = v
            out.append((k, v))
        return out

    def _semh(self, k):
        return self.sem[k] if isinstance(k, str) else self.dsem[k]

    def _mark(self, tok, reads, writes):
        for t in reads:
            t.r.append(tok)
            if len(t.r) > 24:
                best = {}
                for k, v in t.r:
                    if best.get(k, 0) < v:
                        best[k] = v
                t.r = list(best.items())
        for t in writes:
            t.w = tok
            t.r = []

    def op(self, eng, fn, *args, reads=(), writes=(), **kwargs):
        waits = self._waits(eng, reads, writes)
        self.cnt[eng] += 1
        self._mark((eng, self.cnt[eng]), reads, writes)
        sem = self.sem[eng]
        wl = [(self._semh(k), v) for k, v in waits]

        def emit(e, fn=fn, wl=wl, sem=sem, args=args, kwargs=kwargs):
            for s, v in wl:
                e.wait_ge(s, v)
            getattr(e, fn)(*args, **kwargs).then_inc(sem, 1)

        self.q[eng].append(emit)

    def dma(self, eng, fn, *args, reads=(), writes=(), di=None, **kwargs):
        if di is None:
            di = self.ndma % len(self.dsem)
            self.ndma += 1
        waits = self._waits(eng, reads, writes)
        self.dval[di] += 16
        self._mark((di, self.dval[di]), reads, writes)
        sem = self.dsem[di]
        wl = [(self._semh(k), v) for k, v in waits]

        def emit(e, fn=fn, wl=wl, sem=sem, args=args, kwargs=kwargs):
            for s, v in wl:
                e.wait_ge(s, v)
            getattr(e, fn)(*args, **kwargs).then_inc(sem, 16)

        self.q[eng].append(emit)

    def barrier(self):
        for e in self.ENG:
            wl = []
            kn = self.known[e]
            for k in self.ENG:
                if k != e and self.cnt[k] > kn.get(k, 0):
                    kn[k] = self.cnt[k]
                    wl.append((self.sem[k], self.cnt[k]))
            for i, v in enumerate(self.dval):
                if v > kn.get(i, 0):
                    kn[i] = v
                    wl.append((self.dsem[i], v))
            if self.same_sync and e != "pe" and self.cnt[e] > kn.get(e, 0):
                kn[e] = self.cnt[e]
                wl.append((self.sem[e], self.cnt[e]))

            def emit(en, wl=wl):
                for s, v in wl:
                    en.wait_ge(s, v)

            self.q[e].append(emit)

    def finish(self):
        nc = self.nc
        q = self.q
        with nc.Block() as block:
            @block.tensor
            def _(e):
                for f in q["pe"]:
                    f(e)

            @block.scalar
            def _(e):
                for f in q["act"]:
                    f(e)

            @block.vector
            def _(e):
                for f in q["dve"]:
                    f(e)

            @block.gpsimd
            def _(e):
                for f in q["pool"]:
                    f(e)

            @block.sync
            def _(e):
                for f in q["sp"]:
                    f(e)
        self.stack.close()


class Rot:
    def __init__(self, views):
        self.v = views
        self.t = [TT() for _ in views]
        self.i = 0

    def next(self):
        i = self.i % len(self.v)
        self.i += 1
        return self.v[i], self.t[i]


def _chunk(W, cols):
    return np.ascontiguousarray(W[:, cols].reshape(8, 128, len(cols)).transpose(1, 0, 2))


def _swap_idx(base, n, grp):
    idx = np.arange(n)
    half = grp // 2
    return base + (idx // grp) * grp + ((idx % grp) + half) % grp


def _rope_table(grp):
    half = grp // 2
    inv = (THETA ** (-np.arange(0, grp, 2, dtype=np.float32) / grp)).astype(np.float32)
    pos = np.arange(S, dtype=np.float32)
    ang = (pos[:, None] * inv[None, :]).astype(np.float32)
    cos = np.cos(ang).astype(np.float32).T
    sin = np.sin(ang).astype(np.float32).T
    r = np.arange(128)
    i = r % half
    sign = np.where((r % grp) < half, -1.0, 1.0).astype(np.float32)
    C = cos[i]
    Sg = sin[i] * sign[:, None]
    t = np.stack([C, Sg], axis=1)
    t = t.reshape(128, 2, 4, 512).transpose(0, 2, 1, 3)
    return np.ascontiguousarray(t.reshape(128, 4 * 2 * 512)).astype(np.float32)


def _consts():
    c = {}
    k = np.arange(128)[:, None]
    q = np.arange(256)[None, :]
    c["causal"] = (q[:, :128] >= k).astype(np.float32)
    c["band"] = ((q >= k) & (q <= k + 128)).astype(np.float32)
    su = (q[:, :128] > k).astype(np.float32)
    iu = (q[:, :128] >= k).astype(np.float32)
    c["uppers"] = np.concatenate([su, iu], axis=1)
    c["ident"] = np.eye(128, dtype=np.float32)
    blk = np.zeros((128, 128), np.float32)
    blk[:64, :64] = 1.0
    blk[64:, 64:] = 1.0
    c["blk"] = blk
    c["lowers"] = (q[:, :128] < k).astype(np.float32)
    return c


def prep_inputs(inp):
    L = DEPTH
    f = lambda a: np.asarray(a, dtype=np.float32)
    sh = {}
    wgu = np.empty((L, 2, 11, 128, 2, 8, 256), np.float32)
    wd = np.empty((L, 2, 8, 128, 22, 128), np.float32)
    for l in range(L):
        for i, nm in enumerate("ab"):
            g = f(inp["ffn_%s_gate" % nm][l]).reshape(8, 128, 11, 256).transpose(2, 1, 0, 3)
            u = f(inp["ffn_%s_up" % nm][l]).reshape(8, 128, 11, 256).transpose(2, 1, 0, 3)
            wgu[l, i, :, :, 0] = g
            wgu[l, i, :, :, 1] = u
            wd[l, i] = f(inp["ffn_%s_down" % nm][l]).reshape(22, 128, 8, 128).transpose(2, 1, 0, 3)
    sh["wgu"] = wgu.reshape(L * 2 * 11, 128, 4096)
    sh["wd"] = wd.reshape(L * 2 * 8, 128, 2816)
    wch = np.empty((L, NCHUNK_W, 128, 8, 128), np.float32)
    for l in range(L):
        W = f(inp["w_in"][l])
        ci = 0
        for base, grp in ((0, 32), (1152, 64)):
            for i in range(3):
                qc = base + 128 * i + np.arange(128)
                kc_ = base + 384 + 128 * i + np.arange(128)
                wch[l, ci + 0] = _chunk(W, qc)
                wch[l, ci + 1] = _chunk(W, _swap_idx(base + 128 * i, 128, grp))
                wch[l, ci + 2] = _chunk(W, kc_)
                wch[l, ci + 3] = _chunk(W, _swap_idx(base + 384 + 128 * i, 128, grp))
                ci += 4
        for c in range(8):
            wch[l, 24 + c] = _chunk(W, 2304 + 128 * c + np.arange(128))
        for i in range(3):
            wch[l, 32 + i] = _chunk(W, 768 + 128 * i + np.arange(128))
            wch[l, 35 + i] = _chunk(W, 1920 + 128 * i + np.arange(128))
        Wo = f(inp["w_out"][l])
        for dc in range(8):
            wch[l, 38 + dc] = _chunk(Wo, 128 * dc + np.arange(128))
    sh["wch"] = wch.reshape(L * NCHUNK_W, 128, 1024)
    sh["ropeA"] = _rope_table(32)
    sh["ropeB"] = _rope_table(64)
    for k_, v_ in _consts().items():
        sh["c_" + k_] = v_
    cols = []

    def addcol(v):
        cols.append(np.asarray(v, np.float32).reshape(128, 1))
        return len(cols) - 1

    idx = {}
    for l in range(L):
        for i in range(3):
            for c in range(8):
                idx[("lng", l, i, c)] = addcol(f(inp["ln_g"][l, i, c * 128:(c + 1) * 128]))
                idx[("lnb", l, i, c)] = addcol(f(inp["ln_b"][l, i, c * 128:(c + 1) * 128]))
        idx[("ga", l)] = addcol(np.tile(f(inp["a_norm_g"][l]), 2))
        idx[("gb", l)] = addcol(np.tile(f(inp["b_norm_g"][l]), 2))
        for c in range(8):
            idx[("mu", l, c)] = addcol(f(inp["c_mu"][l, c * 128:(c + 1) * 128]))
        for hp in range(2):
            sl = slice(hp * 128, hp * 128 + 128)
            idx[("w0", l, hp)] = addcol(f(inp["c_w0"][l, sl]))
            idx[("a0", l, hp)] = addcol(f(inp["c_a0"][l, sl]))
            idx[("kk", l, hp)] = addcol(f(inp["c_k_k"][l, sl]))
            idx[("ka", l, hp)] = addcol(f(inp["c_k_a"][l, sl]))
            idx[("rk", l, hp)] = addcol(f(inp["c_r_k"][l].reshape(256)[sl]))
            idx[("gng", l, hp)] = addcol(f(inp["c_gn_g"][l, sl]))
            idx[("gnb", l, hp)] = addcol(f(inp["c_gn_b"][l, sl]))
            if l > 0:
                idx[("v0", l, hp)] = addcol(f(inp["c_v0"][l - 1, sl]))
    sh["pp"] = np.ascontiguousarray(np.concatenate(cols, axis=1))
    lamrow = np.stack([np.stack([f(inp["a_lam_q1"][l]), f(inp["a_lam_k1"][l]),
                                 f(inp["a_lam_q2"][l]), f(inp["a_lam_k2"][l])]) for l in range(L)])
    sh["lamrow"] = np.ascontiguousarray(lamrow.reshape(1, L * 4 * 32))
    sm = np.zeros((L, 128, 256 + 256 + 64 + 256), np.float32)
    for l in range(L):
        sm[l, 0:64, 0:256] = f(inp["c_w2"][l])
        sm[l, 64:128, 0:256] = f(inp["c_a2"][l])
        sm[l, :, 256:512] = f(inp["c_g2"][l])
        if l > 0:
            sm[l, :, 512:576] = f(inp["c_v1"][l - 1]).reshape(2, 128, 32).transpose(1, 0, 2).reshape(128, 64)
            sm[l, 0:32, 576:832] = f(inp["c_v2"][l - 1])
    sh["csm"] = np.ascontiguousarray(sm.transpose(1, 0, 2).reshape(128, L * 832))
    xs = [np.ascontiguousarray(f(inp["x"][b]).T) for b in range(inp["x"].shape[0])]
    return sh, xs, idx


def build(shapes, idx, depth=DEPTH, dbg=None):
    nc = bass.Bass("TRN2", target_bir_lowering=False)
    dr = {}
    for k, shp in shapes.items():
        dr[k] = nc.dram_tensor(k, list(shp), F32, kind="ExternalInput").ap()
    xT_d = nc.dram_tensor("xT", [D, S], F32, kind="ExternalInput").ap()
    out_d = nc.dram_tensor("out", [D, S], F32, kind="ExternalOutput").ap()
    vf_d = nc.dram_tensor("vf_scratch", [256, S], F32, kind="Internal").ap()
    npp = shapes["pp"][1]

    P = Prog(nc)
    op, dma = P.op, P.dma
    XT = P.sb("XT", [128, 8, S])
    AR = P.sb("AR", [128, ARC])
    CST = P.sb("CST", [128, 1472])
    PP = P.sb("PP", [128, npp])
    CSM = P.sb("CSM", [128, 832])
    LAM = P.sb("LAM", [128, 64 + DEPTH * 128 + 64])
    banks = [P.ps("pb%d" % i, [128, 512]) for i in range(8)]
    PS = Rot([b[:] for b in banks[0:6]])
    PSL = Rot([b[:] for b in banks[6:8]])

    xt_t = [[TT() for _ in range(4)] for _ in range(8)]
    vf_t = [TT(), TT()]
    t_cst, t_pp, t_csm, t_lam, t_rst = TT(), TT(), TT(), TT(), TT()
    CAUS = CST[:, 0:128]
    BAND = CST[:, 128:384]
    UPP = CST[:, 384:640]
    IDENT = CST[:, 640:768]
    BLK = CST[:, 768:896]
    ONESD = CST[:, 896:1024]
    ONES64 = CST[:, 1024:1088]
    ONES = CST[:, 1088:1216]
    LOWS = CST[:, 1216:1344]
    BLK64 = CST[:, 1344:1472]
    dma("sp", "dma_start", out=CAUS, in_=dr["c_causal"], writes=[t_cst])
    dma("sp", "dma_start", out=BAND, in_=dr["c_band"], writes=[t_cst])
    dma("sp", "dma_start", out=UPP, in_=dr["c_uppers"], writes=[t_cst])
    dma("sp", "dma_start", out=IDENT, in_=dr["c_ident"], writes=[t_cst])
    dma("sp", "dma_start", out=BLK, in_=dr["c_blk"], writes=[t_cst])
    dma("sp", "dma_start", out=PP[:], in_=dr["pp"], writes=[t_pp])
    op("dve", "memset", ONESD, 1.0 / 1024.0, writes=[t_cst])
    op("dve", "memset", ONES64, 1.0 / 64.0, writes=[t_cst])
    op("dve", "memset", ONES, 1.0, writes=[t_cst])
    dma("sp", "dma_start", out=LOWS, in_=dr["c_lowers"], writes=[t_cst])
    op("act", "mul", out=BLK64, in_=BLK, mul=1.0 / 64.0, reads=[t_cst], writes=[t_cst])
    for c in range(8):
        dma("sp", "dma_start", out=XT[:, c, :], in_=xT_d[c * 128:(c + 1) * 128, :],
            writes=xt_t[c])
    ONE1 = LAM[:, 0:64]
    op("dve", "memset", LAM[:], 0.0, writes=[t_lam])
    op("dve", "memset", ONE1, 1.0, writes=[t_lam])
    LROW = LAM[0:1, 64:64 + DEPTH * 128]
    dma("sp", "dma_start", out=LROW, in_=dr["lamrow"], writes=[t_lam])
    NEGLAM = LAM[:, 64 + DEPTH * 128:64 + DEPTH * 128 + 8]
    LS = LAM[0:1, 64 + DEPTH * 128 + 8:64 + DEPTH * 128 + 64]
    for l in range(depth):
        b0 = 64 + l * 128
        lam_init = 0.8 - 0.6 * math.exp(-0.3 * l)
        for j in range(2):
            op("dve", "tensor_tensor", out=LS[:, 0:32], in0=LAM[0:1, b0 + 64 * j:b0 + 64 * j + 32],
                                                          in1=LAM[0:1, b0 + 64 * j + 32:b0 + 64 * j + 64], op=ALU.mult,
               reads=[t_lam], writes=[t_lam])
            op("dve", "reduce_sum", out=LS[:, 32 + j:33 + j], in_=LS[:, 0:32], axis=AX.X,
               reads=[t_lam], writes=[t_lam])
        op("act", "activation", out=LS[:, 34:36], in_=LS[:, 32:34], func=AF.Exp, reads=[t_lam], writes=[t_lam])
        op("dve", "scalar_tensor_tensor", out=LS[:, 36:37], in0=LS[:, 35:36], scalar=-lam_init, in1=LS[:, 34:35],
                                                                op0=ALU.add, op1=ALU.subtract, reads=[t_lam], writes=[t_lam])
        pb, pt = PS.next()
        op("pe", "matmul", pb[0:64, 0:1], lhsT=LAM[0:1, 0:64], rhs=LS[:, 36:37], start=True, stop=True,
           reads=[t_lam], writes=[pt])
        op("act", "copy", out=NEGLAM[0:64, l:l + 1], in_=pb[0:64, 0:1], reads=[pt], writes=[t_lam])

    def ppc(key):
        j = idx[key]
        return PP[:, j:j + 1]

    def tok(tp):
        return slice(tp * 512, (tp + 1) * 512)

    def ssl(st, n, step):
        return slice(st, st + step * (n - 1) + 1, step) if step > 1 else slice(st, st + n)

    def layer_norm(l, i, tp, sc, lnk):
        eps = LN_EPS / (ALPHA * ALPHA)
        pm, pmt = PS.next()
        pe2, pe2t = PS.next()
        for c in range(8):
            op("pe", "matmul", pm, lhsT=ONESD, rhs=XT[:, c, tok(tp)], start=(c == 0), stop=(c == 7),
               reads=[t_cst, xt_t[c][tp]], writes=[pmt])
        for c in range(8):
            sq, sqt = sc.next()
            op("act", "activation", out=sq, in_=XT[:, c, tok(tp)], func=AF.Square,
               reads=[xt_t[c][tp]], writes=[sqt])
            op("pe", "matmul", pe2, lhsT=ONESD, rhs=sq, start=(c == 0), stop=(c == 7),
               reads=[t_cst, sqt], writes=[pe2t])
        mean, meant = lnk.next()
        op("act", "copy", out=mean, in_=pm, reads=[pmt], writes=[meant])
        msq, msqt = lnk.next()
        op("dve", "tensor_tensor", out=msq, in0=mean, in1=mean, op=ALU.mult, reads=[meant], writes=[msqt])
        var, vart = lnk.next()
        op("dve", "scalar_tensor_tensor", out=var, in0=pe2, scalar=eps, in1=msq, op0=ALU.add, op1=ALU.subtract,
           reads=[pe2t, msqt], writes=[vart])
        op("act", "activation", out=var, in_=var, func=AF.Ln, reads=[vart], writes=[vart])
        op("act", "activation", out=var, in_=var, func=AF.Exp, scale=-0.5, reads=[vart], writes=[vart])
        for c in range(8):
            t1, t1t = sc.next()
            op("dve", "tensor_tensor", out=t1, in0=XT[:, c, tok(tp)], in1=mean, op=ALU.subtract,
               reads=[xt_t[c][tp], meant], writes=[t1t])
            op("pool", "tensor_tensor", out=t1, in0=t1, in1=var, op=ALU.mult,
               reads=[t1t, vart], writes=[t1t])
            op("act", "activation", out=XT[:, c, tok(tp)], in_=t1, func=AF.Identity,
                                                        bias=ppc(("lnb", l, i, c)), scale=ppc(("lng", l, i, c)),
               reads=[t1t, t_pp], writes=[xt_t[c][tp]])

    def ffn_phase(l, i):
        P.barrier()
        o = 0
        Hv = AR[:, o:o + 22 * 512].rearrange("p (f t) -> p f t", t=512)
        o += 22 * 512
        h_t = [TT() for _ in range(22)]
        wgu_r = Rot([AR[:, o + k * 4096:o + (k + 1) * 4096].rearrange("p (g kc f) -> p g kc f", g=2, kc=8) for k in range(2)])
        o += 2 * 4096
        wd_r = Rot([AR[:, o + k * 2816:o + (k + 1) * 2816].rearrange("p (f d) -> p f d", d=128) for k in range(2)])
        o += 2 * 2816
        sc = Rot([AR[:, o + k * 512:o + (k + 1) * 512] for k in range(8)])
        o += 8 * 512
        lnk = Rot([AR[:, o + k * 512:o + (k + 1) * 512] for k in range(3)])
        o += 3 * 512
        assert o <= ARC
        lnidx = 0 if i == 0 else 2
        for tp in range(4):
            for fg in range(11):
                wb, wbt = wgu_r.next()
                dma("sp", "dma_start",
                    out=wb, in_=dr["wgu"][(l * 2 + i) * 11 + fg].rearrange("p (g kc f) -> p g kc f", g=2, kc=8),
                    writes=[wbt])
                for fi in range(2):
                    f = fg * 2 + fi
                    pg, pgt = PS.next()
                    pu, put = PS.next()
                    for g, pp_, ppt in ((0, pg, pgt), (1, pu, put)):
                        for kc in range(8):
                            op("pe", "matmul",
                                pp_, lhsT=wb[:, g, kc, fi * 128:(fi + 1) * 128], rhs=XT[:, kc, tok(tp)],
                                start=(kc == 0), stop=(kc == 7),
                               reads=[wbt, xt_t[kc][tp]], writes=[ppt])
                    sg, sgt = sc.next()
                    op("act", "activation", out=sg, in_=pg, func=AF.Silu, reads=[pgt], writes=[sgt])
                    op("dve", "tensor_tensor", out=Hv[:, f, :], in0=sg, in1=pu, op=ALU.mult,
                       reads=[sgt, put], writes=[h_t[f]])
            for dc in range(8):
                wb, wbt = wd_r.next()
                dma("sp", "dma_start",
                    out=wb, in_=dr["wd"][(l * 2 + i) * 8 + dc].rearrange("p (f d) -> p f d", d=128), writes=[wbt])
                py, pyt = PS.next()
                for f in range(22):
                    op("pe", "matmul", py, lhsT=wb[:, f, :], rhs=Hv[:, f, :],
                                                                   start=(f == 0), stop=(f == 21),
                       reads=[wbt, h_t[f]], writes=[pyt])
                op("dve", "scalar_tensor_tensor",
                    out=XT[:, dc, tok(tp)], in0=py, scalar=0.5 / ALPHA, in1=XT[:, dc, tok(tp)], op0=ALU.mult, op1=ALU.add,
                   reads=[pyt, xt_t[dc][tp]], writes=[xt_t[dc][tp]])
            layer_norm(l, lnidx, tp, sc, lnk)


    def mixer_phase(l, stage):
        P.barrier()
        OT = AR[:, 0:16384].rearrange("p (c t) -> p c t", t=S)
        ot_t = [[TT() for _ in range(4)] for _ in range(8)]
        o = 16384
        rope_r = Rot([AR[:, o + k * 1024:o + (k + 1) * 1024].rearrange("p (c t) -> p c t", c=2) for k in range(1)])
        o += 1024
        KZ = AR[:, o:o + 2048]
        kz_t = TT()
        o += 2048
        QT = AR[:, o:o + 2048]
        KT = AR[:, o + 2048:o + 4096]
        qt_t = [TT() for _ in range(4)]
        kt_t = [TT() for _ in range(4)]
        o += 4096
        VA = AR[:, o:o + 2080].rearrange("p (j h d) -> p j h d", h=2, d=65)
        va_t = TT()
        o += 2080
        w_r = Rot([AR[:, o + k * 1024:o + (k + 1) * 1024].rearrange("p (kc f) -> p kc f", kc=8) for k in range(2)])
        o += 2048
        EB = o
        o += 1536
        SB_ = o
        o += 2048
        assert o <= ARC, o
        e512 = Rot([AR[:, EB + k * 512:EB + (k + 1) * 512] for k in range(3)])
        scr = Rot([AR[:, SB_ + k * 512:SB_ + (k + 1) * 512] for k in range(4)])
        lam_init = 0.8 - 0.6 * math.exp(-0.3 * l)

        def load_w(ci):
            w, wt = w_r.next()
            dma("sp", "dma_start", out=w, in_=dr["wch"][l * NCHUNK_W + ci].rearrange("p (kc f) -> p kc f", kc=8), writes=[wt])
            return w, wt

        def proj_rope(ci, dst, dst_t, rname):
            w1, w1t = load_w(ci)
            w2, w2t = load_w(ci + 1)
            for tp in range(4):
                rp, rpt = rope_r.next()
                dma("sp", "dma_start", out=rp, in_=dr[rname][:, tp * 1024:(tp + 1) * 1024].rearrange("p (c t) -> p c t", c=2),
                    writes=[rpt])
                p1, p1t = PS.next()
                p2, p2t = PS.next()
                for w, wt, pp_, ppt in ((w1, w1t, p1, p1t), (w2, w2t, p2, p2t)):
                    for kc in range(8):
                        op("pe", "matmul", pp_, lhsT=w[:, kc, :], rhs=XT[:, kc, tok(tp)], start=(kc == 0), stop=(kc == 7),
                           reads=[wt, xt_t[kc][tp]], writes=[ppt])
                a, at = scr.next()
                b, bt = scr.next()
                op("dve", "tensor_tensor", out=a, in0=p1, in1=rp[:, 0, :], op=ALU.mult, reads=[p1t, rpt], writes=[at])
                op("dve", "tensor_tensor", out=b, in0=p2, in1=rp[:, 1, :], op=ALU.mult, reads=[p2t, rpt], writes=[bt])
                op("pool", "tensor_tensor", out=dst[:, tok(tp)], in0=a, in1=b, op=ALU.add, reads=[at, bt], writes=[dst_t[tp]])

        def proj_v(ci, dil):
            wv, wvt = load_w(ci)
            op("pool", "memset", VA[:, :, :, 64:65], 1.0, writes=[va_t])
            nt = 16 // dil
            for r in range(dil):
                for a in range(nt):
                    st = r + dil * 128 * a
                    pv, pvt = PS.next()
                    for kc in range(8):
                        lhs = XT[:, kc, ssl(st, 128, dil)]
                        op("pe", "matmul", pv[:, 0:128], lhsT=lhs, rhs=wv[:, kc, :], start=(kc == 0), stop=(kc == 7),
                           reads=[wvt] + xt_t[kc], writes=[pvt])
                    op("act", "copy", out=VA[:, r * nt + a, :, 0:64], in_=pv[:, 0:128].rearrange("p (h d) -> p h d", h=2),
                       reads=[pvt], writes=[va_t])

        def phase_a(i):
            proj_rope(4 * i, QT, qt_t, "ropeA")
            proj_rope(4 * i + 2, KT, kt_t, "ropeA")
            proj_v(32 + i, 1)
            op("pool", "tensor_copy", out=KZ[64:128, :], in_=KT[64:128, :], reads=kt_t, writes=[kz_t])
            op("pool", "memset", KZ[64:96, :], 0.0, writes=[kz_t])
            sca = 32 ** -0.5
            for hl in range(2):
                for Qp in range(4):
                    nums = []
                    for m in range(2):
                        r0 = 32 * (2 * hl + m)
                        num, numt = PSL.next()
                        nums.append((num, numt))
                        last = 4 * Qp + 3
                        for j in range(last + 1):
                            c0 = max(128 * j, 512 * Qp)
                            n = 512 * Qp + 512 - c0
                            se, set_ = PS.next()
                            if hl == 1 and m == 1:
                                op("pe", "matmul", se[:, 0:n], lhsT=KZ[64:128, 128 * j:128 * j + 128], rhs=QT[64:128, c0:c0 + n],
                                   start=True, stop=True, reads=[kz_t] + qt_t[c0 // 512:Qp + 1], writes=[set_])
                            else:
                                op("pe", "matmul", se[:, 0:n], lhsT=KT[r0:r0 + 32, 128 * j:128 * j + 128], rhs=QT[r0:r0 + 32, c0:c0 + n],
                                   start=True, stop=True, reads=[kt_t[j // 4]] + qt_t[c0 // 512:Qp + 1], writes=[set_])
                            E, Et = e512.next()
                            op("act", "activation", out=E[:, 0:n], in_=se[:, 0:n], func=AF.Exp, scale=sca, reads=[set_], writes=[Et])
                            if 128 * j >= 512 * Qp:
                                op("pool", "tensor_tensor", out=E[:, 0:128], in0=E[:, 0:128], in1=CAUS, op=ALU.mult,
                                   reads=[Et, t_cst], writes=[Et])
                            op("pe", "matmul", num[0:65, c0 - 512 * Qp:512], lhsT=VA[:, j, hl, :], rhs=E[:, 0:n],
                               start=(j == 0), stop=(j == last), reads=[va_t, Et], writes=[numt])
                    (n1, n1t), (n2, n2t) = nums
                    X0, X0t = scr.next()
                    X1, X1t = scr.next()
                    X2, X2t = scr.next()
                    for (nn, nnt, prt) in ((n1, n1t, 64), (n2, n2t, 32)):
                        op("act", "activation", out=X0[prt:prt + 1, :], in_=nn[64:65, :], func=AF.Ln, reads=[nnt], writes=[X0t])
                        op("act", "activation", out=X0[prt:prt + 1, :], in_=X0[prt:prt + 1, :], func=AF.Exp, scale=-1.0,
                           reads=[X0t], writes=[X0t])
                    for (prt, Xd, Xdt) in ((64, X1, X1t), (32, X2, X2t)):
                        pb, pbt = PS.next()
                        op("pe", "matmul", pb[0:64, :], lhsT=ONE1[prt:prt + 1, 0:64], rhs=X0[prt:prt + 1, :], start=True, stop=True,
                           reads=[t_lam, X0t], writes=[pbt])
                        op("act", "copy", out=Xd[0:64, :], in_=pb[0:64, :], reads=[pbt], writes=[Xdt])
                    op("dve", "tensor_tensor", out=X1[0:64, :], in0=n1[0:64, :], in1=X1[0:64, :], op=ALU.mult, reads=[n1t, X1t], writes=[X1t])
                    op("dve", "tensor_tensor", out=X2[0:64, :], in0=n2[0:64, :], in1=X2[0:64, :], op=ALU.mult, reads=[n2t, X2t], writes=[X2t])
                    op("dve", "scalar_tensor_tensor", out=X1[0:64, :], in0=X2[0:64, :], scalar=NEGLAM[0:64, l:l + 1], in1=X1[0:64, :],
                       op0=ALU.mult, op1=ALU.add, reads=[X1t, X2t, t_lam], writes=[X1t])
                    op("act", "activation", out=X2[0:64, :], in_=X1[0:64, :], func=AF.Square, reads=[X1t], writes=[X2t])
                    pm, pmt = PS.next()
                    op("pe", "matmul", pm[0:64, :], lhsT=ONES64[0:64, 0:64], rhs=X2[0:64, :], start=True, stop=True,
                       reads=[t_cst, X2t], writes=[pmt])
                    op("dve", "tensor_scalar", out=X2[0:64, :], in0=pm[0:64, :], scalar1=RMS_EPS, scalar2=None, op0=ALU.add,
                       reads=[pmt], writes=[X2t])
                    op("act", "activation", out=X2[0:64, :], in_=X2[0:64, :], func=AF.Ln, reads=[X2t], writes=[X2t])
                    op("act", "activation", out=X2[0:64, :], in_=X2[0:64, :], func=AF.Exp, scale=-0.5, reads=[X2t], writes=[X2t])
                    op("dve", "scalar_tensor_tensor", out=X1[0:64, :], in0=X1[0:64, :], scalar=ppc(("ga", l))[0:64, :], in1=X2[0:64, :],
                       op0=ALU.mult, op1=ALU.mult, reads=[X1t, X2t, t_pp], writes=[X1t])
                    op("dve", "tensor_scalar", out=OT[64 * hl:64 * hl + 64, i, tok(Qp)], in0=X1[0:64, :], scalar1=1.0 - lam_init,
                       scalar2=None, op0=ALU.mult, reads=[X1t], writes=[ot_t[i][Qp]])


        def phase_b(i):
            P.barrier()
            RS = KZ
            rs_t = [TT(), TT()]
            e256 = Rot([AR[:, EB + k * 256:EB + (k + 1) * 256] for k in range(6)])
            proj_rope(12 + 4 * i, QT, qt_t, "ropeB")
            proj_rope(12 + 4 * i + 2, KT, kt_t, "ropeB")
            scb = 64 ** -0.5
            oc = 3 + i
            for dil in (1, 4, 16):
                proj_v(35 + i, dil)
                L = S // dil
                nt = L // 128
                for hl in range(2):
                    rows = slice(64 * hl, 64 * hl + 64)
                    rowp = 64 if hl == 0 else 32
                    for r in range(dil):
                        Es = {}

                        def get_E(a):
                            if a in Es:
                                return Es[a]
                            nq = min(256, L - 128 * a)
                            st = r + dil * 128 * a
                            se, set_ = PS.next()
                            op("pe", "matmul", se[:, 0:nq], lhsT=KT[rows, ssl(st, 128, dil)], rhs=QT[rows, ssl(st, nq, dil)],
                               start=True, stop=True, reads=kt_t + qt_t, writes=[set_])
                            E, Et = e256.next()
                            op("act", "activation", out=E[:, 0:nq], in_=se[:, 0:nq], func=AF.Exp, scale=scb, reads=[set_], writes=[Et])
                            op("pool", "tensor_tensor", out=E[:, 0:nq], in0=E[:, 0:nq], in1=BAND[:, 0:nq], op=ALU.mult,
                               reads=[Et, t_cst], writes=[Et])
                            Es[a] = (E, Et, nq)
                            return Es[a]

                        for g in range((nt + 3) // 4):
                            ncols = min(512, L - 512 * g)
                            alist = list(range(max(4 * g - 1, 0), min(4 * g + 3, nt - 1) + 1))
                            for a in alist:
                                get_E(a)
                            num, numt = PSL.next()
                            for ai, a in enumerate(alist):
                                E, Et, nq = Es[a]
                                lo = max(128 * a, 512 * g)
                                hi = min(128 * a + nq, 512 * g + ncols)
                                op("pe", "matmul", num[0:65, lo - 512 * g:hi - 512 * g], lhsT=VA[:, r * nt + a, hl, :],
                                   rhs=E[:, lo - 128 * a:hi - 128 * a], start=(ai == 0), stop=(ai == len(alist) - 1),
                                   reads=[va_t, Et], writes=[numt])
                            p0 = r + dil * 512 * g
                            ps_ = ssl(p0, ncols, dil)
                            if dil == 1:
                                op("dve", "tensor_copy", out=OT[rows, oc, ps_], in_=num[0:64, 0:ncols], reads=[numt], writes=ot_t[oc])
                                op("act", "copy", out=RS[rowp:rowp + 1, ps_], in_=num[64:65, 0:ncols], reads=[numt], writes=[rs_t[hl]])
                            else:
                                op("dve", "tensor_tensor", out=OT[rows, oc, ps_], in0=num[0:64, 0:ncols], in1=OT[rows, oc, ps_], op=ALU.add,
                                   reads=[numt] + ot_t[oc], writes=ot_t[oc])
                                op("dve", "tensor_tensor", out=RS[rowp:rowp + 1, ps_], in0=num[64:65, 0:ncols], in1=RS[rowp:rowp + 1, ps_],
                                   op=ALU.add, reads=[numt, rs_t[hl]], writes=[rs_t[hl]])
            P.barrier()
            pst = Rot([AR[:, EB + k * 512:EB + (k + 1) * 512] for k in range(3)])
            for hl in range(2):
                rows = slice(64 * hl, 64 * hl + 64)
                rowp = 64 if hl == 0 else 32
                for tp in range(4):
                    T0, T0t = pst.next()
                    T1, T1t = pst.next()
                    T2, T2t = pst.next()
                    op("act", "activation", out=T0[rows, :], in_=OT[rows, oc, tok(tp)], func=AF.Square, reads=[ot_t[oc][tp]], writes=[T0t])
                    op("act", "activation", out=T1[rowp:rowp + 1, :], in_=RS[rowp:rowp + 1, tok(tp)], func=AF.Square,
                       scale=math.sqrt(RMS_EPS), reads=[rs_t[hl]], writes=[T1t])
                    pm, pmt = PS.next()
                    op("pe", "matmul", pm[0:64, :], lhsT=ONES64[rows, 0:64], rhs=T0[rows, :], start=True, stop=False,
                       reads=[t_cst, T0t], writes=[pmt])
                    op("pe", "matmul", pm[0:64, :], lhsT=ONE1[rowp:rowp + 1, 0:64], rhs=T1[rowp:rowp + 1, :], start=False, stop=True,
                       reads=[t_lam, T1t], writes=[pmt])
                    op("act", "activation", out=T2[rows, :], in_=pm[0:64, :], func=AF.Ln, reads=[pmt], writes=[T2t])
                    op("act", "activation", out=T2[rows, :], in_=T2[rows, :], func=AF.Exp, scale=-0.5, reads=[T2t], writes=[T2t])
                    op("dve", "scalar_tensor_tensor", out=OT[rows, oc, tok(tp)], in0=OT[rows, oc, tok(tp)], scalar=ppc(("gb", l))[rows, :],
                       in1=T2[rows, :], op0=ALU.mult, op1=ALU.mult, reads=[ot_t[oc][tp], T2t, t_pp], writes=[ot_t[oc][tp]])
            P.barrier()


        def phase_c(hp):
            P.barrier()
            oc = 6 + hp
            base = 16384
            F = [AR[:, base + k * 2048:base + (k + 1) * 2048] for k in range(6)]
            ft = [[TT() for _ in range(4)] for _ in range(6)]
            sm = base + 6 * 2048
            w1 = Rot([AR[:, sm:sm + 1024].rearrange("p (kc f) -> p kc f", kc=8)])
            RAWP = AR[:, sm + 1024:sm + 1024 + 516]
            rawp_t = TT()
            TA = AR[:, sm + 1540:sm + 2052]
            TB = AR[:, sm + 2052:sm + 2564]
            ta_t, tb_t = TT(), TT()
            GC = AR[:, sm + 2564:sm + 2580]
            gc_t = TT()
            assert sm + 2580 <= ARC
            dma("sp", "dma_start", out=CSM[:], in_=dr["csm"][:, l * 832:(l + 1) * 832], writes=[t_csm])

            def loadw1(ci):
                w, wt = w1.next()
                dma("sp", "dma_start", out=w, in_=dr["wch"][l * NCHUNK_W + ci].rearrange("p (kc f) -> p kc f", kc=8), writes=[wt])
                return w, wt

            def proj_lerp(cidx, dst, dst_t):
                w, wt = loadw1(24 + cidx)
                op("pool", "memset", RAWP[:, 0:1], 0.0, writes=[rawp_t])
                mu = ppc(("mu", l, cidx))
                for tp in range(4):
                    pp_, ppt = PS.next()
                    for kc in range(8):
                        op("pe", "matmul", pp_, lhsT=w[:, kc, :], rhs=XT[:, kc, tok(tp)], start=(kc == 0), stop=(kc == 7),
                           reads=[wt, xt_t[kc][tp]], writes=[ppt])
                    op("act", "copy", out=RAWP[:, 1:513], in_=pp_, reads=[ppt], writes=[rawp_t])
                    op("dve", "tensor_tensor", out=TA, in0=RAWP[:, 0:512], in1=RAWP[:, 1:513], op=ALU.subtract,
                       reads=[rawp_t], writes=[ta_t])
                    op("dve", "scalar_tensor_tensor", out=dst[:, tok(tp)], in0=TA, scalar=mu, in1=RAWP[:, 1:513],
                       op0=ALU.mult, op1=ALU.add, reads=[ta_t, rawp_t, t_pp], writes=[dst_t[tp]])
                    op("act", "copy", out=RAWP[:, 0:1], in_=RAWP[:, 512:513], reads=[rawp_t], writes=[rawp_t])

            Kt, Bt, KKt, Rt, Vt = F[0], F[2], F[3], F[4], F[5]
            LW = F[1]
            proj_lerp(4 + hp, F[5], ft[5])
            if l == 0:
                dma("sp", "dma_start", out=vf_d[hp * 128:(hp + 1) * 128, :], in_=F[5], reads=ft[5], writes=[vf_t[hp]])
            else:
                proj_lerp(4 + (1 - hp), F[4], ft[4])
                vch = {hp: (F[5], ft[5]), 1 - hp: (F[4], ft[4])}
                for tp in range(4):
                    p32, p32t = PS.next()
                    for kc in range(2):
                        vv, vvt = vch[kc]
                        op("pe", "matmul", p32[0:32, :], lhsT=CSM[:, 512 + kc * 32:512 + kc * 32 + 32], rhs=vv[:, tok(tp)],
                           start=(kc == 0), stop=(kc == 1), reads=[t_csm, vvt[tp]], writes=[p32t])
                    op("act", "copy", out=TB[0:32, :], in_=p32[0:32, :], reads=[p32t], writes=[tb_t])
                    pg, pgt = PS.next()
                    op("pe", "matmul", pg, lhsT=CSM[0:32, 576 + hp * 128:576 + hp * 128 + 128], rhs=TB[0:32, :], start=True, stop=True,
                       reads=[t_csm, tb_t], writes=[pgt])
                    op("act", "activation", out=TB, in_=pg, func=AF.Sigmoid, bias=ppc(("v0", l, hp)), scale=1.0,
                       reads=[pgt, t_pp], writes=[tb_t])
                    dma("sp", "dma_start", out=TA, in_=vf_d[hp * 128:(hp + 1) * 128, tok(tp)], reads=[vf_t[hp]], writes=[ta_t])
                    op("dve", "tensor_tensor", out=TA, in0=TA, in1=F[5][:, tok(tp)], op=ALU.subtract, reads=[ta_t, ft[5][tp]], writes=[ta_t])
                    op("dve", "tensor_tensor", out=TA, in0=TA, in1=TB, op=ALU.mult, reads=[ta_t, tb_t], writes=[ta_t])
                    op("dve", "tensor_tensor", out=F[5][:, tok(tp)], in0=F[5][:, tok(tp)], in1=TA, op=ALU.add,
                       reads=[ta_t, ft[5][tp]], writes=[ft[5][tp]])
            proj_lerp(6, F[0], ft[0])
            for tp in range(4):
                op("act", "activation", out=F[0][0:64, tok(tp)], in_=F[0][0:64, tok(tp)], func=AF.Tanh, reads=[ft[0][tp]], writes=[ft[0][tp]])
                pw, pwt = PS.next()
                op("pe", "matmul", pw, lhsT=CSM[0:64, hp * 128:hp * 128 + 128], rhs=F[0][0:64, tok(tp)], start=True, stop=True,
                   reads=[t_csm, ft[0][tp]], writes=[pwt])
                op("act", "activation", out=F[1][:, tok(tp)], in_=pw, func=AF.Sigmoid, bias=ppc(("w0", l, hp)), scale=1.0,
                   reads=[pwt, t_pp], writes=[ft[1][tp]])
                op("pool", "tensor_scalar", out=F[1][:, tok(tp)], in0=F[1][:, tok(tp)], scalar1=-math.exp(-0.5), scalar2=None, op0=ALU.mult,
                   reads=[ft[1][tp]], writes=[ft[1][tp]])
                pa, pat = PS.next()
                op("pe", "matmul", pa, lhsT=CSM[64:128, hp * 128:hp * 128 + 128], rhs=F[0][64:128, tok(tp)], start=True, stop=True,
                   reads=[t_csm, ft[0][tp]], writes=[pat])
                op("act", "activation", out=F[2][:, tok(tp)], in_=pa, func=AF.Sigmoid, bias=ppc(("a0", l, hp)), scale=1.0,
                   reads=[pat, t_pp], writes=[ft[2][tp]])
            proj_lerp(2 + hp, F[0], ft[0])
            for tp in range(4):
                tk = tok(tp)
                op("dve", "tensor_scalar", out=F[3][:, tk], in0=F[0][:, tk], scalar1=ppc(("kk", l, hp)), scalar2=None, op0=ALU.mult,
                   reads=[ft[0][tp], t_pp], writes=[ft[3][tp]])
                op("act", "activation", out=TA, in_=F[3][:, tk], func=AF.Square, reads=[ft[3][tp]], writes=[ta_t])
                pq, pqt = PS.next()
                op("pe", "matmul", pq, lhsT=BLK, rhs=TA, start=True, stop=True, reads=[t_cst, ta_t], writes=[pqt])
                op("act", "activation", out=TB, in_=pq, func=AF.Sqrt, reads=[pqt], writes=[tb_t])
                op("dve", "tensor_scalar", out=TB, in0=TB, scalar1=1e-12, scalar2=None, op0=ALU.max, reads=[tb_t], writes=[tb_t])
                op("dve", "reciprocal", out=TB, in_=TB, reads=[tb_t], writes=[tb_t])
                op("dve", "tensor_tensor", out=F[3][:, tk], in0=F[3][:, tk], in1=TB, op=ALU.mult, reads=[ft[3][tp], tb_t], writes=[ft[3][tp]])
                op("dve", "tensor_scalar", out=TA, in0=F[2][:, tk], scalar1=-1.0, scalar2=ppc(("ka", l, hp)), op0=ALU.add, op1=ALU.mult,
                   reads=[ft[2][tp], t_pp], writes=[ta_t])
                op("dve", "scalar_tensor_tensor", out=F[0][:, tk], in0=TA, scalar=1.0, in1=F[0][:, tk], op0=ALU.add, op1=ALU.mult,
                   reads=[ta_t, ft[0][tp]], writes=[ft[0][tp]])
                op("pool", "tensor_tensor", out=F[2][:, tk], in0=F[2][:, tk], in1=F[3][:, tk], op=ALU.mult,
                   reads=[ft[2][tp], ft[3][tp]], writes=[ft[2][tp]])
            proj_lerp(hp, F[4], ft[4])
            for tp in range(4):
                tk = tok(tp)
                op("dve", "scalar_tensor_tensor", out=TA, in0=F[4][:, tk], scalar=ppc(("rk", l, hp)), in1=F[0][:, tk], op0=ALU.mult, op1=ALU.mult,
                   reads=[ft[4][tp], ft[0][tp], t_pp], writes=[ta_t])
                pq, pqt = PS.next()
                op("pe", "matmul", pq, lhsT=BLK, rhs=TA, start=True, stop=True, reads=[t_cst, ta_t], writes=[pqt])
                op("dve", "tensor_tensor", out=OT[:, oc, tk], in0=pq, in1=F[5][:, tk], op=ALU.mult, reads=[pqt, ft[5][tp]], writes=[ot_t[oc][tp]])
            for tp in range(4):
                tk = tok(tp)
                for cc in range(4):
                    cs = slice(tp * 512 + cc * 128, tp * 512 + cc * 128 + 128)
                    op("dve", "tensor_tensor_scan", out=TA[:, cc * 128:(cc + 1) * 128], data0=ONES, data1=F[1][:, cs], initial=0.0,
                       op0=ALU.mult, op1=ALU.add, reads=[t_cst, ft[1][tp]], writes=[ta_t])
                op("dve", "tensor_tensor", out=TB, in0=TA, in1=F[1][:, tk], op=ALU.subtract, reads=[ta_t, ft[1][tp]], writes=[tb_t])
                op("act", "activation", out=TB, in_=TB, func=AF.Exp, reads=[tb_t], writes=[tb_t])
                op("dve", "tensor_tensor", out=F[3][:, tk], in0=F[3][:, tk], in1=TB, op=ALU.mult, reads=[ft[3][tp], tb_t], writes=[ft[3][tp]])
                op("act", "activation", out=TB, in_=TA, func=AF.Exp, reads=[ta_t], writes=[tb_t])
                op("dve", "tensor_tensor", out=F[4][:, tk], in0=F[4][:, tk], in1=TB, op=ALU.mult, reads=[ft[4][tp], tb_t], writes=[ft[4][tp]])
                op("act", "copy", out=GC[:, tp * 4:tp * 4 + 4], in_=TB[:, 127:512:128], reads=[tb_t], writes=[gc_t])
                op("act", "activation", out=TB, in_=TA, func=AF.Exp, scale=-1.0, reads=[ta_t], writes=[tb_t])
                op("dve", "tensor_tensor", out=F[0][:, tk], in0=F[0][:, tk], in1=TB, op=ALU.mult, reads=[ft[0][tp], tb_t], writes=[ft[0][tp]])
                op("pool", "tensor_tensor", out=F[2][:, tk], in0=F[2][:, tk], in1=TB, op=ALU.mult, reads=[ft[2][tp], tb_t], writes=[ft[2][tp]])
            P.barrier()
            def cut(region, n, width):
                out = []
                for _ in range(n):
                    out.append(AR[:, region[0]:region[0] + width])
                    region[0] += width
                return out
            reg1 = [base + 2048]
            reg2 = [sm]
            KhF, BhF = cut(reg1, 2, 128)
            KhT, BhT, VT = cut(reg1, 3, 128)
            MB = cut(reg1, 2, 256)
            MK = cut(reg1, 2, 256)
            assert reg1[0] <= base + 4096
            inv = [cut(reg2, 6, 128) for _ in range(2)]
            Wsb = cut(reg2, 2, 64)
            Usb = cut(reg2, 2, 64)
            ST = cut(reg2, 1, 64)[0]
            YC = cut(reg2, 1, 128)[0]
            G1 = cut(reg2, 3, 128)
            assert reg2[0] <= sm + 2564, reg2[0]
            t_kh, t_bh, t_kht, t_bht, t_vt = TT(), TT(), TT(), TT(), TT()
            t_mb, t_mk = [TT(), TT()], [TT(), TT()]
            t_inv = [[TT() for _ in range(6)] for _ in range(2)]
            t_w, t_u = [TT(), TT()], [TT(), TT()]
            t_st = [TT(), TT()]
            t_yc = TT()
            t_g1 = [TT() for _ in range(3)]
            fall = [sum(ft[k], []) if False else ft[k] for k in range(6)]
            op("pool", "memset", ST, 0.0, writes=t_st)
            for c in range(16):
                cs = slice(c * 128, c * 128 + 128)
                tp = c // 4
                op("dve", "tensor_scalar", out=KhF, in0=F[0][:, cs], scalar1=GC[:, c:c + 1], scalar2=None, op0=ALU.mult,
                   reads=[ft[0][tp], gc_t], writes=[t_kh])
                op("pool", "tensor_scalar", out=BhF, in0=F[2][:, cs], scalar1=GC[:, c:c + 1], scalar2=None, op0=ALU.mult,
                   reads=[ft[2][tp], gc_t], writes=[t_bh])
                for (src, srct, dstT, dstt) in ((KhF, [t_kh], KhT, t_kht), (BhF, [t_bh], BhT, t_bht), (F[5][:, cs], [ft[5][tp]], VT, t_vt)):
                    ptr, ptrt = PS.next()
                    op("pe", "transpose", ptr[:, 0:128], src, IDENT, reads=srct + [t_cst], writes=[ptrt])
                    op("act", "copy", out=dstT, in_=ptr[:, 0:128], reads=[ptrt], writes=[dstt])
                for hl in range(2):
                    rows = slice(64 * hl, 64 * hl + 64)
                    hc = slice(64 * hl, 64 * hl + 64)
                    for (lt, ltt, Mx, Mxt) in ((F[2], ft[2][tp], MB[hl], t_mb[hl]), (F[0], ft[0][tp], MK[hl], t_mk[hl])):
                        pmx, pmxt = PS.next()
                        op("pe", "matmul", pmx[:, 0:128], lhsT=lt[rows, cs], rhs=F[3][rows, cs], start=True, stop=True,
                           reads=[ltt, ft[3][tp]], writes=[pmxt])
                        op("pe", "matmul", pmx[:, 128:256], lhsT=lt[rows, cs], rhs=F[4][rows, cs], start=True, stop=True,
                           reads=[ltt, ft[4][tp]], writes=[pmxt])
                        op("dve", "tensor_tensor", out=Mx, in0=pmx[:, 0:256], in1=UPP, op=ALU.mult, reads=[pmxt, t_cst], writes=[Mxt])
                    Pm, PTm, Zm, Pm2, PTm2, Zm2 = inv[hl]
                    tPm, tPTm, tZm, tPm2, tPTm2, tZm2 = t_inv[hl]
                    op("act", "mul", out=Pm, in_=MB[hl][:, 0:128], mul=-1.0, reads=[t_mb[hl]], writes=[tPm])
                    pnt, pntt = PS.next()
                    op("pe", "matmul", pnt[:, 0:128], lhsT=F[3][rows, cs], rhs=F[2][rows, cs], start=True, stop=True,
                       reads=[ft[3][tp], ft[2][tp]], writes=[pntt])
                    op("dve", "scalar_tensor_tensor", out=PTm, in0=pnt[:, 0:128], scalar=-1.0, in1=LOWS, op0=ALU.mult, op1=ALU.mult,
                       reads=[pntt, t_cst], writes=[tPTm])
                    op("dve", "tensor_tensor", out=Zm, in0=Pm, in1=IDENT, op=ALU.add, reads=[tPm, t_cst], writes=[tZm])
                    cur = (Pm, tPm, PTm, tPTm, Zm, tZm)
                    nxt = (Pm2, tPm2, PTm2, tPTm2, Zm2, tZm2)
                    for stg in range(6):
                        cP, ctP, cPT, ctPT, cZ, ctZ = cur
                        nP, ntP, nPT, ntPT, nZ, ntZ = nxt
                        if stg < 5:
                            pp2, pp2t = PS.next()
                            op("pe", "matmul", pp2[:, 0:128], lhsT=cPT, rhs=cP, start=True, stop=True, reads=[ctP, ctPT], writes=[pp2t])
                            op("act", "copy", out=nP, in_=pp2[:, 0:128], reads=[pp2t], writes=[ntP])
                        ppt2, ppt2t = PS.next()
                        op("pe", "matmul", ppt2[:, 0:128], lhsT=cP, rhs=cPT, start=True, stop=True, reads=[ctP, ctPT], writes=[ppt2t])
                        op("act", "copy", out=nPT, in_=ppt2[:, 0:128], reads=[ppt2t], writes=[ntPT])
                        pz, pzt = PS.next()
                        op("pe", "matmul", pz[:, 0:128], lhsT=nPT, rhs=cZ, start=True, stop=True, reads=[ntPT, ctZ], writes=[pzt])
                        op("dve", "tensor_tensor", out=nZ, in0=pz[:, 0:128], in1=cZ, op=ALU.add, reads=[pzt, ctZ], writes=[ntZ])
                        cur, nxt = nxt, cur
                    TTm, tTT = cur[4], cur[5]
                    pw, pwt = PS.next()
                    op("pe", "matmul", pw[:, 0:64], lhsT=F[3][rows, cs], rhs=ST[rows, :], start=True, stop=False,
                       reads=[ft[3][tp], t_st[hl]], writes=[pwt])
                    op("pe", "matmul", pw[:, 0:64], lhsT=MK[hl][:, 0:128], rhs=VT[:, hc], start=False, stop=True,
                       reads=[t_mk[hl], t_vt], writes=[pwt])
                    op("act", "copy", out=Wsb[hl], in_=pw[:, 0:64], reads=[pwt], writes=[t_w[hl]])
                    pu, put = PS.next()
                    op("pe", "matmul", pu[:, 0:64], lhsT=TTm, rhs=Wsb[hl], start=True, stop=True, reads=[tTT, t_w[hl]], writes=[put])
                    op("act", "mul", out=Usb[hl], in_=pu[:, 0:64], mul=-1.0, reads=[put], writes=[t_u[hl]])
                    py, pyt = PS.next()
                    op("pe", "matmul", py[0:64, 0:128], lhsT=ST[rows, :], rhs=F[4][rows, cs], start=True, stop=False,
                       reads=[t_st[hl], ft[4][tp]], writes=[pyt])
                    op("pe", "matmul", py[0:64, 0:128], lhsT=Usb[hl], rhs=MB[hl][:, 128:256], start=False, stop=False,
                       reads=[t_u[hl], t_mb[hl]], writes=[pyt])
                    op("pe", "matmul", py[0:64, 0:128], lhsT=VT[:, hc], rhs=MK[hl][:, 128:256], start=False, stop=True,
                       reads=[t_vt, t_mk[hl]], writes=[pyt])
                    op("act", "copy", out=YC[rows, :], in_=py[0:64, 0:128], reads=[pyt], writes=[t_yc])
                    pn, pnt_ = PS.next()
                    op("pe", "matmul", pn[0:64, 0:64], lhsT=BhT[:, hc], rhs=Usb[hl], start=True, stop=False,
                       reads=[t_bht, t_u[hl]], writes=[pnt_])
                    op("pe", "matmul", pn[0:64, 0:64], lhsT=KhT[:, hc], rhs=VT[:, hc], start=False, stop=True,
                       reads=[t_kht, t_vt], writes=[pnt_])
                    op("dve", "scalar_tensor_tensor", out=ST[rows, :], in0=ST[rows, :], scalar=GC[rows, c:c + 1], in1=pn[0:64, 0:64],
                       op0=ALU.mult, op1=ALU.add, reads=[t_st[hl], gc_t, pnt_], writes=[t_st[hl]])
                pmu, pmut = PS.next()
                op("pe", "matmul", pmu[:, 0:128], lhsT=BLK64, rhs=YC, start=True, stop=True, reads=[t_cst, t_yc], writes=[pmut])
                op("act", "activation", out=G1[0], in_=YC, func=AF.Square, reads=[t_yc], writes=[t_g1[0]])
                pe2, pe2t = PS.next()
                op("pe", "matmul", pe2[:, 0:128], lhsT=BLK64, rhs=G1[0], start=True, stop=True, reads=[t_cst, t_g1[0]], writes=[pe2t])
                op("act", "copy", out=G1[1], in_=pmu[:, 0:128], reads=[pmut], writes=[t_g1[1]])
                op("dve", "tensor_tensor", out=G1[0], in0=G1[1], in1=G1[1], op=ALU.mult, reads=[t_g1[1]], writes=[t_g1[0]])
                op("dve", "scalar_tensor_tensor", out=G1[2], in0=pe2[:, 0:128], scalar=GN_EPS, in1=G1[0], op0=ALU.add, op1=ALU.subtract,
                   reads=[pe2t, t_g1[0]], writes=[t_g1[2]])
                op("act", "activation", out=G1[2], in_=G1[2], func=AF.Ln, reads=[t_g1[2]], writes=[t_g1[2]])
                op("act", "activation", out=G1[2], in_=G1[2], func=AF.Exp, scale=-0.5, reads=[t_g1[2]], writes=[t_g1[2]])
                op("dve", "tensor_tensor", out=G1[0], in0=YC, in1=G1[1], op=ALU.subtract, reads=[t_yc, t_g1[1]], writes=[t_g1[0]])
                op("dve", "tensor_tensor", out=G1[0], in0=G1[0], in1=G1[2], op=ALU.mult, reads=[t_g1[0], t_g1[2]], writes=[t_g1[0]])
                op("act", "activation", out=G1[0], in_=G1[0], func=AF.Identity, bias=ppc(("gnb", l, hp)), scale=ppc(("gng", l, hp)),
                   reads=[t_g1[0], t_pp], writes=[t_g1[0]])
                op("pool", "tensor_tensor", out=OT[:, oc, cs], in0=OT[:, oc, cs], in1=G1[0], op=ALU.add,
                   reads=[ot_t[oc][tp], t_g1[0]], writes=[ot_t[oc][tp]])
            P.barrier()
            ft0 = [TT() for _ in range(4)]
            rawp_t2 = TT()
            w1b = Rot([AR[:, sm:sm + 1024].rearrange("p (kc f) -> p kc f", kc=8)])
            w, wt = w1b.next()
            dma("sp", "dma_start", out=w, in_=dr["wch"][l * NCHUNK_W + 24 + 7].rearrange("p (kc f) -> p kc f", kc=8), writes=[wt])
            op("pool", "memset", RAWP[:, 0:1], 0.0, writes=[rawp_t2])
            mu = ppc(("mu", l, 7))
            ta2, tb2 = TT(), TT()
            for tp in range(4):
                pp_, ppt = PS.next()
                for kc in range(8):
                    op("pe", "matmul", pp_, lhsT=w[:, kc, :], rhs=XT[:, kc, tok(tp)], start=(kc == 0), stop=(kc == 7),
                       reads=[wt, xt_t[kc][tp]], writes=[ppt])
                op("act", "copy", out=RAWP[:, 1:513], in_=pp_, reads=[ppt], writes=[rawp_t2])
                op("dve", "tensor_tensor", out=TA, in0=RAWP[:, 0:512], in1=RAWP[:, 1:513], op=ALU.subtract, reads=[rawp_t2], writes=[ta2])
                op("dve", "scalar_tensor_tensor", out=TB, in0=TA, scalar=mu, in1=RAWP[:, 1:513], op0=ALU.mult, op1=ALU.add,
                   reads=[ta2, rawp_t2, t_pp], writes=[tb2])
                op("act", "copy", out=RAWP[:, 0:1], in_=RAWP[:, 512:513], reads=[rawp_t2], writes=[rawp_t2])
                op("act", "activation", out=TB, in_=TB, func=AF.Sigmoid, reads=[tb2], writes=[tb2])
                pg, pgt = PS.next()
                op("pe", "matmul", pg, lhsT=CSM[:, 256 + hp * 128:256 + hp * 128 + 128], rhs=TB, start=True, stop=True,
                   reads=[t_csm, tb2], writes=[pgt])
                op("dve", "tensor_tensor", out=OT[:, oc, tok(tp)], in0=pg, in1=OT[:, oc, tok(tp)], op=ALU.mult,
                   reads=[pgt, ot_t[oc][tp]], writes=[ot_t[oc][tp]])
            P.barrier()

        if stage in ("A", "mix", "full"):
            for i in range(3):
                phase_a(i)
        else:
            for c in range(0, 3):
                op("pool", "memset", OT[:, c, :], 0.0, writes=ot_t[c])
        if stage in ("B", "mix", "full"):
            for i in range(3):
                phase_b(i)
        else:
            for c in range(3, 6):
                op("pool", "memset", OT[:, c, :], 0.0, writes=ot_t[c])
        if stage in ("C", "mix", "full"):
            for hp in range(2):
                phase_c(hp)
        else:
            for c in range(6, 8):
                op("pool", "memset", OT[:, c, :], 0.0, writes=ot_t[c])
        return OT, ot_t, w_r

    def wout_ln(l, OT, ot_t, w_r):
        sc = Rot([AR[:, 16384 + k * 512:16384 + (k + 1) * 512] for k in range(8)])
        lnk = Rot([AR[:, 16384 + 4096 + k * 512:16384 + 4096 + (k + 1) * 512] for k in range(3)])
        wo_r = Rot([AR[:, 16384 + 6144 + k * 1024:16384 + 6144 + (k + 1) * 1024].rearrange("p (kc f) -> p kc f", kc=8) for k in range(2)])
        P.barrier()
        for dc in range(8):
            w, wt = wo_r.next()
            dma("sp", "dma_start", out=w, in_=dr["wch"][l * NCHUNK_W + 38 + dc].rearrange("p (kc f) -> p kc f", kc=8), writes=[wt])
            for tp in range(4):
                py, pyt = PS.next()
                for kc in range(8):
                    op("pe", "matmul", py, lhsT=w[:, kc, :], rhs=OT[:, kc, tok(tp)], start=(kc == 0), stop=(kc == 7),
                       reads=[wt, ot_t[kc][tp]], writes=[pyt])
                op("dve", "scalar_tensor_tensor", out=XT[:, dc, tok(tp)], in0=py, scalar=1.0 / ALPHA, in1=XT[:, dc, tok(tp)],
                   op0=ALU.mult, op1=ALU.add, reads=[pyt, xt_t[dc][tp]], writes=[xt_t[dc][tp]])
        for tp in range(4):
            layer_norm(l, 1, tp, sc, lnk)

    def dump_o(OT, ot_t):
        for c in range(8):
            dma("sp", "dma_start", out=out_d[c * 128:(c + 1) * 128, :], in_=OT[:, c, :], reads=ot_t[c])

    def dump_x():
        for c in range(8):
            dma("sp", "dma_start", out=out_d[c * 128:(c + 1) * 128, :], in_=XT[:, c, :], reads=xt_t[c])

    dumped = False
    stage = dbg[1] if dbg else "full"
    for l in range(depth):
        ffn_phase(l, 0)
        if dbg == ("x", "ffn_a") and l == depth - 1:
            break
        OT, ot_t, w_r = mixer_phase(l, stage)
        if dbg is not None and dbg[0] == "o" and l == depth - 1:
            dump_o(OT, ot_t)
            dumped = True
            break
        wout_ln(l, OT, ot_t, w_r)
        if dbg == ("x", "mixln") and l == depth - 1:
            break
        ffn_phase(l, 1)
    if not dumped:
        dump_x()
    P.barrier()
    P.finish()
    return nc


_CACHE = {}


def kernel(**inputs):
    sh, xs, idx = prep_inputs(inputs)
    shapes = {k: v.shape for k, v in sh.items()}
    nc = build(shapes, idx)
    n = len(xs)
    in_maps = []
    for b in range(n):
        m = dict(sh)
        m["xT"] = xs[b]
        in_maps.append(m)
    res = run_bass_kernel_spmd(nc, in_maps, core_ids=list(range(n)))
    out = np.stack([np.ascontiguousarray(res.results[b]["out"].T) for b in range(n)], axis=0)
    return out.astype(np.float32)
```

```python
import contextlib
import math
import numpy as np
import concourse.bass as bass
import concourse.mybir as mybir
from concourse.bass_utils import run_bass_kernel_spmd

F32 = mybir.dt.float32
F32R = mybir.dt.float32r
AF = mybir.ActivationFunctionType
ALU = mybir.AluOpType
AX = mybir.AxisListType

DEPTH = 4
D = 1024
S = 2048
FF = 2816
NFC = 22
ALPHA = (2 * DEPTH) ** 0.25
LN_EPS = 1e-5
RMS_EPS = 1e-5
GN_EPS = 64e-5
THETA = 10000.0
NCHUNK_W = 38 + 8
ARC = 31300


class TT:
    __slots__ = ("w", "r")

    def __init__(self):
        self.w = None
        self.r = []


class Prog:
    ENG = ("pe", "act", "dve", "pool", "sp")

    def __init__(self, nc, same_sync=True, n_dma_sems=40):
        self.nc = nc
        self.same_sync = same_sync
        self.q = {e: [] for e in self.ENG}
        self.cnt = {e: 0 for e in self.ENG}
        self.known = {e: {} for e in self.ENG}
        self.stack = contextlib.ExitStack()
        self.sem = {}
        for e in self.ENG:
            self.sem[e] = self.stack.enter_context(nc.semaphore("s_" + e))
        self.dsem = []
        self.dval = []
        for i in range(n_dma_sems):
            self.dsem.append(self.stack.enter_context(nc.semaphore("d%d" % i)))
            self.dval.append(0)
        self.ndma = 0

    def sb(self, name, shape, dt=F32):
        return self.stack.enter_context(self.nc.sbuf_tensor(name, shape, dt))

    def ps(self, name, shape, dt=F32):
        return self.stack.enter_context(self.nc.psum_tensor(name, shape, dt))

    def _waits(self, eng, reads, writes):
        deps = {}
        for t in reads:
            if t.w is not None:
                k, v = t.w
                if deps.get(k, 0) < v:
                    deps[k] = v
        for t in writes:
            if t.w is not None:
                k, v = t.w
                if deps.get(k, 0) < v:
                    deps[k] = v
            for (k, v) in t.r:
                if deps.get(k, 0) < v:
                    deps[k] = v
        out = []
        kn = self.known[eng]
        for k, v in deps.items():
            if k == eng and (eng == "pe" or not self.same_sync):
                continue
            if kn.get(k, 0) >= v:
                continue
            kn[k] = v
            out.append((k, v))
        return out

    def _semh(self, k):
        return self.sem[k] if isinstance(k, str) else self.dsem[k]

    def _mark(self, tok, reads, writes):
        for t in reads:
            t.r.append(tok)
            if len(t.r) > 24:
                best = {}
                for k, v in t.r:
                    if best.get(k, 0) < v:
                        best[k] = v
                t.r = list(best.items())
        for t in writes:
            t.w = tok
            t.r = []

    def op(self, eng, fn, *args, reads=(), writes=(), **kwargs):
        waits = self._waits(eng, reads, writes)
        self.cnt[eng] += 1
        self._mark((eng, self.cnt[eng]), reads, writes)
        sem = self.sem[eng]
        wl = [(self._semh(k), v) for k, v in waits]

        def emit(e, fn=fn, wl=wl, sem=sem, args=args, kwargs=kwargs):
            for s, v in wl:
                e.wait_ge(s, v)
            getattr(e, fn)(*args, **kwargs).then_inc(sem, 1)

        self.q[eng].append(emit)

    def dma(self, eng, fn, *args, reads=(), writes=(), di=None, **kwargs):
        if di is None:
            di = self.ndma % len(self.dsem)
            self.ndma += 1
        waits = self._waits(eng, reads, writes)
        self.dval[di] += 16
        self._mark((di, self.dval[di]), reads, writes)
        sem = self.dsem[di]
        wl = [(self._semh(k), v) for k, v in waits]

        def emit(e, fn=fn, wl=wl, sem=sem, args=args, kwargs=kwargs):
            for s, v in wl:
                e.wait_ge(s, v)
            getattr(e, fn)(*args, **kwargs).then_inc(sem, 16)

        self.q[eng].append(emit)

    def barrier(self):
        for e in self.ENG:
            wl = []
            kn = self.known[e]
            for k in self.ENG:
                if k != e and self.cnt[k] > kn.get(k, 0):
                    kn[k] = self.cnt[k]
                    wl.append((self.sem[k], self.cnt[k]))
            for i, v in enumerate(self.dval):
                if v > kn.get(i, 0):
                    kn[i] = v
                    wl.append((self.dsem[i], v))
            if self.same_sync and e != "pe" and self.cnt[e] > kn.get(e, 0):
                kn[e] = self.cnt[e]
                wl.append((self.sem[e], self.cnt[e]))

            def emit(en, wl=wl):
                for s, v in wl:
                    en.wait_ge(s, v)

            self.q[e].append(emit)

    def phase(self):
        return contextlib.ExitStack()

    def ph_sb(self, st, name, shape, dt=F32):
        self.uid = getattr(self, "uid", 0) + 1
        return st.enter_context(self.nc.sbuf_tensor("%s_%d" % (name, self.uid), shape, dt))

    def end_phase(self, st):
        self.barrier()
        self.flush()
        st.close()

    def finish(self):
        self.flush()
        self.stack.close()

    def flush(self):
        nc = self.nc
        q = self.q
        self.q = {e: [] for e in self.ENG}
        with nc.Block() as block:
            @block.tensor
            def _(e):
                for f in q["pe"]:
                    f(e)

            @block.scalar
            def _(e):
                for f in q["act"]:
                    f(e)

            @block.vector
            def _(e):
                for f in q["dve"]:
                    f(e)

            @block.gpsimd
            def _(e):
                for f in q["pool"]:
                    f(e)

            @block.sync
            def _(e):
                for f in q["sp"]:
                    f(e)


class Rot:
    def __init__(self, views):
        self.v = views
        self.t = [TT() for _ in views]
        self.i = 0

    def next(self):
        i = self.i % len(self.v)
        self.i += 1
        return self.v[i], self.t[i]


def _chunk(W, cols):
    return np.ascontiguousarray(W[:, cols].reshape(8, 128, len(cols)).transpose(1, 0, 2))


def _swap_idx(base, n, grp):
    idx = np.arange(n)
    half = grp // 2
    return base + (idx // grp) * grp + ((idx % grp) + half) % grp


def _rope_table(grp):
    half = grp // 2
    inv = (THETA ** (-np.arange(0, grp, 2, dtype=np.float32) / grp)).astype(np.float32)
    pos = np.arange(S, dtype=np.float32)
    ang = (pos[:, None] * inv[None, :]).astype(np.float32)
    cos = np.cos(ang).astype(np.float32).T
    sin = np.sin(ang).astype(np.float32).T
    r = np.arange(128)
    i = r % half
    sign = np.where((r % grp) < half, -1.0, 1.0).astype(np.float32)
    C = cos[i]
    Sg = sin[i] * sign[:, None]
    t = np.stack([C, Sg], axis=1)
    t = t.reshape(128, 2, 4, 512).transpose(0, 2, 1, 3)
    return np.ascontiguousarray(t.reshape(128, 4 * 2 * 512)).astype(np.float32)


def _consts():
    c = {}
    k = np.arange(128)[:, None]
    q = np.arange(256)[None, :]
    c["causal"] = (q[:, :128] >= k).astype(np.float32)
    c["band"] = ((q >= k) & (q <= k + 128)).astype(np.float32)
    su = (q[:, :128] > k).astype(np.float32)
    iu = (q[:, :128] >= k).astype(np.float32)
    c["uppers"] = np.concatenate([su, iu], axis=1)
    c["ident"] = np.eye(128, dtype=np.float32)
    blk = np.zeros((128, 128), np.float32)
    blk[:64, :64] = 1.0
    blk[64:, 64:] = 1.0
    c["blk"] = blk
    c["lowers"] = (q[:, :128] < k).astype(np.float32)
    return c


def prep_inputs(inp):
    L = DEPTH
    f = lambda a: np.asarray(a, dtype=np.float32)
    sh = {}
    wgu = np.empty((L, 2, 22, 128, 2, 8, 128), np.float32)
    wd = np.empty((L, 2, 8, 128, 22, 128), np.float32)
    ffw = {"a": (inp["ffn_a_gate"], inp["ffn_a_up"], inp["ffn_a_down"]),
           "b": (inp["ffn_b_gate"], inp["ffn_b_up"], inp["ffn_b_down"])}
    for l in range(L):
        for i, nm in enumerate("ab"):
            g = f(ffw[nm][0][l]).reshape(8, 128, 22, 128).transpose(2, 1, 0, 3)
            u = f(ffw[nm][1][l]).reshape(8, 128, 22, 128).transpose(2, 1, 0, 3)
            wgu[l, i, :, :, 0] = g
            wgu[l, i, :, :, 1] = u
            wd[l, i] = f(ffw[nm][2][l]).reshape(22, 128, 8, 128).transpose(2, 1, 0, 3)
    sh["wgu"] = wgu.reshape(L * 2 * 22, 128, 2048)
    sh["wd"] = wd.reshape(L * 2 * 8, 128, 2816)
    wch = np.empty((L, NCHUNK_W, 128, 8, 128), np.float32)
    for l in range(L):
        W = f(inp["w_in"][l])
        ci = 0
        for base, grp in ((0, 32), (1152, 64)):
            for i in range(3):
                qc = base + 128 * i + np.arange(128)
                kc_ = base + 384 + 128 * i + np.arange(128)
                wch[l, ci + 0] = _chunk(W, qc)
                wch[l, ci + 1] = _chunk(W, _swap_idx(base + 128 * i, 128, grp))
                wch[l, ci + 2] = _chunk(W, kc_)
                wch[l, ci + 3] = _chunk(W, _swap_idx(base + 384 + 128 * i, 128, grp))
                ci += 4
        for c in range(8):
            wch[l, 24 + c] = _chunk(W, 2304 + 128 * c + np.arange(128))
        for i in range(3):
            wch[l, 32 + i] = _chunk(W, 768 + 128 * i + np.arange(128))
            wch[l, 35 + i] = _chunk(W, 1920 + 128 * i + np.arange(128))
        Wo = f(inp["w_out"][l])
        for dc in range(8):
            wch[l, 38 + dc] = _chunk(Wo, 128 * dc + np.arange(128))
    sh["wch"] = wch.reshape(L * NCHUNK_W, 128, 1024)
    sh["ropeA"] = _rope_table(32)
    sh["ropeB"] = _rope_table(64)
    for k_, v_ in _consts().items():
        sh["c_" + k_] = v_
    cols = []

    def addcol(v):
        cols.append(np.asarray(v, np.float32).reshape(128, 1))
        return len(cols) - 1

    idx = {}
    for l in range(L):
        for i in range(3):
            for c in range(8):
                idx[("lng", l, i, c)] = addcol(f(inp["ln_g"][l, i, c * 128:(c + 1) * 128]))
                idx[("lnb", l, i, c)] = addcol(f(inp["ln_b"][l, i, c * 128:(c + 1) * 128]))
        idx[("ga", l)] = addcol(np.tile(f(inp["a_norm_g"][l]), 2))
        idx[("gb", l)] = addcol(np.tile(f(inp["b_norm_g"][l]), 2))
        for c in range(8):
            idx[("mu", l, c)] = addcol(f(inp["c_mu"][l, c * 128:(c + 1) * 128]))
        for hp in range(2):
            sl = slice(hp * 128, hp * 128 + 128)
            idx[("w0", l, hp)] = addcol(f(inp["c_w0"][l, sl]))
            idx[("a0", l, hp)] = addcol(f(inp["c_a0"][l, sl]))
            idx[("kk", l, hp)] = addcol(f(inp["c_k_k"][l, sl]))
            idx[("ka", l, hp)] = addcol(f(inp["c_k_a"][l, sl]))
            idx[("rk", l, hp)] = addcol(f(inp["c_r_k"][l].reshape(256)[sl]))
            idx[("gng", l, hp)] = addcol(f(inp["c_gn_g"][l, sl]))
            idx[("gnb", l, hp)] = addcol(f(inp["c_gn_b"][l, sl]))
            if l > 0:
                idx[("v0", l, hp)] = addcol(f(inp["c_v0"][l - 1, sl]))
    sh["pp"] = np.ascontiguousarray(np.concatenate(cols, axis=1))
    lamrow = np.stack([np.stack([f(inp["a_lam_q1"][l]), f(inp["a_lam_k1"][l]),
                                 f(inp["a_lam_q2"][l]), f(inp["a_lam_k2"][l])]) for l in range(L)])
    sh["lamrow"] = np.ascontiguousarray(lamrow.reshape(1, L * 4 * 32))
    sm = np.zeros((L, 128, 256 + 256 + 64 + 256), np.float32)
    for l in range(L):
        sm[l, 0:64, 0:256] = f(inp["c_w2"][l])
        sm[l, 64:128, 0:256] = f(inp["c_a2"][l])
        sm[l, :, 256:512] = f(inp["c_g2"][l])
        if l > 0:
            sm[l, :, 512:576] = f(inp["c_v1"][l - 1]).reshape(2, 128, 32).transpose(1, 0, 2).reshape(128, 64)
            sm[l, 0:32, 576:832] = f(inp["c_v2"][l - 1])
    sh["csm"] = np.ascontiguousarray(sm.transpose(1, 0, 2).reshape(128, L * 832))
    xs = [np.ascontiguousarray(f(inp["x"][b]).T) for b in range(inp["x"].shape[0])]
    return sh, xs, idx


def build(shapes, idx, depth=DEPTH, dbg=None):
    nc = bass.Bass("TRN2", target_bir_lowering=False)
    dr = {}
    for k, shp in shapes.items():
        dr[k] = nc.dram_tensor(k, list(shp), F32, kind="ExternalInput").ap()
    xT_d = nc.dram_tensor("xT", [D, S], F32, kind="ExternalInput").ap()
    out_d = nc.dram_tensor("out", [D, S], F32, kind="ExternalOutput").ap()
    vf_d = nc.dram_tensor("vf_scratch", [256, S], F32, kind="Internal").ap()
    npp = shapes["pp"][1]

    P = Prog(nc)
    op, dma = P.op, P.dma
    XT = P.sb("XT", [128, 8, S], F32R)
    XF = XT[:].bitcast(F32)
    AR = None
    OTt = None
    CST = P.sb("CST", [128, 1472])
    PP = P.sb("PP", [128, npp])
    CSM = P.sb("CSM", [128, 832])
    LAM = P.sb("LAM", [128, 64 + DEPTH * 128 + 64])
    banks = [P.ps("pb%d" % i, [128, 512]) for i in range(8)]
    PS = Rot([b[:] for b in banks[0:6]])
    PSL = Rot([b[:] for b in banks[6:8]])

    xt_t = [[TT() for _ in range(4)] for _ in range(8)]
    vf_t = [TT(), TT()]
    t_cst, t_pp, t_csm, t_lam, t_rst = TT(), TT(), TT(), TT(), TT()
    CAUS = CST[:, 0:128]
    BAND = CST[:, 128:384]
    UPP = CST[:, 384:640]
    IDENT = CST[:, 640:768]
    BLK = CST[:, 768:896]
    ONESD = CST[:, 896:1024]
    ONES64 = CST[:, 1024:1088]
    ONES = CST[:, 1088:1216]
    LOWS = CST[:, 1216:1344]
    BLK64 = CST[:, 1344:1472]
    dma("sp", "dma_start", out=CAUS, in_=dr["c_causal"], writes=[t_cst])
    dma("sp", "dma_start", out=BAND, in_=dr["c_band"], writes=[t_cst])
    dma("sp", "dma_start", out=UPP, in_=dr["c_uppers"], writes=[t_cst])
    dma("sp", "dma_start", out=IDENT, in_=dr["c_ident"], writes=[t_cst])
    dma("sp", "dma_start", out=BLK, in_=dr["c_blk"], writes=[t_cst])
    dma("sp", "dma_start", out=PP[:], in_=dr["pp"], writes=[t_pp])
    op("dve", "memset", ONESD, 1.0 / 1024.0, writes=[t_cst])
    op("dve", "memset", ONES64, 1.0 / 64.0, writes=[t_cst])
    op("dve", "memset", ONES, 1.0, writes=[t_cst])
    dma("sp", "dma_start", out=LOWS, in_=dr["c_lowers"], writes=[t_cst])
    op("act", "mul", out=BLK64, in_=BLK, mul=1.0 / 64.0, reads=[t_cst], writes=[t_cst])
    for c in range(8):
        dma("pool", "dma_start", out=XT[:, c, :], in_=xT_d[c * 128:(c + 1) * 128, :].bitcast(F32R),
            writes=xt_t[c])
    ONE1 = LAM[:, 0:64]
    op("dve", "memset", LAM[:], 0.0, writes=[t_lam])
    op("dve", "memset", ONE1, 1.0, writes=[t_lam])
    LROW = LAM[0:1, 64:64 + DEPTH * 128]
    dma("sp", "dma_start", out=LROW, in_=dr["lamrow"], writes=[t_lam])
    NEGLAM = LAM[:, 64 + DEPTH * 128:64 + DEPTH * 128 + 8]
    LS = LAM[0:1, 64 + DEPTH * 128 + 8:64 + DEPTH * 128 + 64]
    for l in range(depth):
        b0 = 64 + l * 128
        lam_init = 0.8 - 0.6 * math.exp(-0.3 * l)
        for j in range(2):
            op("dve", "tensor_tensor", out=LS[:, 0:32], in0=LAM[0:1, b0 + 64 * j:b0 + 64 * j + 32],
                                                          in1=LAM[0:1, b0 + 64 * j + 32:b0 + 64 * j + 64], op=ALU.mult,
               reads=[t_lam], writes=[t_lam])
            op("dve", "reduce_sum", out=LS[:, 32 + j:33 + j], in_=LS[:, 0:32], axis=AX.X,
               reads=[t_lam], writes=[t_lam])
        op("act", "activation", out=LS[:, 34:36], in_=LS[:, 32:34], func=AF.Exp, reads=[t_lam], writes=[t_lam])
        op("dve", "scalar_tensor_tensor", out=LS[:, 36:37], in0=LS[:, 35:36], scalar=-lam_init, in1=LS[:, 34:35],
                                                                op0=ALU.add, op1=ALU.subtract, reads=[t_lam], writes=[t_lam])
        pb, pt = PS.next()
        op("pe", "matmul", pb[0:64, 0:1], lhsT=LAM[0:1, 0:64], rhs=LS[:, 36:37], start=True, stop=True,
           reads=[t_lam], writes=[pt])
        op("act", "copy", out=NEGLAM[0:64, l:l + 1], in_=pb[0:64, 0:1], reads=[pt], writes=[t_lam])

    def ppc(key):
        j = idx[key]
        return PP[:, j:j + 1]

    def tok(tp):
        return slice(tp * 512, (tp + 1) * 512)

    def ssl(st, n, step):
        return slice(st, st + step * (n - 1) + 1, step) if step > 1 else slice(st, st + n)

    def layer_norm(l, i, tp, sc, lnk):
        eps = LN_EPS / (ALPHA * ALPHA)
        pm, pmt = PS.next()
        pe2, pe2t = PS.next()
        for c in range(8):
            op("pe", "matmul", pm, lhsT=ONESD, rhs=XF[:, c, tok(tp)], start=(c == 0), stop=(c == 7),
               reads=[t_cst, xt_t[c][tp]], writes=[pmt])
        for c in range(8):
            sq, sqt = sc.next()
            op("act", "activation", out=sq, in_=XF[:, c, tok(tp)], func=AF.Square,
               reads=[xt_t[c][tp]], writes=[sqt])
            op("pe", "matmul", pe2, lhsT=ONESD, rhs=sq, start=(c == 0), stop=(c == 7),
               reads=[t_cst, sqt], writes=[pe2t])
        mean, meant = lnk.next()
        op("act", "copy", out=mean, in_=pm, reads=[pmt], writes=[meant])
        msq, msqt = lnk.next()
        op("dve", "tensor_tensor", out=msq, in0=mean, in1=mean, op=ALU.mult, reads=[meant], writes=[msqt])
        var, vart = lnk.next()
        op("dve", "scalar_tensor_tensor", out=var, in0=pe2, scalar=eps, in1=msq, op0=ALU.add, op1=ALU.subtract,
           reads=[pe2t, msqt], writes=[vart])
        op("act", "activation", out=var, in_=var, func=AF.Ln, reads=[vart], writes=[vart])
        op("act", "activation", out=var, in_=var, func=AF.Exp, scale=-0.5, reads=[vart], writes=[vart])
        for c in range(8):
            t1, t1t = sc.next()
            op("dve", "tensor_tensor", out=t1, in0=XF[:, c, tok(tp)], in1=mean, op=ALU.subtract,
               reads=[xt_t[c][tp], meant], writes=[t1t])
            op("dve", "tensor_tensor", out=t1, in0=t1, in1=var, op=ALU.mult,
               reads=[t1t, vart], writes=[t1t])
            op("act", "activation", out=XT[:, c, tok(tp)], in_=t1, func=AF.Identity,
                                                        bias=ppc(("lnb", l, i, c)), scale=ppc(("lng", l, i, c)),
               reads=[t1t, t_pp], writes=[xt_t[c][tp]])

    def ffn_phase(l, i):
        st = P.phase()
        FR = P.ph_sb(st, "FR", [128, 11264 + 2 * 2048 + 2 * 2816], F32R)
        FT = P.ph_sb(st, "FT", [128, 11 * 512], F32)
        o = 0
        Hv = FR[:, o:o + 22 * 512].rearrange("p (f t) -> p f t", t=512)
        o += 22 * 512
        h_t = [TT() for _ in range(22)]
        wgu_r = Rot([FR[:, o + k * 2048:o + (k + 1) * 2048].rearrange("p (g kc f) -> p g kc f", g=2, kc=8) for k in range(2)])
        o += 2 * 2048
        wd_r = Rot([FR[:, o + k * 2816:o + (k + 1) * 2816].rearrange("p (f d) -> p f d", d=128) for k in range(2)])
        o += 2 * 2816
        sc = Rot([FT[:, k * 512:(k + 1) * 512] for k in range(8)])
        lnk = Rot([FT[:, (8 + k) * 512:(9 + k) * 512] for k in range(3)])
        lnidx = 0 if i == 0 else 2
        for tp in range(4):
            for f in range(22):
                wb, wbt = wgu_r.next()
                dma("pool", "dma_start", out=wb,
                    in_=dr["wgu"][(l * 2 + i) * 22 + f].bitcast(F32R).rearrange("p (g kc f) -> p g kc f", g=2, kc=8), writes=[wbt])
                pg, pgt = PS.next()
                pu, put = PS.next()
                for g, pp_, ppt in ((0, pg, pgt), (1, pu, put)):
                    for kc in range(8):
                        op("pe", "matmul", pp_, lhsT=wb[:, g, kc, :], rhs=XT[:, kc, tok(tp)], start=(kc == 0), stop=(kc == 7),
                           reads=[wbt, xt_t[kc][tp]], writes=[ppt])
                sg, sgt = sc.next()
                op("act", "activation", out=sg, in_=pg, func=AF.Silu, reads=[pgt], writes=[sgt])
                op("dve", "tensor_tensor", out=Hv[:, f, :], in0=sg, in1=pu, op=ALU.mult, reads=[sgt, put], writes=[h_t[f]])
            for dc in range(8):
                wb, wbt = wd_r.next()
                dma("pool", "dma_start", out=wb,
                    in_=dr["wd"][(l * 2 + i) * 8 + dc].bitcast(F32R).rearrange("p (f d) -> p f d", d=128), writes=[wbt])
                py, pyt = PS.next()
                for f in range(22):
                    op("pe", "matmul", py, lhsT=wb[:, f, :], rhs=Hv[:, f, :], start=(f == 0), stop=(f == 21),
                       reads=[wbt, h_t[f]], writes=[pyt])
                op("dve", "scalar_tensor_tensor", out=XT[:, dc, tok(tp)], in0=py, scalar=0.5 / ALPHA, in1=XF[:, dc, tok(tp)],
                   op0=ALU.mult, op1=ALU.add, reads=[pyt, xt_t[dc][tp]], writes=[xt_t[dc][tp]])
            layer_norm(l, lnidx, tp, sc, lnk)
        return st

    def mixer_phase(l, stage):
        OT = OTt[:].rearrange("p (c t) -> p c t", t=S)
        ot_t = [[TT() for _ in range(4)] for _ in range(8)]
        QT = KT = KZ = VA = RS = None
        w_r = rope_r = e512 = e256 = scr = None
        SCR0 = None
        qt_t = kt_t = None
        va_t = kz_t = None

        def setup_ab(is_a):
            nonlocal QT, KT, KZ, VA, RS, w_r, rope_r, e512, e256, scr, SCR0, qt_t, kt_t, va_t, kz_t
            st = P.phase()
            AQ = P.ph_sb(st, "AQ", [128, 11808 if is_a else 9760], F32R)
            ABp = P.ph_sb(st, "ABp", [128, 3072 if is_a else 5120], F32)
            o = 0
            QT = AQ[:, o:o + 2048]
            KT = AQ[:, o + 2048:o + 4096]
            o += 4096
            qt_t = [TT() for _ in range(4)]
            kt_t = [TT() for _ in range(4)]
            VA = AQ[:, o:o + 2080].rearrange("p (j h d) -> p j h d", h=2, d=65)
            va_t = TT()
            o += 2080
            w_r = Rot([AQ[:, o + k * 1024:o + (k + 1) * 1024].rearrange("p (kc f) -> p kc f", kc=8) for k in range(2)])
            o += 2048
            e512 = Rot([AQ[:, o + k * 512:o + (k + 1) * 512] for k in range(3)])
            e256 = Rot([AQ[:, o + k * 256:o + (k + 1) * 256] for k in range(6)])
            o += 1536
            if is_a:
                KZ = AQ[:, o:o + 2048]
                kz_t = TT()
                o += 2048
            rope_r = Rot([ABp[:, 0:1024].rearrange("p (c t) -> p c t", c=2)])
            scr = Rot([ABp[:, 1024 + k * 512:1024 + (k + 1) * 512] for k in range(4)])
            SCR0 = ABp
            if not is_a:
                RS = ABp[:, 3072:5120]
            return st

        lam_init = 0.8 - 0.6 * math.exp(-0.3 * l)

        def load_w(ci):
            w, wt = w_r.next()
            dma("pool", "dma_start", out=w, in_=dr["wch"][l * NCHUNK_W + ci].bitcast(F32R).rearrange("p (kc f) -> p kc f", kc=8),
                writes=[wt])
            return w, wt

        def proj_rope(ci, dst, dst_t, rname):
            w1, w1t = load_w(ci)
            w2, w2t = load_w(ci + 1)
            for tp in range(4):
                rp, rpt = rope_r.next()
                dma("sp", "dma_start", out=rp, in_=dr[rname][:, tp * 1024:(tp + 1) * 1024].rearrange("p (c t) -> p c t", c=2),
                    writes=[rpt])
                p1, p1t = PS.next()
                p2, p2t = PS.next()
                for w, wt, pp_, ppt in ((w1, w1t, p1, p1t), (w2, w2t, p2, p2t)):
                    for kc in range(8):
                        op("pe", "matmul", pp_, lhsT=w[:, kc, :], rhs=XT[:, kc, tok(tp)], start=(kc == 0), stop=(kc == 7),
                           reads=[wt, xt_t[kc][tp]], writes=[ppt])
                a, at = scr.next()
                b, bt = scr.next()
                op("dve", "tensor_tensor", out=a, in0=p1, in1=rp[:, 0, :], op=ALU.mult, reads=[p1t, rpt], writes=[at])
                op("dve", "tensor_tensor", out=b, in0=p2, in1=rp[:, 1, :], op=ALU.mult, reads=[p2t, rpt], writes=[bt])
                op("pool", "tensor_tensor", out=dst[:, tok(tp)], in0=a, in1=b, op=ALU.add, reads=[at, bt], writes=[dst_t[tp]])

        def proj_v(ci, dil):
            wv, wvt = load_w(ci)
            op("act", "copy", out=VA[:, :, :, 64:65], in_=ONES[:, 0:32].rearrange("p (j h d) -> p j h d", h=2, d=1),
               reads=[t_cst], writes=[va_t])
            nt = 16 // dil
            for r in range(dil):
                for a in range(nt):
                    st = r + dil * 128 * a
                    pv, pvt = PS.next()
                    for kc in range(8):
                        lhs = XT[:, kc, ssl(st, 128, dil)]
                        op("pe", "matmul", pv[:, 0:128], lhsT=lhs, rhs=wv[:, kc, :], start=(kc == 0), stop=(kc == 7),
                           reads=[wvt] + xt_t[kc], writes=[pvt])
                    op("act", "copy", out=VA[:, r * nt + a, :, 0:64], in_=pv[:, 0:128].rearrange("p (h d) -> p h d", h=2),
                       reads=[pvt], writes=[va_t])

        def phase_a(i):
            st = setup_ab(True)
            proj_rope(4 * i, QT, qt_t, "ropeA")
            proj_rope(4 * i + 2, KT, kt_t, "ropeA")
            proj_v(32 + i, 1)
            op("pool", "tensor_copy", out=KZ[96:128, :], in_=KT[96:128, :].bitcast(F32), reads=kt_t, writes=[kz_t])
            op("pool", "tensor_scalar", out=KZ[64:96, :], in0=KT[64:96, :].bitcast(F32), scalar1=0.0, scalar2=None, op0=ALU.mult,
               reads=kt_t, writes=[kz_t])
            sca = 32 ** -0.5
            for hl in range(2):
                for Qp in range(4):
                    nums = []
                    for m in range(2):
                        r0 = 32 * (2 * hl + m)
                        num, numt = PSL.next()
                        nums.append((num, numt))
                        last = 4 * Qp + 3
                        pend = []

                        def do_pv(item, num=num, numt=numt, last=last):
                            j, E, Et, c0, n = item
                            op("pe", "matmul", num[0:65, c0 - 512 * Qp:512], lhsT=VA[:, j, hl, :], rhs=E[:, 0:n],
                               start=(j == 0), stop=(j == last), reads=[va_t, Et], writes=[numt])

                        for j in range(last + 1):
                            c0 = max(128 * j, 512 * Qp)
                            n = 512 * Qp + 512 - c0
                            se, set_ = PS.next()
                            if hl == 1 and m == 1:
                                op("pe", "matmul", se[:, 0:n], lhsT=KZ[64:128, 128 * j:128 * j + 128], rhs=QT[64:128, c0:c0 + n],
                                   start=True, stop=True, reads=[kz_t] + qt_t[c0 // 512:Qp + 1], writes=[set_])
                            else:
                                op("pe", "matmul", se[:, 0:n], lhsT=KT[r0:r0 + 32, 128 * j:128 * j + 128], rhs=QT[r0:r0 + 32, c0:c0 + n],
                                   start=True, stop=True, reads=[kt_t[j // 4]] + qt_t[c0 // 512:Qp + 1], writes=[set_])
                            E, Et = e512.next()
                            op("act", "activation", out=E[:, 0:n], in_=se[:, 0:n], func=AF.Exp, scale=sca, reads=[set_], writes=[Et])
                            if 128 * j >= 512 * Qp:
                                op("pool", "tensor_tensor", out=E[:, 0:128], in0=E[:, 0:128].bitcast(F32), in1=CAUS, op=ALU.mult,
                                   reads=[Et, t_cst], writes=[Et])
                            pend.append((j, E, Et, c0, n))
                            if len(pend) > 2:
                                do_pv(pend.pop(0))
                        while pend:
                            do_pv(pend.pop(0))
                    (n1, n1t), (n2, n2t) = nums
                    X0, X0t = scr.next()
                    X1, X1t = scr.next()
                    X2, X2t = scr.next()
                    for (nn, nnt, prt) in ((n1, n1t, 64), (n2, n2t, 32)):
                        op("act", "activation", out=X0[prt:prt + 1, :], in_=nn[64:65, :], func=AF.Ln, reads=[nnt], writes=[X0t])
                        op("act", "activation", out=X0[prt:prt + 1, :], in_=X0[prt:prt + 1, :], func=AF.Exp, scale=-1.0,
                           reads=[X0t], writes=[X0t])
                    for (prt, Xd, Xdt) in ((64, X1, X1t), (32, X2, X2t)):
                        pb, pbt = PS.next()
                        op("pe", "matmul", pb[0:64, :], lhsT=ONE1[prt:prt + 1, 0:64], rhs=X0[prt:prt + 1, :], start=True, stop=True,
                           reads=[t_lam, X0t], writes=[pbt])
                        op("act", "copy", out=Xd[0:64, :], in_=pb[0:64, :], reads=[pbt], writes=[Xdt])
                    op("dve", "tensor_tensor", out=X1[0:64, :], in0=n1[0:64, :], in1=X1[0:64, :], op=ALU.mult, reads=[n1t, X1t], writes=[X1t])
                    op("dve", "tensor_tensor", out=X2[0:64, :], in0=n2[0:64, :], in1=X2[0:64, :], op=ALU.mult, reads=[n2t, X2t], writes=[X2t])
                    op("dve", "scalar_tensor_tensor", out=X1[0:64, :], in0=X2[0:64, :], scalar=NEGLAM[0:64, l:l + 1], in1=X1[0:64, :],
                       op0=ALU.mult, op1=ALU.add, reads=[X1t, X2t, t_lam], writes=[X1t])
                    op("act", "activation", out=X2[0:64, :], in_=X1[0:64, :], func=AF.Square, reads=[X1t], writes=[X2t])
                    pm, pmt = PS.next()
                    op("pe", "matmul", pm[0:64, :], lhsT=ONES64[0:64, 0:64], rhs=X2[0:64, :], start=True, stop=True,
                       reads=[t_cst, X2t], writes=[pmt])
                    op("dve", "tensor_scalar", out=X2[0:64, :], in0=pm[0:64, :], scalar1=RMS_EPS, scalar2=None, op0=ALU.add,
                       reads=[pmt], writes=[X2t])
                    op("act", "activation", out=X2[0:64, :], in_=X2[0:64, :], func=AF.Ln, reads=[X2t], writes=[X2t])
                    op("act", "activation", out=X2[0:64, :], in_=X2[0:64, :], func=AF.Exp, scale=-0.5, reads=[X2t], writes=[X2t])
                    op("dve", "scalar_tensor_tensor", out=X1[0:64, :], in0=X1[0:64, :], scalar=ppc(("ga", l))[0:64, :], in1=X2[0:64, :],
                       op0=ALU.mult, op1=ALU.mult, reads=[X1t, X2t, t_pp], writes=[X1t])
                    op("dve", "tensor_scalar", out=OT[64 * hl:64 * hl + 64, i, tok(Qp)], in0=X1[0:64, :], scalar1=1.0 - lam_init,
                       scalar2=None, op0=ALU.mult, reads=[X1t], writes=[ot_t[i][Qp]])
            P.end_phase(st)


        def phase_b(i):
            st = setup_ab(False)
            rs_t = [TT(), TT()]
            proj_rope(12 + 4 * i, QT, qt_t, "ropeB")
            proj_rope(12 + 4 * i + 2, KT, kt_t, "ropeB")
            scb = 64 ** -0.5
            oc = 3 + i
            for dil in (1, 4, 16):
                proj_v(35 + i, dil)
                L = S // dil
                nt = L // 128
                for hl in range(2):
                    rows = slice(64 * hl, 64 * hl + 64)
                    rowp = 64 if hl == 0 else 32
                    for r in range(dil):
                        Es = {}

                        def get_E(a):
                            if a in Es:
                                return Es[a]
                            nq = min(256, L - 128 * a)
                            st = r + dil * 128 * a
                            se, set_ = PS.next()
                            op("pe", "matmul", se[:, 0:nq], lhsT=KT[rows, ssl(st, 128, dil)], rhs=QT[rows, ssl(st, nq, dil)],
                               start=True, stop=True, reads=kt_t + qt_t, writes=[set_])
                            E, Et = e256.next()
                            op("act", "activation", out=E[:, 0:nq], in_=se[:, 0:nq], func=AF.Exp, scale=scb, reads=[set_], writes=[Et])
                            op("pool", "tensor_tensor", out=E[:, 0:nq], in0=E[:, 0:nq].bitcast(F32), in1=BAND[:, 0:nq], op=ALU.mult,
                               reads=[Et, t_cst], writes=[Et])
                            Es[a] = (E, Et, nq)
                            return Es[a]

                        for g in range((nt + 3) // 4):
                            ncols = min(512, L - 512 * g)
                            alist = list(range(max(4 * g - 1, 0), min(4 * g + 3, nt - 1) + 1))
                            for a in alist:
                                get_E(a)
                            num, numt = PSL.next()
                            for ai, a in enumerate(alist):
                                E, Et, nq = Es[a]
                                lo = max(128 * a, 512 * g)
                                hi = min(128 * a + nq, 512 * g + ncols)
                                op("pe", "matmul", num[0:65, lo - 512 * g:hi - 512 * g], lhsT=VA[:, r * nt + a, hl, :],
                                   rhs=E[:, lo - 128 * a:hi - 128 * a], start=(ai == 0), stop=(ai == len(alist) - 1),
                                   reads=[va_t, Et], writes=[numt])
                            p0 = r + dil * 512 * g
                            ps_ = ssl(p0, ncols, dil)
                            if dil == 1:
                                op("dve", "tensor_copy", out=OT[rows, oc, ps_], in_=num[0:64, 0:ncols], reads=[numt], writes=ot_t[oc])
                                op("act", "copy", out=RS[rowp:rowp + 1, ps_], in_=num[64:65, 0:ncols], reads=[numt], writes=[rs_t[hl]])
                            else:
                                op("dve", "tensor_tensor", out=OT[rows, oc, ps_], in0=num[0:64, 0:ncols], in1=OT[rows, oc, ps_], op=ALU.add,
                                   reads=[numt] + ot_t[oc], writes=ot_t[oc])
                                op("dve", "tensor_tensor", out=RS[rowp:rowp + 1, ps_], in0=num[64:65, 0:ncols], in1=RS[rowp:rowp + 1, ps_],
                                   op=ALU.add, reads=[numt, rs_t[hl]], writes=[rs_t[hl]])
            P.barrier()
            pst = Rot([SCR0[:, 1024 + k * 512:1024 + (k + 1) * 512] for k in range(3)])
            for hl in range(2):
                rows = slice(64 * hl, 64 * hl + 64)
                rowp = 64 if hl == 0 else 32
                for tp in range(4):
                    T0, T0t = pst.next()
                    T1, T1t = pst.next()
                    T2, T2t = pst.next()
                    op("act", "activation", out=T0[rows, :], in_=OT[rows, oc, tok(tp)], func=AF.Square, reads=[ot_t[oc][tp]], writes=[T0t])
                    op("act", "activation", out=T1[rowp:rowp + 1, :], in_=RS[rowp:rowp + 1, tok(tp)], func=AF.Square,
                       scale=math.sqrt(RMS_EPS), reads=[rs_t[hl]], writes=[T1t])
                    pm, pmt = PS.next()
                    op("pe", "matmul", pm[0:64, :], lhsT=ONES64[rows, 0:64], rhs=T0[rows, :], start=True, stop=False,
                       reads=[t_cst, T0t], writes=[pmt])
                    op("pe", "matmul", pm[0:64, :], lhsT=ONE1[rowp:rowp + 1, 0:64], rhs=T1[rowp:rowp + 1, :], start=False, stop=True,
                       reads=[t_lam, T1t], writes=[pmt])
                    op("act", "activation", out=T2[rows, :], in_=pm[0:64, :], func=AF.Ln, reads=[pmt], writes=[T2t])
                    op("act", "activation", out=T2[rows, :], in_=T2[rows, :], func=AF.Exp, scale=-0.5, reads=[T2t], writes=[T2t])
                    op("dve", "scalar_tensor_tensor", out=OT[rows, oc, tok(tp)], in0=OT[rows, oc, tok(tp)], scalar=ppc(("gb", l))[rows, :],
                       in1=T2[rows, :], op0=ALU.mult, op1=ALU.mult, reads=[ot_t[oc][tp], T2t, t_pp], writes=[ot_t[oc][tp]])
            P.end_phase(st)


        def phase_c(hp):
            st = P.phase()
            AR = P.ph_sb(st, "CF", [128, 14916], F32)
            oc = 6 + hp
            base = 0
            F = [AR[:, base + k * 2048:base + (k + 1) * 2048] for k in range(6)]
            ft = [[TT() for _ in range(4)] for _ in range(6)]
            sm = base + 6 * 2048
            w1 = Rot([AR[:, sm:sm + 1024].rearrange("p (kc f) -> p kc f", kc=8)])
            RAWP = AR[:, sm + 1024:sm + 1024 + 516]
            rawp_t = TT()
            TA = AR[:, sm + 1540:sm + 2052]
            TB = AR[:, sm + 2052:sm + 2564]
            ta_t, tb_t = TT(), TT()
            GC = AR[:, sm + 2564:sm + 2580]
            gc_t = TT()
            assert sm + 2580 <= 14916
            dma("sp", "dma_start", out=CSM[:], in_=dr["csm"][:, l * 832:(l + 1) * 832], writes=[t_csm])

            def loadw1(ci):
                w, wt = w1.next()
                dma("sp", "dma_start", out=w, in_=dr["wch"][l * NCHUNK_W + ci].rearrange("p (kc f) -> p kc f", kc=8), writes=[wt])
                return w, wt

            def proj_lerp(cidx, dst, dst_t):
                w, wt = loadw1(24 + cidx)
                op("pool", "memset", RAWP[:, 0:1], 0.0, writes=[rawp_t])
                mu = ppc(("mu", l, cidx))
                for tp in range(4):
                    pp_, ppt = PS.next()
                    for kc in range(8):
                        op("pe", "matmul", pp_, lhsT=w[:, kc, :], rhs=XF[:, kc, tok(tp)], start=(kc == 0), stop=(kc == 7),
                           reads=[wt, xt_t[kc][tp]], writes=[ppt])
                    op("act", "copy", out=RAWP[:, 1:513], in_=pp_, reads=[ppt], writes=[rawp_t])
                    op("dve", "tensor_tensor", out=TA, in0=RAWP[:, 0:512], in1=RAWP[:, 1:513], op=ALU.subtract,
                       reads=[rawp_t], writes=[ta_t])
                    op("dve", "scalar_tensor_tensor", out=dst[:, tok(tp)], in0=TA, scalar=mu, in1=RAWP[:, 1:513],
                       op0=ALU.mult, op1=ALU.add, reads=[ta_t, rawp_t, t_pp], writes=[dst_t[tp]])
                    op("act", "copy", out=RAWP[:, 0:1], in_=RAWP[:, 512:513], reads=[rawp_t], writes=[rawp_t])

            Kt, Bt, KKt, Rt, Vt = F[0], F[2], F[3], F[4], F[5]
            LW = F[1]
            proj_lerp(4 + hp, F[5], ft[5])
            if l == 0:
                dma("sp", "dma_start", out=vf_d[hp * 128:(hp + 1) * 128, :], in_=F[5], reads=ft[5], writes=[vf_t[hp]])
            else:
                proj_lerp(4 + (1 - hp), F[4], ft[4])
                vch = {hp: (F[5], ft[5]), 1 - hp: (F[4], ft[4])}
                for tp in range(4):
                    p32, p32t = PS.next()
                    for kc in range(2):
                        vv, vvt = vch[kc]
                        op("pe", "matmul", p32[0:32, :], lhsT=CSM[:, 512 + kc * 32:512 + kc * 32 + 32], rhs=vv[:, tok(tp)],
                           start=(kc == 0), stop=(kc == 1), reads=[t_csm, vvt[tp]], writes=[p32t])
                    op("act", "copy", out=TB[0:32, :], in_=p32[0:32, :], reads=[p32t], writes=[tb_t])
                    pg, pgt = PS.next()
                    op("pe", "matmul", pg, lhsT=CSM[0:32, 576 + hp * 128:576 + hp * 128 + 128], rhs=TB[0:32, :], start=True, stop=True,
                       reads=[t_csm, tb_t], writes=[pgt])
                    op("act", "activation", out=TB, in_=pg, func=AF.Sigmoid, bias=ppc(("v0", l, hp)), scale=1.0,
                       reads=[pgt, t_pp], writes=[tb_t])
                    dma("sp", "dma_start", out=TA, in_=vf_d[hp * 128:(hp + 1) * 128, tok(tp)], reads=[vf_t[hp]], writes=[ta_t])
                    op("dve", "tensor_tensor", out=TA, in0=TA, in1=F[5][:, tok(tp)], op=ALU.subtract, reads=[ta_t, ft[5][tp]], writes=[ta_t])
                    op("dve", "tensor_tensor", out=TA, in0=TA, in1=TB, op=ALU.mult, reads=[ta_t, tb_t], writes=[ta_t])
                    op("dve", "tensor_tensor", out=F[5][:, tok(tp)], in0=F[5][:, tok(tp)], in1=TA, op=ALU.add,
                       reads=[ta_t, ft[5][tp]], writes=[ft[5][tp]])
            proj_lerp(6, F[0], ft[0])
            for tp in range(4):
                op("act", "activation", out=F[0][0:64, tok(tp)], in_=F[0][0:64, tok(tp)], func=AF.Tanh, reads=[ft[0][tp]], writes=[ft[0][tp]])
                pw, pwt = PS.next()
                op("pe", "matmul", pw, lhsT=CSM[0:64, hp * 128:hp * 128 + 128], rhs=F[0][0:64, tok(tp)], start=True, stop=True,
                   reads=[t_csm, ft[0][tp]], writes=[pwt])
                op("act", "activation", out=F[1][:, tok(tp)], in_=pw, func=AF.Sigmoid, bias=ppc(("w0", l, hp)), scale=1.0,
                   reads=[pwt, t_pp], writes=[ft[1][tp]])
                op("pool", "tensor_scalar", out=F[1][:, tok(tp)], in0=F[1][:, tok(tp)], scalar1=-math.exp(-0.5), scalar2=None, op0=ALU.mult,
                   reads=[ft[1][tp]], writes=[ft[1][tp]])
                pa, pat = PS.next()
                op("pe", "matmul", pa, lhsT=CSM[64:128, hp * 128:hp * 128 + 128], rhs=F[0][64:128, tok(tp)], start=True, stop=True,
                   reads=[t_csm, ft[0][tp]], writes=[pat])
                op("act", "activation", out=F[2][:, tok(tp)], in_=pa, func=AF.Sigmoid, bias=ppc(("a0", l, hp)), scale=1.0,
                   reads=[pat, t_pp], writes=[ft[2][tp]])
            proj_lerp(2 + hp, F[0], ft[0])
            for tp in range(4):
                tk = tok(tp)
                op("dve", "tensor_scalar", out=F[3][:, tk], in0=F[0][:, tk], scalar1=ppc(("kk", l, hp)), scalar2=None, op0=ALU.mult,
                   reads=[ft[0][tp], t_pp], writes=[ft[3][tp]])
                op("act", "activation", out=TA, in_=F[3][:, tk], func=AF.Square, reads=[ft[3][tp]], writes=[ta_t])
                pq, pqt = PS.next()
                op("pe", "matmul", pq, lhsT=BLK, rhs=TA, start=True, stop=True, reads=[t_cst, ta_t], writes=[pqt])
                op("act", "activation", out=TB, in_=pq, func=AF.Sqrt, reads=[pqt], writes=[tb_t])
                op("dve", "tensor_scalar", out=TB, in0=TB, scalar1=1e-12, scalar2=None, op0=ALU.max, reads=[tb_t], writes=[tb_t])
                op("dve", "reciprocal", out=TB, in_=TB, reads=[tb_t], writes=[tb_t])
                op("dve", "tensor_tensor", out=F[3][:, tk], in0=F[3][:, tk], in1=TB, op=ALU.mult, reads=[ft[3][tp], tb_t], writes=[ft[3][tp]])
                op("dve", "tensor_scalar", out=TA, in0=F[2][:, tk], scalar1=-1.0, scalar2=ppc(("ka", l, hp)), op0=ALU.add, op1=ALU.mult,
                   reads=[ft[2][tp], t_pp], writes=[ta_t])
                op("dve", "scalar_tensor_tensor", out=F[0][:, tk], in0=TA, scalar=1.0, in1=F[0][:, tk], op0=ALU.add, op1=ALU.mult,
                   reads=[ta_t, ft[0][tp]], writes=[ft[0][tp]])
                op("pool", "tensor_tensor", out=F[2][:, tk], in0=F[2][:, tk], in1=F[3][:, tk], op=ALU.mult,
                   reads=[ft[2][tp], ft[3][tp]], writes=[ft[2][tp]])
            proj_lerp(hp, F[4], ft[4])
            for tp in range(4):
                tk = tok(tp)
                op("dve", "scalar_tensor_tensor", out=TA, in0=F[4][:, tk], scalar=ppc(("rk", l, hp)), in1=F[0][:, tk], op0=ALU.mult, op1=ALU.mult,
                   reads=[ft[4][tp], ft[0][tp], t_pp], writes=[ta_t])
                pq, pqt = PS.next()
                op("pe", "matmul", pq, lhsT=BLK, rhs=TA, start=True, stop=True, reads=[t_cst, ta_t], writes=[pqt])
                op("dve", "tensor_tensor", out=OT[:, oc, tk], in0=pq, in1=F[5][:, tk], op=ALU.mult, reads=[pqt, ft[5][tp]], writes=[ot_t[oc][tp]])
            for tp in range(4):
                tk = tok(tp)
                for cc in range(4):
                    cs = slice(tp * 512 + cc * 128, tp * 512 + cc * 128 + 128)
                    op("dve", "tensor_tensor_scan", out=TA[:, cc * 128:(cc + 1) * 128], data0=ONES, data1=F[1][:, cs], initial=0.0,
                       op0=ALU.mult, op1=ALU.add, reads=[t_cst, ft[1][tp]], writes=[ta_t])
                op("dve", "tensor_tensor", out=TB, in0=TA, in1=F[1][:, tk], op=ALU.subtract, reads=[ta_t, ft[1][tp]], writes=[tb_t])
                op("act", "activation", out=TB, in_=TB, func=AF.Exp, reads=[tb_t], writes=[tb_t])
                op("dve", "tensor_tensor", out=F[3][:, tk], in0=F[3][:, tk], in1=TB, op=ALU.mult, reads=[ft[3][tp], tb_t], writes=[ft[3][tp]])
                op("act", "activation", out=TB, in_=TA, func=AF.Exp, reads=[ta_t], writes=[tb_t])
                op("dve", "tensor_tensor", out=F[4][:, tk], in0=F[4][:, tk], in1=TB, op=ALU.mult, reads=[ft[4][tp], tb_t], writes=[ft[4][tp]])
                op("act", "copy", out=GC[:, tp * 4:tp * 4 + 4], in_=TB[:, 127:512:128], reads=[tb_t], writes=[gc_t])
                op("act", "activation", out=TB, in_=TA, func=AF.Exp, scale=-1.0, reads=[ta_t], writes=[tb_t])
                op("dve", "tensor_tensor", out=F[0][:, tk], in0=F[0][:, tk], in1=TB, op=ALU.mult, reads=[ft[0][tp], tb_t], writes=[ft[0][tp]])
                op("pool", "tensor_tensor", out=F[2][:, tk], in0=F[2][:, tk], in1=TB, op=ALU.mult, reads=[ft[2][tp], tb_t], writes=[ft[2][tp]])
            P.barrier()
            def cut(region, n, width):
                out = []
                for _ in range(n):
                    out.append(AR[:, region[0]:region[0] + width])
                    region[0] += width
                return out
            reg1 = [base + 2048]
            reg2 = [sm]
            KhF, BhF = cut(reg1, 2, 128)
            KhT, BhT, VT = cut(reg1, 3, 128)
            MB = cut(reg1, 2, 256)
            MK = cut(reg1, 2, 256)
            assert reg1[0] <= base + 4096
            inv = [cut(reg2, 6, 128) for _ in range(2)]
            Wsb = cut(reg2, 2, 64)
            Usb = cut(reg2, 2, 64)
            ST = cut(reg2, 1, 64)[0]
            YC = cut(reg2, 1, 128)[0]
            G1 = cut(reg2, 3, 128)
            assert reg2[0] <= sm + 2564, reg2[0]
            t_kh, t_bh, t_kht, t_bht, t_vt = TT(), TT(), TT(), TT(), TT()
            t_mb, t_mk = [TT(), TT()], [TT(), TT()]
            t_inv = [[TT() for _ in range(6)] for _ in range(2)]
            t_w, t_u = [TT(), TT()], [TT(), TT()]
            t_st = [TT(), TT()]
            t_yc = TT()
            t_g1 = [TT() for _ in range(3)]
            fall = [sum(ft[k], []) if False else ft[k] for k in range(6)]
            op("pool", "memset", ST, 0.0, writes=t_st)
            for c in range(16):
                cs = slice(c * 128, c * 128 + 128)
                tp = c // 4
                op("dve", "tensor_scalar", out=KhF, in0=F[0][:, cs], scalar1=GC[:, c:c + 1], scalar2=None, op0=ALU.mult,
                   reads=[ft[0][tp], gc_t], writes=[t_kh])
                op("pool", "tensor_scalar", out=BhF, in0=F[2][:, cs], scalar1=GC[:, c:c + 1], scalar2=None, op0=ALU.mult,
                   reads=[ft[2][tp], gc_t], writes=[t_bh])
                for (src, srct, dstT, dstt) in ((KhF, [t_kh], KhT, t_kht), (BhF, [t_bh], BhT, t_bht), (F[5][:, cs], [ft[5][tp]], VT, t_vt)):
                    ptr, ptrt = PS.next()
                    op("pe", "transpose", ptr[:, 0:128], src, IDENT, reads=srct + [t_cst], writes=[ptrt])
                    op("act", "copy", out=dstT, in_=ptr[:, 0:128], reads=[ptrt], writes=[dstt])
                RW = [slice(0, 64), slice(64, 128)]
                for hl in range(2):
                    rows = RW[hl]
                    for (lt, ltt, Mx, Mxt) in ((F[2], ft[2][tp], MB[hl], t_mb[hl]), (F[0], ft[0][tp], MK[hl], t_mk[hl])):
                        pmx, pmxt = PS.next()
                        op("pe", "matmul", pmx[:, 0:128], lhsT=lt[rows, cs], rhs=F[3][rows, cs], start=True, stop=True,
                           reads=[ltt, ft[3][tp]], writes=[pmxt])
                        op("pe", "matmul", pmx[:, 128:256], lhsT=lt[rows, cs], rhs=F[4][rows, cs], start=True, stop=True,
                           reads=[ltt, ft[4][tp]], writes=[pmxt])
                        op("dve", "tensor_tensor", out=Mx, in0=pmx[:, 0:256], in1=UPP, op=ALU.mult, reads=[pmxt, t_cst], writes=[Mxt])
                    Pm, PTm, Zm, Pm2, PTm2, Zm2 = inv[hl]
                    tPm, tPTm, tZm, tPm2, tPTm2, tZm2 = t_inv[hl]
                    op("act", "mul", out=Pm, in_=MB[hl][:, 0:128], mul=-1.0, reads=[t_mb[hl]], writes=[tPm])
                    pnt, pntt = PS.next()
                    op("pe", "matmul", pnt[:, 0:128], lhsT=F[3][rows, cs], rhs=F[2][rows, cs], start=True, stop=True,
                       reads=[ft[3][tp], ft[2][tp]], writes=[pntt])
                    op("dve", "scalar_tensor_tensor", out=PTm, in0=pnt[:, 0:128], scalar=-1.0, in1=LOWS, op0=ALU.mult, op1=ALU.mult,
                       reads=[pntt, t_cst], writes=[tPTm])
                    op("pool", "tensor_tensor", out=Zm, in0=Pm, in1=IDENT, op=ALU.add, reads=[tPm, t_cst], writes=[tZm])
                curs = [(inv[h][0], t_inv[h][0], inv[h][1], t_inv[h][1], inv[h][2], t_inv[h][2]) for h in range(2)]
                nxts = [(inv[h][3], t_inv[h][3], inv[h][4], t_inv[h][4], inv[h][5], t_inv[h][5]) for h in range(2)]
                for stg in range(6):
                    for hl in range(2):
                        cP, ctP, cPT, ctPT, cZ, ctZ = curs[hl]
                        nP, ntP, nPT, ntPT, nZ, ntZ = nxts[hl]
                        ppt2, ppt2t = PS.next()
                        op("pe", "matmul", ppt2[:, 0:128], lhsT=cP, rhs=cPT, start=True, stop=True, reads=[ctP, ctPT], writes=[ppt2t])
                        op("act", "copy", out=nPT, in_=ppt2[:, 0:128], reads=[ppt2t], writes=[ntPT])
                        if stg < 5:
                            pp2, pp2t = PS.next()
                            op("pe", "matmul", pp2[:, 0:128], lhsT=cPT, rhs=cP, start=True, stop=True, reads=[ctP, ctPT], writes=[pp2t])
                            op("pool" if False else "act", "copy", out=nP, in_=pp2[:, 0:128], reads=[pp2t], writes=[ntP])
                    for hl in range(2):
                        cP, ctP, cPT, ctPT, cZ, ctZ = curs[hl]
                        nP, ntP, nPT, ntPT, nZ, ntZ = nxts[hl]
                        pz, pzt = PS.next()
                        op("pe", "matmul", pz[:, 0:128], lhsT=nPT, rhs=cZ, start=True, stop=True, reads=[ntPT, ctZ], writes=[pzt])
                        op("dve", "tensor_tensor", out=nZ, in0=pz[:, 0:128], in1=cZ, op=ALU.add, reads=[pzt, ctZ], writes=[ntZ])
                        curs[hl], nxts[hl] = nxts[hl], curs[hl]
                TTs = [(curs[h][4], curs[h][5]) for h in range(2)]
                pws = []
                for hl in range(2):
                    rows, hc = RW[hl], RW[hl]
                    pw, pwt = PS.next()
                    op("pe", "matmul", pw[:, 0:64], lhsT=F[3][rows, cs], rhs=ST[rows, :], start=True, stop=False,
                       reads=[ft[3][tp], t_st[hl]], writes=[pwt])
                    op("pe", "matmul", pw[:, 0:64], lhsT=MK[hl][:, 0:128], rhs=VT[:, hc], start=False, stop=True,
                       reads=[t_mk[hl], t_vt], writes=[pwt])
                    op("act", "copy", out=Wsb[hl], in_=pw[:, 0:64], reads=[pwt], writes=[t_w[hl]])
                for hl in range(2):
                    pu, put = PS.next()
                    op("pe", "matmul", pu[:, 0:64], lhsT=TTs[hl][0], rhs=Wsb[hl], start=True, stop=True, reads=[TTs[hl][1], t_w[hl]], writes=[put])
                    op("act", "mul", out=Usb[hl], in_=pu[:, 0:64], mul=-1.0, reads=[put], writes=[t_u[hl]])
                for hl in range(2):
                    rows, hc = RW[hl], RW[hl]
                    py, pyt = PS.next()
                    op("pe", "matmul", py[0:64, 0:128], lhsT=ST[rows, :], rhs=F[4][rows, cs], start=True, stop=False,
                       reads=[t_st[hl], ft[4][tp]], writes=[pyt])
                    op("pe", "matmul", py[0:64, 0:128], lhsT=Usb[hl], rhs=MB[hl][:, 128:256], start=False, stop=False,
                       reads=[t_u[hl], t_mb[hl]], writes=[pyt])
                    op("pe", "matmul", py[0:64, 0:128], lhsT=VT[:, hc], rhs=MK[hl][:, 128:256], start=False, stop=True,
                       reads=[t_vt, t_mk[hl]], writes=[pyt])
                    op("act", "copy", out=YC[rows, :], in_=py[0:64, 0:128], reads=[pyt], writes=[t_yc])
                    pn, pnt_ = PS.next()
                    op("pe", "matmul", pn[0:64, 0:64], lhsT=BhT[:, hc], rhs=Usb[hl], start=True, stop=False,
                       reads=[t_bht, t_u[hl]], writes=[pnt_])
                    op("pe", "matmul", pn[0:64, 0:64], lhsT=KhT[:, hc], rhs=VT[:, hc], start=False, stop=True,
                       reads=[t_kht, t_vt], writes=[pnt_])
                    op("dve", "scalar_tensor_tensor", out=ST[rows, :], in0=ST[rows, :], scalar=GC[rows, c:c + 1], in1=pn[0:64, 0:64],
                       op0=ALU.mult, op1=ALU.add, reads=[t_st[hl], gc_t, pnt_], writes=[t_st[hl]])
                pmu, pmut = PS.next()
                op("pe", "matmul", pmu[:, 0:128], lhsT=BLK64, rhs=YC, start=True, stop=True, reads=[t_cst, t_yc], writes=[pmut])
                op("act", "activation", out=G1[0], in_=YC, func=AF.Square, reads=[t_yc], writes=[t_g1[0]])
                pe2, pe2t = PS.next()
                op("pe", "matmul", pe2[:, 0:128], lhsT=BLK64, rhs=G1[0], start=True, stop=True, reads=[t_cst, t_g1[0]], writes=[pe2t])
                op("act", "copy", out=G1[1], in_=pmu[:, 0:128], reads=[pmut], writes=[t_g1[1]])
                op("dve", "tensor_tensor", out=G1[0], in0=G1[1], in1=G1[1], op=ALU.mult, reads=[t_g1[1]], writes=[t_g1[0]])
                op("dve", "scalar_tensor_tensor", out=G1[2], in0=pe2[:, 0:128], scalar=GN_EPS, in1=G1[0], op0=ALU.add, op1=ALU.subtract,
                   reads=[pe2t, t_g1[0]], writes=[t_g1[2]])
                op("act", "activation", out=G1[2], in_=G1[2], func=AF.Ln, reads=[t_g1[2]], writes=[t_g1[2]])
                op("act", "activation", out=G1[2], in_=G1[2], func=AF.Exp, scale=-0.5, reads=[t_g1[2]], writes=[t_g1[2]])
                op("dve", "tensor_tensor", out=G1[0], in0=YC, in1=G1[1], op=ALU.subtract, reads=[t_yc, t_g1[1]], writes=[t_g1[0]])
                op("dve", "tensor_tensor", out=G1[0], in0=G1[0], in1=G1[2], op=ALU.mult, reads=[t_g1[0], t_g1[2]], writes=[t_g1[0]])
                op("act", "activation", out=G1[0], in_=G1[0], func=AF.Identity, bias=ppc(("gnb", l, hp)), scale=ppc(("gng", l, hp)),
                   reads=[t_g1[0], t_pp], writes=[t_g1[0]])
                op("pool", "tensor_tensor", out=OT[:, oc, cs], in0=OT[:, oc, cs], in1=G1[0], op=ALU.add,
                   reads=[ot_t[oc][tp], t_g1[0]], writes=[ot_t[oc][tp]])
            P.barrier()
            ft0 = [TT() for _ in range(4)]
            rawp_t2 = TT()
            w1b = Rot([AR[:, sm:sm + 1024].rearrange("p (kc f) -> p kc f", kc=8)])
            w, wt = w1b.next()
            dma("sp", "dma_start", out=w, in_=dr["wch"][l * NCHUNK_W + 24 + 7].rearrange("p (kc f) -> p kc f", kc=8), writes=[wt])
            op("pool", "memset", RAWP[:, 0:1], 0.0, writes=[rawp_t2])
            mu = ppc(("mu", l, 7))
            ta2, tb2 = TT(), TT()
            for tp in range(4):
                pp_, ppt = PS.next()
                for kc in range(8):
                    op("pe", "matmul", pp_, lhsT=w[:, kc, :], rhs=XF[:, kc, tok(tp)], start=(kc == 0), stop=(kc == 7),
                       reads=[wt, xt_t[kc][tp]], writes=[ppt])
                op("act", "copy", out=RAWP[:, 1:513], in_=pp_, reads=[ppt], writes=[rawp_t2])
                op("dve", "tensor_tensor", out=TA, in0=RAWP[:, 0:512], in1=RAWP[:, 1:513], op=ALU.subtract, reads=[rawp_t2], writes=[ta2])
                op("dve", "scalar_tensor_tensor", out=TB, in0=TA, scalar=mu, in1=RAWP[:, 1:513], op0=ALU.mult, op1=ALU.add,
                   reads=[ta2, rawp_t2, t_pp], writes=[tb2])
                op("act", "copy", out=RAWP[:, 0:1], in_=RAWP[:, 512:513], reads=[rawp_t2], writes=[rawp_t2])
                op("act", "activation", out=TB, in_=TB, func=AF.Sigmoid, reads=[tb2], writes=[tb2])
                pg, pgt = PS.next()
                op("pe", "matmul", pg, lhsT=CSM[:, 256 + hp * 128:256 + hp * 128 + 128], rhs=TB, start=True, stop=True,
                   reads=[t_csm, tb2], writes=[pgt])
                op("dve", "tensor_tensor", out=OT[:, oc, tok(tp)], in0=pg, in1=OT[:, oc, tok(tp)], op=ALU.mult,
                   reads=[pgt, ot_t[oc][tp]], writes=[ot_t[oc][tp]])
            P.end_phase(st)

        if stage in ("A", "mix", "full"):
            for i in range(3):
                phase_a(i)
        else:
            for c in range(0, 3):
                op("pool", "memset", OT[:, c, :], 0.0, writes=ot_t[c])
        if stage in ("B", "mix", "full"):
            for i in range(3):
                phase_b(i)
        else:
            for c in range(3, 6):
                op("pool", "memset", OT[:, c, :], 0.0, writes=ot_t[c])
        if stage in ("C", "mix", "full"):
            for hp in range(2):
                phase_c(hp)
        else:
            for c in range(6, 8):
                op("pool", "memset", OT[:, c, :], 0.0, writes=ot_t[c])
        return OT, ot_t, None

    def wout_ln(l, OT, ot_t, w_r):
        st = P.phase()
        WL = P.ph_sb(st, "WL", [128, 4096 + 1536 + 2048], F32)
        sc = Rot([WL[:, k * 512:(k + 1) * 512] for k in range(8)])
        lnk = Rot([WL[:, 4096 + k * 512:4096 + (k + 1) * 512] for k in range(3)])
        wo_r = Rot([WL[:, 5632 + k * 1024:5632 + (k + 1) * 1024].rearrange("p (kc f) -> p kc f", kc=8) for k in range(2)])
        for dc in range(8):
            w, wt = wo_r.next()
            dma("sp", "dma_start", out=w, in_=dr["wch"][l * NCHUNK_W + 38 + dc].rearrange("p (kc f) -> p kc f", kc=8), writes=[wt])
            for tp in range(4):
                py, pyt = PS.next()
                for kc in range(8):
                    op("pe", "matmul", py, lhsT=w[:, kc, :], rhs=OT[:, kc, tok(tp)], start=(kc == 0), stop=(kc == 7),
                       reads=[wt, ot_t[kc][tp]], writes=[pyt])
                op("dve", "scalar_tensor_tensor", out=XT[:, dc, tok(tp)], in0=py, scalar=1.0 / ALPHA, in1=XF[:, dc, tok(tp)],
                   op0=ALU.mult, op1=ALU.add, reads=[pyt, xt_t[dc][tp]], writes=[xt_t[dc][tp]])
        for tp in range(4):
            layer_norm(l, 1, tp, sc, lnk)
        return st

    def dump_o(OT, ot_t):
        for c in range(8):
            dma("sp", "dma_start", out=out_d[c * 128:(c + 1) * 128, :], in_=OT[:, c, :], reads=ot_t[c])

    def dump_x():
        for c in range(8):
            dma("sp", "dma_start", out=out_d[c * 128:(c + 1) * 128, :], in_=XF[:, c, :], reads=xt_t[c])

    dumped = False
    stage = dbg[1] if dbg else "full"
    for l in range(depth):
        st = ffn_phase(l, 0)
        if dbg == ("x", "ffn_a") and l == depth - 1:
            dump_x()
            dumped = True
            P.end_phase(st)
            break
        P.end_phase(st)
        stm = P.phase()
        OTt = P.ph_sb(stm, "OT", [128, 16384], F32)
        OT, ot_t, w_r = mixer_phase(l, stage)
        if dbg is not None and dbg[0] == "o" and l == depth - 1:
            dump_o(OT, ot_t)
            dumped = True
            P.end_phase(stm)
            break
        st = wout_ln(l, OT, ot_t, w_r)
        if dbg == ("x", "mixln") and l == depth - 1:
            dump_x()
            dumped = True
            P.end_phase(st)
            P.end_phase(stm)
            break
        P.end_phase(st)
        P.end_phase(stm)
        st = ffn_phase(l, 1)
        if l == depth - 1:
            dump_x()
            dumped = True
        P.end_phase(st)
    P.barrier()
    P.finish()
    return nc


_CACHE = {}


def kernel(**inputs):
    sh, xs, idx = prep_inputs(inputs)
    shapes = {k: v.shape for k, v in sh.items()}
    nc = build(shapes, idx)
    n = len(xs)
    in_maps = []
    for b in range(n):
        m = dict(sh)
        m["xT"] = xs[b]
        in_maps.append(m)
    res = run_bass_kernel_spmd(nc, in_maps, core_ids=list(range(n)))
    out = np.stack([np.ascontiguousarray(res.results[b]["out"].T) for b in range(n)], axis=0)
    return out.astype(np.float32)
```

```python
import contextlib
import math
import numpy as np
import concourse.bass as bass
import concourse.mybir as mybir
from concourse.bass_utils import run_bass_kernel_spmd

F32 = mybir.dt.float32
F32R = mybir.dt.float32r
AF = mybir.ActivationFunctionType
ALU = mybir.AluOpType
AX = mybir.AxisListType

DEPTH = 4
D = 1024
S = 2048
FF = 2816
NFC = 22
ALPHA = (2 * DEPTH) ** 0.25
LN_EPS = 1e-5
RMS_EPS = 1e-5
GN_EPS = 64e-5
THETA = 10000.0
NCHUNK_W = 38 + 8
ARC = 31300


class TT:
    __slots__ = ("w", "r")

    def __init__(self):
        self.w = None
        self.r = []


class Prog:
    ENG = ("pe", "act", "dve", "pool", "sp")

    def __init__(self, nc, same_sync=True, n_dma_sems=40):
        self.nc = nc
        self.same_sync = same_sync
        self.q = {e: [] for e in self.ENG}
        self.cnt = {e: 0 for e in self.ENG}
        self.known = {e: {} for e in self.ENG}
        self.stack = contextlib.ExitStack()
        self.sem = {}
        for e in self.ENG:
            self.sem[e] = self.stack.enter_context(nc.semaphore("s_" + e))
        self.dsem = []
        self.dval = []
        for i in range(n_dma_sems):
            self.dsem.append(self.stack.enter_context(nc.semaphore("d%d" % i)))
            self.dval.append(0)
        self.ndma = 0

    def sb(self, name, shape, dt=F32):
        return self.stack.enter_context(self.nc.sbuf_tensor(name, shape, dt))

    def ps(self, name, shape, dt=F32):
        return self.stack.enter_context(self.nc.psum_tensor(name, shape, dt))

    def _waits(self, eng, reads, writes):
        deps = {}
        for t in reads:
            if t.w is not None:
                k, v = t.w
                if deps.get(k, 0) < v:
                    deps[k] = v
        for t in writes:
            if t.w is not None:
                k, v = t.w
                if deps.get(k, 0) < v:
                    deps[k] = v
            for (k, v) in t.r:
                if deps.get(k, 0) < v:
                    deps[k] = v
        out = []
        kn = self.known[eng]
        for k, v in deps.items():
            if k == eng and (eng == "pe" or not self.same_sync):
                continue
            if kn.get(k, 0) >= v:
                continue
            kn[k] = v
            out.append((k, v))
        return out

    def _semh(self, k):
        return self.sem[k] if isinstance(k, str) else self.dsem[k]

    def _mark(self, tok, reads, writes):
        for t in reads:
            t.r.append(tok)
            if len(t.r) > 24:
                best = {}
                for k, v in t.r:
                    if best.get(k, 0) < v:
                        best[k] = v
                t.r = list(best.items())
        for t in writes:
            t.w = tok
            t.r = []

    def op(self, eng, fn, *args, reads=(), writes=(), **kwargs):
        waits = self._waits(eng, reads, writes)
        self.cnt[eng] += 1
        self._mark((eng, self.cnt[eng]), reads, writes)
        sem = self.sem[eng]
        wl = [(self._semh(k), v) for k, v in waits]

        def emit(e, fn=fn, wl=wl, sem=sem, args=args, kwargs=kwargs):
            for s, v in wl:
                e.wait_ge(s, v)
            getattr(e, fn)(*args, **kwargs).then_inc(sem, 1)

        self.q[eng].append(emit)

    def dma(self, eng, fn, *args, reads=(), writes=(), di=None, **kwargs):
        if di is None:
            di = self.ndma % len(self.dsem)
            self.ndma += 1
        waits = self._waits(eng, reads, writes)
        self.dval[di] += 16
        self._mark((di, self.dval[di]), reads, writes)
        sem = self.dsem[di]
        wl = [(self._semh(k), v) for k, v in waits]

        def emit(e, fn=fn, wl=wl, sem=sem, args=args, kwargs=kwargs):
            for s, v in wl:
                e.wait_ge(s, v)
            getattr(e, fn)(*args, **kwargs).then_inc(sem, 16)

        self.q[eng].append(emit)

    def barrier(self):
        for e in self.ENG:
            wl = []
            kn = self.known[e]
            for k in self.ENG:
                if k != e and self.cnt[k] > kn.get(k, 0):
                    kn[k] = self.cnt[k]
                    wl.append((self.sem[k], self.cnt[k]))
            for i, v in enumerate(self.dval):
                if v > kn.get(i, 0):
                    kn[i] = v
                    wl.append((self.dsem[i], v))
            if self.same_sync and e != "pe" and self.cnt[e] > kn.get(e, 0):
                kn[e] = self.cnt[e]
                wl.append((self.sem[e], self.cnt[e]))

            def emit(en, wl=wl):
                for s, v in wl:
                    en.wait_ge(s, v)

            self.q[e].append(emit)

    def phase(self):
        return contextlib.ExitStack()

    def ph_sb(self, st, name, shape, dt=F32):
        self.uid = getattr(self, "uid", 0) + 1
        return st.enter_context(self.nc.sbuf_tensor("%s_%d" % (name, self.uid), shape, dt))

    def end_phase(self, st):
        self.barrier()
        self.flush()
        st.close()

    def finish(self):
        self.flush()
        self.stack.close()

    def flush(self):
        nc = self.nc
        q = self.q
        self.q = {e: [] for e in self.ENG}
        with nc.Block() as block:
            @block.tensor
            def _(e):
                for f in q["pe"]:
                    f(e)

            @block.scalar
            def _(e):
                for f in q["act"]:
                    f(e)

            @block.vector
            def _(e):
                for f in q["dve"]:
                    f(e)

            @block.gpsimd
            def _(e):
                for f in q["pool"]:
                    f(e)

            @block.sync
            def _(e):
                for f in q["sp"]:
                    f(e)


class Rot:
    def __init__(self, views, tts=None):
        self.v = views
        self.t = tts if tts is not None else [TT() for _ in views]
        self.i = 0

    def next(self):
        i = self.i % len(self.v)
        self.i += 1
        return self.v[i], self.t[i]


def _chunk(W, cols):
    return np.ascontiguousarray(W[:, cols].reshape(8, 128, len(cols)).transpose(1, 0, 2))


def _swap_idx(base, n, grp):
    idx = np.arange(n)
    half = grp // 2
    return base + (idx // grp) * grp + ((idx % grp) + half) % grp


def _rope_table(grp):
    half = grp // 2
    inv = (THETA ** (-np.arange(0, grp, 2, dtype=np.float32) / grp)).astype(np.float32)
    pos = np.arange(S, dtype=np.float32)
    ang = (pos[:, None] * inv[None, :]).astype(np.float32)
    cos = np.cos(ang).astype(np.float32).T
    sin = np.sin(ang).astype(np.float32).T
    r = np.arange(128)
    i = r % half
    sign = np.where((r % grp) < half, -1.0, 1.0).astype(np.float32)
    C = cos[i]
    Sg = sin[i] * sign[:, None]
    t = np.stack([C, Sg], axis=1)
    t = t.reshape(128, 2, 4, 512).transpose(0, 2, 1, 3)
    return np.ascontiguousarray(t.reshape(128, 4 * 2 * 512)).astype(np.float32)


def _consts():
    c = {}
    k = np.arange(128)[:, None]
    q = np.arange(256)[None, :]
    c["causal"] = (q[:, :128] >= k).astype(np.float32)
    c["band"] = ((q >= k) & (q <= k + 128)).astype(np.float32)
    su = (q[:, :128] > k).astype(np.float32)
    iu = (q[:, :128] >= k).astype(np.float32)
    c["uppers"] = np.concatenate([su, iu], axis=1)
    c["ident"] = np.eye(128, dtype=np.float32)
    blk = np.zeros((128, 128), np.float32)
    blk[:64, :64] = 1.0
    blk[64:, 64:] = 1.0
    c["blk"] = blk
    c["lowers"] = (q[:, :128] < k).astype(np.float32)
    return c


def prep_inputs(inp):
    L = DEPTH
    f = lambda a: np.asarray(a, dtype=np.float32)
    sh = {}
    wgu = np.empty((L, 2, 22, 128, 2, 8, 128), np.float32)
    wd = np.empty((L, 2, 8, 128, 22, 128), np.float32)
    ffw = {"a": (inp["ffn_a_gate"], inp["ffn_a_up"], inp["ffn_a_down"]),
           "b": (inp["ffn_b_gate"], inp["ffn_b_up"], inp["ffn_b_down"])}
    for l in range(L):
        for i, nm in enumerate("ab"):
            g = f(ffw[nm][0][l]).reshape(8, 128, 22, 128).transpose(2, 1, 0, 3)
            u = f(ffw[nm][1][l]).reshape(8, 128, 22, 128).transpose(2, 1, 0, 3)
            wgu[l, i, :, :, 0] = g
            wgu[l, i, :, :, 1] = u
            wd[l, i] = f(ffw[nm][2][l]).reshape(22, 128, 8, 128).transpose(2, 1, 0, 3)
    sh["wgu"] = wgu.reshape(L * 2 * 22, 128, 2048)
    sh["wd"] = wd.reshape(L * 2 * 8, 128, 2816)
    wch = np.empty((L, NCHUNK_W, 128, 8, 128), np.float32)
    for l in range(L):
        W = f(inp["w_in"][l])
        ci = 0
        for base, grp in ((0, 32), (1152, 64)):
            for i in range(3):
                qc = base + 128 * i + np.arange(128)
                kc_ = base + 384 + 128 * i + np.arange(128)
                wch[l, ci + 0] = _chunk(W, qc)
                wch[l, ci + 1] = _chunk(W, _swap_idx(base + 128 * i, 128, grp))
                wch[l, ci + 2] = _chunk(W, kc_)
                wch[l, ci + 3] = _chunk(W, _swap_idx(base + 384 + 128 * i, 128, grp))
                ci += 4
        for c in range(8):
            wch[l, 24 + c] = _chunk(W, 2304 + 128 * c + np.arange(128))
        for i in range(3):
            wch[l, 32 + i] = _chunk(W, 768 + 128 * i + np.arange(128))
            wch[l, 35 + i] = _chunk(W, 1920 + 128 * i + np.arange(128))
        Wo = f(inp["w_out"][l])
        for dc in range(8):
            wch[l, 38 + dc] = _chunk(Wo, 128 * dc + np.arange(128))
    sh["wch"] = wch.reshape(L * NCHUNK_W, 128, 1024)
    sh["ropeA"] = _rope_table(32)
    sh["ropeB"] = _rope_table(64)
    for k_, v_ in _consts().items():
        sh["c_" + k_] = v_
    cols = []

    def addcol(v):
        cols.append(np.asarray(v, np.float32).reshape(128, 1))
        return len(cols) - 1

    idx = {}
    for l in range(L):
        for i in range(3):
            for c in range(8):
                idx[("lng", l, i, c)] = addcol(f(inp["ln_g"][l, i, c * 128:(c + 1) * 128]))
                idx[("lnb", l, i, c)] = addcol(f(inp["ln_b"][l, i, c * 128:(c + 1) * 128]))
        idx[("ga", l)] = addcol(np.tile(f(inp["a_norm_g"][l]), 2))
        idx[("gb", l)] = addcol(np.tile(f(inp["b_norm_g"][l]), 2))
        for c in range(8):
            idx[("mu", l, c)] = addcol(f(inp["c_mu"][l, c * 128:(c + 1) * 128]))
        for hp in range(2):
            sl = slice(hp * 128, hp * 128 + 128)
            idx[("w0", l, hp)] = addcol(f(inp["c_w0"][l, sl]))
            idx[("a0", l, hp)] = addcol(f(inp["c_a0"][l, sl]))
            idx[("kk", l, hp)] = addcol(f(inp["c_k_k"][l, sl]))
            idx[("ka", l, hp)] = addcol(f(inp["c_k_a"][l, sl]))
            idx[("rk", l, hp)] = addcol(f(inp["c_r_k"][l].reshape(256)[sl]))
            idx[("gng", l, hp)] = addcol(f(inp["c_gn_g"][l, sl]))
            idx[("gnb", l, hp)] = addcol(f(inp["c_gn_b"][l, sl]))
            if l > 0:
                idx[("v0", l, hp)] = addcol(f(inp["c_v0"][l - 1, sl]))
    sh["pp"] = np.ascontiguousarray(np.concatenate(cols, axis=1))
    lamrow = np.stack([np.stack([f(inp["a_lam_q1"][l]), f(inp["a_lam_k1"][l]),
                                 f(inp["a_lam_q2"][l]), f(inp["a_lam_k2"][l])]) for l in range(L)])
    sh["lamrow"] = np.ascontiguousarray(lamrow.reshape(1, L * 4 * 32))
    sm = np.zeros((L, 128, 256 + 256 + 64 + 256), np.float32)
    for l in range(L):
        sm[l, 0:64, 0:256] = f(inp["c_w2"][l])
        sm[l, 64:128, 0:256] = f(inp["c_a2"][l])
        sm[l, :, 256:512] = f(inp["c_g2"][l])
        if l > 0:
            sm[l, :, 512:576] = f(inp["c_v1"][l - 1]).reshape(2, 128, 32).transpose(1, 0, 2).reshape(128, 64)
            sm[l, 0:32, 576:832] = f(inp["c_v2"][l - 1])
    sh["csm"] = np.ascontiguousarray(sm.transpose(1, 0, 2).reshape(128, L * 832))
    xs = [np.ascontiguousarray(f(inp["x"][b]).T) for b in range(inp["x"].shape[0])]
    return sh, xs, idx


def build(shapes, idx, depth=DEPTH, dbg=None):
    nc = bass.Bass("TRN2", target_bir_lowering=False)
    dr = {}
    for k, shp in shapes.items():
        dr[k] = nc.dram_tensor(k, list(shp), F32, kind="ExternalInput").ap()
    xT_d = nc.dram_tensor("xT", [D, S], F32, kind="ExternalInput").ap()
    out_d = nc.dram_tensor("out", [D, S], F32, kind="ExternalOutput").ap()
    vf_d = nc.dram_tensor("vf_scratch", [256, S], F32, kind="Internal").ap()
    npp = shapes["pp"][1]

    import os as _os
    P = Prog(nc, same_sync=(_os.environ.get('NOSAME') is None))
    op, dma = P.op, P.dma
    XT = P.sb("XT", [128, 8, S], F32R)
    XF = XT[:].bitcast(F32)
    AR = None
    OTt = None
    CST = P.sb("CST", [128, 1472])
    PP = P.sb("PP", [128, npp])
    CSM = P.sb("CSM", [128, 832])
    LAM = P.sb("LAM", [128, 64 + DEPTH * 128 + 64])
    banks = [P.ps("pb%d" % i, [128, 512]) for i in range(8)]
    bank_t = [TT() for _ in range(8)]
    PS = Rot([b[:] for b in banks[0:6]], bank_t[0:6])
    PSL = Rot([b[:] for b in banks[6:8]], bank_t[6:8])
    PSA = Rot([b[:] for b in banks[0:4]], bank_t[0:4])
    PSLA = Rot([b[:] for b in banks[4:8]], bank_t[4:8])

    xt_t = [[TT() for _ in range(4)] for _ in range(8)]
    vf_t = [TT(), TT()]
    t_cst, t_pp, t_csm, t_lam, t_rst = TT(), TT(), TT(), TT(), TT()
    CAUS = CST[:, 0:128]
    BAND = CST[:, 128:384]
    UPP = CST[:, 384:640]
    IDENT = CST[:, 640:768]
    BLK = CST[:, 768:896]
    ONESD = CST[:, 896:1024]
    ONES64 = CST[:, 1024:1088]
    ONES = CST[:, 1088:1216]
    LOWS = CST[:, 1216:1344]
    BLK64 = CST[:, 1344:1472]
    dma("sp", "dma_start", out=CAUS, in_=dr["c_causal"], writes=[t_cst])
    dma("sp", "dma_start", out=BAND, in_=dr["c_band"], writes=[t_cst])
    dma("sp", "dma_start", out=UPP, in_=dr["c_uppers"], writes=[t_cst])
    dma("sp", "dma_start", out=IDENT, in_=dr["c_ident"], writes=[t_cst])
    dma("sp", "dma_start", out=BLK, in_=dr["c_blk"], writes=[t_cst])
    dma("sp", "dma_start", out=PP[:], in_=dr["pp"], writes=[t_pp])
    op("dve", "memset", ONESD, 1.0 / 1024.0, writes=[t_cst])
    op("dve", "memset", ONES64, 1.0 / 64.0, writes=[t_cst])
    op("dve", "memset", ONES, 1.0, writes=[t_cst])
    dma("sp", "dma_start", out=LOWS, in_=dr["c_lowers"], writes=[t_cst])
    op("act", "mul", out=BLK64, in_=BLK, mul=1.0 / 64.0, reads=[t_cst], writes=[t_cst])
    for c in range(8):
        dma("pool", "dma_start", out=XT[:, c, :], in_=xT_d[c * 128:(c + 1) * 128, :].bitcast(F32R),
            writes=xt_t[c])
    ONE1 = LAM[:, 0:64]
    op("dve", "memset", LAM[:], 0.0, writes=[t_lam])
    op("dve", "memset", ONE1, 1.0, writes=[t_lam])
    LROW = LAM[0:1, 64:64 + DEPTH * 128]
    dma("sp", "dma_start", out=LROW, in_=dr["lamrow"], writes=[t_lam])
    NEGLAM = LAM[:, 64 + DEPTH * 128:64 + DEPTH * 128 + 8]
    LS = LAM[0:1, 64 + DEPTH * 128 + 8:64 + DEPTH * 128 + 64]
    for l in range(depth):
        b0 = 64 + l * 128
        lam_init = 0.8 - 0.6 * math.exp(-0.3 * l)
        for j in range(2):
            op("dve", "tensor_tensor", out=LS[:, 0:32], in0=LAM[0:1, b0 + 64 * j:b0 + 64 * j + 32],
                                                          in1=LAM[0:1, b0 + 64 * j + 32:b0 + 64 * j + 64], op=ALU.mult,
               reads=[t_lam], writes=[t_lam])
            op("dve", "reduce_sum", out=LS[:, 32 + j:33 + j], in_=LS[:, 0:32], axis=AX.X,
               reads=[t_lam], writes=[t_lam])
        op("act", "activation", out=LS[:, 34:36], in_=LS[:, 32:34], func=AF.Exp, reads=[t_lam], writes=[t_lam])
        op("dve", "scalar_tensor_tensor", out=LS[:, 36:37], in0=LS[:, 35:36], scalar=-lam_init, in1=LS[:, 34:35],
                                                                op0=ALU.add, op1=ALU.subtract, reads=[t_lam], writes=[t_lam])
        pb, pt = PS.next()
        op("pe", "matmul", pb[0:64, 0:1], lhsT=LAM[0:1, 0:64], rhs=LS[:, 36:37], start=True, stop=True,
           reads=[t_lam], writes=[pt])
        op("act", "copy", out=NEGLAM[0:64, l:l + 1], in_=pb[0:64, 0:1], reads=[pt], writes=[t_lam])

    def ppc(key):
        j = idx[key]
        return PP[:, j:j + 1]

    def tok(tp):
        return slice(tp * 512, (tp + 1) * 512)

    def ssl(st, n, step):
        return slice(st, st + step * (n - 1) + 1, step) if step > 1 else slice(st, st + n)

    def layer_norm(l, i, tp, sc, lnk):
        eps = LN_EPS / (ALPHA * ALPHA)
        pm, pmt = PS.next()
        pe2, pe2t = PS.next()
        for c in range(8):
            op("pe", "matmul", pm, lhsT=ONESD, rhs=XF[:, c, tok(tp)], start=(c == 0), stop=(c == 7),
               reads=[t_cst, xt_t[c][tp]], writes=[pmt])
        for c in range(8):
            sq, sqt = sc.next()
            op("act", "activation", out=sq, in_=XF[:, c, tok(tp)], func=AF.Square,
               reads=[xt_t[c][tp]], writes=[sqt])
            op("pe", "matmul", pe2, lhsT=ONESD, rhs=sq, start=(c == 0), stop=(c == 7),
               reads=[t_cst, sqt], writes=[pe2t])
        mean, meant = lnk.next()
        op("act", "copy", out=mean, in_=pm, reads=[pmt], writes=[meant])
        msq, msqt = lnk.next()
        op("dve", "tensor_tensor", out=msq, in0=mean, in1=mean, op=ALU.mult, reads=[meant], writes=[msqt])
        var, vart = lnk.next()
        op("dve", "scalar_tensor_tensor", out=var, in0=pe2, scalar=eps, in1=msq, op0=ALU.add, op1=ALU.subtract,
           reads=[pe2t, msqt], writes=[vart])
        op("act", "activation", out=var, in_=var, func=AF.Ln, reads=[vart], writes=[vart])
        op("act", "activation", out=var, in_=var, func=AF.Exp, scale=-0.5, reads=[vart], writes=[vart])
        for c in range(8):
            t1, t1t = sc.next()
            op("dve", "tensor_tensor", out=t1, in0=XF[:, c, tok(tp)], in1=mean, op=ALU.subtract,
               reads=[xt_t[c][tp], meant], writes=[t1t])
            op("dve", "tensor_tensor", out=t1, in0=t1, in1=var, op=ALU.mult,
               reads=[t1t, vart], writes=[t1t])
            op("act", "activation", out=XT[:, c, tok(tp)], in_=t1, func=AF.Identity,
                                                        bias=ppc(("lnb", l, i, c)), scale=ppc(("lng", l, i, c)),
               reads=[t1t, t_pp], writes=[xt_t[c][tp]])

    def ffn_phase(l, i):
        st = P.phase()
        FR = P.ph_sb(st, "FR", [128, 11264 + 4 * 2048 + 2 * 2816], F32R)
        FT = P.ph_sb(st, "FT", [128, 11 * 512], F32)
        o = 0
        Hv = FR[:, o:o + 22 * 512].rearrange("p (f t) -> p f t", t=512)
        o += 22 * 512
        h_t = [TT() for _ in range(22)]
        wgu_r = Rot([FR[:, o + k * 2048:o + (k + 1) * 2048].rearrange("p (g kc f) -> p g kc f", g=2, kc=8) for k in range(4)])
        o += 4 * 2048
        wd_r = Rot([FR[:, o + k * 2816:o + (k + 1) * 2816].rearrange("p (f d) -> p f d", d=128) for k in range(2)])
        o += 2 * 2816
        sc = Rot([FT[:, k * 512:(k + 1) * 512] for k in range(8)])
        lnk = Rot([FT[:, (8 + k) * 512:(9 + k) * 512] for k in range(3)])
        lnidx = 0 if i == 0 else 2
        def p1(tp):
            for f in range(22):
                wb, wbt = wgu_r.next()
                dma("pool", "dma_start", out=wb,
                    in_=dr["wgu"][(l * 2 + i) * 22 + f].bitcast(F32R).rearrange("p (g kc f) -> p g kc f", g=2, kc=8), writes=[wbt])
                pg, pgt = PS.next()
                pu, put = PS.next()
                for g, pp_, ppt in ((0, pg, pgt), (1, pu, put)):
                    for kc in range(8):
                        op("pe", "matmul", pp_, lhsT=wb[:, g, kc, :], rhs=XT[:, kc, tok(tp)], start=(kc == 0), stop=(kc == 7),
                           reads=[wbt, xt_t[kc][tp]], writes=[ppt])
                sg, sgt = sc.next()
                op("act", "activation", out=sg, in_=pg, func=AF.Silu, reads=[pgt], writes=[sgt])
                op("dve", "tensor_tensor", out=Hv[:, f, :], in0=sg, in1=pu, op=ALU.mult, reads=[sgt, put], writes=[h_t[f]])

        def p2(tp):
            for dc in range(8):
                wb, wbt = wd_r.next()
                dma("pool", "dma_start", out=wb,
                    in_=dr["wd"][(l * 2 + i) * 8 + dc].bitcast(F32R).rearrange("p (f d) -> p f d", d=128), writes=[wbt])
                py, pyt = PS.next()
                for f in range(22):
                    op("pe", "matmul", py, lhsT=wb[:, f, :], rhs=Hv[:, f, :], start=(f == 0), stop=(f == 21),
                       reads=[wbt, h_t[f]], writes=[pyt])
                op("dve", "scalar_tensor_tensor", out=XT[:, dc, tok(tp)], in0=py, scalar=0.5 / ALPHA, in1=XF[:, dc, tok(tp)],
                   op0=ALU.mult, op1=ALU.add, reads=[pyt, xt_t[dc][tp]], writes=[xt_t[dc][tp]])

        for tp in range(4):
            p1(tp)
            if tp > 0:
                layer_norm(l, lnidx, tp - 1, sc, lnk)
            p2(tp)
        layer_norm(l, lnidx, 3, sc, lnk)
        return st

    def mixer_phase(l, stage):
        OT = OTt[:].rearrange("p (c t) -> p c t", t=S)
        ot_t = [[TT() for _ in range(4)] for _ in range(8)]
        QT = KT = KZ = VA = RS = None
        w_r = rope_r = e512 = e256 = scr = None
        SCR0 = None
        qt_t = kt_t = None
        va_t = kz_t = None

        def setup_ab(is_a):
            nonlocal QT, KT, KZ, VA, RS, w_r, rope_r, e512, e256, scr, SCR0, qt_t, kt_t, va_t, kz_t
            st = P.phase()
            AQ = P.ph_sb(st, "AQ", [128, 11808 if is_a else 9760 + 768], F32R)
            ABp = P.ph_sb(st, "ABp", [128, 3072 if is_a else 5120], F32)
            o = 0
            QT = AQ[:, o:o + 2048]
            KT = AQ[:, o + 2048:o + 4096]
            o += 4096
            qt_t = [TT() for _ in range(4)]
            kt_t = [TT() for _ in range(4)]
            VA = AQ[:, o:o + 2080].rearrange("p (j h d) -> p j h d", h=2, d=65)
            va_t = TT()
            o += 2080
            w_r = Rot([AQ[:, o + k * 1024:o + (k + 1) * 1024].rearrange("p (kc f) -> p kc f", kc=8) for k in range(2)])
            o += 2048
            e512 = Rot([AQ[:, o + k * 512:o + (k + 1) * 512] for k in range(3)])
            e256 = Rot([AQ[:, o + k * 256:o + (k + 1) * 256] for k in range(6 if is_a else 9)])
            o += 1536 if is_a else 2304
            if is_a:
                KZ = AQ[:, o:o + 2048]
                kz_t = TT()
                o += 2048
            rope_r = Rot([ABp[:, 0:1024].rearrange("p (c t) -> p c t", c=2)])
            scr = Rot([ABp[:, 1024 + k * 512:1024 + (k + 1) * 512] for k in range(4)])
            SCR0 = ABp
            if not is_a:
                RS = ABp[:, 3072:5120]
            return st

        lam_init = 0.8 - 0.6 * math.exp(-0.3 * l)

        def load_w(ci):
            w, wt = w_r.next()
            dma("pool", "dma_start", out=w, in_=dr["wch"][l * NCHUNK_W + ci].bitcast(F32R).rearrange("p (kc f) -> p kc f", kc=8),
                writes=[wt])
            return w, wt

        def proj_rope(ci, dst, dst_t, rname):
            w1, w1t = load_w(ci)
            w2, w2t = load_w(ci + 1)
            for tp in range(4):
                rp, rpt = rope_r.next()
                dma("sp", "dma_start", out=rp, in_=dr[rname][:, tp * 1024:(tp + 1) * 1024].rearrange("p (c t) -> p c t", c=2),
                    writes=[rpt])
                p1, p1t = PS.next()
                p2, p2t = PS.next()
                for w, wt, pp_, ppt in ((w1, w1t, p1, p1t), (w2, w2t, p2, p2t)):
                    for kc in range(8):
                        op("pe", "matmul", pp_, lhsT=w[:, kc, :], rhs=XT[:, kc, tok(tp)], start=(kc == 0), stop=(kc == 7),
                           reads=[wt, xt_t[kc][tp]], writes=[ppt])
                a, at = scr.next()
                b, bt = scr.next()
                op("dve", "tensor_tensor", out=a, in0=p1, in1=rp[:, 0, :], op=ALU.mult, reads=[p1t, rpt], writes=[at])
                op("dve", "tensor_tensor", out=b, in0=p2, in1=rp[:, 1, :], op=ALU.mult, reads=[p2t, rpt], writes=[bt])
                op("pool", "tensor_tensor", out=dst[:, tok(tp)], in0=a, in1=b, op=ALU.add, reads=[at, bt], writes=[dst_t[tp]])

        def proj_v(ci, dil):
            wv, wvt = load_w(ci)
            op("act", "copy", out=VA[:, :, :, 64:65], in_=ONES[:, 0:32].rearrange("p (j h d) -> p j h d", h=2, d=1),
               reads=[t_cst], writes=[va_t])
            nt = 16 // dil
            for r in range(dil):
                for a in range(nt):
                    st = r + dil * 128 * a
                    pv, pvt = PS.next()
                    for kc in range(8):
                        lhs = XT[:, kc, ssl(st, 128, dil)]
                        op("pe", "matmul", pv[:, 0:128], lhsT=lhs, rhs=wv[:, kc, :], start=(kc == 0), stop=(kc == 7),
                           reads=[wvt] + xt_t[kc], writes=[pvt])
                    op("act", "copy", out=VA[:, r * nt + a, :, 0:64], in_=pv[:, 0:128].rearrange("p (h d) -> p h d", h=2),
                       reads=[pvt], writes=[va_t])

        def phase_a(i):
            st = setup_ab(True)
            proj_rope(4 * i, QT, qt_t, "ropeA")
            proj_rope(4 * i + 2, KT, kt_t, "ropeA")
            proj_v(32 + i, 1)
            op("pool", "tensor_copy", out=KZ[96:128, :], in_=KT[96:128, :].bitcast(F32), reads=kt_t, writes=[kz_t])
            op("pool", "tensor_scalar", out=KZ[64:96, :], in0=KT[64:96, :].bitcast(F32), scalar1=0.0, scalar2=None, op0=ALU.mult,
               reads=kt_t, writes=[kz_t])
            sca = 32 ** -0.5
            bg = []

            def step_bg():
                if bg:
                    try:
                        next(bg[0])
                    except StopIteration:
                        bg.pop(0)

            def drain_bg(keep=0):
                while len(bg) > keep:
                    step_bg()

            def main_block(hl, Qp, nums):
                for m in range(2):
                    r0 = 32 * (2 * hl + m)
                    num, numt = PSLA.next()
                    nums.append((num, numt))
                    last = 4 * Qp + 3
                    pend = []

                    def do_pv(item, num=num, numt=numt, last=last):
                        j, E, Et, c0, n = item
                        op("pe", "matmul", num[0:65, c0 - 512 * Qp:512], lhsT=VA[:, j, hl, :], rhs=E[:, 0:n],
                           start=(j == 0), stop=(j == last), reads=[va_t, Et], writes=[numt])

                    for j in range(last + 1):
                        c0 = max(128 * j, 512 * Qp)
                        n = 512 * Qp + 512 - c0
                        se, set_ = PSA.next()
                        if hl == 1 and m == 1:
                            op("pe", "matmul", se[:, 0:n], lhsT=KZ[64:128, 128 * j:128 * j + 128], rhs=QT[64:128, c0:c0 + n],
                               start=True, stop=True, reads=[kz_t] + qt_t[c0 // 512:Qp + 1], writes=[set_])
                        else:
                            op("pe", "matmul", se[:, 0:n], lhsT=KT[r0:r0 + 32, 128 * j:128 * j + 128], rhs=QT[r0:r0 + 32, c0:c0 + n],
                               start=True, stop=True, reads=[kt_t[j // 4]] + qt_t[c0 // 512:Qp + 1], writes=[set_])
                        E, Et = e512.next()
                        op("act", "activation", out=E[:, 0:n], in_=se[:, 0:n], func=AF.Exp, scale=sca, reads=[set_], writes=[Et])
                        if 128 * j >= 512 * Qp:
                            op("pool", "tensor_tensor", out=E[:, 0:128], in0=E[:, 0:128].bitcast(F32), in1=CAUS, op=ALU.mult,
                               reads=[Et, t_cst], writes=[Et])
                        pend.append((j, E, Et, c0, n))
                        if len(pend) > 2:
                            do_pv(pend.pop(0))
                        yield
                    while pend:
                        do_pv(pend.pop(0))
                        yield

            def post_block(hl, Qp, nums):
                (n1, n1t), (n2, n2t) = nums
                X0, X0t = scr.next()
                X1, X1t = scr.next()
                X2, X2t = scr.next()
                for (nn, nnt, prt) in ((n1, n1t, 64), (n2, n2t, 32)):
                    op("act", "activation", out=X0[prt:prt + 1, :], in_=nn[64:65, :], func=AF.Ln, reads=[nnt], writes=[X0t])
                    yield
                    op("act", "activation", out=X0[prt:prt + 1, :], in_=X0[prt:prt + 1, :], func=AF.Exp, scale=-1.0,
                       reads=[X0t], writes=[X0t])
                    yield
                for (prt, Xd, Xdt) in ((64, X1, X1t), (32, X2, X2t)):
                    pb, pbt = PSA.next()
                    op("pe", "matmul", pb[0:64, :], lhsT=ONE1[prt:prt + 1, 0:64], rhs=X0[prt:prt + 1, :], start=True, stop=True,
                       reads=[t_lam, X0t], writes=[pbt])
                    yield
                    op("act", "copy", out=Xd[0:64, :], in_=pb[0:64, :], reads=[pbt], writes=[Xdt])
                    yield
                op("dve", "tensor_tensor", out=X1[0:64, :], in0=n1[0:64, :], in1=X1[0:64, :], op=ALU.mult, reads=[n1t, X1t], writes=[X1t])
                yield
                op("dve", "tensor_tensor", out=X2[0:64, :], in0=n2[0:64, :], in1=X2[0:64, :], op=ALU.mult, reads=[n2t, X2t], writes=[X2t])
                yield
                op("dve", "scalar_tensor_tensor", out=X1[0:64, :], in0=X2[0:64, :], scalar=NEGLAM[0:64, l:l + 1], in1=X1[0:64, :],
                   op0=ALU.mult, op1=ALU.add, reads=[X1t, X2t, t_lam], writes=[X1t])
                yield
                op("act", "activation", out=X2[0:64, :], in_=X1[0:64, :], func=AF.Square, reads=[X1t], writes=[X2t])
                yield
                pm, pmt = PSA.next()
                op("pe", "matmul", pm[0:64, :], lhsT=ONES64[0:64, 0:64], rhs=X2[0:64, :], start=True, stop=True,
                   reads=[t_cst, X2t], writes=[pmt])
                yield
                op("dve", "tensor_scalar", out=X2[0:64, :], in0=pm[0:64, :], scalar1=RMS_EPS, scalar2=None, op0=ALU.add,
                   reads=[pmt], writes=[X2t])
                yield
                op("act", "activation", out=X2[0:64, :], in_=X2[0:64, :], func=AF.Ln, reads=[X2t], writes=[X2t])
                yield
                op("act", "activation", out=X2[0:64, :], in_=X2[0:64, :], func=AF.Exp, scale=-0.5, reads=[X2t], writes=[X2t])
                yield
                op("dve", "scalar_tensor_tensor", out=X1[0:64, :], in0=X1[0:64, :], scalar=ppc(("ga", l))[0:64, :], in1=X2[0:64, :],
                   op0=ALU.mult, op1=ALU.mult, reads=[X1t, X2t, t_pp], writes=[X1t])
                yield
                op("dve", "tensor_scalar", out=OT[64 * hl:64 * hl + 64, i, tok(Qp)], in0=X1[0:64, :], scalar1=1.0 - lam_init,
                   scalar2=None, op0=ALU.mult, reads=[X1t], writes=[ot_t[i][Qp]])
                yield

            for hl in range(2):
                for Qp in range(4):
                    nums = []
                    drain_bg(keep=1)
                    cnt = 0
                    for _ in main_block(hl, Qp, nums):
                        cnt += 1
                        if cnt % 2 == 0:
                            step_bg()
                    drain_bg(keep=0) if False else None
                    bg.append(post_block(hl, Qp, nums))
            drain_bg(0)
            P.end_phase(st)


        def phase_b(i):
            st = setup_ab(False)
            rs_t = [TT(), TT()]
            proj_rope(12 + 4 * i, QT, qt_t, "ropeB")
            proj_rope(12 + 4 * i + 2, KT, kt_t, "ropeB")
            scb = 64 ** -0.5
            oc = 3 + i
            for dil in (1, 4, 16):
                proj_v(35 + i, dil)
                L = S // dil
                nt = L // 128
                pieces = []
                for hl in range(2):
                    for r in range(dil):
                        Es = {}
                        for g in range((nt + 3) // 4):
                            pieces.append((hl, r, g, Es))

                def stage1(pc):
                    hl, r, g, Es = pc
                    rows = slice(64 * hl, 64 * hl + 64)
                    for a in range(max(4 * g - 1, 0), min(4 * g + 3, nt - 1) + 1):
                        if a in Es:
                            continue
                        nq = min(256, L - 128 * a)
                        st_ = r + dil * 128 * a
                        se, set_ = PS.next()
                        op("pe", "matmul", se[:, 0:nq], lhsT=KT[rows, ssl(st_, 128, dil)], rhs=QT[rows, ssl(st_, nq, dil)],
                           start=True, stop=True, reads=kt_t + qt_t, writes=[set_])
                        E, Et = e256.next()
                        op("act", "activation", out=E[:, 0:nq], in_=se[:, 0:nq], func=AF.Exp, scale=scb, reads=[set_], writes=[Et])
                        op("pool", "tensor_tensor", out=E[:, 0:nq], in0=E[:, 0:nq].bitcast(F32), in1=BAND[:, 0:nq], op=ALU.mult,
                           reads=[Et, t_cst], writes=[Et])
                        Es[a] = (E, Et, nq)

                def stage2(pc):
                    hl, r, g, Es = pc
                    rows = slice(64 * hl, 64 * hl + 64)
                    rowp = 64 if hl == 0 else 32
                    ncols = min(512, L - 512 * g)
                    alist = list(range(max(4 * g - 1, 0), min(4 * g + 3, nt - 1) + 1))
                    num, numt = PSL.next()
                    for ai, a in enumerate(alist):
                        E, Et, nq = Es[a]
                        lo = max(128 * a, 512 * g)
                        hi = min(128 * a + nq, 512 * g + ncols)
                        op("pe", "matmul", num[0:65, lo - 512 * g:hi - 512 * g], lhsT=VA[:, r * nt + a, hl, :],
                           rhs=E[:, lo - 128 * a:hi - 128 * a], start=(ai == 0), stop=(ai == len(alist) - 1),
                           reads=[va_t, Et], writes=[numt])
                    p0 = r + dil * 512 * g
                    ps_ = ssl(p0, ncols, dil)
                    if dil == 1:
                        op("dve", "tensor_copy", out=OT[rows, oc, ps_], in_=num[0:64, 0:ncols], reads=[numt], writes=ot_t[oc])
                        op("act", "copy", out=RS[rowp:rowp + 1, ps_], in_=num[64:65, 0:ncols], reads=[numt], writes=[rs_t[hl]])
                    else:
                        op("dve", "tensor_tensor", out=OT[rows, oc, ps_], in0=num[0:64, 0:ncols], in1=OT[rows, oc, ps_], op=ALU.add,
                           reads=[numt] + ot_t[oc], writes=ot_t[oc])
                        op("dve", "tensor_tensor", out=RS[rowp:rowp + 1, ps_], in0=num[64:65, 0:ncols], in1=RS[rowp:rowp + 1, ps_],
                           op=ALU.add, reads=[numt, rs_t[hl]], writes=[rs_t[hl]])

                stage1(pieces[0])
                for k in range(len(pieces)):
                    if k + 1 < len(pieces):
                        stage1(pieces[k + 1])
                    stage2(pieces[k])
            P.barrier()
            chains = []

            def post_chain(hl, tp, X, Xt):
                rows = slice(64 * hl, 64 * hl + 64)
                rowp = 64 if hl == 0 else 32
                op("act", "activation", out=X[rows, :], in_=OT[rows, oc, tok(tp)], func=AF.Square, reads=[ot_t[oc][tp]], writes=[Xt])
                yield
                op("act", "activation", out=X[rowp:rowp + 1, :], in_=RS[rowp:rowp + 1, tok(tp)], func=AF.Square,
                   scale=math.sqrt(RMS_EPS), reads=[rs_t[hl]], writes=[Xt])
                yield
                pm, pmt = PS.next()
                op("pe", "matmul", pm[0:64, :], lhsT=ONES64[rows, 0:64], rhs=X[rows, :], start=True, stop=False,
                   reads=[t_cst, Xt], writes=[pmt])
                op("pe", "matmul", pm[0:64, :], lhsT=ONE1[rowp:rowp + 1, 0:64], rhs=X[rowp:rowp + 1, :], start=False, stop=True,
                   reads=[t_lam, Xt], writes=[pmt])
                yield
                op("act", "activation", out=X[rows, :], in_=pm[0:64, :], func=AF.Ln, reads=[pmt], writes=[Xt])
                yield
                op("act", "activation", out=X[rows, :], in_=X[rows, :], func=AF.Exp, scale=-0.5, reads=[Xt], writes=[Xt])
                yield
                op("dve", "scalar_tensor_tensor", out=OT[rows, oc, tok(tp)], in0=OT[rows, oc, tok(tp)], scalar=ppc(("gb", l))[rows, :],
                   in1=X[rows, :], op0=ALU.mult, op1=ALU.mult, reads=[ot_t[oc][tp], Xt, t_pp], writes=[ot_t[oc][tp]])
                yield

            xt4 = [(SCR0[:, 1024 + k * 512:1024 + (k + 1) * 512], TT()) for k in range(4)]
            for hl in range(2):
                gens = [post_chain(hl, tp, xt4[tp][0], xt4[tp][1]) for tp in range(4)]
                while gens:
                    for gsn in list(gens):
                        try:
                            next(gsn)
                        except StopIteration:
                            gens.remove(gsn)
            P.end_phase(st)


        def phase_c(hp):
            st = P.phase()
            AR = P.ph_sb(st, "CF", [128, 14916], F32)
            oc = 6 + hp
            base = 0
            F = [AR[:, base + k * 2048:base + (k + 1) * 2048] for k in range(6)]
            ft = [[TT() for _ in range(4)] for _ in range(6)]
            sm = base + 6 * 2048
            w1 = Rot([AR[:, sm:sm + 1024].rearrange("p (kc f) -> p kc f", kc=8)])
            RAWP = AR[:, sm + 1024:sm + 1024 + 516]
            rawp_t = TT()
            TA = AR[:, sm + 1540:sm + 2052]
            TB = AR[:, sm + 2052:sm + 2564]
            ta_t, tb_t = TT(), TT()
            GC = AR[:, sm + 2564:sm + 2580]
            gc_t = TT()
            assert sm + 2580 <= 14916
            dma("sp", "dma_start", out=CSM[:], in_=dr["csm"][:, l * 832:(l + 1) * 832], writes=[t_csm])

            def loadw1(ci):
                w, wt = w1.next()
                dma("sp", "dma_start", out=w, in_=dr["wch"][l * NCHUNK_W + ci].rearrange("p (kc f) -> p kc f", kc=8), writes=[wt])
                return w, wt

            def proj_lerp(cidx, dst, dst_t):
                w, wt = loadw1(24 + cidx)
                op("pool", "memset", RAWP[:, 0:1], 0.0, writes=[rawp_t])
                mu = ppc(("mu", l, cidx))
                for tp in range(4):
                    pp_, ppt = PS.next()
                    for kc in range(8):
                        op("pe", "matmul", pp_, lhsT=w[:, kc, :], rhs=XF[:, kc, tok(tp)], start=(kc == 0), stop=(kc == 7),
                           reads=[wt, xt_t[kc][tp]], writes=[ppt])
                    op("act", "copy", out=RAWP[:, 1:513], in_=pp_, reads=[ppt], writes=[rawp_t])
                    op("dve", "tensor_tensor", out=TA, in0=RAWP[:, 0:512], in1=RAWP[:, 1:513], op=ALU.subtract,
                       reads=[rawp_t], writes=[ta_t])
                    op("dve", "scalar_tensor_tensor", out=dst[:, tok(tp)], in0=TA, scalar=mu, in1=RAWP[:, 1:513],
                       op0=ALU.mult, op1=ALU.add, reads=[ta_t, rawp_t, t_pp], writes=[dst_t[tp]])
                    op("act", "copy", out=RAWP[:, 0:1], in_=RAWP[:, 512:513], reads=[rawp_t], writes=[rawp_t])

            Kt, Bt, KKt, Rt, Vt = F[0], F[2], F[3], F[4], F[5]
            LW = F[1]
            proj_lerp(4 + hp, F[5], ft[5])
            if l == 0:
                dma("sp", "dma_start", out=vf_d[hp * 128:(hp + 1) * 128, :], in_=F[5], reads=ft[5], writes=[vf_t[hp]])
            else:
                proj_lerp(4 + (1 - hp), F[4], ft[4])
                vch = {hp: (F[5], ft[5]), 1 - hp: (F[4], ft[4])}
                for tp in range(4):
                    p32, p32t = PS.next()
                    for kc in range(2):
                        vv, vvt = vch[kc]
                        op("pe", "matmul", p32[0:32, :], lhsT=CSM[:, 512 + kc * 32:512 + kc * 32 + 32], rhs=vv[:, tok(tp)],
                           start=(kc == 0), stop=(kc == 1), reads=[t_csm, vvt[tp]], writes=[p32t])
                    op("act", "copy", out=TB[0:32, :], in_=p32[0:32, :], reads=[p32t], writes=[tb_t])
                    pg, pgt = PS.next()
                    op("pe", "matmul", pg, lhsT=CSM[0:32, 576 + hp * 128:576 + hp * 128 + 128], rhs=TB[0:32, :], start=True, stop=True,
                       reads=[t_csm, tb_t], writes=[pgt])
                    op("act", "activation", out=TB, in_=pg, func=AF.Sigmoid, bias=ppc(("v0", l, hp)), scale=1.0,
                       reads=[pgt, t_pp], writes=[tb_t])
                    dma("sp", "dma_start", out=TA, in_=vf_d[hp * 128:(hp + 1) * 128, tok(tp)], reads=[vf_t[hp]], writes=[ta_t])
                    op("dve", "tensor_tensor", out=TA, in0=TA, in1=F[5][:, tok(tp)], op=ALU.subtract, reads=[ta_t, ft[5][tp]], writes=[ta_t])
                    op("dve", "tensor_tensor", out=TA, in0=TA, in1=TB, op=ALU.mult, reads=[ta_t, tb_t], writes=[ta_t])
                    op("dve", "tensor_tensor", out=F[5][:, tok(tp)], in0=F[5][:, tok(tp)], in1=TA, op=ALU.add,
                       reads=[ta_t, ft[5][tp]], writes=[ft[5][tp]])
            proj_lerp(6, F[0], ft[0])
            for tp in range(4):
                op("act", "activation", out=F[0][0:64, tok(tp)], in_=F[0][0:64, tok(tp)], func=AF.Tanh, reads=[ft[0][tp]], writes=[ft[0][tp]])
                pw, pwt = PS.next()
                op("pe", "matmul", pw, lhsT=CSM[0:64, hp * 128:hp * 128 + 128], rhs=F[0][0:64, tok(tp)], start=True, stop=True,
                   reads=[t_csm, ft[0][tp]], writes=[pwt])
                op("act", "activation", out=F[1][:, tok(tp)], in_=pw, func=AF.Sigmoid, bias=ppc(("w0", l, hp)), scale=1.0,
                   reads=[pwt, t_pp], writes=[ft[1][tp]])
                op("pool", "tensor_scalar", out=F[1][:, tok(tp)], in0=F[1][:, tok(tp)], scalar1=-math.exp(-0.5), scalar2=None, op0=ALU.mult,
                   reads=[ft[1][tp]], writes=[ft[1][tp]])
                pa, pat = PS.next()
                op("pe", "matmul", pa, lhsT=CSM[64:128, hp * 128:hp * 128 + 128], rhs=F[0][64:128, tok(tp)], start=True, stop=True,
                   reads=[t_csm, ft[0][tp]], writes=[pat])
                op("act", "activation", out=F[2][:, tok(tp)], in_=pa, func=AF.Sigmoid, bias=ppc(("a0", l, hp)), scale=1.0,
                   reads=[pat, t_pp], writes=[ft[2][tp]])
            proj_lerp(2 + hp, F[0], ft[0])
            for tp in range(4):
                tk = tok(tp)
                op("dve", "tensor_scalar", out=F[3][:, tk], in0=F[0][:, tk], scalar1=ppc(("kk", l, hp)), scalar2=None, op0=ALU.mult,
                   reads=[ft[0][tp], t_pp], writes=[ft[3][tp]])
                op("act", "activation", out=TA, in_=F[3][:, tk], func=AF.Square, reads=[ft[3][tp]], writes=[ta_t])
                pq, pqt = PS.next()
                op("pe", "matmul", pq, lhsT=BLK, rhs=TA, start=True, stop=True, reads=[t_cst, ta_t], writes=[pqt])
                op("act", "activation", out=TB, in_=pq, func=AF.Sqrt, reads=[pqt], writes=[tb_t])
                op("dve", "tensor_scalar", out=TB, in0=TB, scalar1=1e-12, scalar2=None, op0=ALU.max, reads=[tb_t], writes=[tb_t])
                op("dve", "reciprocal", out=TB, in_=TB, reads=[tb_t], writes=[tb_t])
                op("dve", "tensor_tensor", out=F[3][:, tk], in0=F[3][:, tk], in1=TB, op=ALU.mult, reads=[ft[3][tp], tb_t], writes=[ft[3][tp]])
                op("dve", "tensor_scalar", out=TA, in0=F[2][:, tk], scalar1=-1.0, scalar2=ppc(("ka", l, hp)), op0=ALU.add, op1=ALU.mult,
                   reads=[ft[2][tp], t_pp], writes=[ta_t])
                op("dve", "scalar_tensor_tensor", out=F[0][:, tk], in0=TA, scalar=1.0, in1=F[0][:, tk], op0=ALU.add, op1=ALU.mult,
                   reads=[ta_t, ft[0][tp]], writes=[ft[0][tp]])
                op("pool", "tensor_tensor", out=F[2][:, tk], in0=F[2][:, tk], in1=F[3][:, tk], op=ALU.mult,
                   reads=[ft[2][tp], ft[3][tp]], writes=[ft[2][tp]])
            proj_lerp(hp, F[4], ft[4])
            for tp in range(4):
                tk = tok(tp)
                op("dve", "scalar_tensor_tensor", out=TA, in0=F[4][:, tk], scalar=ppc(("rk", l, hp)), in1=F[0][:, tk], op0=ALU.mult, op1=ALU.mult,
                   reads=[ft[4][tp], ft[0][tp], t_pp], writes=[ta_t])
                pq, pqt = PS.next()
                op("pe", "matmul", pq, lhsT=BLK, rhs=TA, start=True, stop=True, reads=[t_cst, ta_t], writes=[pqt])
                op("dve", "tensor_tensor", out=OT[:, oc, tk], in0=pq, in1=F[5][:, tk], op=ALU.mult, reads=[pqt, ft[5][tp]], writes=[ot_t[oc][tp]])
            for tp in range(4):
                tk = tok(tp)
                for cc in range(4):
                    cs = slice(tp * 512 + cc * 128, tp * 512 + cc * 128 + 128)
                    op("dve", "tensor_tensor_scan", out=TA[:, cc * 128:(cc + 1) * 128], data0=ONES, data1=F[1][:, cs], initial=0.0,
                       op0=ALU.mult, op1=ALU.add, reads=[t_cst, ft[1][tp]], writes=[ta_t])
                op("dve", "tensor_tensor", out=TB, in0=TA, in1=F[1][:, tk], op=ALU.subtract, reads=[ta_t, ft[1][tp]], writes=[tb_t])
                op("act", "activation", out=TB, in_=TB, func=AF.Exp, reads=[tb_t], writes=[tb_t])
                op("dve", "tensor_tensor", out=F[3][:, tk], in0=F[3][:, tk], in1=TB, op=ALU.mult, reads=[ft[3][tp], tb_t], writes=[ft[3][tp]])
                op("act", "activation", out=TB, in_=TA, func=AF.Exp, reads=[ta_t], writes=[tb_t])
                op("dve", "tensor_tensor", out=F[4][:, tk], in0=F[4][:, tk], in1=TB, op=ALU.mult, reads=[ft[4][tp], tb_t], writes=[ft[4][tp]])
                op("act", "copy", out=GC[:, tp * 4:tp * 4 + 4], in_=TB[:, 127:512:128], reads=[tb_t], writes=[gc_t])
                op("act", "activation", out=TB, in_=TA, func=AF.Exp, scale=-1.0, reads=[ta_t], writes=[tb_t])
                op("dve", "tensor_tensor", out=F[0][:, tk], in0=F[0][:, tk], in1=TB, op=ALU.mult, reads=[ft[0][tp], tb_t], writes=[ft[0][tp]])
                op("pool", "tensor_tensor", out=F[2][:, tk], in0=F[2][:, tk], in1=TB, op=ALU.mult, reads=[ft[2][tp], tb_t], writes=[ft[2][tp]])
            P.barrier()
            def cut(region, n, width):
                out = []
                for _ in range(n):
                    out.append(AR[:, region[0]:region[0] + width])
                    region[0] += width
                return out
            reg1 = [base + 2048]
            reg2 = [sm]
            KhF, BhF = cut(reg1, 2, 128)
            KhT, BhT, VT = cut(reg1, 3, 128)
            MB = cut(reg1, 2, 256)
            MK = cut(reg1, 2, 256)
            assert reg1[0] <= base + 4096
            inv = [cut(reg2, 6, 128) for _ in range(2)]
            Wsb = cut(reg2, 2, 64)
            Usb = cut(reg2, 2, 64)
            ST = cut(reg2, 1, 64)[0]
            YC = cut(reg2, 1, 128)[0]
            G1 = cut(reg2, 3, 128)
            assert reg2[0] <= sm + 2564, reg2[0]
            t_kh, t_bh, t_kht, t_bht, t_vt = TT(), TT(), TT(), TT(), TT()
            t_mb, t_mk = [TT(), TT()], [TT(), TT()]
            t_inv = [[TT() for _ in range(6)] for _ in range(2)]
            t_w, t_u = [TT(), TT()], [TT(), TT()]
            t_st = [TT(), TT()]
            t_yc = TT()
            t_g1 = [TT() for _ in range(3)]
            fall = [sum(ft[k], []) if False else ft[k] for k in range(6)]
            op("pool", "memset", ST, 0.0, writes=t_st)
            for c in range(16):
                cs = slice(c * 128, c * 128 + 128)
                tp = c // 4
                op("dve", "tensor_scalar", out=KhF, in0=F[0][:, cs], scalar1=GC[:, c:c + 1], scalar2=None, op0=ALU.mult,
                   reads=[ft[0][tp], gc_t], writes=[t_kh])
                op("pool", "tensor_scalar", out=BhF, in0=F[2][:, cs], scalar1=GC[:, c:c + 1], scalar2=None, op0=ALU.mult,
                   reads=[ft[2][tp], gc_t], writes=[t_bh])
                for (src, srct, dstT, dstt) in ((KhF, [t_kh], KhT, t_kht), (BhF, [t_bh], BhT, t_bht), (F[5][:, cs], [ft[5][tp]], VT, t_vt)):
                    ptr, ptrt = PS.next()
                    op("pe", "transpose", ptr[:, 0:128], src, IDENT, reads=srct + [t_cst], writes=[ptrt])
                    op("act", "copy", out=dstT, in_=ptr[:, 0:128], reads=[ptrt], writes=[dstt])
                RW = [slice(0, 64), slice(64, 128)]
                for hl in range(2):
                    rows = RW[hl]
                    for (lt, ltt, Mx, Mxt) in ((F[2], ft[2][tp], MB[hl], t_mb[hl]), (F[0], ft[0][tp], MK[hl], t_mk[hl])):
                        pmx, pmxt = PS.next()
                        op("pe", "matmul", pmx[:, 0:128], lhsT=lt[rows, cs], rhs=F[3][rows, cs], start=True, stop=True,
                           reads=[ltt, ft[3][tp]], writes=[pmxt])
                        op("pe", "matmul", pmx[:, 128:256], lhsT=lt[rows, cs], rhs=F[4][rows, cs], start=True, stop=True,
                           reads=[ltt, ft[4][tp]], writes=[pmxt])
                        op("dve", "tensor_tensor", out=Mx, in0=pmx[:, 0:256], in1=UPP, op=ALU.mult, reads=[pmxt, t_cst], writes=[Mxt])
                    Pm, PTm, Zm, Pm2, PTm2, Zm2 = inv[hl]
                    tPm, tPTm, tZm, tPm2, tPTm2, tZm2 = t_inv[hl]
                    op("act", "mul", out=Pm, in_=MB[hl][:, 0:128], mul=-1.0, reads=[t_mb[hl]], writes=[tPm])
                    pnt, pntt = PS.next()
                    op("pe", "matmul", pnt[:, 0:128], lhsT=F[3][rows, cs], rhs=F[2][rows, cs], start=True, stop=True,
                       reads=[ft[3][tp], ft[2][tp]], writes=[pntt])
                    op("dve", "scalar_tensor_tensor", out=PTm, in0=pnt[:, 0:128], scalar=-1.0, in1=LOWS, op0=ALU.mult, op1=ALU.mult,
                       reads=[pntt, t_cst], writes=[tPTm])
                    op("pool", "tensor_tensor", out=Zm, in0=Pm, in1=IDENT, op=ALU.add, reads=[tPm, t_cst], writes=[tZm])
                curs = [(inv[h][0], t_inv[h][0], inv[h][1], t_inv[h][1], inv[h][2], t_inv[h][2]) for h in range(2)]
                nxts = [(inv[h][3], t_inv[h][3], inv[h][4], t_inv[h][4], inv[h][5], t_inv[h][5]) for h in range(2)]
                for stg in range(6):
                    for hl in range(2):
                        cP, ctP, cPT, ctPT, cZ, ctZ = curs[hl]
                        nP, ntP, nPT, ntPT, nZ, ntZ = nxts[hl]
                        ppt2, ppt2t = PS.next()
                        op("pe", "matmul", ppt2[:, 0:128], lhsT=cP, rhs=cPT, start=True, stop=True, reads=[ctP, ctPT], writes=[ppt2t])
                        op("act", "copy", out=nPT, in_=ppt2[:, 0:128], reads=[ppt2t], writes=[ntPT])
                        if stg < 5:
                            pp2, pp2t = PS.next()
                            op("pe", "matmul", pp2[:, 0:128], lhsT=cPT, rhs=cP, start=True, stop=True, reads=[ctP, ctPT], writes=[pp2t])
                            op("pool" if False else "act", "copy", out=nP, in_=pp2[:, 0:128], reads=[pp2t], writes=[ntP])
                    for hl in range(2):
                        cP, ctP, cPT, ctPT, cZ, ctZ = curs[hl]
                        nP, ntP, nPT, ntPT, nZ, ntZ = nxts[hl]
                        pz, pzt = PS.next()
                        op("pe", "matmul", pz[:, 0:128], lhsT=nPT, rhs=cZ, start=True, stop=True, reads=[ntPT, ctZ], writes=[pzt])
                        op("dve", "tensor_tensor", out=nZ, in0=pz[:, 0:128], in1=cZ, op=ALU.add, reads=[pzt, ctZ], writes=[ntZ])
                        curs[hl], nxts[hl] = nxts[hl], curs[hl]
                TTs = [(curs[h][4], curs[h][5]) for h in range(2)]
                pws = []
                for hl in range(2):
                    rows, hc = RW[hl], RW[hl]
                    pw, pwt = PS.next()
                    op("pe", "matmul", pw[:, 0:64], lhsT=F[3][rows, cs], rhs=ST[rows, :], start=True, stop=False,
                       reads=[ft[3][tp], t_st[hl]], writes=[pwt])
                    op("pe", "matmul", pw[:, 0:64], lhsT=MK[hl][:, 0:128], rhs=VT[:, hc], start=False, stop=True,
                       reads=[t_mk[hl], t_vt], writes=[pwt])
                    op("act", "copy", out=Wsb[hl], in_=pw[:, 0:64], reads=[pwt], writes=[t_w[hl]])
                for hl in range(2):
                    pu, put = PS.next()
                    op("pe", "matmul", pu[:, 0:64], lhsT=TTs[hl][0], rhs=Wsb[hl], start=True, stop=True, reads=[TTs[hl][1], t_w[hl]], writes=[put])
                    op("act", "mul", out=Usb[hl], in_=pu[:, 0:64], mul=-1.0, reads=[put], writes=[t_u[hl]])
                for hl in range(2):
                    rows, hc = RW[hl], RW[hl]
                    py, pyt = PS.next()
                    op("pe", "matmul", py[0:64, 0:128], lhsT=ST[rows, :], rhs=F[4][rows, cs], start=True, stop=False,
                       reads=[t_st[hl], ft[4][tp]], writes=[pyt])
                    op("pe", "matmul", py[0:64, 0:128], lhsT=Usb[hl], rhs=MB[hl][:, 128:256], start=False, stop=False,
                       reads=[t_u[hl], t_mb[hl]], writes=[pyt])
                    op("pe", "matmul", py[0:64, 0:128], lhsT=VT[:, hc], rhs=MK[hl][:, 128:256], start=False, stop=True,
                       reads=[t_vt, t_mk[hl]], writes=[pyt])
                    op("act", "copy", out=YC[rows, :], in_=py[0:64, 0:128], reads=[pyt], writes=[t_yc])
                    pn, pnt_ = PS.next()
                    op("pe", "matmul", pn[0:64, 0:64], lhsT=BhT[:, hc], rhs=Usb[hl], start=True, stop=False,
                       reads=[t_bht, t_u[hl]], writes=[pnt_])
                    op("pe", "matmul", pn[0:64, 0:64], lhsT=KhT[:, hc], rhs=VT[:, hc], start=False, stop=True,
                       reads=[t_kht, t_vt], writes=[pnt_])
                    op("dve", "scalar_tensor_tensor", out=ST[rows, :], in0=ST[rows, :], scalar=GC[rows, c:c + 1], in1=pn[0:64, 0:64],
                       op0=ALU.mult, op1=ALU.add, reads=[t_st[hl], gc_t, pnt_], writes=[t_st[hl]])
                pmu, pmut = PS.next()
                op("pe", "matmul", pmu[:, 0:128], lhsT=BLK64, rhs=YC, start=True, stop=True, reads=[t_cst, t_yc], writes=[pmut])
                op("act", "activation", out=G1[0], in_=YC, func=AF.Square, reads=[t_yc], writes=[t_g1[0]])
                pe2, pe2t = PS.next()
                op("pe", "matmul", pe2[:, 0:128], lhsT=BLK64, rhs=G1[0], start=True, stop=True, reads=[t_cst, t_g1[0]], writes=[pe2t])
                op("act", "copy", out=G1[1], in_=pmu[:, 0:128], reads=[pmut], writes=[t_g1[1]])
                op("dve", "tensor_tensor", out=G1[0], in0=G1[1], in1=G1[1], op=ALU.mult, reads=[t_g1[1]], writes=[t_g1[0]])
                op("dve", "scalar_tensor_tensor", out=G1[2], in0=pe2[:, 0:128], scalar=GN_EPS, in1=G1[0], op0=ALU.add, op1=ALU.subtract,
                   reads=[pe2t, t_g1[0]], writes=[t_g1[2]])
                op("act", "activation", out=G1[2], in_=G1[2], func=AF.Ln, reads=[t_g1[2]], writes=[t_g1[2]])
                op("act", "activation", out=G1[2], in_=G1[2], func=AF.Exp, scale=-0.5, reads=[t_g1[2]], writes=[t_g1[2]])
                op("dve", "tensor_tensor", out=G1[0], in0=YC, in1=G1[1], op=ALU.subtract, reads=[t_yc, t_g1[1]], writes=[t_g1[0]])
                op("dve", "tensor_tensor", out=G1[0], in0=G1[0], in1=G1[2], op=ALU.mult, reads=[t_g1[0], t_g1[2]], writes=[t_g1[0]])
                op("act", "activation", out=G1[0], in_=G1[0], func=AF.Identity, bias=ppc(("gnb", l, hp)), scale=ppc(("gng", l, hp)),
                   reads=[t_g1[0], t_pp], writes=[t_g1[0]])
                op("pool", "tensor_tensor", out=OT[:, oc, cs], in0=OT[:, oc, cs], in1=G1[0], op=ALU.add,
                   reads=[ot_t[oc][tp], t_g1[0]], writes=[ot_t[oc][tp]])
            P.barrier()
            ft0 = [TT() for _ in range(4)]
            rawp_t2 = TT()
            w1b = Rot([AR[:, sm:sm + 1024].rearrange("p (kc f) -> p kc f", kc=8)])
            w, wt = w1b.next()
            dma("sp", "dma_start", out=w, in_=dr["wch"][l * NCHUNK_W + 24 + 7].rearrange("p (kc f) -> p kc f", kc=8), writes=[wt])
            op("pool", "memset", RAWP[:, 0:1], 0.0, writes=[rawp_t2])
            mu = ppc(("mu", l, 7))
            ta2, tb2 = TT(), TT()
            for tp in range(4):
                pp_, ppt = PS.next()
                for kc in range(8):
                    op("pe", "matmul", pp_, lhsT=w[:, kc, :], rhs=XF[:, kc, tok(tp)], start=(kc == 0), stop=(kc == 7),
                       reads=[wt, xt_t[kc][tp]], writes=[ppt])
                op("act", "copy", out=RAWP[:, 1:513], in_=pp_, reads=[ppt], writes=[rawp_t2])
                op("dve", "tensor_tensor", out=TA, in0=RAWP[:, 0:512], in1=RAWP[:, 1:513], op=ALU.subtract, reads=[rawp_t2], writes=[ta2])
                op("dve", "scalar_tensor_tensor", out=TB, in0=TA, scalar=mu, in1=RAWP[:, 1:513], op0=ALU.mult, op1=ALU.add,
                   reads=[ta2, rawp_t2, t_pp], writes=[tb2])
                op("act", "copy", out=RAWP[:, 0:1], in_=RAWP[:, 512:513], reads=[rawp_t2], writes=[rawp_t2])
                op("act", "activation", out=TB, in_=TB, func=AF.Sigmoid, reads=[tb2], writes=[tb2])
                pg, pgt = PS.next()
                op("pe", "matmul", pg, lhsT=CSM[:, 256 + hp * 128:256 + hp * 128 + 128], rhs=TB, start=True, stop=True,
                   reads=[t_csm, tb2], writes=[pgt])
                op("dve", "tensor_tensor", out=OT[:, oc, tok(tp)], in0=pg, in1=OT[:, oc, tok(tp)], op=ALU.mult,
                   reads=[pgt, ot_t[oc][tp]], writes=[ot_t[oc][tp]])
            P.end_phase(st)

        if stage in ("A", "mix", "full"):
            for i in range(3):
                phase_a(i)
        else:
            for c in range(0, 3):
                op("pool", "memset", OT[:, c, :], 0.0, writes=ot_t[c])
        if stage in ("B", "mix", "full"):
            for i in range(3):
                phase_b(i)
        else:
            for c in range(3, 6):
                op("pool", "memset", OT[:, c, :], 0.0, writes=ot_t[c])
        if stage in ("C", "mix", "full"):
            for hp in range(2):
                phase_c(hp)
        else:
            for c in range(6, 8):
                op("pool", "memset", OT[:, c, :], 0.0, writes=ot_t[c])
        return OT, ot_t, None

    def wout_ln(l, OT, ot_t, w_r):
        st = P.phase()
        WL = P.ph_sb(st, "WL", [128, 4096 + 1536 + 2048], F32)
        sc = Rot([WL[:, k * 512:(k + 1) * 512] for k in range(8)])
        lnk = Rot([WL[:, 4096 + k * 512:4096 + (k + 1) * 512] for k in range(3)])
        wo_r = Rot([WL[:, 5632 + k * 1024:5632 + (k + 1) * 1024].rearrange("p (kc f) -> p kc f", kc=8) for k in range(2)])
        for dc in range(8):
            w, wt = wo_r.next()
            dma("sp", "dma_start", out=w, in_=dr["wch"][l * NCHUNK_W + 38 + dc].rearrange("p (kc f) -> p kc f", kc=8), writes=[wt])
            for tp in range(4):
                py, pyt = PS.next()
                for kc in range(8):
                    op("pe", "matmul", py, lhsT=w[:, kc, :], rhs=OT[:, kc, tok(tp)], start=(kc == 0), stop=(kc == 7),
                       reads=[wt, ot_t[kc][tp]], writes=[pyt])
                op("dve", "scalar_tensor_tensor", out=XT[:, dc, tok(tp)], in0=py, scalar=1.0 / ALPHA, in1=XF[:, dc, tok(tp)],
                   op0=ALU.mult, op1=ALU.add, reads=[pyt, xt_t[dc][tp]], writes=[xt_t[dc][tp]])
        for tp in range(4):
            layer_norm(l, 1, tp, sc, lnk)
        return st

    def dump_o(OT, ot_t):
        for c in range(8):
            dma("sp", "dma_start", out=out_d[c * 128:(c + 1) * 128, :], in_=OT[:, c, :], reads=ot_t[c])

    def dump_x():
        for c in range(8):
            dma("sp", "dma_start", out=out_d[c * 128:(c + 1) * 128, :], in_=XF[:, c, :], reads=xt_t[c])

    dumped = False
    stage = dbg[1] if dbg else "full"
    for l in range(depth):
        st = ffn_phase(l, 0)
        if dbg == ("x", "ffn_a") and l == depth - 1:
            dump_x()
            dumped = True
            P.end_phase(st)
            break
        P.end_phase(st)
        stm = P.phase()
        OTt = P.ph_sb(stm, "OT", [128, 16384], F32)
        OT, ot_t, w_r = mixer_phase(l, stage)
        if dbg is not None and dbg[0] == "o" and l == depth - 1:
            dump_o(OT, ot_t)
            dumped = True
            P.end_phase(stm)
            break
        st = wout_ln(l, OT, ot_t, w_r)
        if dbg == ("x", "mixln") and l == depth - 1:
            dump_x()
            dumped = True
            P.end_phase(st)
            P.end_phase(stm)
            break
        P.end_phase(st)
        P.end_phase(stm)
        st = ffn_phase(l, 1)
        if l == depth - 1:
            dump_x()
            dumped = True
        P.end_phase(st)
    P.barrier()
    P.finish()
    return nc


_CACHE = {}


def kernel(**inputs):
    sh, xs, idx = prep_inputs(inputs)
    shapes = {k: v.shape for k, v in sh.items()}
    nc = build(shapes, idx)
    n = len(xs)
    in_maps = []
    for b in range(n):
        m = dict(sh)
        m["xT"] = xs[b]
        in_maps.append(m)
    res = run_bass_kernel_spmd(nc, in_maps, core_ids=list(range(n)))
    out = np.stack([np.ascontiguousarray(res.results[b]["out"].T) for b in range(n)], axis=0)
    return out.astype(np.float32)
```

```python
import contextlib
import math
import numpy as np
import concourse.bass as bass
import concourse.mybir as mybir
from concourse.bass_utils import run_bass_kernel_spmd

F32 = mybir.dt.float32
F32R = mybir.dt.float32r
AF = mybir.ActivationFunctionType
ALU = mybir.AluOpType
AX = mybir.AxisListType

DEPTH = 4
D = 1024
S = 2048
FF = 2816
NFC = 22
ALPHA = (2 * DEPTH) ** 0.25
LN_EPS = 1e-5
RMS_EPS = 1e-5
GN_EPS = 64e-5
THETA = 10000.0
NCHUNK_W = 38 + 8
ARC = 31300
CFW = 14916 + 1024


class TT:
    __slots__ = ("w", "r")

    def __init__(self):
        self.w = None
        self.r = []


class Prog:
    ENG = ("pe", "act", "dve", "pool", "sp")

    def __init__(self, nc, same_sync=True, n_dma_sems=40):
        self.nc = nc
        self.same_sync = same_sync
        self.q = {e: [] for e in self.ENG}
        self.cnt = {e: 0 for e in self.ENG}
        self.known = {e: {} for e in self.ENG}
        self.stack = contextlib.ExitStack()
        self.sem = {}
        for e in self.ENG:
            self.sem[e] = self.stack.enter_context(nc.semaphore("s_" + e))
        self.dsem = []
        self.dval = []
        for i in range(n_dma_sems):
            self.dsem.append(self.stack.enter_context(nc.semaphore("d%d" % i)))
            self.dval.append(0)
        self.ndma = 0

    def sb(self, name, shape, dt=F32):
        return self.stack.enter_context(self.nc.sbuf_tensor(name, shape, dt))

    def ps(self, name, shape, dt=F32):
        return self.stack.enter_context(self.nc.psum_tensor(name, shape, dt))

    def _waits(self, eng, reads, writes):
        deps = {}
        for t in reads:
            if t.w is not None:
                k, v = t.w
                if deps.get(k, 0) < v:
                    deps[k] = v
        for t in writes:
            if t.w is not None:
                k, v = t.w
                if deps.get(k, 0) < v:
                    deps[k] = v
            for (k, v) in t.r:
                if deps.get(k, 0) < v:
                    deps[k] = v
        out = []
        kn = self.known[eng]
        for k, v in deps.items():
            if k == eng and (eng == "pe" or not self.same_sync):
                continue
            if kn.get(k, 0) >= v:
                continue
            kn[k] = v
            out.append((k, v))
        return out

    def _semh(self, k):
        return self.sem[k] if isinstance(k, str) else self.dsem[k]

    def _mark(self, tok, reads, writes):
        for t in reads:
            t.r.append(tok)
            if len(t.r) > 24:
                best = {}
                for k, v in t.r:
                    if best.get(k, 0) < v:
                        best[k] = v
                t.r = list(best.items())
        for t in writes:
            t.w = tok
            t.r = []

    def op(self, eng, fn, *args, reads=(), writes=(), **kwargs):
        waits = self._waits(eng, reads, writes)
        self.cnt[eng] += 1
        self._mark((eng, self.cnt[eng]), reads, writes)
        sem = self.sem[eng]
        wl = [(self._semh(k), v) for k, v in waits]

        def emit(e, fn=fn, wl=wl, sem=sem, args=args, kwargs=kwargs):
            for s, v in wl:
                e.wait_ge(s, v)
            getattr(e, fn)(*args, **kwargs).then_inc(sem, 1)

        self.q[eng].append(emit)

    def dma(self, eng, fn, *args, reads=(), writes=(), di=None, **kwargs):
        if di is None:
            di = self.ndma % len(self.dsem)
            self.ndma += 1
        waits = self._waits(eng, reads, writes)
        self.dval[di] += 16
        self._mark((di, self.dval[di]), reads, writes)
        sem = self.dsem[di]
        wl = [(self._semh(k), v) for k, v in waits]

        def emit(e, fn=fn, wl=wl, sem=sem, args=args, kwargs=kwargs):
            for s, v in wl:
                e.wait_ge(s, v)
            getattr(e, fn)(*args, **kwargs).then_inc(sem, 16)

        self.q[eng].append(emit)

    def barrier(self):
        for e in self.ENG:
            wl = []
            kn = self.known[e]
            for k in self.ENG:
                if k != e and self.cnt[k] > kn.get(k, 0):
                    kn[k] = self.cnt[k]
                    wl.append((self.sem[k], self.cnt[k]))
            for i, v in enumerate(self.dval):
                if v > kn.get(i, 0):
                    kn[i] = v
                    wl.append((self.dsem[i], v))
            if self.same_sync and e != "pe" and self.cnt[e] > kn.get(e, 0):
                kn[e] = self.cnt[e]
                wl.append((self.sem[e], self.cnt[e]))

            def emit(en, wl=wl):
                for s, v in wl:
                    en.wait_ge(s, v)

            self.q[e].append(emit)

    def phase(self):
        return contextlib.ExitStack()

    def ph_sb(self, st, name, shape, dt=F32):
        self.uid = getattr(self, "uid", 0) + 1
        return st.enter_context(self.nc.sbuf_tensor("%s_%d" % (name, self.uid), shape, dt))

    def end_phase(self, st):
        self.barrier()
        self.flush()
        st.close()

    def finish(self):
        self.flush()
        self.stack.close()

    def flush(self):
        nc = self.nc
        q = self.q
        self.q = {e: [] for e in self.ENG}
        with nc.Block() as block:
            @block.tensor
            def _(e):
                for f in q["pe"]:
                    f(e)

            @block.scalar
            def _(e):
                for f in q["act"]:
                    f(e)

            @block.vector
            def _(e):
                for f in q["dve"]:
                    f(e)

            @block.gpsimd
            def _(e):
                for f in q["pool"]:
                    f(e)

            @block.sync
            def _(e):
                for f in q["sp"]:
                    f(e)


class Rot:
    def __init__(self, views, tts=None):
        self.v = views
        self.t = tts if tts is not None else [TT() for _ in views]
        self.i = 0

    def next(self):
        i = self.i % len(self.v)
        self.i += 1
        return self.v[i], self.t[i]


def _chunk(W, cols):
    return np.ascontiguousarray(W[:, cols].reshape(8, 128, len(cols)).transpose(1, 0, 2))


def _swap_idx(base, n, grp):
    idx = np.arange(n)
    half = grp // 2
    return base + (idx // grp) * grp + ((idx % grp) + half) % grp


def _rope_table(grp):
    half = grp // 2
    inv = (THETA ** (-np.arange(0, grp, 2, dtype=np.float32) / grp)).astype(np.float32)
    pos = np.arange(S, dtype=np.float32)
    ang = (pos[:, None] * inv[None, :]).astype(np.float32)
    cos = np.cos(ang).astype(np.float32).T
    sin = np.sin(ang).astype(np.float32).T
    r = np.arange(128)
    i = r % half
    sign = np.where((r % grp) < half, -1.0, 1.0).astype(np.float32)
    C = cos[i]
    Sg = sin[i] * sign[:, None]
    t = np.stack([C, Sg], axis=1)
    t = t.reshape(128, 2, 4, 512).transpose(0, 2, 1, 3)
    return np.ascontiguousarray(t.reshape(128, 4 * 2 * 512)).astype(np.float32)


def _consts():
    c = {}
    k = np.arange(128)[:, None]
    q = np.arange(256)[None, :]
    c["causal"] = (q[:, :128] >= k).astype(np.float32)
    c["band"] = ((q >= k) & (q <= k + 128)).astype(np.float32)
    su = (q[:, :128] > k).astype(np.float32)
    iu = (q[:, :128] >= k).astype(np.float32)
    c["uppers"] = np.concatenate([su, iu], axis=1)
    c["ident"] = np.eye(128, dtype=np.float32)
    blk = np.zeros((128, 128), np.float32)
    blk[:64, :64] = 1.0
    blk[64:, 64:] = 1.0
    c["blk"] = blk
    c["lowers"] = (q[:, :128] < k).astype(np.float32)
    return c


def prep_inputs(inp):
    L = DEPTH
    f = lambda a: np.asarray(a, dtype=np.float32)
    sh = {}
    wgu = np.empty((L, 2, 22, 128, 2, 8, 128), np.float32)
    wd = np.empty((L, 2, 8, 128, 22, 128), np.float32)
    ffw = {"a": (inp["ffn_a_gate"], inp["ffn_a_up"], inp["ffn_a_down"]),
           "b": (inp["ffn_b_gate"], inp["ffn_b_up"], inp["ffn_b_down"])}
    for l in range(L):
        for i, nm in enumerate("ab"):
            g = f(ffw[nm][0][l]).reshape(8, 128, 22, 128).transpose(2, 1, 0, 3)
            u = f(ffw[nm][1][l]).reshape(8, 128, 22, 128).transpose(2, 1, 0, 3)
            wgu[l, i, :, :, 0] = g
            wgu[l, i, :, :, 1] = u
            wd[l, i] = f(ffw[nm][2][l]).reshape(22, 128, 8, 128).transpose(2, 1, 0, 3)
    sh["wgu"] = wgu.reshape(L * 2 * 22, 128, 2048)
    sh["wd"] = wd.reshape(L * 2 * 8, 128, 2816)
    wch = np.empty((L, NCHUNK_W, 128, 8, 128), np.float32)
    for l in range(L):
        W = f(inp["w_in"][l])
        ci = 0
        for base, grp in ((0, 32), (1152, 64)):
            for i in range(3):
                qc = base + 128 * i + np.arange(128)
                kc_ = base + 384 + 128 * i + np.arange(128)
                wch[l, ci + 0] = _chunk(W, qc)
                wch[l, ci + 1] = _chunk(W, _swap_idx(base + 128 * i, 128, grp))
                wch[l, ci + 2] = _chunk(W, kc_)
                wch[l, ci + 3] = _chunk(W, _swap_idx(base + 384 + 128 * i, 128, grp))
                ci += 4
        for c in range(8):
            wch[l, 24 + c] = _chunk(W, 2304 + 128 * c + np.arange(128))
        for i in range(3):
            wch[l, 32 + i] = _chunk(W, 768 + 128 * i + np.arange(128))
            wch[l, 35 + i] = _chunk(W, 1920 + 128 * i + np.arange(128))
        Wo = f(inp["w_out"][l])
        for dc in range(8):
            wch[l, 38 + dc] = _chunk(Wo, 128 * dc + np.arange(128))
    sh["wch"] = wch.reshape(L * NCHUNK_W, 128, 1024)
    sh["ropeA"] = _rope_table(32)
    sh["ropeB"] = _rope_table(64)
    for k_, v_ in _consts().items():
        sh["c_" + k_] = v_
    cols = []

    def addcol(v):
        cols.append(np.asarray(v, np.float32).reshape(128, 1))
        return len(cols) - 1

    idx = {}
    for l in range(L):
        for i in range(3):
            for c in range(8):
                idx[("lng", l, i, c)] = addcol(f(inp["ln_g"][l, i, c * 128:(c + 1) * 128]))
                idx[("lnb", l, i, c)] = addcol(f(inp["ln_b"][l, i, c * 128:(c + 1) * 128]))
        idx[("ga", l)] = addcol(np.tile(f(inp["a_norm_g"][l]), 2))
        idx[("gb", l)] = addcol(np.tile(f(inp["b_norm_g"][l]), 2))
        for c in range(8):
            idx[("mu", l, c)] = addcol(f(inp["c_mu"][l, c * 128:(c + 1) * 128]))
        for hp in range(2):
            sl = slice(hp * 128, hp * 128 + 128)
            idx[("w0", l, hp)] = addcol(f(inp["c_w0"][l, sl]))
            idx[("a0", l, hp)] = addcol(f(inp["c_a0"][l, sl]))
            idx[("kk", l, hp)] = addcol(f(inp["c_k_k"][l, sl]))
            idx[("ka", l, hp)] = addcol(f(inp["c_k_a"][l, sl]))
            idx[("rk", l, hp)] = addcol(f(inp["c_r_k"][l].reshape(256)[sl]))
            idx[("gng", l, hp)] = addcol(f(inp["c_gn_g"][l, sl]))
            idx[("gnb", l, hp)] = addcol(f(inp["c_gn_b"][l, sl]))
            if l > 0:
                idx[("v0", l, hp)] = addcol(f(inp["c_v0"][l - 1, sl]))
    sh["pp"] = np.ascontiguousarray(np.concatenate(cols, axis=1))
    lamrow = np.stack([np.stack([f(inp["a_lam_q1"][l]), f(inp["a_lam_k1"][l]),
                                 f(inp["a_lam_q2"][l]), f(inp["a_lam_k2"][l])]) for l in range(L)])
    sh["lamrow"] = np.ascontiguousarray(lamrow.reshape(1, L * 4 * 32))
    sm = np.zeros((L, 128, 256 + 256 + 64 + 256), np.float32)
    for l in range(L):
        sm[l, 0:64, 0:256] = f(inp["c_w2"][l])
        sm[l, 64:128, 0:256] = f(inp["c_a2"][l])
        sm[l, :, 256:512] = f(inp["c_g2"][l])
        if l > 0:
            sm[l, :, 512:576] = f(inp["c_v1"][l - 1]).reshape(2, 128, 32).transpose(1, 0, 2).reshape(128, 64)
            sm[l, 0:32, 576:832] = f(inp["c_v2"][l - 1])
    sh["csm"] = np.ascontiguousarray(sm.transpose(1, 0, 2).reshape(128, L * 832))
    xs = [np.ascontiguousarray(f(inp["x"][b]).T) for b in range(inp["x"].shape[0])]
    return sh, xs, idx


def build(shapes, idx, depth=DEPTH, dbg=None):
    nc = bass.Bass("TRN2", target_bir_lowering=False)
    dr = {}
    for k, shp in shapes.items():
        dr[k] = nc.dram_tensor(k, list(shp), F32, kind="ExternalInput").ap()
    xT_d = nc.dram_tensor("xT", [D, S], F32, kind="ExternalInput").ap()
    out_d = nc.dram_tensor("out", [D, S], F32, kind="ExternalOutput").ap()
    vf_d = nc.dram_tensor("vf_scratch", [256, S], F32, kind="Internal").ap()
    npp = shapes["pp"][1]

    import os as _os
    P = Prog(nc, same_sync=(_os.environ.get('NOSAME') is None))
    op, dma = P.op, P.dma
    XT = P.sb("XT", [128, 8, S], F32R)
    XF = XT[:].bitcast(F32)
    AR = None
    OTt = None
    CST = P.sb("CST", [128, 1472])
    PP = P.sb("PP", [128, npp])
    CSM = P.sb("CSM", [128, 832])
    LAM = P.sb("LAM", [128, 64 + DEPTH * 128 + 64])
    banks = [P.ps("pb%d" % i, [128, 512]) for i in range(8)]
    bank_t = [TT() for _ in range(8)]
    PS = Rot([b[:] for b in banks[0:6]], bank_t[0:6])
    PSL = Rot([b[:] for b in banks[6:8]], bank_t[6:8])
    PSA = Rot([b[:] for b in banks[0:4]], bank_t[0:4])
    PSLA = Rot([b[:] for b in banks[4:8]], bank_t[4:8])

    xt_t = [[TT() for _ in range(4)] for _ in range(8)]
    vf_t = [TT(), TT()]
    t_cst, t_pp, t_csm, t_lam, t_rst = TT(), TT(), TT(), TT(), TT()
    CAUS = CST[:, 0:128]
    BAND = CST[:, 128:384]
    UPP = CST[:, 384:640]
    IDENT = CST[:, 640:768]
    BLK = CST[:, 768:896]
    ONESD = CST[:, 896:1024]
    ONES64 = CST[:, 1024:1088]
    ONES = CST[:, 1088:1216]
    LOWS = CST[:, 1216:1344]
    BLK64 = CST[:, 1344:1472]
    dma("sp", "dma_start", out=CAUS, in_=dr["c_causal"], writes=[t_cst])
    dma("sp", "dma_start", out=BAND, in_=dr["c_band"], writes=[t_cst])
    dma("sp", "dma_start", out=UPP, in_=dr["c_uppers"], writes=[t_cst])
    dma("sp", "dma_start", out=IDENT, in_=dr["c_ident"], writes=[t_cst])
    dma("sp", "dma_start", out=BLK, in_=dr["c_blk"], writes=[t_cst])
    dma("sp", "dma_start", out=PP[:], in_=dr["pp"], writes=[t_pp])
    op("dve", "memset", ONESD, 1.0 / 1024.0, writes=[t_cst])
    op("dve", "memset", ONES64, 1.0 / 64.0, writes=[t_cst])
    op("dve", "memset", ONES, 1.0, writes=[t_cst])
    dma("sp", "dma_start", out=LOWS, in_=dr["c_lowers"], writes=[t_cst])
    op("act", "mul", out=BLK64, in_=BLK, mul=1.0 / 64.0, reads=[t_cst], writes=[t_cst])
    for c in range(8):
        dma("pool", "dma_start", out=XT[:, c, :], in_=xT_d[c * 128:(c + 1) * 128, :].bitcast(F32R),
            writes=xt_t[c])
    ONE1 = LAM[:, 0:64]
    op("dve", "memset", LAM[:], 0.0, writes=[t_lam])
    op("dve", "memset", ONE1, 1.0, writes=[t_lam])
    LROW = LAM[0:1, 64:64 + DEPTH * 128]
    dma("sp", "dma_start", out=LROW, in_=dr["lamrow"], writes=[t_lam])
    NEGLAM = LAM[:, 64 + DEPTH * 128:64 + DEPTH * 128 + 8]
    LS = LAM[0:1, 64 + DEPTH * 128 + 8:64 + DEPTH * 128 + 64]
    for l in range(depth):
        b0 = 64 + l * 128
        lam_init = 0.8 - 0.6 * math.exp(-0.3 * l)
        for j in range(2):
            op("dve", "tensor_tensor", out=LS[:, 0:32], in0=LAM[0:1, b0 + 64 * j:b0 + 64 * j + 32],
                                                          in1=LAM[0:1, b0 + 64 * j + 32:b0 + 64 * j + 64], op=ALU.mult,
               reads=[t_lam], writes=[t_lam])
            op("dve", "reduce_sum", out=LS[:, 32 + j:33 + j], in_=LS[:, 0:32], axis=AX.X,
               reads=[t_lam], writes=[t_lam])
        op("act", "activation", out=LS[:, 34:36], in_=LS[:, 32:34], func=AF.Exp, reads=[t_lam], writes=[t_lam])
        op("dve", "scalar_tensor_tensor", out=LS[:, 36:37], in0=LS[:, 35:36], scalar=-lam_init, in1=LS[:, 34:35],
                                                                op0=ALU.add, op1=ALU.subtract, reads=[t_lam], writes=[t_lam])
        pb, pt = PS.next()
        op("pe", "matmul", pb[0:64, 0:1], lhsT=LAM[0:1, 0:64], rhs=LS[:, 36:37], start=True, stop=True,
           reads=[t_lam], writes=[pt])
        op("act", "copy", out=NEGLAM[0:64, l:l + 1], in_=pb[0:64, 0:1], reads=[pt], writes=[t_lam])

    def ppc(key):
        j = idx[key]
        return PP[:, j:j + 1]

    def tok(tp):
        return slice(tp * 512, (tp + 1) * 512)

    def ssl(st, n, step):
        return slice(st, st + step * (n - 1) + 1, step) if step > 1 else slice(st, st + n)

    def layer_norm(l, i, tp, sc, lnk):
        eps = LN_EPS / (ALPHA * ALPHA)
        pm, pmt = PS.next()
        pe2, pe2t = PS.next()
        for c in range(8):
            op("pe", "matmul", pm, lhsT=ONESD, rhs=XF[:, c, tok(tp)], start=(c == 0), stop=(c == 7),
               reads=[t_cst, xt_t[c][tp]], writes=[pmt])
        for c in range(8):
            sq, sqt = sc.next()
            op("act", "activation", out=sq, in_=XF[:, c, tok(tp)], func=AF.Square,
               reads=[xt_t[c][tp]], writes=[sqt])
            op("pe", "matmul", pe2, lhsT=ONESD, rhs=sq, start=(c == 0), stop=(c == 7),
               reads=[t_cst, sqt], writes=[pe2t])
        mean, meant = lnk.next()
        op("act", "copy", out=mean, in_=pm, reads=[pmt], writes=[meant])
        msq, msqt = lnk.next()
        op("dve", "tensor_tensor", out=msq, in0=mean, in1=mean, op=ALU.mult, reads=[meant], writes=[msqt])
        var, vart = lnk.next()
        op("dve", "scalar_tensor_tensor", out=var, in0=pe2, scalar=eps, in1=msq, op0=ALU.add, op1=ALU.subtract,
           reads=[pe2t, msqt], writes=[vart])
        op("act", "activation", out=var, in_=var, func=AF.Ln, reads=[vart], writes=[vart])
        op("act", "activation", out=var, in_=var, func=AF.Exp, scale=-0.5, reads=[vart], writes=[vart])
        for c in range(8):
            t1, t1t = sc.next()
            op("dve", "tensor_tensor", out=t1, in0=XF[:, c, tok(tp)], in1=mean, op=ALU.subtract,
               reads=[xt_t[c][tp], meant], writes=[t1t])
            op("dve", "tensor_tensor", out=t1, in0=t1, in1=var, op=ALU.mult,
               reads=[t1t, vart], writes=[t1t])
            op("act", "activation", out=XT[:, c, tok(tp)], in_=t1, func=AF.Identity,
                                                        bias=ppc(("lnb", l, i, c)), scale=ppc(("lng", l, i, c)),
               reads=[t1t, t_pp], writes=[xt_t[c][tp]])

    def ffn_phase(l, i):
        st = P.phase()
        FR = P.ph_sb(st, "FR", [128, 11264 + 4 * 2048 + 2 * 2816], F32R)
        FT = P.ph_sb(st, "FT", [128, 11 * 512], F32)
        o = 0
        Hv = FR[:, o:o + 22 * 512].rearrange("p (f t) -> p f t", t=512)
        o += 22 * 512
        h_t = [TT() for _ in range(22)]
        wgu_r = Rot([FR[:, o + k * 2048:o + (k + 1) * 2048].rearrange("p (g kc f) -> p g kc f", g=2, kc=8) for k in range(4)])
        o += 4 * 2048
        wd_r = Rot([FR[:, o + k * 2816:o + (k + 1) * 2816].rearrange("p (f d) -> p f d", d=128) for k in range(2)])
        o += 2 * 2816
        sc = Rot([FT[:, k * 512:(k + 1) * 512] for k in range(8)])
        lnk = Rot([FT[:, (8 + k) * 512:(9 + k) * 512] for k in range(3)])
        lnidx = 0 if i == 0 else 2
        def p1(tp):
            for f in range(22):
                wb, wbt = wgu_r.next()
                dma("pool", "dma_start", out=wb,
                    in_=dr["wgu"][(l * 2 + i) * 22 + f].bitcast(F32R).rearrange("p (g kc f) -> p g kc f", g=2, kc=8), writes=[wbt])
                pg, pgt = PS.next()
                pu, put = PS.next()
                for g, pp_, ppt in ((0, pg, pgt), (1, pu, put)):
                    for kc in range(8):
                        op("pe", "matmul", pp_, lhsT=wb[:, g, kc, :], rhs=XT[:, kc, tok(tp)], start=(kc == 0), stop=(kc == 7),
                           reads=[wbt, xt_t[kc][tp]], writes=[ppt])
                sg, sgt = sc.next()
                op("act", "activation", out=sg, in_=pg, func=AF.Silu, reads=[pgt], writes=[sgt])
                op("dve", "tensor_tensor", out=Hv[:, f, :], in0=sg, in1=pu, op=ALU.mult, reads=[sgt, put], writes=[h_t[f]])

        def p2(tp):
            for dc in range(8):
                wb, wbt = wd_r.next()
                dma("pool", "dma_start", out=wb,
                    in_=dr["wd"][(l * 2 + i) * 8 + dc].bitcast(F32R).rearrange("p (f d) -> p f d", d=128), writes=[wbt])
                py, pyt = PS.next()
                for f in range(22):
                    op("pe", "matmul", py, lhsT=wb[:, f, :], rhs=Hv[:, f, :], start=(f == 0), stop=(f == 21),
                       reads=[wbt, h_t[f]], writes=[pyt])
                op("dve", "scalar_tensor_tensor", out=XT[:, dc, tok(tp)], in0=py, scalar=0.5 / ALPHA, in1=XF[:, dc, tok(tp)],
                   op0=ALU.mult, op1=ALU.add, reads=[pyt, xt_t[dc][tp]], writes=[xt_t[dc][tp]])

        for tp in range(4):
            p1(tp)
            if tp > 0:
                layer_norm(l, lnidx, tp - 1, sc, lnk)
            p2(tp)
        layer_norm(l, lnidx, 3, sc, lnk)
        return st

    def mixer_phase(l, stage):
        OT = OTt[:].rearrange("p (c t) -> p c t", t=S)
        ot_t = [[TT() for _ in range(4)] for _ in range(8)]
        QT = KT = KZ = VA = RS = None
        w_r = rope_r = e512 = e256 = scr = None
        SCR0 = None
        qt_t = kt_t = None
        va_t = kz_t = None

        def setup_ab(is_a):
            nonlocal QT, KT, KZ, VA, RS, w_r, rope_r, e512, e256, scr, SCR0, qt_t, kt_t, va_t, kz_t
            st = P.phase()
            AQ = P.ph_sb(st, "AQ", [128, 11808 if is_a else 9760 + 768], F32R)
            ABp = P.ph_sb(st, "ABp", [128, 3072 if is_a else 5120], F32)
            o = 0
            QT = AQ[:, o:o + 2048]
            KT = AQ[:, o + 2048:o + 4096]
            o += 4096
            qt_t = [TT() for _ in range(4)]
            kt_t = [TT() for _ in range(4)]
            VA = AQ[:, o:o + 2080].rearrange("p (j h d) -> p j h d", h=2, d=65)
            va_t = TT()
            o += 2080
            w_r = Rot([AQ[:, o + k * 1024:o + (k + 1) * 1024].rearrange("p (kc f) -> p kc f", kc=8) for k in range(2)])
            o += 2048
            e512 = Rot([AQ[:, o + k * 512:o + (k + 1) * 512] for k in range(3)])
            e256 = Rot([AQ[:, o + k * 256:o + (k + 1) * 256] for k in range(6 if is_a else 9)])
            o += 1536 if is_a else 2304
            if is_a:
                KZ = AQ[:, o:o + 2048]
                kz_t = TT()
                o += 2048
            rope_r = Rot([ABp[:, 0:1024].rearrange("p (c t) -> p c t", c=2)])
            scr = Rot([ABp[:, 1024 + k * 512:1024 + (k + 1) * 512] for k in range(4)])
            SCR0 = ABp
            if not is_a:
                RS = ABp[:, 3072:5120]
            return st

        lam_init = 0.8 - 0.6 * math.exp(-0.3 * l)

        def load_w(ci):
            w, wt = w_r.next()
            dma("pool", "dma_start", out=w, in_=dr["wch"][l * NCHUNK_W + ci].bitcast(F32R).rearrange("p (kc f) -> p kc f", kc=8),
                writes=[wt])
            return w, wt

        def proj_rope(ci, dst, dst_t, rname):
            w1, w1t = load_w(ci)
            w2, w2t = load_w(ci + 1)
            for tp in range(4):
                rp, rpt = rope_r.next()
                dma("sp", "dma_start", out=rp, in_=dr[rname][:, tp * 1024:(tp + 1) * 1024].rearrange("p (c t) -> p c t", c=2),
                    writes=[rpt])
                p1, p1t = PS.next()
                p2, p2t = PS.next()
                for w, wt, pp_, ppt in ((w1, w1t, p1, p1t), (w2, w2t, p2, p2t)):
                    for kc in range(8):
                        op("pe", "matmul", pp_, lhsT=w[:, kc, :], rhs=XT[:, kc, tok(tp)], start=(kc == 0), stop=(kc == 7),
                           reads=[wt, xt_t[kc][tp]], writes=[ppt])
                a, at = scr.next()
                b, bt = scr.next()
                op("dve", "tensor_tensor", out=a, in0=p1, in1=rp[:, 0, :], op=ALU.mult, reads=[p1t, rpt], writes=[at])
                op("dve", "tensor_tensor", out=b, in0=p2, in1=rp[:, 1, :], op=ALU.mult, reads=[p2t, rpt], writes=[bt])
                op("pool", "tensor_tensor", out=dst[:, tok(tp)], in0=a, in1=b, op=ALU.add, reads=[at, bt], writes=[dst_t[tp]])

        def proj_v(ci, dil):
            wv, wvt = load_w(ci)
            op("act", "copy", out=VA[:, :, :, 64:65], in_=ONES[:, 0:32].rearrange("p (j h d) -> p j h d", h=2, d=1),
               reads=[t_cst], writes=[va_t])
            nt = 16 // dil
            for r in range(dil):
                for a in range(nt):
                    st = r + dil * 128 * a
                    pv, pvt = PS.next()
                    for kc in range(8):
                        lhs = XT[:, kc, ssl(st, 128, dil)]
                        op("pe", "matmul", pv[:, 0:128], lhsT=lhs, rhs=wv[:, kc, :], start=(kc == 0), stop=(kc == 7),
                           reads=[wvt] + xt_t[kc], writes=[pvt])
                    op("act", "copy", out=VA[:, r * nt + a, :, 0:64], in_=pv[:, 0:128].rearrange("p (h d) -> p h d", h=2),
                       reads=[pvt], writes=[va_t])

        def phase_a(i):
            st = setup_ab(True)
            proj_rope(4 * i, QT, qt_t, "ropeA")
            proj_rope(4 * i + 2, KT, kt_t, "ropeA")
            proj_v(32 + i, 1)
            op("pool", "tensor_copy", out=KZ[96:128, :], in_=KT[96:128, :].bitcast(F32), reads=kt_t, writes=[kz_t])
            op("pool", "tensor_scalar", out=KZ[64:96, :], in0=KT[64:96, :].bitcast(F32), scalar1=0.0, scalar2=None, op0=ALU.mult,
               reads=kt_t, writes=[kz_t])
            sca = 32 ** -0.5
            bg = []

            def step_bg():
                if bg:
                    try:
                        next(bg[0])
                    except StopIteration:
                        bg.pop(0)

            def drain_bg(keep=0):
                while len(bg) > keep:
                    step_bg()

            def main_block(hl, Qp, nums):
                for m in range(2):
                    r0 = 32 * (2 * hl + m)
                    num, numt = PSLA.next()
                    nums.append((num, numt))
                    last = 4 * Qp + 3
                    pend = []

                    def do_pv(item, num=num, numt=numt, last=last):
                        j, E, Et, c0, n = item
                        op("pe", "matmul", num[0:65, c0 - 512 * Qp:512], lhsT=VA[:, j, hl, :], rhs=E[:, 0:n],
                           start=(j == 0), stop=(j == last), reads=[va_t, Et], writes=[numt])

                    for j in range(last + 1):
                        c0 = max(128 * j, 512 * Qp)
                        n = 512 * Qp + 512 - c0
                        se, set_ = PSA.next()
                        if hl == 1 and m == 1:
                            op("pe", "matmul", se[:, 0:n], lhsT=KZ[64:128, 128 * j:128 * j + 128], rhs=QT[64:128, c0:c0 + n],
                               start=True, stop=True, reads=[kz_t] + qt_t[c0 // 512:Qp + 1], writes=[set_])
                        else:
                            op("pe", "matmul", se[:, 0:n], lhsT=KT[r0:r0 + 32, 128 * j:128 * j + 128], rhs=QT[r0:r0 + 32, c0:c0 + n],
                               start=True, stop=True, reads=[kt_t[j // 4]] + qt_t[c0 // 512:Qp + 1], writes=[set_])
                        E, Et = e512.next()
                        op("act", "activation", out=E[:, 0:n], in_=se[:, 0:n], func=AF.Exp, scale=sca, reads=[set_], writes=[Et])
                        if 128 * j >= 512 * Qp:
                            op("pool", "tensor_tensor", out=E[:, 0:128], in0=E[:, 0:128].bitcast(F32), in1=CAUS, op=ALU.mult,
                               reads=[Et, t_cst], writes=[Et])
                        pend.append((j, E, Et, c0, n))
                        if len(pend) > 2:
                            do_pv(pend.pop(0))
                        yield
                    while pend:
                        do_pv(pend.pop(0))
                        yield

            def post_block(hl, Qp, nums):
                (n1, n1t), (n2, n2t) = nums
                X0, X0t = scr.next()
                X1, X1t = scr.next()
                X2, X2t = scr.next()
                for (nn, nnt, prt) in ((n1, n1t, 64), (n2, n2t, 32)):
                    op("act", "activation", out=X0[prt:prt + 1, :], in_=nn[64:65, :], func=AF.Ln, reads=[nnt], writes=[X0t])
                    yield
                    op("act", "activation", out=X0[prt:prt + 1, :], in_=X0[prt:prt + 1, :], func=AF.Exp, scale=-1.0,
                       reads=[X0t], writes=[X0t])
                    yield
                for (prt, Xd, Xdt) in ((64, X1, X1t), (32, X2, X2t)):
                    pb, pbt = PSA.next()
                    op("pe", "matmul", pb[0:64, :], lhsT=ONE1[prt:prt + 1, 0:64], rhs=X0[prt:prt + 1, :], start=True, stop=True,
                       reads=[t_lam, X0t], writes=[pbt])
                    yield
                    op("act", "copy", out=Xd[0:64, :], in_=pb[0:64, :], reads=[pbt], writes=[Xdt])
                    yield
                op("dve", "tensor_tensor", out=X1[0:64, :], in0=n1[0:64, :], in1=X1[0:64, :], op=ALU.mult, reads=[n1t, X1t], writes=[X1t])
                yield
                op("dve", "tensor_tensor", out=X2[0:64, :], in0=n2[0:64, :], in1=X2[0:64, :], op=ALU.mult, reads=[n2t, X2t], writes=[X2t])
                yield
                op("dve", "scalar_tensor_tensor", out=X1[0:64, :], in0=X2[0:64, :], scalar=NEGLAM[0:64, l:l + 1], in1=X1[0:64, :],
                   op0=ALU.mult, op1=ALU.add, reads=[X1t, X2t, t_lam], writes=[X1t])
                yield
                op("act", "activation", out=X2[0:64, :], in_=X1[0:64, :], func=AF.Square, reads=[X1t], writes=[X2t])
                yield
                pm, pmt = PSA.next()
                op("pe", "matmul", pm[0:64, :], lhsT=ONES64[0:64, 0:64], rhs=X2[0:64, :], start=True, stop=True,
                   reads=[t_cst, X2t], writes=[pmt])
                yield
                op("dve", "tensor_scalar", out=X2[0:64, :], in0=pm[0:64, :], scalar1=RMS_EPS, scalar2=None, op0=ALU.add,
                   reads=[pmt], writes=[X2t])
                yield
                op("act", "activation", out=X2[0:64, :], in_=X2[0:64, :], func=AF.Ln, reads=[X2t], writes=[X2t])
                yield
                op("act", "activation", out=X2[0:64, :], in_=X2[0:64, :], func=AF.Exp, scale=-0.5, reads=[X2t], writes=[X2t])
                yield
                op("dve", "scalar_tensor_tensor", out=X1[0:64, :], in0=X1[0:64, :], scalar=ppc(("ga", l))[0:64, :], in1=X2[0:64, :],
                   op0=ALU.mult, op1=ALU.mult, reads=[X1t, X2t, t_pp], writes=[X1t])
                yield
                op("dve", "tensor_scalar", out=OT[64 * hl:64 * hl + 64, i, tok(Qp)], in0=X1[0:64, :], scalar1=1.0 - lam_init,
                   scalar2=None, op0=ALU.mult, reads=[X1t], writes=[ot_t[i][Qp]])
                yield

            for hl in range(2):
                for Qp in range(4):
                    nums = []
                    drain_bg(keep=1)
                    cnt = 0
                    for _ in main_block(hl, Qp, nums):
                        cnt += 1
                        if cnt % 2 == 0:
                            step_bg()
                    drain_bg(keep=0) if False else None
                    bg.append(post_block(hl, Qp, nums))
            drain_bg(0)
            P.end_phase(st)


        def phase_b(i):
            st = setup_ab(False)
            rs_t = [TT(), TT()]
            proj_rope(12 + 4 * i, QT, qt_t, "ropeB")
            proj_rope(12 + 4 * i + 2, KT, kt_t, "ropeB")
            scb = 64 ** -0.5
            oc = 3 + i
            for dil in (1, 4, 16):
                proj_v(35 + i, dil)
                L = S // dil
                nt = L // 128
                pieces = []
                for hl in range(2):
                    for r in range(dil):
                        Es = {}
                        for g in range((nt + 3) // 4):
                            pieces.append((hl, r, g, Es))

                def stage1(pc):
                    hl, r, g, Es = pc
                    rows = slice(64 * hl, 64 * hl + 64)
                    for a in range(max(4 * g - 1, 0), min(4 * g + 3, nt - 1) + 1):
                        if a in Es:
                            continue
                        nq = min(256, L - 128 * a)
                        st_ = r + dil * 128 * a
                        se, set_ = PS.next()
                        op("pe", "matmul", se[:, 0:nq], lhsT=KT[rows, ssl(st_, 128, dil)], rhs=QT[rows, ssl(st_, nq, dil)],
                           start=True, stop=True, reads=kt_t + qt_t, writes=[set_])
                        E, Et = e256.next()
                        op("act", "activation", out=E[:, 0:nq], in_=se[:, 0:nq], func=AF.Exp, scale=scb, reads=[set_], writes=[Et])
                        op("pool", "tensor_tensor", out=E[:, 0:nq], in0=E[:, 0:nq].bitcast(F32), in1=BAND[:, 0:nq], op=ALU.mult,
                           reads=[Et, t_cst], writes=[Et])
                        Es[a] = (E, Et, nq)

                def stage2(pc):
                    hl, r, g, Es = pc
                    rows = slice(64 * hl, 64 * hl + 64)
                    rowp = 64 if hl == 0 else 32
                    ncols = min(512, L - 512 * g)
                    alist = list(range(max(4 * g - 1, 0), min(4 * g + 3, nt - 1) + 1))
                    num, numt = PSL.next()
                    for ai, a in enumerate(alist):
                        E, Et, nq = Es[a]
                        lo = max(128 * a, 512 * g)
                        hi = min(128 * a + nq, 512 * g + ncols)
                        op("pe", "matmul", num[0:65, lo - 512 * g:hi - 512 * g], lhsT=VA[:, r * nt + a, hl, :],
                           rhs=E[:, lo - 128 * a:hi - 128 * a], start=(ai == 0), stop=(ai == len(alist) - 1),
                           reads=[va_t, Et], writes=[numt])
                    p0 = r + dil * 512 * g
                    ps_ = ssl(p0, ncols, dil)
                    if dil == 1:
                        op("dve", "tensor_copy", out=OT[rows, oc, ps_], in_=num[0:64, 0:ncols], reads=[numt], writes=ot_t[oc])
                        op("act", "copy", out=RS[rowp:rowp + 1, ps_], in_=num[64:65, 0:ncols], reads=[numt], writes=[rs_t[hl]])
                    else:
                        op("dve", "tensor_tensor", out=OT[rows, oc, ps_], in0=num[0:64, 0:ncols], in1=OT[rows, oc, ps_], op=ALU.add,
                           reads=[numt] + ot_t[oc], writes=ot_t[oc])
                        op("dve", "tensor_tensor", out=RS[rowp:rowp + 1, ps_], in0=num[64:65, 0:ncols], in1=RS[rowp:rowp + 1, ps_],
                           op=ALU.add, reads=[numt, rs_t[hl]], writes=[rs_t[hl]])

                stage1(pieces[0])
                for k in range(len(pieces)):
                    if k + 1 < len(pieces):
                        stage1(pieces[k + 1])
                    stage2(pieces[k])
            P.barrier()
            chains = []

            def post_chain(hl, tp, X, Xt):
                rows = slice(64 * hl, 64 * hl + 64)
                rowp = 64 if hl == 0 else 32
                op("act", "activation", out=X[rows, :], in_=OT[rows, oc, tok(tp)], func=AF.Square, reads=[ot_t[oc][tp]], writes=[Xt])
                yield
                op("act", "activation", out=X[rowp:rowp + 1, :], in_=RS[rowp:rowp + 1, tok(tp)], func=AF.Square,
                   scale=math.sqrt(RMS_EPS), reads=[rs_t[hl]], writes=[Xt])
                yield
                pm, pmt = PS.next()
                op("pe", "matmul", pm[0:64, :], lhsT=ONES64[rows, 0:64], rhs=X[rows, :], start=True, stop=False,
                   reads=[t_cst, Xt], writes=[pmt])
                op("pe", "matmul", pm[0:64, :], lhsT=ONE1[rowp:rowp + 1, 0:64], rhs=X[rowp:rowp + 1, :], start=False, stop=True,
                   reads=[t_lam, Xt], writes=[pmt])
                yield
                op("act", "activation", out=X[rows, :], in_=pm[0:64, :], func=AF.Ln, reads=[pmt], writes=[Xt])
                yield
                op("act", "activation", out=X[rows, :], in_=X[rows, :], func=AF.Exp, scale=-0.5, reads=[Xt], writes=[Xt])
                yield
                op("dve", "scalar_tensor_tensor", out=OT[rows, oc, tok(tp)], in0=OT[rows, oc, tok(tp)], scalar=ppc(("gb", l))[rows, :],
                   in1=X[rows, :], op0=ALU.mult, op1=ALU.mult, reads=[ot_t[oc][tp], Xt, t_pp], writes=[ot_t[oc][tp]])
                yield

            xt4 = [(SCR0[:, 1024 + k * 512:1024 + (k + 1) * 512], TT()) for k in range(4)]
            for hl in range(2):
                gens = [post_chain(hl, tp, xt4[tp][0], xt4[tp][1]) for tp in range(4)]
                while gens:
                    for gsn in list(gens):
                        try:
                            next(gsn)
                        except StopIteration:
                            gens.remove(gsn)
            P.end_phase(st)


        def phase_c(hp):
            st = P.phase()
            AR = P.ph_sb(st, "CF", [128, CFW], F32)
            oc = 6 + hp
            base = 0
            F = [AR[:, base + k * 2048:base + (k + 1) * 2048] for k in range(6)]
            ft = [[TT() for _ in range(4)] for _ in range(6)]
            sm = base + 6 * 2048
            CW = P.ph_sb(st, "CW", [128, 1024], F32R)
            w1 = Rot([CW[:, 0:1024].rearrange("p (kc f) -> p kc f", kc=8)])
            RAWPS = [AR[:, sm:sm + 516], AR[:, sm + 1024:sm + 1024 + 516]]
            rawp_ts = [TT(), TT()]
            RAWP = RAWPS[1]
            rawp_t = rawp_ts[1]
            TA = AR[:, sm + 1540:sm + 2052]
            TB = AR[:, sm + 2052:sm + 2564]
            ta_t, tb_t = TT(), TT()
            GC = AR[:, sm + 2564:sm + 2580]
            gc_t = TT()
            assert sm + 2580 <= CFW
            dma("sp", "dma_start", out=CSM[:], in_=dr["csm"][:, l * 832:(l + 1) * 832], writes=[t_csm])

            def loadw1(ci):
                w, wt = w1.next()
                dma("pool", "dma_start", out=w, in_=dr["wch"][l * NCHUNK_W + ci].bitcast(F32R).rearrange("p (kc f) -> p kc f", kc=8),
                    writes=[wt])
                return w, wt

            def proj_lerp(cidx, dst, dst_t):
                w, wt = loadw1(24 + cidx)
                mu = ppc(("mu", l, cidx))
                op("pool", "memset", RAWPS[0][:, 0:1], 0.0, writes=[rawp_ts[0]])
                for tp in range(4):
                    RP, RPt = RAWPS[tp % 2], rawp_ts[tp % 2]
                    RN, RNt = RAWPS[(tp + 1) % 2], rawp_ts[(tp + 1) % 2]
                    pp_, ppt = PS.next()
                    for kc in range(8):
                        op("pe", "matmul", pp_, lhsT=w[:, kc, :], rhs=XT[:, kc, tok(tp)], start=(kc == 0), stop=(kc == 7),
                           reads=[wt, xt_t[kc][tp]], writes=[ppt])
                    op("act", "copy", out=RP[:, 1:513], in_=pp_, reads=[ppt], writes=[RPt])
                    if tp < 3:
                        op("act", "copy", out=RN[:, 0:1], in_=RP[:, 512:513], reads=[RPt], writes=[RNt])
                    op("dve", "tensor_tensor", out=TA, in0=RP[:, 0:512], in1=RP[:, 1:513], op=ALU.subtract,
                       reads=[RPt], writes=[ta_t])
                    op("dve", "scalar_tensor_tensor", out=dst[:, tok(tp)], in0=TA, scalar=mu, in1=RP[:, 1:513],
                       op0=ALU.mult, op1=ALU.add, reads=[ta_t, RPt, t_pp], writes=[dst_t[tp]])

            Kt, Bt, KKt, Rt, Vt = F[0], F[2], F[3], F[4], F[5]
            LW = F[1]
            proj_lerp(4 + hp, F[5], ft[5])
            if l == 0:
                dma("sp", "dma_start", out=vf_d[hp * 128:(hp + 1) * 128, :], in_=F[5], reads=ft[5], writes=[vf_t[hp]])
            else:
                proj_lerp(4 + (1 - hp), F[4], ft[4])
                vch = {hp: (F[5], ft[5]), 1 - hp: (F[4], ft[4])}
                for tp in range(4):
                    p32, p32t = PS.next()
                    for kc in range(2):
                        vv, vvt = vch[kc]
                        op("pe", "matmul", p32[0:32, :], lhsT=CSM[:, 512 + kc * 32:512 + kc * 32 + 32], rhs=vv[:, tok(tp)],
                           start=(kc == 0), stop=(kc == 1), reads=[t_csm, vvt[tp]], writes=[p32t])
                    op("act", "copy", out=TB[0:32, :], in_=p32[0:32, :], reads=[p32t], writes=[tb_t])
                    pg, pgt = PS.next()
                    op("pe", "matmul", pg, lhsT=CSM[0:32, 576 + hp * 128:576 + hp * 128 + 128], rhs=TB[0:32, :], start=True, stop=True,
                       reads=[t_csm, tb_t], writes=[pgt])
                    op("act", "activation", out=TB, in_=pg, func=AF.Sigmoid, bias=ppc(("v0", l, hp)), scale=1.0,
                       reads=[pgt, t_pp], writes=[tb_t])
                    dma("sp", "dma_start", out=TA, in_=vf_d[hp * 128:(hp + 1) * 128, tok(tp)], reads=[vf_t[hp]], writes=[ta_t])
                    op("dve", "tensor_tensor", out=TA, in0=TA, in1=F[5][:, tok(tp)], op=ALU.subtract, reads=[ta_t, ft[5][tp]], writes=[ta_t])
                    op("dve", "tensor_tensor", out=TA, in0=TA, in1=TB, op=ALU.mult, reads=[ta_t, tb_t], writes=[ta_t])
                    op("dve", "tensor_tensor", out=F[5][:, tok(tp)], in0=F[5][:, tok(tp)], in1=TA, op=ALU.add,
                       reads=[ta_t, ft[5][tp]], writes=[ft[5][tp]])
            proj_lerp(6, F[0], ft[0])
            for tp in range(4):
                op("act", "activation", out=F[0][0:64, tok(tp)], in_=F[0][0:64, tok(tp)], func=AF.Tanh, reads=[ft[0][tp]], writes=[ft[0][tp]])
                pw, pwt = PS.next()
                op("pe", "matmul", pw, lhsT=CSM[0:64, hp * 128:hp * 128 + 128], rhs=F[0][0:64, tok(tp)], start=True, stop=True,
                   reads=[t_csm, ft[0][tp]], writes=[pwt])
                op("act", "activation", out=F[1][:, tok(tp)], in_=pw, func=AF.Sigmoid, bias=ppc(("w0", l, hp)), scale=1.0,
                   reads=[pwt, t_pp], writes=[ft[1][tp]])
                op("pool", "tensor_scalar", out=F[1][:, tok(tp)], in0=F[1][:, tok(tp)], scalar1=-math.exp(-0.5), scalar2=None, op0=ALU.mult,
                   reads=[ft[1][tp]], writes=[ft[1][tp]])
                pa, pat = PS.next()
                op("pe", "matmul", pa, lhsT=CSM[64:128, hp * 128:hp * 128 + 128], rhs=F[0][64:128, tok(tp)], start=True, stop=True,
                   reads=[t_csm, ft[0][tp]], writes=[pat])
                op("act", "activation", out=F[2][:, tok(tp)], in_=pa, func=AF.Sigmoid, bias=ppc(("a0", l, hp)), scale=1.0,
                   reads=[pat, t_pp], writes=[ft[2][tp]])
            proj_lerp(2 + hp, F[0], ft[0])
            for tp in range(4):
                tk = tok(tp)
                op("dve", "tensor_scalar", out=F[3][:, tk], in0=F[0][:, tk], scalar1=ppc(("kk", l, hp)), scalar2=None, op0=ALU.mult,
                   reads=[ft[0][tp], t_pp], writes=[ft[3][tp]])
                op("act", "activation", out=TA, in_=F[3][:, tk], func=AF.Square, reads=[ft[3][tp]], writes=[ta_t])
                pq, pqt = PS.next()
                op("pe", "matmul", pq, lhsT=BLK, rhs=TA, start=True, stop=True, reads=[t_cst, ta_t], writes=[pqt])
                op("act", "activation", out=TB, in_=pq, func=AF.Sqrt, reads=[pqt], writes=[tb_t])
                op("dve", "tensor_scalar", out=TB, in0=TB, scalar1=1e-12, scalar2=None, op0=ALU.max, reads=[tb_t], writes=[tb_t])
                op("dve", "reciprocal", out=TB, in_=TB, reads=[tb_t], writes=[tb_t])
                op("dve", "tensor_tensor", out=F[3][:, tk], in0=F[3][:, tk], in1=TB, op=ALU.mult, reads=[ft[3][tp], tb_t], writes=[ft[3][tp]])
                op("dve", "tensor_scalar", out=TA, in0=F[2][:, tk], scalar1=-1.0, scalar2=ppc(("ka", l, hp)), op0=ALU.add, op1=ALU.mult,
                   reads=[ft[2][tp], t_pp], writes=[ta_t])
                op("dve", "scalar_tensor_tensor", out=F[0][:, tk], in0=TA, scalar=1.0, in1=F[0][:, tk], op0=ALU.add, op1=ALU.mult,
                   reads=[ta_t, ft[0][tp]], writes=[ft[0][tp]])
                op("pool", "tensor_tensor", out=F[2][:, tk], in0=F[2][:, tk], in1=F[3][:, tk], op=ALU.mult,
                   reads=[ft[2][tp], ft[3][tp]], writes=[ft[2][tp]])
            proj_lerp(hp, F[4], ft[4])
            for tp in range(4):
                tk = tok(tp)
                op("dve", "scalar_tensor_tensor", out=TA, in0=F[4][:, tk], scalar=ppc(("rk", l, hp)), in1=F[0][:, tk], op0=ALU.mult, op1=ALU.mult,
                   reads=[ft[4][tp], ft[0][tp], t_pp], writes=[ta_t])
                pq, pqt = PS.next()
                op("pe", "matmul", pq, lhsT=BLK, rhs=TA, start=True, stop=True, reads=[t_cst, ta_t], writes=[pqt])
                op("dve", "tensor_tensor", out=OT[:, oc, tk], in0=pq, in1=F[5][:, tk], op=ALU.mult, reads=[pqt, ft[5][tp]], writes=[ot_t[oc][tp]])
            for tp in range(4):
                tk = tok(tp)
                for cc in range(4):
                    cs = slice(tp * 512 + cc * 128, tp * 512 + cc * 128 + 128)
                    op("dve", "tensor_tensor_scan", out=TA[:, cc * 128:(cc + 1) * 128], data0=ONES, data1=F[1][:, cs], initial=0.0,
                       op0=ALU.mult, op1=ALU.add, reads=[t_cst, ft[1][tp]], writes=[ta_t])
                op("dve", "tensor_tensor", out=TB, in0=TA, in1=F[1][:, tk], op=ALU.subtract, reads=[ta_t, ft[1][tp]], writes=[tb_t])
                op("act", "activation", out=TB, in_=TB, func=AF.Exp, reads=[tb_t], writes=[tb_t])
                op("dve", "tensor_tensor", out=F[3][:, tk], in0=F[3][:, tk], in1=TB, op=ALU.mult, reads=[ft[3][tp], tb_t], writes=[ft[3][tp]])
                op("act", "activation", out=TB, in_=TA, func=AF.Exp, reads=[ta_t], writes=[tb_t])
                op("dve", "tensor_tensor", out=F[4][:, tk], in0=F[4][:, tk], in1=TB, op=ALU.mult, reads=[ft[4][tp], tb_t], writes=[ft[4][tp]])
                op("act", "copy", out=GC[:, tp * 4:tp * 4 + 4], in_=TB[:, 127:512:128], reads=[tb_t], writes=[gc_t])
                op("act", "activation", out=TB, in_=TA, func=AF.Exp, scale=-1.0, reads=[ta_t], writes=[tb_t])
                op("dve", "tensor_tensor", out=F[0][:, tk], in0=F[0][:, tk], in1=TB, op=ALU.mult, reads=[ft[0][tp], tb_t], writes=[ft[0][tp]])
                op("pool", "tensor_tensor", out=F[2][:, tk], in0=F[2][:, tk], in1=TB, op=ALU.mult, reads=[ft[2][tp], tb_t], writes=[ft[2][tp]])
            P.barrier()
            def cut(region, n, width):
                out = []
                for _ in range(n):
                    out.append(AR[:, region[0]:region[0] + width])
                    region[0] += width
                return out
            reg1 = [base + 2048]
            reg2 = [sm]
            reg3 = [sm + 2580]
            KhF, BhF = cut(reg1, 2, 128)
            MB = [cut(reg1, 2, 256) for _ in range(2)]
            TK = [cut(reg1, 3, 128) for _ in range(2)]
            assert reg1[0] <= base + 4096
            MK = [cut(reg3, 2, 256) for _ in range(2)]
            assert reg3[0] <= CFW, reg3[0]
            inv = [[cut(reg2, 3, 128) for _ in range(2)] for _ in range(2)]
            Wsb = cut(reg2, 2, 64)
            Usb = cut(reg2, 2, 64)
            ST = cut(reg2, 1, 64)[0]
            YC = cut(reg2, 1, 128)[0]
            G1 = cut(reg2, 3, 128)
            assert reg2[0] <= sm + 2564, reg2[0]
            t_kh, t_bh = TT(), TT()
            t_tk = [[TT(), TT(), TT()] for _ in range(2)]
            t_mb = [[TT(), TT()] for _ in range(2)]
            t_mk = [[TT(), TT()] for _ in range(2)]
            t_inv = [[[TT() for _ in range(3)] for _ in range(2)] for _ in range(2)]
            t_w, t_u = [TT(), TT()], [TT(), TT()]
            t_st = [TT(), TT()]
            t_yc = TT()
            t_g1 = [TT() for _ in range(3)]
            RW = [slice(0, 64), slice(64, 128)]
            op("pool", "memset", ST, 0.0, writes=t_st)

            def build(c):
                bf = c % 2
                cs = slice(c * 128, c * 128 + 128)
                tp = c // 4
                KhT, BhT, VT = TK[bf]
                t_kht, t_bht, t_vt = t_tk[bf]
                op("dve", "tensor_scalar", out=KhF, in0=F[0][:, cs], scalar1=GC[:, c:c + 1], scalar2=None, op0=ALU.mult,
                   reads=[ft[0][tp], gc_t], writes=[t_kh])
                op("pool", "tensor_scalar", out=BhF, in0=F[2][:, cs], scalar1=GC[:, c:c + 1], scalar2=None, op0=ALU.mult,
                   reads=[ft[2][tp], gc_t], writes=[t_bh])
                yield
                for (src, srct, dstT, dstt) in ((KhF, [t_kh], KhT, t_kht), (BhF, [t_bh], BhT, t_bht), (F[5][:, cs], [ft[5][tp]], VT, t_vt)):
                    ptr, ptrt = PS.next()
                    op("pe", "transpose", ptr[:, 0:128], src, IDENT, reads=srct + [t_cst], writes=[ptrt])
                    op("act", "copy", out=dstT, in_=ptr[:, 0:128], reads=[ptrt], writes=[dstt])
                    yield
                for hl in range(2):
                    rows = RW[hl]
                    for (lt, ltt, Mx, Mxt) in ((F[2], ft[2][tp], MB[bf][hl], t_mb[bf][hl]), (F[0], ft[0][tp], MK[bf][hl], t_mk[bf][hl])):
                        pmx, pmxt = PS.next()
                        op("pe", "matmul", pmx[:, 0:128], lhsT=lt[rows, cs], rhs=F[3][rows, cs], start=True, stop=True,
                           reads=[ltt, ft[3][tp]], writes=[pmxt])
                        op("pe", "matmul", pmx[:, 128:256], lhsT=lt[rows, cs], rhs=F[4][rows, cs], start=True, stop=True,
                           reads=[ltt, ft[4][tp]], writes=[pmxt])
                        op("dve", "tensor_tensor", out=Mx, in0=pmx[:, 0:256], in1=UPP, op=ALU.mult, reads=[pmxt, t_cst], writes=[Mxt])
                        yield
                    Pm, PTm, Zm = inv[bf][hl]
                    tPm, tPTm, tZm = t_inv[bf][hl]
                    op("act", "mul", out=Pm, in_=MB[bf][hl][:, 0:128], mul=-1.0, reads=[t_mb[bf][hl]], writes=[tPm])
                    pnt, pntt = PS.next()
                    op("pe", "matmul", pnt[:, 0:128], lhsT=F[3][rows, cs], rhs=F[2][rows, cs], start=True, stop=True,
                       reads=[ft[3][tp], ft[2][tp]], writes=[pntt])
                    op("dve", "scalar_tensor_tensor", out=PTm, in0=pnt[:, 0:128], scalar=-1.0, in1=LOWS, op0=ALU.mult, op1=ALU.mult,
                       reads=[pntt, t_cst], writes=[tPTm])
                    op("pool", "tensor_tensor", out=Zm, in0=Pm, in1=IDENT, op=ALU.add, reads=[tPm, t_cst], writes=[tZm])
                    yield
                for stg in range(6):
                    for hl in range(2):
                        Pm, PTm, Zm = inv[bf][hl]
                        tPm, tPTm, tZm = t_inv[bf][hl]
                        ppt2, ppt2t = PS.next()
                        op("pe", "matmul", ppt2[:, 0:128], lhsT=Pm, rhs=PTm, start=True, stop=True, reads=[tPm, tPTm], writes=[ppt2t])
                        if stg < 5:
                            pp2, pp2t = PS.next()
                            op("pe", "matmul", pp2[:, 0:128], lhsT=PTm, rhs=Pm, start=True, stop=True, reads=[tPm, tPTm], writes=[pp2t])
                        op("act", "copy", out=PTm, in_=ppt2[:, 0:128], reads=[ppt2t], writes=[tPTm])
                        if stg < 5:
                            op("act", "copy", out=Pm, in_=pp2[:, 0:128], reads=[pp2t], writes=[tPm])
                        yield
                    for hl in range(2):
                        Pm, PTm, Zm = inv[bf][hl]
                        tPm, tPTm, tZm = t_inv[bf][hl]
                        pz, pzt = PS.next()
                        op("pe", "matmul", pz[:, 0:128], lhsT=PTm, rhs=Zm, start=True, stop=True, reads=[tPTm, tZm], writes=[pzt])
                        op("dve", "tensor_tensor", out=Zm, in0=pz[:, 0:128], in1=Zm, op=ALU.add, reads=[pzt, tZm], writes=[tZm])
                        yield

            def seq(c):
                bf = c % 2
                cs = slice(c * 128, c * 128 + 128)
                tp = c // 4
                KhT, BhT, VT = TK[bf]
                t_kht, t_bht, t_vt = t_tk[bf]
                for hl in range(2):
                    rows, hc = RW[hl], RW[hl]
                    pw, pwt = PS.next()
                    op("pe", "matmul", pw[:, 0:64], lhsT=F[3][rows, cs], rhs=ST[rows, :], start=True, stop=False,
                       reads=[ft[3][tp], t_st[hl]], writes=[pwt])
                    op("pe", "matmul", pw[:, 0:64], lhsT=MK[bf][hl][:, 0:128], rhs=VT[:, hc], start=False, stop=True,
                       reads=[t_mk[bf][hl], t_vt], writes=[pwt])
                    op("act", "copy", out=Wsb[hl], in_=pw[:, 0:64], reads=[pwt], writes=[t_w[hl]])
                    yield
                for hl in range(2):
                    pu, put = PS.next()
                    op("pe", "matmul", pu[:, 0:64], lhsT=inv[bf][hl][2], rhs=Wsb[hl], start=True, stop=True,
                       reads=[t_inv[bf][hl][2], t_w[hl]], writes=[put])
                    op("act", "mul", out=Usb[hl], in_=pu[:, 0:64], mul=-1.0, reads=[put], writes=[t_u[hl]])
                    yield
                for hl in range(2):
                    rows, hc = RW[hl], RW[hl]
                    py, pyt = PS.next()
                    op("pe", "matmul", py[0:64, 0:128], lhsT=ST[rows, :], rhs=F[4][rows, cs], start=True, stop=False,
                       reads=[t_st[hl], ft[4][tp]], writes=[pyt])
                    op("pe", "matmul", py[0:64, 0:128], lhsT=Usb[hl], rhs=MB[bf][hl][:, 128:256], start=False, stop=False,
                       reads=[t_u[hl], t_mb[bf][hl]], writes=[pyt])
                    op("pe", "matmul", py[0:64, 0:128], lhsT=VT[:, hc], rhs=MK[bf][hl][:, 128:256], start=False, stop=True,
                       reads=[t_vt, t_mk[bf][hl]], writes=[pyt])
                    op("act", "copy", out=YC[rows, :], in_=py[0:64, 0:128], reads=[pyt], writes=[t_yc])
                    pn, pnt_ = PS.next()
                    op("pe", "matmul", pn[0:64, 0:64], lhsT=BhT[:, hc], rhs=Usb[hl], start=True, stop=False,
                       reads=[t_bht, t_u[hl]], writes=[pnt_])
                    op("pe", "matmul", pn[0:64, 0:64], lhsT=KhT[:, hc], rhs=VT[:, hc], start=False, stop=True,
                       reads=[t_kht, t_vt], writes=[pnt_])
                    op("dve", "scalar_tensor_tensor", out=ST[rows, :], in0=ST[rows, :], scalar=GC[rows, c:c + 1], in1=pn[0:64, 0:64],
                       op0=ALU.mult, op1=ALU.add, reads=[t_st[hl], gc_t, pnt_], writes=[t_st[hl]])
                    yield
                pmu, pmut = PS.next()
                op("pe", "matmul", pmu[:, 0:128], lhsT=BLK64, rhs=YC, start=True, stop=True, reads=[t_cst, t_yc], writes=[pmut])
                op("act", "activation", out=G1[0], in_=YC, func=AF.Square, reads=[t_yc], writes=[t_g1[0]])
                yield
                pe2, pe2t = PS.next()
                op("pe", "matmul", pe2[:, 0:128], lhsT=BLK64, rhs=G1[0], start=True, stop=True, reads=[t_cst, t_g1[0]], writes=[pe2t])
                op("act", "copy", out=G1[1], in_=pmu[:, 0:128], reads=[pmut], writes=[t_g1[1]])
                yield
                op("dve", "tensor_tensor", out=G1[0], in0=G1[1], in1=G1[1], op=ALU.mult, reads=[t_g1[1]], writes=[t_g1[0]])
                op("dve", "scalar_tensor_tensor", out=G1[2], in0=pe2[:, 0:128], scalar=GN_EPS, in1=G1[0], op0=ALU.add, op1=ALU.subtract,
                   reads=[pe2t, t_g1[0]], writes=[t_g1[2]])
                yield
                op("act", "activation", out=G1[2], in_=G1[2], func=AF.Ln, reads=[t_g1[2]], writes=[t_g1[2]])
                op("act", "activation", out=G1[2], in_=G1[2], func=AF.Exp, scale=-0.5, reads=[t_g1[2]], writes=[t_g1[2]])
                yield
                op("dve", "tensor_tensor", out=G1[0], in0=YC, in1=G1[1], op=ALU.subtract, reads=[t_yc, t_g1[1]], writes=[t_g1[0]])
                op("dve", "tensor_tensor", out=G1[0], in0=G1[0], in1=G1[2], op=ALU.mult, reads=[t_g1[0], t_g1[2]], writes=[t_g1[0]])
                yield
                op("act", "activation", out=G1[0], in_=G1[0], func=AF.Identity, bias=ppc(("gnb", l, hp)), scale=ppc(("gng", l, hp)),
                   reads=[t_g1[0], t_pp], writes=[t_g1[0]])
                op("pool", "tensor_tensor", out=OT[:, oc, cs], in0=OT[:, oc, cs], in1=G1[0], op=ALU.add,
                   reads=[ot_t[oc][tp], t_g1[0]], writes=[ot_t[oc][tp]])
                yield

            for _ in build(0):
                pass
            NCH = int(_os.environ.get("CSCAN", "16"))
            for c in range(NCH):
                gb = build(c + 1) if c + 1 < NCH else iter(())
                gs = seq(c)
                alive = [gb, gs]
                while alive:
                    for gsn in list(alive):
                        try:
                            next(gsn)
                        except StopIteration:
                            alive.remove(gsn)
            P.barrier()
            ft0 = [TT() for _ in range(4)]
            rawp_t2 = TT()
            w, wt = w1.next()
            dma("pool", "dma_start", out=w, in_=dr["wch"][l * NCHUNK_W + 24 + 7].bitcast(F32R).rearrange("p (kc f) -> p kc f", kc=8),
                writes=[wt])
            op("pool", "memset", RAWP[:, 0:1], 0.0, writes=[rawp_t2])
            mu = ppc(("mu", l, 7))
            ta2, tb2 = TT(), TT()
            for tp in range(4):
                pp_, ppt = PS.next()
                for kc in range(8):
                    op("pe", "matmul", pp_, lhsT=w[:, kc, :], rhs=XT[:, kc, tok(tp)], start=(kc == 0), stop=(kc == 7),
                       reads=[wt, xt_t[kc][tp]], writes=[ppt])
                op("act", "copy", out=RAWP[:, 1:513], in_=pp_, reads=[ppt], writes=[rawp_t2])
                op("dve", "tensor_tensor", out=TA, in0=RAWP[:, 0:512], in1=RAWP[:, 1:513], op=ALU.subtract, reads=[rawp_t2], writes=[ta2])
                op("dve", "scalar_tensor_tensor", out=TB, in0=TA, scalar=mu, in1=RAWP[:, 1:513], op0=ALU.mult, op1=ALU.add,
                   reads=[ta2, rawp_t2, t_pp], writes=[tb2])
                op("act", "copy", out=RAWP[:, 0:1], in_=RAWP[:, 512:513], reads=[rawp_t2], writes=[rawp_t2])
                op("act", "activation", out=TB, in_=TB, func=AF.Sigmoid, reads=[tb2], writes=[tb2])
                pg, pgt = PS.next()
                op("pe", "matmul", pg, lhsT=CSM[:, 256 + hp * 128:256 + hp * 128 + 128], rhs=TB, start=True, stop=True,
                   reads=[t_csm, tb2], writes=[pgt])
                op("dve", "tensor_tensor", out=OT[:, oc, tok(tp)], in0=pg, in1=OT[:, oc, tok(tp)], op=ALU.mult,
                   reads=[pgt, ot_t[oc][tp]], writes=[ot_t[oc][tp]])
            P.end_phase(st)

        if stage in ("A", "mix", "full"):
            for i in range(3):
                phase_a(i)
        else:
            for c in range(0, 3):
                op("pool", "memset", OT[:, c, :], 0.0, writes=ot_t[c])
        if stage in ("B", "mix", "full"):
            for i in range(3):
                phase_b(i)
        else:
            for c in range(3, 6):
                op("pool", "memset", OT[:, c, :], 0.0, writes=ot_t[c])
        if stage in ("C", "mix", "full"):
            for hp in range(2):
                phase_c(hp)
        else:
            for c in range(6, 8):
                op("pool", "memset", OT[:, c, :], 0.0, writes=ot_t[c])
        return OT, ot_t, None

    def wout_ln(l, OT, ot_t, w_r):
        st = P.phase()
        WL = P.ph_sb(st, "WL", [128, 4096 + 1536 + 2048], F32)
        sc = Rot([WL[:, k * 512:(k + 1) * 512] for k in range(8)])
        lnk = Rot([WL[:, 4096 + k * 512:4096 + (k + 1) * 512] for k in range(3)])
        wo_r = Rot([WL[:, 5632 + k * 1024:5632 + (k + 1) * 1024].rearrange("p (kc f) -> p kc f", kc=8) for k in range(2)])
        for dc in range(8):
            w, wt = wo_r.next()
            dma("sp", "dma_start", out=w, in_=dr["wch"][l * NCHUNK_W + 38 + dc].rearrange("p (kc f) -> p kc f", kc=8), writes=[wt])
            for tp in range(4):
                py, pyt = PS.next()
                for kc in range(8):
                    op("pe", "matmul", py, lhsT=w[:, kc, :], rhs=OT[:, kc, tok(tp)], start=(kc == 0), stop=(kc == 7),
                       reads=[wt, ot_t[kc][tp]], writes=[pyt])
                op("dve", "scalar_tensor_tensor", out=XT[:, dc, tok(tp)], in0=py, scalar=1.0 / ALPHA, in1=XF[:, dc, tok(tp)],
                   op0=ALU.mult, op1=ALU.add, reads=[pyt, xt_t[dc][tp]], writes=[xt_t[dc][tp]])
        for tp in range(4):
            layer_norm(l, 1, tp, sc, lnk)
        return st

    def dump_o(OT, ot_t):
        for c in range(8):
            dma("sp", "dma_start", out=out_d[c * 128:(c + 1) * 128, :], in_=OT[:, c, :], reads=ot_t[c])

    def dump_x():
        for c in range(8):
            dma("sp", "dma_start", out=out_d[c * 128:(c + 1) * 128, :], in_=XF[:, c, :], reads=xt_t[c])

    dumped = False
    stage = dbg[1] if dbg else "full"
    for l in range(depth):
        st = ffn_phase(l, 0)
        if dbg == ("x", "ffn_a") and l == depth - 1:
            dump_x()
            dumped = True
            P.end_phase(st)
            break
        P.end_phase(st)
        stm = P.phase()
        OTt = P.ph_sb(stm, "OT", [128, 16384], F32)
        OT, ot_t, w_r = mixer_phase(l, stage)
        if dbg is not None and dbg[0] == "o" and l == depth - 1:
            dump_o(OT, ot_t)
            dumped = True
            P.end_phase(stm)
            break
        st = wout_ln(l, OT, ot_t, w_r)
        if dbg == ("x", "mixln") and l == depth - 1:
            dump_x()
            dumped = True
            P.end_phase(st)
            P.end_phase(stm)
            break
        P.end_phase(st)
        P.end_phase(stm)
        st = ffn_phase(l, 1)
        if l == depth - 1:
            dump_x()
            dumped = True
        P.end_phase(st)
    P.barrier()
    P.finish()
    return nc


_CACHE = {}


def kernel(**inputs):
    sh, xs, idx = prep_inputs(inputs)
    shapes = {k: v.shape for k, v in sh.items()}
    nc = build(shapes, idx)
    n = len(xs)
    in_maps = []
    for b in range(n):
        m = dict(sh)
        m["xT"] = xs[b]
        in_maps.append(m)
    res = run_bass_kernel_spmd(nc, in_maps, core_ids=list(range(n)))
    out = np.stack([np.ascontiguousarray(res.results[b]["out"].T) for b in range(n)], axis=0)
    return out.astype(np.float32)
```

```python
import contextlib
import math
import numpy as np
import concourse.bass as bass
import concourse.mybir as mybir
from concourse.bass_utils import run_bass_kernel_spmd

F32 = mybir.dt.float32
F32R = mybir.dt.float32r
AF = mybir.ActivationFunctionType
ALU = mybir.AluOpType
AX = mybir.AxisListType

DEPTH = 4
D = 1024
S = 2048
FF = 2816
NFC = 22
ALPHA = (2 * DEPTH) ** 0.25
LN_EPS = 1e-5
RMS_EPS = 1e-5
GN_EPS = 64e-5
THETA = 10000.0
NCHUNK_W = 38 + 8
ARC = 31300
CFW = 14916 + 1024


class TT:
    __slots__ = ("w", "r")

    def __init__(self):
        self.w = None
        self.r = []


class Prog:
    ENG = ("pe", "act", "dve", "pool", "sp")

    def __init__(self, nc, same_sync=True, n_dma_sems=40):
        self.nc = nc
        self.same_sync = same_sync
        self.q = {e: [] for e in self.ENG}
        self.cnt = {e: 0 for e in self.ENG}
        self.known = {e: {} for e in self.ENG}
        self.stack = contextlib.ExitStack()
        self.sem = {}
        for e in self.ENG:
            self.sem[e] = self.stack.enter_context(nc.semaphore("s_" + e))
        self.dsem = []
        self.dval = []
        for i in range(n_dma_sems):
            self.dsem.append(self.stack.enter_context(nc.semaphore("d%d" % i)))
            self.dval.append(0)
        self.ndma = 0

    def sb(self, name, shape, dt=F32):
        return self.stack.enter_context(self.nc.sbuf_tensor(name, shape, dt))

    def ps(self, name, shape, dt=F32):
        return self.stack.enter_context(self.nc.psum_tensor(name, shape, dt))

    def _waits(self, eng, reads, writes):
        deps = {}
        for t in reads:
            if t.w is not None:
                k, v = t.w
                if deps.get(k, 0) < v:
                    deps[k] = v
        for t in writes:
            if t.w is not None:
                k, v = t.w
                if deps.get(k, 0) < v:
                    deps[k] = v
            for (k, v) in t.r:
                if deps.get(k, 0) < v:
                    deps[k] = v
        out = []
        kn = self.known[eng]
        for k, v in deps.items():
            if k == eng and (eng == "pe" or not self.same_sync):
                continue
            if kn.get(k, 0) >= v:
                continue
            kn[k] = v
            out.append((k, v))
        return out

    def _semh(self, k):
        return self.sem[k] if isinstance(k, str) else self.dsem[k]

    def _mark(self, tok, reads, writes):
        for t in reads:
            t.r.append(tok)
            if len(t.r) > 24:
                best = {}
                for k, v in t.r:
                    if best.get(k, 0) < v:
                        best[k] = v
                t.r = list(best.items())
        for t in writes:
            t.w = tok
            t.r = []

    def op(self, eng, fn, *args, reads=(), writes=(), **kwargs):
        waits = self._waits(eng, reads, writes)
        self.cnt[eng] += 1
        self._mark((eng, self.cnt[eng]), reads, writes)
        sem = self.sem[eng]
        wl = [(self._semh(k), v) for k, v in waits]

        def emit(e, fn=fn, wl=wl, sem=sem, args=args, kwargs=kwargs):
            for s, v in wl:
                e.wait_ge(s, v)
            getattr(e, fn)(*args, **kwargs).then_inc(sem, 1)

        self.q[eng].append(emit)

    def dma(self, eng, fn, *args, reads=(), writes=(), di=None, **kwargs):
        if di is None:
            di = self.ndma % len(self.dsem)
            self.ndma += 1
        waits = self._waits(eng, reads, writes)
        self.dval[di] += 16
        self._mark((di, self.dval[di]), reads, writes)
        sem = self.dsem[di]
        wl = [(self._semh(k), v) for k, v in waits]

        def emit(e, fn=fn, wl=wl, sem=sem, args=args, kwargs=kwargs):
            for s, v in wl:
                e.wait_ge(s, v)
            getattr(e, fn)(*args, **kwargs).then_inc(sem, 16)

        self.q[eng].append(emit)

    def barrier(self):
        for e in self.ENG:
            wl = []
            kn = self.known[e]
            for k in self.ENG:
                if k != e and self.cnt[k] > kn.get(k, 0):
                    kn[k] = self.cnt[k]
                    wl.append((self.sem[k], self.cnt[k]))
            for i, v in enumerate(self.dval):
                if v > kn.get(i, 0):
                    kn[i] = v
                    wl.append((self.dsem[i], v))
            if self.same_sync and e != "pe" and self.cnt[e] > kn.get(e, 0):
                kn[e] = self.cnt[e]
                wl.append((self.sem[e], self.cnt[e]))

            def emit(en, wl=wl):
                for s, v in wl:
                    en.wait_ge(s, v)

            self.q[e].append(emit)

    def phase(self):
        return contextlib.ExitStack()

    def ph_sb(self, st, name, shape, dt=F32):
        self.uid = getattr(self, "uid", 0) + 1
        return st.enter_context(self.nc.sbuf_tensor("%s_%d" % (name, self.uid), shape, dt))

    def end_phase(self, st):
        self.barrier()
        self.flush()
        st.close()

    def finish(self):
        self.flush()
        self.stack.close()

    def flush(self):
        nc = self.nc
        q = self.q
        self.q = {e: [] for e in self.ENG}
        with nc.Block() as block:
            @block.tensor
            def _(e):
                for f in q["pe"]:
                    f(e)

            @block.scalar
            def _(e):
                for f in q["act"]:
                    f(e)

            @block.vector
            def _(e):
                for f in q["dve"]:
                    f(e)

            @block.gpsimd
            def _(e):
                for f in q["pool"]:
                    f(e)

            @block.sync
            def _(e):
                for f in q["sp"]:
                    f(e)


class Rot:
    def __init__(self, views, tts=None):
        self.v = views
        self.t = tts if tts is not None else [TT() for _ in views]
        self.i = 0

    def next(self):
        i = self.i % len(self.v)
        self.i += 1
        return self.v[i], self.t[i]


def _chunk(W, cols):
    return np.ascontiguousarray(W[:, cols].reshape(8, 128, len(cols)).transpose(1, 0, 2))


def _swap_idx(base, n, grp):
    idx = np.arange(n)
    half = grp // 2
    return base + (idx // grp) * grp + ((idx % grp) + half) % grp


def _rope_table(grp):
    half = grp // 2
    inv = (THETA ** (-np.arange(0, grp, 2, dtype=np.float32) / grp)).astype(np.float32)
    pos = np.arange(S, dtype=np.float32)
    ang = (pos[:, None] * inv[None, :]).astype(np.float32)
    cos = np.cos(ang).astype(np.float32).T
    sin = np.sin(ang).astype(np.float32).T
    r = np.arange(128)
    i = r % half
    sign = np.where((r % grp) < half, -1.0, 1.0).astype(np.float32)
    C = cos[i]
    Sg = sin[i] * sign[:, None]
    t = np.stack([C, Sg], axis=1)
    t = t.reshape(128, 2, 4, 512).transpose(0, 2, 1, 3)
    return np.ascontiguousarray(t.reshape(128, 4 * 2 * 512)).astype(np.float32)


def _consts():
    c = {}
    k = np.arange(128)[:, None]
    q = np.arange(256)[None, :]
    c["causal"] = (q[:, :128] >= k).astype(np.float32)
    c["band"] = ((q >= k) & (q <= k + 128)).astype(np.float32)
    su = (q[:, :128] > k).astype(np.float32)
    iu = (q[:, :128] >= k).astype(np.float32)
    c["uppers"] = np.concatenate([su, iu], axis=1)
    c["ident"] = np.eye(128, dtype=np.float32)
    blk = np.zeros((128, 128), np.float32)
    blk[:64, :64] = 1.0
    blk[64:, 64:] = 1.0
    c["blk"] = blk
    c["lowers"] = (q[:, :128] < k).astype(np.float32)
    return c


def prep_inputs(inp):
    L = DEPTH
    f = lambda a: np.asarray(a, dtype=np.float32)
    sh = {}
    wgu = np.empty((L, 2, 22, 128, 2, 8, 128), np.float32)
    wd = np.empty((L, 2, 8, 128, 22, 128), np.float32)
    ffw = {"a": (inp["ffn_a_gate"], inp["ffn_a_up"], inp["ffn_a_down"]),
           "b": (inp["ffn_b_gate"], inp["ffn_b_up"], inp["ffn_b_down"])}
    for l in range(L):
        for i, nm in enumerate("ab"):
            g = f(ffw[nm][0][l]).reshape(8, 128, 22, 128).transpose(2, 1, 0, 3)
            u = f(ffw[nm][1][l]).reshape(8, 128, 22, 128).transpose(2, 1, 0, 3)
            wgu[l, i, :, :, 0] = g
            wgu[l, i, :, :, 1] = u
            wd[l, i] = f(ffw[nm][2][l]).reshape(22, 128, 8, 128).transpose(2, 1, 0, 3)
    sh["wgu"] = wgu.reshape(L * 2 * 22, 128, 2048)
    sh["wd"] = wd.reshape(L * 2 * 8, 128, 2816)
    wch = np.empty((L, NCHUNK_W, 128, 8, 128), np.float32)
    for l in range(L):
        W = f(inp["w_in"][l])
        ci = 0
        for base, grp in ((0, 32), (1152, 64)):
            for i in range(3):
                qc = base + 128 * i + np.arange(128)
                kc_ = base + 384 + 128 * i + np.arange(128)
                wch[l, ci + 0] = _chunk(W, qc)
                wch[l, ci + 1] = _chunk(W, _swap_idx(base + 128 * i, 128, grp))
                wch[l, ci + 2] = _chunk(W, kc_)
                wch[l, ci + 3] = _chunk(W, _swap_idx(base + 384 + 128 * i, 128, grp))
                ci += 4
        for c in range(8):
            wch[l, 24 + c] = _chunk(W, 2304 + 128 * c + np.arange(128))
        for i in range(3):
            wch[l, 32 + i] = _chunk(W, 768 + 128 * i + np.arange(128))
            wch[l, 35 + i] = _chunk(W, 1920 + 128 * i + np.arange(128))
        Wo = f(inp["w_out"][l])
        for dc in range(8):
            wch[l, 38 + dc] = _chunk(Wo, 128 * dc + np.arange(128))
    sh["wch"] = wch.reshape(L * NCHUNK_W, 128, 1024)
    sh["ropeA"] = _rope_table(32)
    sh["ropeB"] = _rope_table(64)
    for k_, v_ in _consts().items():
        sh["c_" + k_] = v_
    cols = []

    def addcol(v):
        cols.append(np.asarray(v, np.float32).reshape(128, 1))
        return len(cols) - 1

    idx = {}
    for l in range(L):
        for i in range(3):
            for c in range(8):
                idx[("lng", l, i, c)] = addcol(f(inp["ln_g"][l, i, c * 128:(c + 1) * 128]))
                idx[("lnb", l, i, c)] = addcol(f(inp["ln_b"][l, i, c * 128:(c + 1) * 128]))
        idx[("ga", l)] = addcol(np.tile(f(inp["a_norm_g"][l]), 2))
        idx[("gb", l)] = addcol(np.tile(f(inp["b_norm_g"][l]), 2))
        for c in range(8):
            idx[("mu", l, c)] = addcol(f(inp["c_mu"][l, c * 128:(c + 1) * 128]))
        for hp in range(2):
            sl = slice(hp * 128, hp * 128 + 128)
            idx[("w0", l, hp)] = addcol(f(inp["c_w0"][l, sl]))
            idx[("a0", l, hp)] = addcol(f(inp["c_a0"][l, sl]))
            idx[("kk", l, hp)] = addcol(f(inp["c_k_k"][l, sl]))
            idx[("ka", l, hp)] = addcol(f(inp["c_k_a"][l, sl]))
            idx[("rk", l, hp)] = addcol(f(inp["c_r_k"][l].reshape(256)[sl]))
            idx[("gng", l, hp)] = addcol(f(inp["c_gn_g"][l, sl]))
            idx[("gnb", l, hp)] = addcol(f(inp["c_gn_b"][l, sl]))
            if l > 0:
                idx[("v0", l, hp)] = addcol(f(inp["c_v0"][l - 1, sl]))
    sh["pp"] = np.ascontiguousarray(np.concatenate(cols, axis=1))
    lamrow = np.stack([np.stack([f(inp["a_lam_q1"][l]), f(inp["a_lam_k1"][l]),
                                 f(inp["a_lam_q2"][l]), f(inp["a_lam_k2"][l])]) for l in range(L)])
    sh["lamrow"] = np.ascontiguousarray(lamrow.reshape(1, L * 4 * 32))
    sm = np.zeros((L, 128, 256 + 256 + 64 + 256), np.float32)
    for l in range(L):
        sm[l, 0:64, 0:256] = f(inp["c_w2"][l])
        sm[l, 64:128, 0:256] = f(inp["c_a2"][l])
        sm[l, :, 256:512] = f(inp["c_g2"][l])
        if l > 0:
            sm[l, :, 512:576] = f(inp["c_v1"][l - 1]).reshape(2, 128, 32).transpose(1, 0, 2).reshape(128, 64)
            sm[l, 0:32, 576:832] = f(inp["c_v2"][l - 1])
    sh["csm"] = np.ascontiguousarray(sm.transpose(1, 0, 2).reshape(128, L * 832))
    xs = [np.ascontiguousarray(f(inp["x"][b]).T) for b in range(inp["x"].shape[0])]
    return sh, xs, idx


def build(shapes, idx, depth=DEPTH, dbg=None):
    nc = bass.Bass("TRN2", target_bir_lowering=False)
    dr = {}
    for k, shp in shapes.items():
        dr[k] = nc.dram_tensor(k, list(shp), F32, kind="ExternalInput").ap()
    xT_d = nc.dram_tensor("xT", [D, S], F32, kind="ExternalInput").ap()
    out_d = nc.dram_tensor("out", [D, S], F32, kind="ExternalOutput").ap()
    vf_d = nc.dram_tensor("vf_scratch", [256, S], F32, kind="Internal").ap()
    npp = shapes["pp"][1]

    import os as _os
    P = Prog(nc, same_sync=(_os.environ.get('NOSAME') is None))
    op, dma = P.op, P.dma
    XT = P.sb("XT", [128, 8, S], F32R)
    XF = XT[:].bitcast(F32)
    AR = None
    OTt = None
    CST = P.sb("CST", [128, 1472])
    PP = P.sb("PP", [128, npp])
    CSM = P.sb("CSM", [128, 832])
    LAM = P.sb("LAM", [128, 64 + DEPTH * 128 + 64])
    banks = [P.ps("pb%d" % i, [128, 512]) for i in range(8)]
    bank_t = [TT() for _ in range(8)]
    PS = Rot([b[:] for b in banks[0:6]], bank_t[0:6])
    PSL = Rot([b[:] for b in banks[6:8]], bank_t[6:8])
    PSA = Rot([b[:] for b in banks[0:4]], bank_t[0:4])
    PSLA = Rot([b[:] for b in banks[4:8]], bank_t[4:8])

    xt_t = [[TT() for _ in range(4)] for _ in range(8)]
    vf_t = [TT(), TT()]
    t_cst, t_pp, t_csm, t_lam, t_rst = TT(), TT(), TT(), TT(), TT()
    CAUS = CST[:, 0:128]
    BAND = CST[:, 128:384]
    UPP = CST[:, 384:640]
    IDENT = CST[:, 640:768]
    BLK = CST[:, 768:896]
    ONESD = CST[:, 896:1024]
    ONES64 = CST[:, 1024:1088]
    ONES = CST[:, 1088:1216]
    LOWS = CST[:, 1216:1344]
    BLK64 = CST[:, 1344:1472]
    dma("sp", "dma_start", out=CAUS, in_=dr["c_causal"], writes=[t_cst])
    dma("sp", "dma_start", out=BAND, in_=dr["c_band"], writes=[t_cst])
    dma("sp", "dma_start", out=UPP, in_=dr["c_uppers"], writes=[t_cst])
    dma("sp", "dma_start", out=IDENT, in_=dr["c_ident"], writes=[t_cst])
    dma("sp", "dma_start", out=BLK, in_=dr["c_blk"], writes=[t_cst])
    dma("sp", "dma_start", out=PP[:], in_=dr["pp"], writes=[t_pp])
    op("dve", "memset", ONESD, 1.0 / 1024.0, writes=[t_cst])
    op("dve", "memset", ONES64, 1.0 / 64.0, writes=[t_cst])
    op("dve", "memset", ONES, 1.0, writes=[t_cst])
    dma("sp", "dma_start", out=LOWS, in_=dr["c_lowers"], writes=[t_cst])
    op("act", "mul", out=BLK64, in_=BLK, mul=1.0 / 64.0, reads=[t_cst], writes=[t_cst])
    for c in range(8):
        dma("pool", "dma_start", out=XT[:, c, :], in_=xT_d[c * 128:(c + 1) * 128, :].bitcast(F32R),
            writes=xt_t[c])
    ONE1 = LAM[:, 0:64]
    op("dve", "memset", LAM[:], 0.0, writes=[t_lam])
    op("dve", "memset", ONE1, 1.0, writes=[t_lam])
    LROW = LAM[0:1, 64:64 + DEPTH * 128]
    dma("sp", "dma_start", out=LROW, in_=dr["lamrow"], writes=[t_lam])
    NEGLAM = LAM[:, 64 + DEPTH * 128:64 + DEPTH * 128 + 8]
    LS = LAM[0:1, 64 + DEPTH * 128 + 8:64 + DEPTH * 128 + 64]
    for l in range(depth):
        b0 = 64 + l * 128
        lam_init = 0.8 - 0.6 * math.exp(-0.3 * l)
        for j in range(2):
            op("dve", "tensor_tensor", out=LS[:, 0:32], in0=LAM[0:1, b0 + 64 * j:b0 + 64 * j + 32],
                                                          in1=LAM[0:1, b0 + 64 * j + 32:b0 + 64 * j + 64], op=ALU.mult,
               reads=[t_lam], writes=[t_lam])
            op("dve", "reduce_sum", out=LS[:, 32 + j:33 + j], in_=LS[:, 0:32], axis=AX.X,
               reads=[t_lam], writes=[t_lam])
        op("act", "activation", out=LS[:, 34:36], in_=LS[:, 32:34], func=AF.Exp, reads=[t_lam], writes=[t_lam])
        op("dve", "scalar_tensor_tensor", out=LS[:, 36:37], in0=LS[:, 35:36], scalar=-lam_init, in1=LS[:, 34:35],
                                                                op0=ALU.add, op1=ALU.subtract, reads=[t_lam], writes=[t_lam])
        pb, pt = PS.next()
        op("pe", "matmul", pb[0:64, 0:1], lhsT=LAM[0:1, 0:64], rhs=LS[:, 36:37], start=True, stop=True,
           reads=[t_lam], writes=[pt])
        op("act", "copy", out=NEGLAM[0:64, l:l + 1], in_=pb[0:64, 0:1], reads=[pt], writes=[t_lam])

    def ppc(key):
        j = idx[key]
        return PP[:, j:j + 1]

    def tok(tp):
        return slice(tp * 512, (tp + 1) * 512)

    def ssl(st, n, step):
        return slice(st, st + step * (n - 1) + 1, step) if step > 1 else slice(st, st + n)

    def layer_norm(l, i, tp, sc, lnk):
        eps = LN_EPS / (ALPHA * ALPHA)
        pm, pmt = PS.next()
        pe2, pe2t = PS.next()
        for c in range(8):
            op("pe", "matmul", pm, lhsT=ONESD, rhs=XF[:, c, tok(tp)], start=(c == 0), stop=(c == 7),
               reads=[t_cst, xt_t[c][tp]], writes=[pmt])
        for c in range(8):
            sq, sqt = sc.next()
            op("act", "activation", out=sq, in_=XF[:, c, tok(tp)], func=AF.Square,
               reads=[xt_t[c][tp]], writes=[sqt])
            op("pe", "matmul", pe2, lhsT=ONESD, rhs=sq, start=(c == 0), stop=(c == 7),
               reads=[t_cst, sqt], writes=[pe2t])
        mean, meant = lnk.next()
        op("act", "copy", out=mean, in_=pm, reads=[pmt], writes=[meant])
        msq, msqt = lnk.next()
        op("dve", "tensor_tensor", out=msq, in0=mean, in1=mean, op=ALU.mult, reads=[meant], writes=[msqt])
        var, vart = lnk.next()
        op("dve", "scalar_tensor_tensor", out=var, in0=pe2, scalar=eps, in1=msq, op0=ALU.add, op1=ALU.subtract,
           reads=[pe2t, msqt], writes=[vart])
        op("act", "activation", out=var, in_=var, func=AF.Ln, reads=[vart], writes=[vart])
        op("act", "activation", out=var, in_=var, func=AF.Exp, scale=-0.5, reads=[vart], writes=[vart])
        for c in range(8):
            t1, t1t = sc.next()
            op("dve", "tensor_tensor", out=t1, in0=XF[:, c, tok(tp)], in1=mean, op=ALU.subtract,
               reads=[xt_t[c][tp], meant], writes=[t1t])
            op("dve", "tensor_tensor", out=t1, in0=t1, in1=var, op=ALU.mult,
               reads=[t1t, vart], writes=[t1t])
            op("act", "activation", out=XT[:, c, tok(tp)], in_=t1, func=AF.Identity,
                                                        bias=ppc(("lnb", l, i, c)), scale=ppc(("lng", l, i, c)),
               reads=[t1t, t_pp], writes=[xt_t[c][tp]])

    def ffn_phase(l, i):
        st = P.phase()
        FR = P.ph_sb(st, "FR", [128, 11264 + 4 * 2048 + 2 * 2816], F32R)
        FT = P.ph_sb(st, "FT", [128, 11 * 512], F32)
        o = 0
        Hv = FR[:, o:o + 22 * 512].rearrange("p (f t) -> p f t", t=512)
        o += 22 * 512
        h_t = [TT() for _ in range(22)]
        wgu_r = Rot([FR[:, o + k * 2048:o + (k + 1) * 2048].rearrange("p (g kc f) -> p g kc f", g=2, kc=8) for k in range(4)])
        o += 4 * 2048
        wd_r = Rot([FR[:, o + k * 2816:o + (k + 1) * 2816].rearrange("p (f d) -> p f d", d=128) for k in range(2)])
        o += 2 * 2816
        sc = Rot([FT[:, k * 512:(k + 1) * 512] for k in range(8)])
        lnk = Rot([FT[:, (8 + k) * 512:(9 + k) * 512] for k in range(3)])
        lnidx = 0 if i == 0 else 2
        def p1(tp):
            for f in range(22):
                wb, wbt = wgu_r.next()
                dma("pool", "dma_start", out=wb,
                    in_=dr["wgu"][(l * 2 + i) * 22 + f].bitcast(F32R).rearrange("p (g kc f) -> p g kc f", g=2, kc=8), writes=[wbt])
                pg, pgt = PS.next()
                pu, put = PS.next()
                for g, pp_, ppt in ((0, pg, pgt), (1, pu, put)):
                    for kc in range(8):
                        op("pe", "matmul", pp_, lhsT=wb[:, g, kc, :], rhs=XT[:, kc, tok(tp)], start=(kc == 0), stop=(kc == 7),
                           reads=[wbt, xt_t[kc][tp]], writes=[ppt])
                sg, sgt = sc.next()
                op("act", "activation", out=sg, in_=pg, func=AF.Silu, reads=[pgt], writes=[sgt])
                op("dve", "tensor_tensor", out=Hv[:, f, :], in0=sg, in1=pu, op=ALU.mult, reads=[sgt, put], writes=[h_t[f]])

        def p2(tp):
            for dc in range(8):
                wb, wbt = wd_r.next()
                dma("pool", "dma_start", out=wb,
                    in_=dr["wd"][(l * 2 + i) * 8 + dc].bitcast(F32R).rearrange("p (f d) -> p f d", d=128), writes=[wbt])
                py, pyt = PS.next()
                for f in range(22):
                    op("pe", "matmul", py, lhsT=wb[:, f, :], rhs=Hv[:, f, :], start=(f == 0), stop=(f == 21),
                       reads=[wbt, h_t[f]], writes=[pyt])
                op("dve", "scalar_tensor_tensor", out=XT[:, dc, tok(tp)], in0=py, scalar=0.5 / ALPHA, in1=XF[:, dc, tok(tp)],
                   op0=ALU.mult, op1=ALU.add, reads=[pyt, xt_t[dc][tp]], writes=[xt_t[dc][tp]])

        for tp in range(4):
            p1(tp)
            if tp > 0:
                layer_norm(l, lnidx, tp - 1, sc, lnk)
            p2(tp)
        layer_norm(l, lnidx, 3, sc, lnk)
        return st

    def mixer_phase(l, stage):
        OT = OTt[:].rearrange("p (c t) -> p c t", t=S)
        ot_t = [[TT() for _ in range(4)] for _ in range(8)]
        QT = KT = KZ = VA = RS = VTF = vtf_t = EREG = None
        w_r = rope_r = e512 = e256 = scr = None
        SCR0 = None
        qt_t = kt_t = None
        va_t = kz_t = None

        def setup_ab(is_a):
            nonlocal QT, KT, KZ, VA, RS, w_r, rope_r, e512, e256, scr, SCR0, qt_t, kt_t, va_t, kz_t, VTF, vtf_t, EREG
            st = P.phase()
            AQ = P.ph_sb(st, "AQ", [128, 11808 if is_a else 9760 + 768], F32R)
            ABp = P.ph_sb(st, "ABp", [128, 3072 if is_a else 5120], F32)
            o = 0
            QT = AQ[:, o:o + 2048]
            KT = AQ[:, o + 2048:o + 4096]
            o += 4096
            qt_t = [TT() for _ in range(4)]
            kt_t = [TT() for _ in range(4)]
            VA = AQ[:, o:o + 2080].rearrange("p (j h d) -> p j h d", h=2, d=65)
            va_t = TT()
            o += 2080
            w_r = Rot([AQ[:, o + k * 1024:o + (k + 1) * 1024].rearrange("p (kc f) -> p kc f", kc=8) for k in range(2)])
            VTF = AQ[:, o:o + 2048]
            vtf_t = TT()
            o += 2048
            EREG = AQ[:, o:o + 1024]
            e512 = Rot([AQ[:, o + k * 512:o + (k + 1) * 512] for k in range(3)])
            e256 = Rot([AQ[:, o + k * 256:o + (k + 1) * 256] for k in range(6 if is_a else 9)])
            o += 1536 if is_a else 2304
            if is_a:
                KZ = AQ[:, o:o + 2048]
                kz_t = TT()
                o += 2048
            rope_r = Rot([ABp[:, 0:1024].rearrange("p (c t) -> p c t", c=2)])
            scr = Rot([ABp[:, 1024 + k * 512:1024 + (k + 1) * 512] for k in range(4)])
            SCR0 = ABp
            if not is_a:
                RS = ABp[:, 3072:5120]
            return st

        lam_init = 0.8 - 0.6 * math.exp(-0.3 * l)

        def load_w(ci):
            w, wt = w_r.next()
            dma("pool", "dma_start", out=w, in_=dr["wch"][l * NCHUNK_W + ci].bitcast(F32R).rearrange("p (kc f) -> p kc f", kc=8),
                writes=[wt])
            return w, wt

        def proj_rope(ci, dst, dst_t, rname):
            w1, w1t = load_w(ci)
            w2, w2t = load_w(ci + 1)
            for tp in range(4):
                rp, rpt = rope_r.next()
                dma("sp", "dma_start", out=rp, in_=dr[rname][:, tp * 1024:(tp + 1) * 1024].rearrange("p (c t) -> p c t", c=2),
                    writes=[rpt])
                p1, p1t = PS.next()
                p2, p2t = PS.next()
                for w, wt, pp_, ppt in ((w1, w1t, p1, p1t), (w2, w2t, p2, p2t)):
                    for kc in range(8):
                        op("pe", "matmul", pp_, lhsT=w[:, kc, :], rhs=XT[:, kc, tok(tp)], start=(kc == 0), stop=(kc == 7),
                           reads=[wt, xt_t[kc][tp]], writes=[ppt])
                a, at = scr.next()
                b, bt = scr.next()
                op("dve", "tensor_tensor", out=a, in0=p1, in1=rp[:, 0, :], op=ALU.mult, reads=[p1t, rpt], writes=[at])
                op("dve", "tensor_tensor", out=b, in0=p2, in1=rp[:, 1, :], op=ALU.mult, reads=[p2t, rpt], writes=[bt])
                op("pool", "tensor_tensor", out=dst[:, tok(tp)], in0=a, in1=b, op=ALU.add, reads=[at, bt], writes=[dst_t[tp]])

        def proj_vfm(ci):
            P.barrier()
            wv = EREG.rearrange("p (kc f) -> p kc f", kc=8)
            wvt = TT()
            dma("pool", "dma_start", out=wv, in_=dr["wch"][l * NCHUNK_W + ci].bitcast(F32R).rearrange("p (kc f) -> p kc f", kc=8),
                writes=[wvt])
            for tp in range(4):
                pv, pvt = PS.next()
                for kc in range(8):
                    op("pe", "matmul", pv, lhsT=wv[:, kc, :], rhs=XT[:, kc, tok(tp)], start=(kc == 0), stop=(kc == 7),
                       reads=[wvt, xt_t[kc][tp]], writes=[pvt])
                if tp % 2 == 0:
                    op("act", "copy", out=VTF[:, tok(tp)], in_=pv, reads=[pvt], writes=[vtf_t])
                else:
                    op("dve", "tensor_copy", out=VTF[:, tok(tp)], in_=pv, reads=[pvt], writes=[vtf_t])
            P.barrier()

        def proj_v(ci, dil):
            op("act", "copy", out=VA[:, :, :, 64:65], in_=ONES[:, 0:32].rearrange("p (j h d) -> p j h d", h=2, d=1),
               reads=[t_cst], writes=[va_t])
            nt = 16 // dil
            VTFf = VTF.bitcast(F32)
            k = 0
            for r in range(dil):
                for a in range(nt):
                    st = r + dil * 128 * a
                    pv, pvt = PS.next()
                    op("pe", "transpose", pv[:, 0:128], VTFf[:, ssl(st, 128, dil)], IDENT, reads=[vtf_t, t_cst], writes=[pvt])
                    if k % 2 == 0:
                        op("act", "copy", out=VA[:, r * nt + a, :, 0:64], in_=pv[:, 0:128].rearrange("p (h d) -> p h d", h=2),
                           reads=[pvt], writes=[va_t])
                    else:
                        op("dve", "tensor_copy", out=VA[:, r * nt + a, :, 0:64], in_=pv[:, 0:128].rearrange("p (h d) -> p h d", h=2),
                           reads=[pvt], writes=[va_t])
                    k += 1

        def phase_a(i):
            st = setup_ab(True)
            proj_rope(4 * i, QT, qt_t, "ropeA")
            proj_rope(4 * i + 2, KT, kt_t, "ropeA")
            proj_vfm(32 + i)
            proj_v(32 + i, 1)
            op("pool", "tensor_copy", out=KZ[96:128, :], in_=KT[96:128, :].bitcast(F32), reads=kt_t, writes=[kz_t])
            op("pool", "tensor_scalar", out=KZ[64:96, :], in0=KT[64:96, :].bitcast(F32), scalar1=0.0, scalar2=None, op0=ALU.mult,
               reads=kt_t, writes=[kz_t])
            sca = 32 ** -0.5
            bg = []

            def step_bg():
                if bg:
                    try:
                        next(bg[0])
                    except StopIteration:
                        bg.pop(0)

            def drain_bg(keep=0):
                while len(bg) > keep:
                    step_bg()

            def main_block(hl, Qp, nums):
                for m in range(2):
                    r0 = 32 * (2 * hl + m)
                    num, numt = PSLA.next()
                    nums.append((num, numt))
                    last = 4 * Qp + 3
                    pend = []

                    def do_pv(item, num=num, numt=numt, last=last):
                        j, E, Et, c0, n = item
                        op("pe", "matmul", num[0:65, c0 - 512 * Qp:512], lhsT=VA[:, j, hl, :], rhs=E[:, 0:n],
                           start=(j == 0), stop=(j == last), reads=[va_t, Et], writes=[numt])

                    for j in range(last + 1):
                        c0 = max(128 * j, 512 * Qp)
                        n = 512 * Qp + 512 - c0
                        se, set_ = PSA.next()
                        if hl == 1 and m == 1:
                            op("pe", "matmul", se[:, 0:n], lhsT=KZ[64:128, 128 * j:128 * j + 128], rhs=QT[64:128, c0:c0 + n],
                               start=True, stop=True, reads=[kz_t] + qt_t[c0 // 512:Qp + 1], writes=[set_])
                        else:
                            op("pe", "matmul", se[:, 0:n], lhsT=KT[r0:r0 + 32, 128 * j:128 * j + 128], rhs=QT[r0:r0 + 32, c0:c0 + n],
                               start=True, stop=True, reads=[kt_t[j // 4]] + qt_t[c0 // 512:Qp + 1], writes=[set_])
                        E, Et = e512.next()
                        op("act", "activation", out=E[:, 0:n], in_=se[:, 0:n], func=AF.Exp, scale=sca, reads=[set_], writes=[Et])
                        if 128 * j >= 512 * Qp:
                            op("pool", "tensor_tensor", out=E[:, 0:128], in0=E[:, 0:128].bitcast(F32), in1=CAUS, op=ALU.mult,
                               reads=[Et, t_cst], writes=[Et])
                        pend.append((j, E, Et, c0, n))
                        if len(pend) > 2:
                            do_pv(pend.pop(0))
                        yield
                    while pend:
                        do_pv(pend.pop(0))
                        yield

            def post_block(hl, Qp, nums):
                (n1, n1t), (n2, n2t) = nums
                X0, X0t = scr.next()
                X1, X1t = scr.next()
                X2, X2t = scr.next()
                for (nn, nnt, prt) in ((n1, n1t, 64), (n2, n2t, 32)):
                    op("act", "activation", out=X0[prt:prt + 1, :], in_=nn[64:65, :], func=AF.Ln, reads=[nnt], writes=[X0t])
                    yield
                    op("act", "activation", out=X0[prt:prt + 1, :], in_=X0[prt:prt + 1, :], func=AF.Exp, scale=-1.0,
                       reads=[X0t], writes=[X0t])
                    yield
                for (prt, Xd, Xdt) in ((64, X1, X1t), (32, X2, X2t)):
                    pb, pbt = PSA.next()
                    op("pe", "matmul", pb[0:64, :], lhsT=ONE1[prt:prt + 1, 0:64], rhs=X0[prt:prt + 1, :], start=True, stop=True,
                       reads=[t_lam, X0t], writes=[pbt])
                    yield
                    op("act", "copy", out=Xd[0:64, :], in_=pb[0:64, :], reads=[pbt], writes=[Xdt])
                    yield
                op("dve", "tensor_tensor", out=X1[0:64, :], in0=n1[0:64, :], in1=X1[0:64, :], op=ALU.mult, reads=[n1t, X1t], writes=[X1t])
                yield
                op("dve", "tensor_tensor", out=X2[0:64, :], in0=n2[0:64, :], in1=X2[0:64, :], op=ALU.mult, reads=[n2t, X2t], writes=[X2t])
                yield
                op("dve", "scalar_tensor_tensor", out=X1[0:64, :], in0=X2[0:64, :], scalar=NEGLAM[0:64, l:l + 1], in1=X1[0:64, :],
                   op0=ALU.mult, op1=ALU.add, reads=[X1t, X2t, t_lam], writes=[X1t])
                yield
                op("act", "activation", out=X2[0:64, :], in_=X1[0:64, :], func=AF.Square, reads=[X1t], writes=[X2t])
                yield
                pm, pmt = PSA.next()
                op("pe", "matmul", pm[0:64, :], lhsT=ONES64[0:64, 0:64], rhs=X2[0:64, :], start=True, stop=True,
                   reads=[t_cst, X2t], writes=[pmt])
                yield
                op("dve", "tensor_scalar", out=X2[0:64, :], in0=pm[0:64, :], scalar1=RMS_EPS, scalar2=None, op0=ALU.add,
                   reads=[pmt], writes=[X2t])
                yield
                op("act", "activation", out=X2[0:64, :], in_=X2[0:64, :], func=AF.Ln, reads=[X2t], writes=[X2t])
                yield
                op("act", "activation", out=X2[0:64, :], in_=X2[0:64, :], func=AF.Exp, scale=-0.5, reads=[X2t], writes=[X2t])
                yield
                op("dve", "scalar_tensor_tensor", out=X1[0:64, :], in0=X1[0:64, :], scalar=ppc(("ga", l))[0:64, :], in1=X2[0:64, :],
                   op0=ALU.mult, op1=ALU.mult, reads=[X1t, X2t, t_pp], writes=[X1t])
                yield
                op("dve", "tensor_scalar", out=OT[64 * hl:64 * hl + 64, i, tok(Qp)], in0=X1[0:64, :], scalar1=1.0 - lam_init,
                   scalar2=None, op0=ALU.mult, reads=[X1t], writes=[ot_t[i][Qp]])
                yield

            for hl in range(2):
                for Qp in range(4):
                    nums = []
                    drain_bg(keep=1)
                    cnt = 0
                    for _ in main_block(hl, Qp, nums):
                        cnt += 1
                        if cnt % 2 == 0:
                            step_bg()
                    drain_bg(keep=0) if False else None
                    bg.append(post_block(hl, Qp, nums))
            drain_bg(0)
            P.end_phase(st)


        def phase_b(i):
            st = setup_ab(False)
            rs_t = [TT(), TT()]
            proj_rope(12 + 4 * i, QT, qt_t, "ropeB")
            proj_rope(12 + 4 * i + 2, KT, kt_t, "ropeB")
            scb = 64 ** -0.5
            oc = 3 + i
            proj_vfm(35 + i)
            for dil in (1, 4, 16):
                proj_v(35 + i, dil)
                L = S // dil
                nt = L // 128
                pieces = []
                for hl in range(2):
                    for r in range(dil):
                        Es = {}
                        for g in range((nt + 3) // 4):
                            pieces.append((hl, r, g, Es))

                def stage1(pc):
                    hl, r, g, Es = pc
                    rows = slice(64 * hl, 64 * hl + 64)
                    for a in range(max(4 * g - 1, 0), min(4 * g + 3, nt - 1) + 1):
                        if a in Es:
                            continue
                        nq = min(256, L - 128 * a)
                        st_ = r + dil * 128 * a
                        se, set_ = PS.next()
                        op("pe", "matmul", se[:, 0:nq], lhsT=KT[rows, ssl(st_, 128, dil)], rhs=QT[rows, ssl(st_, nq, dil)],
                           start=True, stop=True, reads=kt_t + qt_t, writes=[set_])
                        E, Et = e256.next()
                        op("act", "activation", out=E[:, 0:nq], in_=se[:, 0:nq], func=AF.Exp, scale=scb, reads=[set_], writes=[Et])
                        op("pool", "tensor_tensor", out=E[:, 0:nq], in0=E[:, 0:nq].bitcast(F32), in1=BAND[:, 0:nq], op=ALU.mult,
                           reads=[Et, t_cst], writes=[Et])
                        Es[a] = (E, Et, nq)

                def stage2(pc):
                    hl, r, g, Es = pc
                    rows = slice(64 * hl, 64 * hl + 64)
                    rowp = 64 if hl == 0 else 32
                    ncols = min(512, L - 512 * g)
                    alist = list(range(max(4 * g - 1, 0), min(4 * g + 3, nt - 1) + 1))
                    num, numt = PSL.next()
                    for ai, a in enumerate(alist):
                        E, Et, nq = Es[a]
                        lo = max(128 * a, 512 * g)
                        hi = min(128 * a + nq, 512 * g + ncols)
                        op("pe", "matmul", num[0:65, lo - 512 * g:hi - 512 * g], lhsT=VA[:, r * nt + a, hl, :],
                           rhs=E[:, lo - 128 * a:hi - 128 * a], start=(ai == 0), stop=(ai == len(alist) - 1),
                           reads=[va_t, Et], writes=[numt])
                    p0 = r + dil * 512 * g
                    ps_ = ssl(p0, ncols, dil)
                    if dil == 1:
                        op("dve", "tensor_copy", out=OT[rows, oc, ps_], in_=num[0:64, 0:ncols], reads=[numt], writes=ot_t[oc])
                        op("act", "copy", out=RS[rowp:rowp + 1, ps_], in_=num[64:65, 0:ncols], reads=[numt], writes=[rs_t[hl]])
                    else:
                        op("dve", "tensor_tensor", out=OT[rows, oc, ps_], in0=num[0:64, 0:ncols], in1=OT[rows, oc, ps_], op=ALU.add,
                           reads=[numt] + ot_t[oc], writes=ot_t[oc])
                        op("dve", "tensor_tensor", out=RS[rowp:rowp + 1, ps_], in0=num[64:65, 0:ncols], in1=RS[rowp:rowp + 1, ps_],
                           op=ALU.add, reads=[numt, rs_t[hl]], writes=[rs_t[hl]])

                stage1(pieces[0])
                for k in range(len(pieces)):
                    if k + 1 < len(pieces):
                        stage1(pieces[k + 1])
                    stage2(pieces[k])
            P.barrier()
            chains = []

            def post_chain(hl, tp, X, Xt):
                rows = slice(64 * hl, 64 * hl + 64)
                rowp = 64 if hl == 0 else 32
                op("act", "activation", out=X[rows, :], in_=OT[rows, oc, tok(tp)], func=AF.Square, reads=[ot_t[oc][tp]], writes=[Xt])
                yield
                op("act", "activation", out=X[rowp:rowp + 1, :], in_=RS[rowp:rowp + 1, tok(tp)], func=AF.Square,
                   scale=math.sqrt(RMS_EPS), reads=[rs_t[hl]], writes=[Xt])
                yield
                pm, pmt = PS.next()
                op("pe", "matmul", pm[0:64, :], lhsT=ONES64[rows, 0:64], rhs=X[rows, :], start=True, stop=False,
                   reads=[t_cst, Xt], writes=[pmt])
                op("pe", "matmul", pm[0:64, :], lhsT=ONE1[rowp:rowp + 1, 0:64], rhs=X[rowp:rowp + 1, :], start=False, stop=True,
                   reads=[t_lam, Xt], writes=[pmt])
                yield
                op("act", "activation", out=X[rows, :], in_=pm[0:64, :], func=AF.Ln, reads=[pmt], writes=[Xt])
                yield
                op("act", "activation", out=X[rows, :], in_=X[rows, :], func=AF.Exp, scale=-0.5, reads=[Xt], writes=[Xt])
                yield
                op("dve", "scalar_tensor_tensor", out=OT[rows, oc, tok(tp)], in0=OT[rows, oc, tok(tp)], scalar=ppc(("gb", l))[rows, :],
                   in1=X[rows, :], op0=ALU.mult, op1=ALU.mult, reads=[ot_t[oc][tp], Xt, t_pp], writes=[ot_t[oc][tp]])
                yield

            xt4 = [(SCR0[:, 1024 + k * 512:1024 + (k + 1) * 512], TT()) for k in range(4)]
            for hl in range(2):
                gens = [post_chain(hl, tp, xt4[tp][0], xt4[tp][1]) for tp in range(4)]
                while gens:
                    for gsn in list(gens):
                        try:
                            next(gsn)
                        except StopIteration:
                            gens.remove(gsn)
            P.end_phase(st)


        def phase_c(hp):
            st = P.phase()
            AR = P.ph_sb(st, "CF", [128, CFW], F32)
            oc = 6 + hp
            base = 0
            F = [AR[:, base + k * 2048:base + (k + 1) * 2048] for k in range(6)]
            ft = [[TT() for _ in range(4)] for _ in range(6)]
            sm = base + 6 * 2048
            CW = P.ph_sb(st, "CW", [128, 1024], F32R)
            w1 = Rot([CW[:, 0:1024].rearrange("p (kc f) -> p kc f", kc=8)])
            RAWPS = [AR[:, sm:sm + 516], AR[:, sm + 1024:sm + 1024 + 516]]
            rawp_ts = [TT(), TT()]
            RAWP = RAWPS[1]
            rawp_t = rawp_ts[1]
            TA = AR[:, sm + 1540:sm + 2052]
            TB = AR[:, sm + 2052:sm + 2564]
            ta_t, tb_t = TT(), TT()
            GC = AR[:, sm + 2564:sm + 2580]
            gc_t = TT()
            assert sm + 2580 <= CFW
            dma("sp", "dma_start", out=CSM[:], in_=dr["csm"][:, l * 832:(l + 1) * 832], writes=[t_csm])

            def loadw1(ci):
                w, wt = w1.next()
                dma("pool", "dma_start", out=w, in_=dr["wch"][l * NCHUNK_W + ci].bitcast(F32R).rearrange("p (kc f) -> p kc f", kc=8),
                    writes=[wt])
                return w, wt

            def proj_lerp(cidx, dst, dst_t):
                w, wt = loadw1(24 + cidx)
                mu = ppc(("mu", l, cidx))
                op("pool", "memset", RAWPS[0][:, 0:1], 0.0, writes=[rawp_ts[0]])
                for tp in range(4):
                    RP, RPt = RAWPS[tp % 2], rawp_ts[tp % 2]
                    RN, RNt = RAWPS[(tp + 1) % 2], rawp_ts[(tp + 1) % 2]
                    pp_, ppt = PS.next()
                    for kc in range(8):
                        op("pe", "matmul", pp_, lhsT=w[:, kc, :], rhs=XT[:, kc, tok(tp)], start=(kc == 0), stop=(kc == 7),
                           reads=[wt, xt_t[kc][tp]], writes=[ppt])
                    op("act", "copy", out=RP[:, 1:513], in_=pp_, reads=[ppt], writes=[RPt])
                    if tp < 3:
                        op("act", "copy", out=RN[:, 0:1], in_=RP[:, 512:513], reads=[RPt], writes=[RNt])
                    op("dve", "tensor_tensor", out=TA, in0=RP[:, 0:512], in1=RP[:, 1:513], op=ALU.subtract,
                       reads=[RPt], writes=[ta_t])
                    op("dve", "scalar_tensor_tensor", out=dst[:, tok(tp)], in0=TA, scalar=mu, in1=RP[:, 1:513],
                       op0=ALU.mult, op1=ALU.add, reads=[ta_t, RPt, t_pp], writes=[dst_t[tp]])

            Kt, Bt, KKt, Rt, Vt = F[0], F[2], F[3], F[4], F[5]
            LW = F[1]
            proj_lerp(4 + hp, F[5], ft[5])
            if l == 0:
                dma("sp", "dma_start", out=vf_d[hp * 128:(hp + 1) * 128, :], in_=F[5], reads=ft[5], writes=[vf_t[hp]])
            else:
                proj_lerp(4 + (1 - hp), F[4], ft[4])
                vch = {hp: (F[5], ft[5]), 1 - hp: (F[4], ft[4])}
                for tp in range(4):
                    p32, p32t = PS.next()
                    for kc in range(2):
                        vv, vvt = vch[kc]
                        op("pe", "matmul", p32[0:32, :], lhsT=CSM[:, 512 + kc * 32:512 + kc * 32 + 32], rhs=vv[:, tok(tp)],
                           start=(kc == 0), stop=(kc == 1), reads=[t_csm, vvt[tp]], writes=[p32t])
                    op("act", "copy", out=TB[0:32, :], in_=p32[0:32, :], reads=[p32t], writes=[tb_t])
                    pg, pgt = PS.next()
                    op("pe", "matmul", pg, lhsT=CSM[0:32, 576 + hp * 128:576 + hp * 128 + 128], rhs=TB[0:32, :], start=True, stop=True,
                       reads=[t_csm, tb_t], writes=[pgt])
                    op("act", "activation", out=TB, in_=pg, func=AF.Sigmoid, bias=ppc(("v0", l, hp)), scale=1.0,
                       reads=[pgt, t_pp], writes=[tb_t])
                    dma("sp", "dma_start", out=TA, in_=vf_d[hp * 128:(hp + 1) * 128, tok(tp)], reads=[vf_t[hp]], writes=[ta_t])
                    op("dve", "tensor_tensor", out=TA, in0=TA, in1=F[5][:, tok(tp)], op=ALU.subtract, reads=[ta_t, ft[5][tp]], writes=[ta_t])
                    op("dve", "tensor_tensor", out=TA, in0=TA, in1=TB, op=ALU.mult, reads=[ta_t, tb_t], writes=[ta_t])
                    op("dve", "tensor_tensor", out=F[5][:, tok(tp)], in0=F[5][:, tok(tp)], in1=TA, op=ALU.add,
                       reads=[ta_t, ft[5][tp]], writes=[ft[5][tp]])
            proj_lerp(6, F[0], ft[0])
            for tp in range(4):
                op("act", "activation", out=F[0][0:64, tok(tp)], in_=F[0][0:64, tok(tp)], func=AF.Tanh, reads=[ft[0][tp]], writes=[ft[0][tp]])
                pw, pwt = PS.next()
                op("pe", "matmul", pw, lhsT=CSM[0:64, hp * 128:hp * 128 + 128], rhs=F[0][0:64, tok(tp)], start=True, stop=True,
                   reads=[t_csm, ft[0][tp]], writes=[pwt])
                op("act", "activation", out=F[1][:, tok(tp)], in_=pw, func=AF.Sigmoid, bias=ppc(("w0", l, hp)), scale=1.0,
                   reads=[pwt, t_pp], writes=[ft[1][tp]])
                op("pool", "tensor_scalar", out=F[1][:, tok(tp)], in0=F[1][:, tok(tp)], scalar1=-math.exp(-0.5), scalar2=None, op0=ALU.mult,
                   reads=[ft[1][tp]], writes=[ft[1][tp]])
                pa, pat = PS.next()
                op("pe", "matmul", pa, lhsT=CSM[64:128, hp * 128:hp * 128 + 128], rhs=F[0][64:128, tok(tp)], start=True, stop=True,
                   reads=[t_csm, ft[0][tp]], writes=[pat])
                op("act", "activation", out=F[2][:, tok(tp)], in_=pa, func=AF.Sigmoid, bias=ppc(("a0", l, hp)), scale=1.0,
                   reads=[pat, t_pp], writes=[ft[2][tp]])
            proj_lerp(2 + hp, F[0], ft[0])
            for tp in range(4):
                tk = tok(tp)
                op("dve", "tensor_scalar", out=F[3][:, tk], in0=F[0][:, tk], scalar1=ppc(("kk", l, hp)), scalar2=None, op0=ALU.mult,
                   reads=[ft[0][tp], t_pp], writes=[ft[3][tp]])
                op("act", "activation", out=TA, in_=F[3][:, tk], func=AF.Square, reads=[ft[3][tp]], writes=[ta_t])
                pq, pqt = PS.next()
                op("pe", "matmul", pq, lhsT=BLK, rhs=TA, start=True, stop=True, reads=[t_cst, ta_t], writes=[pqt])
                op("act", "activation", out=TB, in_=pq, func=AF.Sqrt, reads=[pqt], writes=[tb_t])
                op("dve", "tensor_scalar", out=TB, in0=TB, scalar1=1e-12, scalar2=None, op0=ALU.max, reads=[tb_t], writes=[tb_t])
                op("dve", "reciprocal", out=TB, in_=TB, reads=[tb_t], writes=[tb_t])
                op("dve", "tensor_tensor", out=F[3][:, tk], in0=F[3][:, tk], in1=TB, op=ALU.mult, reads=[ft[3][tp], tb_t], writes=[ft[3][tp]])
                op("dve", "tensor_scalar", out=TA, in0=F[2][:, tk], scalar1=-1.0, scalar2=ppc(("ka", l, hp)), op0=ALU.add, op1=ALU.mult,
                   reads=[ft[2][tp], t_pp], writes=[ta_t])
                op("dve", "scalar_tensor_tensor", out=F[0][:, tk], in0=TA, scalar=1.0, in1=F[0][:, tk], op0=ALU.add, op1=ALU.mult,
                   reads=[ta_t, ft[0][tp]], writes=[ft[0][tp]])
                op("pool", "tensor_tensor", out=F[2][:, tk], in0=F[2][:, tk], in1=F[3][:, tk], op=ALU.mult,
                   reads=[ft[2][tp], ft[3][tp]], writes=[ft[2][tp]])
            proj_lerp(hp, F[4], ft[4])
            for tp in range(4):
                tk = tok(tp)
                op("dve", "scalar_tensor_tensor", out=TA, in0=F[4][:, tk], scalar=ppc(("rk", l, hp)), in1=F[0][:, tk], op0=ALU.mult, op1=ALU.mult,
                   reads=[ft[4][tp], ft[0][tp], t_pp], writes=[ta_t])
                pq, pqt = PS.next()
                op("pe", "matmul", pq, lhsT=BLK, rhs=TA, start=True, stop=True, reads=[t_cst, ta_t], writes=[pqt])
                op("dve", "tensor_tensor", out=OT[:, oc, tk], in0=pq, in1=F[5][:, tk], op=ALU.mult, reads=[pqt, ft[5][tp]], writes=[ot_t[oc][tp]])
            for tp in range(4):
                tk = tok(tp)
                for cc in range(4):
                    cs = slice(tp * 512 + cc * 128, tp * 512 + cc * 128 + 128)
                    op("dve", "tensor_tensor_scan", out=TA[:, cc * 128:(cc + 1) * 128], data0=ONES, data1=F[1][:, cs], initial=0.0,
                       op0=ALU.mult, op1=ALU.add, reads=[t_cst, ft[1][tp]], writes=[ta_t])
                op("dve", "tensor_tensor", out=TB, in0=TA, in1=F[1][:, tk], op=ALU.subtract, reads=[ta_t, ft[1][tp]], writes=[tb_t])
                op("act", "activation", out=TB, in_=TB, func=AF.Exp, reads=[tb_t], writes=[tb_t])
                op("dve", "tensor_tensor", out=F[3][:, tk], in0=F[3][:, tk], in1=TB, op=ALU.mult, reads=[ft[3][tp], tb_t], writes=[ft[3][tp]])
                op("act", "activation", out=TB, in_=TA, func=AF.Exp, reads=[ta_t], writes=[tb_t])
                op("dve", "tensor_tensor", out=F[4][:, tk], in0=F[4][:, tk], in1=TB, op=ALU.mult, reads=[ft[4][tp], tb_t], writes=[ft[4][tp]])
                op("act", "copy", out=GC[:, tp * 4:tp * 4 + 4], in_=TB[:, 127:512:128], reads=[tb_t], writes=[gc_t])
                op("act", "activation", out=TB, in_=TA, func=AF.Exp, scale=-1.0, reads=[ta_t], writes=[tb_t])
                op("dve", "tensor_tensor", out=F[0][:, tk], in0=F[0][:, tk], in1=TB, op=ALU.mult, reads=[ft[0][tp], tb_t], writes=[ft[0][tp]])
                op("pool", "tensor_tensor", out=F[2][:, tk], in0=F[2][:, tk], in1=TB, op=ALU.mult, reads=[ft[2][tp], tb_t], writes=[ft[2][tp]])
            P.barrier()
            def cut(region, n, width):
                out = []
                for _ in range(n):
                    out.append(AR[:, region[0]:region[0] + width])
                    region[0] += width
                return out
            reg1 = [base + 2048]
            reg2 = [sm]
            reg3 = [sm + 2580]
            KhF, BhF = cut(reg1, 2, 128)
            MB = [cut(reg1, 2, 256) for _ in range(2)]
            TK = [cut(reg1, 3, 128) for _ in range(2)]
            assert reg1[0] <= base + 4096
            MK = [cut(reg3, 2, 256) for _ in range(2)]
            assert reg3[0] <= CFW, reg3[0]
            inv = [[cut(reg2, 3, 128) for _ in range(2)] for _ in range(2)]
            Wsb = cut(reg2, 2, 64)
            Usb = cut(reg2, 2, 64)
            ST = cut(reg2, 1, 64)[0]
            YC = cut(reg2, 1, 128)[0]
            G1 = cut(reg2, 3, 128)
            assert reg2[0] <= sm + 2564, reg2[0]
            t_kh, t_bh = TT(), TT()
            t_tk = [[TT(), TT(), TT()] for _ in range(2)]
            t_mb = [[TT(), TT()] for _ in range(2)]
            t_mk = [[TT(), TT()] for _ in range(2)]
            t_inv = [[[TT() for _ in range(3)] for _ in range(2)] for _ in range(2)]
            t_w, t_u = [TT(), TT()], [TT(), TT()]
            t_st = [TT(), TT()]
            t_yc = TT()
            t_g1 = [TT() for _ in range(3)]
            RW = [slice(0, 64), slice(64, 128)]
            op("pool", "memset", ST, 0.0, writes=t_st)

            def build(c):
                bf = c % 2
                cs = slice(c * 128, c * 128 + 128)
                tp = c // 4
                KhT, BhT, VT = TK[bf]
                t_kht, t_bht, t_vt = t_tk[bf]
                op("dve", "tensor_scalar", out=KhF, in0=F[0][:, cs], scalar1=GC[:, c:c + 1], scalar2=None, op0=ALU.mult,
                   reads=[ft[0][tp], gc_t], writes=[t_kh])
                op("pool", "tensor_scalar", out=BhF, in0=F[2][:, cs], scalar1=GC[:, c:c + 1], scalar2=None, op0=ALU.mult,
                   reads=[ft[2][tp], gc_t], writes=[t_bh])
                yield
                for (src, srct, dstT, dstt) in ((KhF, [t_kh], KhT, t_kht), (BhF, [t_bh], BhT, t_bht), (F[5][:, cs], [ft[5][tp]], VT, t_vt)):
                    ptr, ptrt = PS.next()
                    op("pe", "transpose", ptr[:, 0:128], src, IDENT, reads=srct + [t_cst], writes=[ptrt])
                    op("act", "copy", out=dstT, in_=ptr[:, 0:128], reads=[ptrt], writes=[dstt])
                    yield
                for hl in range(2):
                    rows = RW[hl]
                    for (lt, ltt, Mx, Mxt) in ((F[2], ft[2][tp], MB[bf][hl], t_mb[bf][hl]), (F[0], ft[0][tp], MK[bf][hl], t_mk[bf][hl])):
                        pmx, pmxt = PS.next()
                        op("pe", "matmul", pmx[:, 0:128], lhsT=lt[rows, cs], rhs=F[3][rows, cs], start=True, stop=True,
                           reads=[ltt, ft[3][tp]], writes=[pmxt])
                        op("pe", "matmul", pmx[:, 128:256], lhsT=lt[rows, cs], rhs=F[4][rows, cs], start=True, stop=True,
                           reads=[ltt, ft[4][tp]], writes=[pmxt])
                        op("dve", "tensor_tensor", out=Mx, in0=pmx[:, 0:256], in1=UPP, op=ALU.mult, reads=[pmxt, t_cst], writes=[Mxt])
                        yield
                    Pm, PTm, Zm = inv[bf][hl]
                    tPm, tPTm, tZm = t_inv[bf][hl]
                    op("act", "mul", out=Pm, in_=MB[bf][hl][:, 0:128], mul=-1.0, reads=[t_mb[bf][hl]], writes=[tPm])
                    pnt, pntt = PS.next()
                    op("pe", "matmul", pnt[:, 0:128], lhsT=F[3][rows, cs], rhs=F[2][rows, cs], start=True, stop=True,
                       reads=[ft[3][tp], ft[2][tp]], writes=[pntt])
                    op("dve", "scalar_tensor_tensor", out=PTm, in0=pnt[:, 0:128], scalar=-1.0, in1=LOWS, op0=ALU.mult, op1=ALU.mult,
                       reads=[pntt, t_cst], writes=[tPTm])
                    op("pool", "tensor_tensor", out=Zm, in0=Pm, in1=IDENT, op=ALU.add, reads=[tPm, t_cst], writes=[tZm])
                    yield
                for stg in range(6):
                    for hl in range(2):
                        Pm, PTm, Zm = inv[bf][hl]
                        tPm, tPTm, tZm = t_inv[bf][hl]
                        ppt2, ppt2t = PS.next()
                        op("pe", "matmul", ppt2[:, 0:128], lhsT=Pm, rhs=PTm, start=True, stop=True, reads=[tPm, tPTm], writes=[ppt2t])
                        if stg < 5:
                            pp2, pp2t = PS.next()
                            op("pe", "matmul", pp2[:, 0:128], lhsT=PTm, rhs=Pm, start=True, stop=True, reads=[tPm, tPTm], writes=[pp2t])
                        op("act", "copy", out=PTm, in_=ppt2[:, 0:128], reads=[ppt2t], writes=[tPTm])
                        if stg < 5:
                            op("act", "copy", out=Pm, in_=pp2[:, 0:128], reads=[pp2t], writes=[tPm])
                        yield
                    for hl in range(2):
                        Pm, PTm, Zm = inv[bf][hl]
                        tPm, tPTm, tZm = t_inv[bf][hl]
                        pz, pzt = PS.next()
                        op("pe", "matmul", pz[:, 0:128], lhsT=PTm, rhs=Zm, start=True, stop=True, reads=[tPTm, tZm], writes=[pzt])
                        op("dve", "tensor_tensor", out=Zm, in0=pz[:, 0:128], in1=Zm, op=ALU.add, reads=[pzt, tZm], writes=[tZm])
                        yield

            def seq(c):
                bf = c % 2
                cs = slice(c * 128, c * 128 + 128)
                tp = c // 4
                KhT, BhT, VT = TK[bf]
                t_kht, t_bht, t_vt = t_tk[bf]
                for hl in range(2):
                    rows, hc = RW[hl], RW[hl]
                    pw, pwt = PS.next()
                    op("pe", "matmul", pw[:, 0:64], lhsT=F[3][rows, cs], rhs=ST[rows, :], start=True, stop=False,
                       reads=[ft[3][tp], t_st[hl]], writes=[pwt])
                    op("pe", "matmul", pw[:, 0:64], lhsT=MK[bf][hl][:, 0:128], rhs=VT[:, hc], start=False, stop=True,
                       reads=[t_mk[bf][hl], t_vt], writes=[pwt])
                    op("act", "copy", out=Wsb[hl], in_=pw[:, 0:64], reads=[pwt], writes=[t_w[hl]])
                    yield
                for hl in range(2):
                    pu, put = PS.next()
                    op("pe", "matmul", pu[:, 0:64], lhsT=inv[bf][hl][2], rhs=Wsb[hl], start=True, stop=True,
                       reads=[t_inv[bf][hl][2], t_w[hl]], writes=[put])
                    op("act", "mul", out=Usb[hl], in_=pu[:, 0:64], mul=-1.0, reads=[put], writes=[t_u[hl]])
                    yield
                for hl in range(2):
                    rows, hc = RW[hl], RW[hl]
                    py, pyt = PS.next()
                    op("pe", "matmul", py[0:64, 0:128], lhsT=ST[rows, :], rhs=F[4][rows, cs], start=True, stop=False,
                       reads=[t_st[hl], ft[4][tp]], writes=[pyt])
                    op("pe", "matmul", py[0:64, 0:128], lhsT=Usb[hl], rhs=MB[bf][hl][:, 128:256], start=False, stop=False,
                       reads=[t_u[hl], t_mb[bf][hl]], writes=[pyt])
                    op("pe", "matmul", py[0:64, 0:128], lhsT=VT[:, hc], rhs=MK[bf][hl][:, 128:256], start=False, stop=True,
                       reads=[t_vt, t_mk[bf][hl]], writes=[pyt])
                    op("act", "copy", out=YC[rows, :], in_=py[0:64, 0:128], reads=[pyt], writes=[t_yc])
                    pn, pnt_ = PS.next()
                    op("pe", "matmul", pn[0:64, 0:64], lhsT=BhT[:, hc], rhs=Usb[hl], start=True, stop=False,
                       reads=[t_bht, t_u[hl]], writes=[pnt_])
                    op("pe", "matmul", pn[0:64, 0:64], lhsT=KhT[:, hc], rhs=VT[:, hc], start=False, stop=True,
                       reads=[t_kht, t_vt], writes=[pnt_])
                    op("dve", "scalar_tensor_tensor", out=ST[rows, :], in0=ST[rows, :], scalar=GC[rows, c:c + 1], in1=pn[0:64, 0:64],
                       op0=ALU.mult, op1=ALU.add, reads=[t_st[hl], gc_t, pnt_], writes=[t_st[hl]])
                    yield
                pmu, pmut = PS.next()
                op("pe", "matmul", pmu[:, 0:128], lhsT=BLK64, rhs=YC, start=True, stop=True, reads=[t_cst, t_yc], writes=[pmut])
                op("act", "activation", out=G1[0], in_=YC, func=AF.Square, reads=[t_yc], writes=[t_g1[0]])
                yield
                pe2, pe2t = PS.next()
                op("pe", "matmul", pe2[:, 0:128], lhsT=BLK64, rhs=G1[0], start=True, stop=True, reads=[t_cst, t_g1[0]], writes=[pe2t])
                op("act", "copy", out=G1[1], in_=pmu[:, 0:128], reads=[pmut], writes=[t_g1[1]])
                yield
                op("dve", "tensor_tensor", out=G1[0], in0=G1[1], in1=G1[1], op=ALU.mult, reads=[t_g1[1]], writes=[t_g1[0]])
                op("dve", "scalar_tensor_tensor", out=G1[2], in0=pe2[:, 0:128], scalar=GN_EPS, in1=G1[0], op0=ALU.add, op1=ALU.subtract,
                   reads=[pe2t, t_g1[0]], writes=[t_g1[2]])
                yield
                op("act", "activation", out=G1[2], in_=G1[2], func=AF.Ln, reads=[t_g1[2]], writes=[t_g1[2]])
                op("act", "activation", out=G1[2], in_=G1[2], func=AF.Exp, scale=-0.5, reads=[t_g1[2]], writes=[t_g1[2]])
                yield
                op("dve", "tensor_tensor", out=G1[0], in0=YC, in1=G1[1], op=ALU.subtract, reads=[t_yc, t_g1[1]], writes=[t_g1[0]])
                op("dve", "tensor_tensor", out=G1[0], in0=G1[0], in1=G1[2], op=ALU.mult, reads=[t_g1[0], t_g1[2]], writes=[t_g1[0]])
                yield
                op("act", "activation", out=G1[0], in_=G1[0], func=AF.Identity, bias=ppc(("gnb", l, hp)), scale=ppc(("gng", l, hp)),
                   reads=[t_g1[0], t_pp], writes=[t_g1[0]])
                op("pool", "tensor_tensor", out=OT[:, oc, cs], in0=OT[:, oc, cs], in1=G1[0], op=ALU.add,
                   reads=[ot_t[oc][tp], t_g1[0]], writes=[ot_t[oc][tp]])
                yield

            for _ in build(0):
                pass
            NCH = int(_os.environ.get("CSCAN", "16"))
            for c in range(NCH):
                gb = build(c + 1) if c + 1 < NCH else iter(())
                gs = seq(c)
                alive = [gb, gs]
                while alive:
                    for gsn in list(alive):
                        try:
                            next(gsn)
                        except StopIteration:
                            alive.remove(gsn)
            P.barrier()
            ft0 = [TT() for _ in range(4)]
            rawp_t2 = TT()
            w, wt = w1.next()
            dma("pool", "dma_start", out=w, in_=dr["wch"][l * NCHUNK_W + 24 + 7].bitcast(F32R).rearrange("p (kc f) -> p kc f", kc=8),
                writes=[wt])
            op("pool", "memset", RAWP[:, 0:1], 0.0, writes=[rawp_t2])
            mu = ppc(("mu", l, 7))
            ta2, tb2 = TT(), TT()
            for tp in range(4):
                pp_, ppt = PS.next()
                for kc in range(8):
                    op("pe", "matmul", pp_, lhsT=w[:, kc, :], rhs=XT[:, kc, tok(tp)], start=(kc == 0), stop=(kc == 7),
                       reads=[wt, xt_t[kc][tp]], writes=[ppt])
                op("act", "copy", out=RAWP[:, 1:513], in_=pp_, reads=[ppt], writes=[rawp_t2])
                op("dve", "tensor_tensor", out=TA, in0=RAWP[:, 0:512], in1=RAWP[:, 1:513], op=ALU.subtract, reads=[rawp_t2], writes=[ta2])
                op("dve", "scalar_tensor_tensor", out=TB, in0=TA, scalar=mu, in1=RAWP[:, 1:513], op0=ALU.mult, op1=ALU.add,
                   reads=[ta2, rawp_t2, t_pp], writes=[tb2])
                op("act", "copy", out=RAWP[:, 0:1], in_=RAWP[:, 512:513], reads=[rawp_t2], writes=[rawp_t2])
                op("act", "activation", out=TB, in_=TB, func=AF.Sigmoid, reads=[tb2], writes=[tb2])
                pg, pgt = PS.next()
                op("pe", "matmul", pg, lhsT=CSM[:, 256 + hp * 128:256 + hp * 128 + 128], rhs=TB, start=True, stop=True,
                   reads=[t_csm, tb2], writes=[pgt])
                op("dve", "tensor_tensor", out=OT[:, oc, tok(tp)], in0=pg, in1=OT[:, oc, tok(tp)], op=ALU.mult,
                   reads=[pgt, ot_t[oc][tp]], writes=[ot_t[oc][tp]])
            P.end_phase(st)

        if stage in ("A", "mix", "full"):
            for i in range(3):
                phase_a(i)
        else:
            for c in range(0, 3):
                op("pool", "memset", OT[:, c, :], 0.0, writes=ot_t[c])
        if stage in ("B", "mix", "full"):
            for i in range(3):
                phase_b(i)
        else:
            for c in range(3, 6):
                op("pool", "memset", OT[:, c, :], 0.0, writes=ot_t[c])
        if stage in ("C", "mix", "full"):
            for hp in range(2):
                phase_c(hp)
        else:
            for c in range(6, 8):
                op("pool", "memset", OT[:, c, :], 0.0, writes=ot_t[c])
        return OT, ot_t, None

    def wout_ln(l, OT, ot_t, w_r):
        st = P.phase()
        WL = P.ph_sb(st, "WL", [128, 4096 + 1536 + 2048], F32)
        sc = Rot([WL[:, k * 512:(k + 1) * 512] for k in range(8)])
        lnk = Rot([WL[:, 4096 + k * 512:4096 + (k + 1) * 512] for k in range(3)])
        wo_r = Rot([WL[:, 5632 + k * 1024:5632 + (k + 1) * 1024].rearrange("p (kc f) -> p kc f", kc=8) for k in range(2)])
        for dc in range(8):
            w, wt = wo_r.next()
            dma("sp", "dma_start", out=w, in_=dr["wch"][l * NCHUNK_W + 38 + dc].rearrange("p (kc f) -> p kc f", kc=8), writes=[wt])
            for tp in range(4):
                py, pyt = PS.next()
                for kc in range(8):
                    op("pe", "matmul", py, lhsT=w[:, kc, :], rhs=OT[:, kc, tok(tp)], start=(kc == 0), stop=(kc == 7),
                       reads=[wt, ot_t[kc][tp]], writes=[pyt])
                op("dve", "scalar_tensor_tensor", out=XT[:, dc, tok(tp)], in0=py, scalar=1.0 / ALPHA, in1=XF[:, dc, tok(tp)],
                   op0=ALU.mult, op1=ALU.add, reads=[pyt, xt_t[dc][tp]], writes=[xt_t[dc][tp]])
        for tp in range(4):
            layer_norm(l, 1, tp, sc, lnk)
        return st

    def dump_o(OT, ot_t):
        for c in range(8):
            dma("sp", "dma_start", out=out_d[c * 128:(c + 1) * 128, :], in_=OT[:, c, :], reads=ot_t[c])

    def dump_x():
        for c in range(8):
            dma("sp", "dma_start", out=out_d[c * 128:(c + 1) * 128, :], in_=XF[:, c, :], reads=xt_t[c])

    dumped = False
    stage = dbg[1] if dbg else "full"
    for l in range(depth):
        st = ffn_phase(l, 0)
        if dbg == ("x", "ffn_a") and l == depth - 1:
            dump_x()
            dumped = True
            P.end_phase(st)
            break
        P.end_phase(st)
        stm = P.phase()
        OTt = P.ph_sb(stm, "OT", [128, 16384], F32)
        OT, ot_t, w_r = mixer_phase(l, stage)
        if dbg is not None and dbg[0] == "o" and l == depth - 1:
            dump_o(OT, ot_t)
            dumped = True
            P.end_phase(stm)
            break
        st = wout_ln(l, OT, ot_t, w_r)
        if dbg == ("x", "mixln") and l == depth - 1:
            dump_x()
            dumped = True
            P.end_phase(st)
            P.end_phase(stm)
            break
        P.end_phase(st)
        P.end_phase(stm)
        st = ffn_phase(l, 1)
        if l == depth - 1:
            dump_x()
            dumped = True
        P.end_phase(st)
    P.barrier()
    P.finish()
    return nc


_CACHE = {}


def kernel(**inputs):
    sh, xs, idx = prep_inputs(inputs)
    shapes = {k: v.shape for k, v in sh.items()}
    nc = build(shapes, idx)
    n = len(xs)
    in_maps = []
    for b in range(n):
        m = dict(sh)
        m["xT"] = xs[b]
        in_maps.append(m)
    res = run_bass_kernel_spmd(nc, in_maps, core_ids=list(range(n)))
    out = np.stack([np.ascontiguousarray(res.results[b]["out"].T) for b in range(n)], axis=0)
    return out.astype(np.float32)
```

```python
import contextlib
import math
import numpy as np
import concourse.bass as bass
import concourse.mybir as mybir
from concourse.bass_utils import run_bass_kernel_spmd

F32 = mybir.dt.float32
F32R = mybir.dt.float32r
AF = mybir.ActivationFunctionType
ALU = mybir.AluOpType
AX = mybir.AxisListType

DEPTH = 4
D = 1024
S = 2048
FF = 2816
NFC = 22
ALPHA = (2 * DEPTH) ** 0.25
LN_EPS = 1e-5
RMS_EPS = 1e-5
GN_EPS = 64e-5
THETA = 10000.0
NCHUNK_W = 38 + 8
ARC = 31300
CFW = 14916 + 1024


class TT:
    __slots__ = ("w", "r")

    def __init__(self):
        self.w = None
        self.r = []


class Prog:
    ENG = ("pe", "act", "dve", "pool", "sp")

    def __init__(self, nc, same_sync=True, n_dma_sems=40):
        self.nc = nc
        self.same_sync = same_sync
        self.q = {e: [] for e in self.ENG}
        self.cnt = {e: 0 for e in self.ENG}
        self.known = {e: {} for e in self.ENG}
        self.stack = contextlib.ExitStack()
        self.sem = {}
        for e in self.ENG:
            self.sem[e] = self.stack.enter_context(nc.semaphore("s_" + e))
        self.dsem = []
        self.dval = []
        for i in range(n_dma_sems):
            self.dsem.append(self.stack.enter_context(nc.semaphore("d%d" % i)))
            self.dval.append(0)
        self.ndma = 0

    def sb(self, name, shape, dt=F32):
        return self.stack.enter_context(self.nc.sbuf_tensor(name, shape, dt))

    def ps(self, name, shape, dt=F32):
        return self.stack.enter_context(self.nc.psum_tensor(name, shape, dt))

    def _waits(self, eng, reads, writes):
        deps = {}
        for t in reads:
            if t.w is not None:
                k, v = t.w
                if deps.get(k, 0) < v:
                    deps[k] = v
        for t in writes:
            if t.w is not None:
                k, v = t.w
                if deps.get(k, 0) < v:
                    deps[k] = v
            for (k, v) in t.r:
                if deps.get(k, 0) < v:
                    deps[k] = v
        out = []
        kn = self.known[eng]
        for k, v in deps.items():
            if k == eng and (eng == "pe" or not self.same_sync):
                continue
            if kn.get(k, 0) >= v:
                continue
            kn[k] = v
            out.append((k, v))
        return out

    def _semh(self, k):
        return self.sem[k] if isinstance(k, str) else self.dsem[k]

    def _mark(self, tok, reads, writes):
        for t in reads:
            t.r.append(tok)
            if len(t.r) > 24:
                best = {}
                for k, v in t.r:
                    if best.get(k, 0) < v:
                        best[k] = v
                t.r = list(best.items())
        for t in writes:
            t.w = tok
            t.r = []

    def op(self, eng, fn, *args, reads=(), writes=(), **kwargs):
        waits = self._waits(eng, reads, writes)
        self.cnt[eng] += 1
        self._mark((eng, self.cnt[eng]), reads, writes)
        sem = self.sem[eng]
        wl = [(self._semh(k), v) for k, v in waits]

        def emit(e, fn=fn, wl=wl, sem=sem, args=args, kwargs=kwargs):
            for s, v in wl:
                e.wait_ge(s, v)
            getattr(e, fn)(*args, **kwargs).then_inc(sem, 1)

        self.q[eng].append(emit)

    def dma(self, eng, fn, *args, reads=(), writes=(), di=None, **kwargs):
        if di is None:
            di = self.ndma % len(self.dsem)
            self.ndma += 1
        waits = self._waits(eng, reads, writes)
        self.dval[di] += 16
        self._mark((di, self.dval[di]), reads, writes)
        sem = self.dsem[di]
        wl = [(self._semh(k), v) for k, v in waits]

        def emit(e, fn=fn, wl=wl, sem=sem, args=args, kwargs=kwargs):
            for s, v in wl:
                e.wait_ge(s, v)
            getattr(e, fn)(*args, **kwargs).then_inc(sem, 16)

        self.q[eng].append(emit)

    def barrier(self):
        for e in self.ENG:
            wl = []
            kn = self.known[e]
            for k in self.ENG:
                if k != e and self.cnt[k] > kn.get(k, 0):
                    kn[k] = self.cnt[k]
                    wl.append((self.sem[k], self.cnt[k]))
            for i, v in enumerate(self.dval):
                if v > kn.get(i, 0):
                    kn[i] = v
                    wl.append((self.dsem[i], v))
            if self.same_sync and e != "pe" and self.cnt[e] > kn.get(e, 0):
                kn[e] = self.cnt[e]
                wl.append((self.sem[e], self.cnt[e]))

            def emit(en, wl=wl):
                for s, v in wl:
                    en.wait_ge(s, v)

            self.q[e].append(emit)

    def phase(self):
        return contextlib.ExitStack()

    def ph_sb(self, st, name, shape, dt=F32):
        self.uid = getattr(self, "uid", 0) + 1
        return st.enter_context(self.nc.sbuf_tensor("%s_%d" % (name, self.uid), shape, dt))

    def end_phase(self, st):
        self.barrier()
        self.flush()
        st.close()

    def finish(self):
        self.flush()
        self.stack.close()

    def flush(self):
        nc = self.nc
        q = self.q
        self.q = {e: [] for e in self.ENG}
        with nc.Block() as block:
            @block.tensor
            def _(e):
                for f in q["pe"]:
                    f(e)

            @block.scalar
            def _(e):
                for f in q["act"]:
                    f(e)

            @block.vector
            def _(e):
                for f in q["dve"]:
                    f(e)

            @block.gpsimd
            def _(e):
                for f in q["pool"]:
                    f(e)

            @block.sync
            def _(e):
                for f in q["sp"]:
                    f(e)


class Rot:
    def __init__(self, views, tts=None):
        self.v = views
        self.t = tts if tts is not None else [TT() for _ in views]
        self.i = 0

    def next(self):
        i = self.i % len(self.v)
        self.i += 1
        return self.v[i], self.t[i]


def _chunk(W, cols):
    return np.ascontiguousarray(W[:, cols].reshape(8, 128, len(cols)).transpose(1, 0, 2))


def _swap_idx(base, n, grp):
    idx = np.arange(n)
    half = grp // 2
    return base + (idx // grp) * grp + ((idx % grp) + half) % grp


def _rope_table(grp):
    half = grp // 2
    inv = (THETA ** (-np.arange(0, grp, 2, dtype=np.float32) / grp)).astype(np.float32)
    pos = np.arange(S, dtype=np.float32)
    ang = (pos[:, None] * inv[None, :]).astype(np.float32)
    cos = np.cos(ang).astype(np.float32).T
    sin = np.sin(ang).astype(np.float32).T
    r = np.arange(128)
    i = r % half
    sign = np.where((r % grp) < half, -1.0, 1.0).astype(np.float32)
    C = cos[i]
    Sg = sin[i] * sign[:, None]
    t = np.stack([C, Sg], axis=1)
    t = t.reshape(128, 2, 4, 512).transpose(0, 2, 1, 3)
    return np.ascontiguousarray(t.reshape(128, 4 * 2 * 512)).astype(np.float32)


def _consts():
    c = {}
    k = np.arange(128)[:, None]
    q = np.arange(256)[None, :]
    c["causal"] = (q[:, :128] >= k).astype(np.float32)
    c["band"] = ((q >= k) & (q <= k + 128)).astype(np.float32)
    su = (q[:, :128] > k).astype(np.float32)
    iu = (q[:, :128] >= k).astype(np.float32)
    c["uppers"] = np.concatenate([su, iu], axis=1)
    c["ident"] = np.eye(128, dtype=np.float32)
    blk = np.zeros((128, 128), np.float32)
    blk[:64, :64] = 1.0
    blk[64:, 64:] = 1.0
    c["blk"] = blk
    c["lowers"] = (q[:, :128] < k).astype(np.float32)
    return c


def prep_inputs(inp):
    L = DEPTH
    f = lambda a: np.asarray(a, dtype=np.float32)
    sh = {}
    wgu = np.empty((L, 2, 22, 128, 2, 8, 128), np.float32)
    wd = np.empty((L, 2, 8, 128, 22, 128), np.float32)
    ffw = {"a": (inp["ffn_a_gate"], inp["ffn_a_up"], inp["ffn_a_down"]),
           "b": (inp["ffn_b_gate"], inp["ffn_b_up"], inp["ffn_b_down"])}
    for l in range(L):
        for i, nm in enumerate("ab"):
            g = f(ffw[nm][0][l]).reshape(8, 128, 22, 128).transpose(2, 1, 0, 3)
            u = f(ffw[nm][1][l]).reshape(8, 128, 22, 128).transpose(2, 1, 0, 3)
            wgu[l, i, :, :, 0] = g
            wgu[l, i, :, :, 1] = u
            wd[l, i] = f(ffw[nm][2][l]).reshape(22, 128, 8, 128).transpose(2, 1, 0, 3)
    sh["wgu"] = wgu.reshape(L * 2 * 22, 128, 2048)
    sh["wd"] = wd.reshape(L * 2 * 8, 128, 2816)
    wch = np.empty((L, NCHUNK_W, 128, 8, 128), np.float32)
    for l in range(L):
        W = f(inp["w_in"][l])
        ci = 0
        for base, grp in ((0, 32), (1152, 64)):
            for i in range(3):
                qc = base + 128 * i + np.arange(128)
                kc_ = base + 384 + 128 * i + np.arange(128)
                wch[l, ci + 0] = _chunk(W, qc)
                wch[l, ci + 1] = _chunk(W, _swap_idx(base + 128 * i, 128, grp))
                wch[l, ci + 2] = _chunk(W, kc_)
                wch[l, ci + 3] = _chunk(W, _swap_idx(base + 384 + 128 * i, 128, grp))
                ci += 4
        for c in range(8):
            wch[l, 24 + c] = _chunk(W, 2304 + 128 * c + np.arange(128))
        for i in range(3):
            wch[l, 32 + i] = _chunk(W, 768 + 128 * i + np.arange(128))
            wch[l, 35 + i] = _chunk(W, 1920 + 128 * i + np.arange(128))
        Wo = f(inp["w_out"][l])
        for dc in range(8):
            wch[l, 38 + dc] = _chunk(Wo, 128 * dc + np.arange(128))
    sh["wch"] = wch.reshape(L * NCHUNK_W, 128, 1024)
    sh["ropeA"] = _rope_table(32)
    sh["ropeB"] = _rope_table(64)
    for k_, v_ in _consts().items():
        sh["c_" + k_] = v_
    cols = []

    def addcol(v):
        cols.append(np.asarray(v, np.float32).reshape(128, 1))
        return len(cols) - 1

    idx = {}
    for l in range(L):
        for i in range(3):
            for c in range(8):
                idx[("lng", l, i, c)] = addcol(f(inp["ln_g"][l, i, c * 128:(c + 1) * 128]))
                idx[("lnb", l, i, c)] = addcol(f(inp["ln_b"][l, i, c * 128:(c + 1) * 128]))
        idx[("ga", l)] = addcol(np.tile(f(inp["a_norm_g"][l]), 2))
        idx[("gb", l)] = addcol(np.tile(f(inp["b_norm_g"][l]), 2))
        for c in range(8):
            idx[("mu", l, c)] = addcol(f(inp["c_mu"][l, c * 128:(c + 1) * 128]))
        for hp in range(2):
            sl = slice(hp * 128, hp * 128 + 128)
            idx[("w0", l, hp)] = addcol(f(inp["c_w0"][l, sl]))
            idx[("a0", l, hp)] = addcol(f(inp["c_a0"][l, sl]))
            idx[("kk", l, hp)] = addcol(f(inp["c_k_k"][l, sl]))
            idx[("ka", l, hp)] = addcol(f(inp["c_k_a"][l, sl]))
            idx[("rk", l, hp)] = addcol(f(inp["c_r_k"][l].reshape(256)[sl]))
            idx[("gng", l, hp)] = addcol(f(inp["c_gn_g"][l, sl]))
            idx[("gnb", l, hp)] = addcol(f(inp["c_gn_b"][l, sl]))
            if l > 0:
                idx[("v0", l, hp)] = addcol(f(inp["c_v0"][l - 1, sl]))
    sh["pp"] = np.ascontiguousarray(np.concatenate(cols, axis=1))
    lamrow = np.stack([np.stack([f(inp["a_lam_q1"][l]), f(inp["a_lam_k1"][l]),
                                 f(inp["a_lam_q2"][l]), f(inp["a_lam_k2"][l])]) for l in range(L)])
    sh["lamrow"] = np.ascontiguousarray(lamrow.reshape(1, L * 4 * 32))
    sm = np.zeros((L, 128, 256 + 256 + 64 + 256), np.float32)
    for l in range(L):
        sm[l, 0:64, 0:256] = f(inp["c_w2"][l])
        sm[l, 64:128, 0:256] = f(inp["c_a2"][l])
        sm[l, :, 256:512] = f(inp["c_g2"][l])
        if l > 0:
            sm[l, :, 512:576] = f(inp["c_v1"][l - 1]).reshape(2, 128, 32).transpose(1, 0, 2).reshape(128, 64)
            sm[l, 0:32, 576:832] = f(inp["c_v2"][l - 1])
    sh["csm"] = np.ascontiguousarray(sm.transpose(1, 0, 2).reshape(128, L * 832))
    xs = [np.ascontiguousarray(f(inp["x"][b]).T) for b in range(inp["x"].shape[0])]
    return sh, xs, idx


def build(shapes, idx, depth=DEPTH, dbg=None):
    nc = bass.Bass("TRN2", target_bir_lowering=False)
    dr = {}
    for k, shp in shapes.items():
        dr[k] = nc.dram_tensor(k, list(shp), F32, kind="ExternalInput").ap()
    xT_d = nc.dram_tensor("xT", [D, S], F32, kind="ExternalInput").ap()
    out_d = nc.dram_tensor("out", [D, S], F32, kind="ExternalOutput").ap()
    vf_d = nc.dram_tensor("vf_scratch", [256, S], F32, kind="Internal").ap()
    npp = shapes["pp"][1]

    import os as _os
    P = Prog(nc, same_sync=(_os.environ.get('NOSAME') is None))
    op, dma = P.op, P.dma
    XT = P.sb("XT", [128, 8, S], F32R)
    XF = XT[:].bitcast(F32)
    AR = None
    OTt = None
    CST = P.sb("CST", [128, 1472])
    PP = P.sb("PP", [128, npp])
    CSM = P.sb("CSM", [128, 832])
    LAM = P.sb("LAM", [128, 64 + DEPTH * 128 + 64])
    banks = [P.ps("pb%d" % i, [128, 512]) for i in range(8)]
    bank_t = [TT() for _ in range(8)]
    PS = Rot([b[:] for b in banks[0:6]], bank_t[0:6])
    PSL = Rot([b[:] for b in banks[6:8]], bank_t[6:8])
    PSA = Rot([b[:] for b in banks[0:4]], bank_t[0:4])
    PSLA = Rot([b[:] for b in banks[4:8]], bank_t[4:8])

    xt_t = [[TT() for _ in range(4)] for _ in range(8)]
    vf_t = [TT(), TT()]
    t_cst, t_pp, t_csm, t_lam, t_rst = TT(), TT(), TT(), TT(), TT()
    CAUS = CST[:, 0:128]
    BAND = CST[:, 128:384]
    UPP = CST[:, 384:640]
    IDENT = CST[:, 640:768]
    BLK = CST[:, 768:896]
    ONESD = CST[:, 896:1024]
    ONES64 = CST[:, 1024:1088]
    ONES = CST[:, 1088:1216]
    LOWS = CST[:, 1216:1344]
    BLK64 = CST[:, 1344:1472]
    dma("sp", "dma_start", out=CAUS, in_=dr["c_causal"], writes=[t_cst])
    dma("sp", "dma_start", out=BAND, in_=dr["c_band"], writes=[t_cst])
    dma("sp", "dma_start", out=UPP, in_=dr["c_uppers"], writes=[t_cst])
    dma("sp", "dma_start", out=IDENT, in_=dr["c_ident"], writes=[t_cst])
    dma("sp", "dma_start", out=BLK, in_=dr["c_blk"], writes=[t_cst])
    dma("sp", "dma_start", out=PP[:], in_=dr["pp"], writes=[t_pp])
    op("dve", "memset", ONESD, 1.0 / 1024.0, writes=[t_cst])
    op("dve", "memset", ONES64, 1.0 / 64.0, writes=[t_cst])
    op("dve", "memset", ONES, 1.0, writes=[t_cst])
    dma("sp", "dma_start", out=LOWS, in_=dr["c_lowers"], writes=[t_cst])
    op("act", "mul", out=BLK64, in_=BLK, mul=1.0 / 64.0, reads=[t_cst], writes=[t_cst])
    for c in range(8):
        dma("pool", "dma_start", out=XT[:, c, :], in_=xT_d[c * 128:(c + 1) * 128, :].bitcast(F32R),
            writes=xt_t[c])
    ONE1 = LAM[:, 0:64]
    op("dve", "memset", LAM[:], 0.0, writes=[t_lam])
    op("dve", "memset", ONE1, 1.0, writes=[t_lam])
    LROW = LAM[0:1, 64:64 + DEPTH * 128]
    dma("sp", "dma_start", out=LROW, in_=dr["lamrow"], writes=[t_lam])
    NEGLAM = LAM[:, 64 + DEPTH * 128:64 + DEPTH * 128 + 8]
    LS = LAM[0:1, 64 + DEPTH * 128 + 8:64 + DEPTH * 128 + 64]
    for l in range(depth):
        b0 = 64 + l * 128
        lam_init = 0.8 - 0.6 * math.exp(-0.3 * l)
        for j in range(2):
            op("dve", "tensor_tensor", out=LS[:, 0:32], in0=LAM[0:1, b0 + 64 * j:b0 + 64 * j + 32],
                                                          in1=LAM[0:1, b0 + 64 * j + 32:b0 + 64 * j + 64], op=ALU.mult,
               reads=[t_lam], writes=[t_lam])
            op("dve", "reduce_sum", out=LS[:, 32 + j:33 + j], in_=LS[:, 0:32], axis=AX.X,
               reads=[t_lam], writes=[t_lam])
        op("act", "activation", out=LS[:, 34:36], in_=LS[:, 32:34], func=AF.Exp, reads=[t_lam], writes=[t_lam])
        op("dve", "scalar_tensor_tensor", out=LS[:, 36:37], in0=LS[:, 35:36], scalar=-lam_init, in1=LS[:, 34:35],
                                                                op0=ALU.add, op1=ALU.subtract, reads=[t_lam], writes=[t_lam])
        pb, pt = PS.next()
        op("pe", "matmul", pb[0:64, 0:1], lhsT=LAM[0:1, 0:64], rhs=LS[:, 36:37], start=True, stop=True,
           reads=[t_lam], writes=[pt])
        op("act", "copy", out=NEGLAM[0:64, l:l + 1], in_=pb[0:64, 0:1], reads=[pt], writes=[t_lam])

    def ppc(key):
        j = idx[key]
        return PP[:, j:j + 1]

    def tok(tp):
        return slice(tp * 512, (tp + 1) * 512)

    def ssl(st, n, step):
        return slice(st, st + step * (n - 1) + 1, step) if step > 1 else slice(st, st + n)

    def layer_norm(l, i, tp, sc, lnk, sqr=None, onesr=None, onest=None):
        eps = LN_EPS / (ALPHA * ALPHA)
        pm, pmt = PS.next()
        pe2, pe2t = PS.next()
        for c in range(8):
            if onesr is not None:
                op("pe", "matmul", pm, lhsT=onesr, rhs=XT[:, c, tok(tp)], start=(c == 0), stop=(c == 7),
                   reads=[onest, xt_t[c][tp]], writes=[pmt])
            else:
                op("pe", "matmul", pm, lhsT=ONESD, rhs=XF[:, c, tok(tp)], start=(c == 0), stop=(c == 7),
                   reads=[t_cst, xt_t[c][tp]], writes=[pmt])
        for c in range(8):
            sq, sqt = (sqr if sqr is not None else sc).next()
            op("act", "activation", out=sq, in_=XF[:, c, tok(tp)], func=AF.Square,
               reads=[xt_t[c][tp]], writes=[sqt])
            if onesr is not None:
                op("pe", "matmul", pe2, lhsT=onesr, rhs=sq, start=(c == 0), stop=(c == 7),
                   reads=[onest, sqt], writes=[pe2t])
            else:
                op("pe", "matmul", pe2, lhsT=ONESD, rhs=sq, start=(c == 0), stop=(c == 7),
                   reads=[t_cst, sqt], writes=[pe2t])
        mean, meant = lnk.next()
        op("act", "copy", out=mean, in_=pm, reads=[pmt], writes=[meant])
        msq, msqt = lnk.next()
        op("dve", "tensor_tensor", out=msq, in0=mean, in1=mean, op=ALU.mult, reads=[meant], writes=[msqt])
        var, vart = lnk.next()
        op("dve", "scalar_tensor_tensor", out=var, in0=pe2, scalar=eps, in1=msq, op0=ALU.add, op1=ALU.subtract,
           reads=[pe2t, msqt], writes=[vart])
        op("act", "activation", out=var, in_=var, func=AF.Ln, reads=[vart], writes=[vart])
        op("act", "activation", out=var, in_=var, func=AF.Exp, scale=-0.5, reads=[vart], writes=[vart])
        for c in range(8):
            t1, t1t = sc.next()
            op("dve", "tensor_tensor", out=t1, in0=XF[:, c, tok(tp)], in1=mean, op=ALU.subtract,
               reads=[xt_t[c][tp], meant], writes=[t1t])
            op("dve", "tensor_tensor", out=t1, in0=t1, in1=var, op=ALU.mult,
               reads=[t1t, vart], writes=[t1t])
            op("act", "activation", out=XT[:, c, tok(tp)], in_=t1, func=AF.Identity,
                                                        bias=ppc(("lnb", l, i, c)), scale=ppc(("lng", l, i, c)),
               reads=[t1t, t_pp], writes=[xt_t[c][tp]])

    def ffn_phase(l, i):
        st = P.phase()
        FR = P.ph_sb(st, "FR", [128, 11264 + 4 * 2048 + 2 * 2816 + 1024 + 128], F32R)
        FT = P.ph_sb(st, "FT", [128, 11 * 512], F32)
        o = 0
        Hv = FR[:, o:o + 22 * 512].rearrange("p (f t) -> p f t", t=512)
        o += 22 * 512
        h_t = [TT() for _ in range(22)]
        wgu_r = Rot([FR[:, o + k * 2048:o + (k + 1) * 2048].rearrange("p (g kc f) -> p g kc f", g=2, kc=8) for k in range(4)])
        o += 4 * 2048
        wd_r = Rot([FR[:, o + k * 2816:o + (k + 1) * 2816].rearrange("p (f d) -> p f d", d=128) for k in range(2)])
        o += 2 * 2816
        sqr = Rot([FR[:, o + k * 512:o + (k + 1) * 512] for k in range(2)])
        o += 1024
        onesr = FR[:, o:o + 128]
        onest = TT()
        op("act", "copy", out=onesr, in_=ONESD, reads=[t_cst], writes=[onest])
        sc = Rot([FT[:, k * 512:(k + 1) * 512] for k in range(8)])
        lnk = Rot([FT[:, (8 + k) * 512:(9 + k) * 512] for k in range(3)])
        lnidx = 0 if i == 0 else 2
        def p1(tp):
            for f in range(22):
                wb, wbt = wgu_r.next()
                dma("pool", "dma_start", out=wb,
                    in_=dr["wgu"][(l * 2 + i) * 22 + f].bitcast(F32R).rearrange("p (g kc f) -> p g kc f", g=2, kc=8), writes=[wbt])
                pg, pgt = PS.next()
                pu, put = PS.next()
                for g, pp_, ppt in ((0, pg, pgt), (1, pu, put)):
                    for kc in range(8):
                        op("pe", "matmul", pp_, lhsT=wb[:, g, kc, :], rhs=XT[:, kc, tok(tp)], start=(kc == 0), stop=(kc == 7),
                           reads=[wbt, xt_t[kc][tp]], writes=[ppt])
                sg, sgt = sc.next()
                op("act", "activation", out=sg, in_=pg, func=AF.Silu, reads=[pgt], writes=[sgt])
                op("dve", "tensor_tensor", out=Hv[:, f, :], in0=sg, in1=pu, op=ALU.mult, reads=[sgt, put], writes=[h_t[f]])

        def p2(tp):
            for dc in range(8):
                wb, wbt = wd_r.next()
                dma("pool", "dma_start", out=wb,
                    in_=dr["wd"][(l * 2 + i) * 8 + dc].bitcast(F32R).rearrange("p (f d) -> p f d", d=128), writes=[wbt])
                py, pyt = PS.next()
                for f in range(22):
                    op("pe", "matmul", py, lhsT=wb[:, f, :], rhs=Hv[:, f, :], start=(f == 0), stop=(f == 21),
                       reads=[wbt, h_t[f]], writes=[pyt])
                op("dve", "scalar_tensor_tensor", out=XT[:, dc, tok(tp)], in0=py, scalar=0.5 / ALPHA, in1=XF[:, dc, tok(tp)],
                   op0=ALU.mult, op1=ALU.add, reads=[pyt, xt_t[dc][tp]], writes=[xt_t[dc][tp]])

        for tp in range(4):
            p1(tp)
            if tp > 0:
                layer_norm(l, lnidx, tp - 1, sc, lnk, sqr, onesr, onest)
            p2(tp)
        layer_norm(l, lnidx, 3, sc, lnk, sqr, onesr, onest)
        return st

    def mixer_phase(l, stage):
        OT = OTt[:].rearrange("p (c t) -> p c t", t=S)
        OTf = OTt[:].bitcast(F32).rearrange("p (c t) -> p c t", t=S)
        ot_t = [[TT() for _ in range(4)] for _ in range(8)]
        QT = KT = KZ = VA = RS = VTF = vtf_t = EREG = None
        w_r = rope_r = e512 = e256 = scr = None
        SCR0 = None
        qt_t = kt_t = None
        va_t = kz_t = None

        def setup_ab(is_a):
            nonlocal QT, KT, KZ, VA, RS, w_r, rope_r, e512, e256, scr, SCR0, qt_t, kt_t, va_t, kz_t, VTF, vtf_t, EREG
            st = P.phase()
            AQ = P.ph_sb(st, "AQ", [128, 11808 if is_a else 9760 + 768], F32R)
            ABp = P.ph_sb(st, "ABp", [128, 3072 if is_a else 5120], F32)
            o = 0
            QT = AQ[:, o:o + 2048]
            KT = AQ[:, o + 2048:o + 4096]
            o += 4096
            qt_t = [TT() for _ in range(4)]
            kt_t = [TT() for _ in range(4)]
            VA = AQ[:, o:o + 2080].rearrange("p (j h d) -> p j h d", h=2, d=65)
            va_t = TT()
            o += 2080
            w_r = Rot([AQ[:, o + k * 1024:o + (k + 1) * 1024].rearrange("p (kc f) -> p kc f", kc=8) for k in range(2)])
            VTF = AQ[:, o:o + 2048]
            vtf_t = TT()
            o += 2048
            EREG = AQ[:, o:o + 1024]
            e512 = Rot([AQ[:, o + k * 512:o + (k + 1) * 512] for k in range(3)])
            e256 = Rot([AQ[:, o + k * 256:o + (k + 1) * 256] for k in range(6 if is_a else 9)])
            o += 1536 if is_a else 2304
            if is_a:
                KZ = AQ[:, o:o + 2048]
                kz_t = TT()
                o += 2048
            rope_r = Rot([ABp[:, 0:1024].rearrange("p (c t) -> p c t", c=2)])
            scr = Rot([ABp[:, 1024 + k * 512:1024 + (k + 1) * 512] for k in range(4)])
            SCR0 = ABp
            if not is_a:
                RS = ABp[:, 3072:5120]
            return st

        lam_init = 0.8 - 0.6 * math.exp(-0.3 * l)

        def load_w(ci):
            w, wt = w_r.next()
            dma("pool", "dma_start", out=w, in_=dr["wch"][l * NCHUNK_W + ci].bitcast(F32R).rearrange("p (kc f) -> p kc f", kc=8),
                writes=[wt])
            return w, wt

        def proj_rope(ci, dst, dst_t, rname):
            w1, w1t = load_w(ci)
            w2, w2t = load_w(ci + 1)
            for tp in range(4):
                rp, rpt = rope_r.next()
                dma("sp", "dma_start", out=rp, in_=dr[rname][:, tp * 1024:(tp + 1) * 1024].rearrange("p (c t) -> p c t", c=2),
                    writes=[rpt])
                p1, p1t = PS.next()
                p2, p2t = PS.next()
                for w, wt, pp_, ppt in ((w1, w1t, p1, p1t), (w2, w2t, p2, p2t)):
                    for kc in range(8):
                        op("pe", "matmul", pp_, lhsT=w[:, kc, :], rhs=XT[:, kc, tok(tp)], start=(kc == 0), stop=(kc == 7),
                           reads=[wt, xt_t[kc][tp]], writes=[ppt])
                a, at = scr.next()
                b, bt = scr.next()
                op("dve", "tensor_tensor", out=a, in0=p1, in1=rp[:, 0, :], op=ALU.mult, reads=[p1t, rpt], writes=[at])
                op("dve", "tensor_tensor", out=b, in0=p2, in1=rp[:, 1, :], op=ALU.mult, reads=[p2t, rpt], writes=[bt])
                op("pool", "tensor_tensor", out=dst[:, tok(tp)], in0=a, in1=b, op=ALU.add, reads=[at, bt], writes=[dst_t[tp]])

        def proj_vfm(ci):
            P.barrier()
            wv = EREG.rearrange("p (kc f) -> p kc f", kc=8)
            wvt = TT()
            dma("pool", "dma_start", out=wv, in_=dr["wch"][l * NCHUNK_W + ci].bitcast(F32R).rearrange("p (kc f) -> p kc f", kc=8),
                writes=[wvt])
            for tp in range(4):
                pv, pvt = PS.next()
                for kc in range(8):
                    op("pe", "matmul", pv, lhsT=wv[:, kc, :], rhs=XT[:, kc, tok(tp)], start=(kc == 0), stop=(kc == 7),
                       reads=[wvt, xt_t[kc][tp]], writes=[pvt])
                if tp % 2 == 0:
                    op("act", "copy", out=VTF[:, tok(tp)], in_=pv, reads=[pvt], writes=[vtf_t])
                else:
                    op("dve", "tensor_copy", out=VTF[:, tok(tp)], in_=pv, reads=[pvt], writes=[vtf_t])
            P.barrier()

        def proj_v(ci, dil):
            op("act", "copy", out=VA[:, :, :, 64:65], in_=ONES[:, 0:32].rearrange("p (j h d) -> p j h d", h=2, d=1),
               reads=[t_cst], writes=[va_t])
            nt = 16 // dil
            VTFf = VTF.bitcast(F32)
            k = 0
            for r in range(dil):
                for a in range(nt):
                    st = r + dil * 128 * a
                    pv, pvt = PS.next()
                    op("pe", "transpose", pv[:, 0:128], VTFf[:, ssl(st, 128, dil)], IDENT, reads=[vtf_t, t_cst], writes=[pvt])
                    if k % 2 == 0:
                        op("act", "copy", out=VA[:, r * nt + a, :, 0:64], in_=pv[:, 0:128].rearrange("p (h d) -> p h d", h=2),
                           reads=[pvt], writes=[va_t])
                    else:
                        op("dve", "tensor_copy", out=VA[:, r * nt + a, :, 0:64], in_=pv[:, 0:128].rearrange("p (h d) -> p h d", h=2),
                           reads=[pvt], writes=[va_t])
                    k += 1

        def phase_a(i):
            st = setup_ab(True)
            proj_rope(4 * i, QT, qt_t, "ropeA")
            proj_rope(4 * i + 2, KT, kt_t, "ropeA")
            proj_vfm(32 + i)
            proj_v(32 + i, 1)
            op("pool", "tensor_copy", out=KZ[96:128, :], in_=KT[96:128, :].bitcast(F32), reads=kt_t, writes=[kz_t])
            op("pool", "tensor_scalar", out=KZ[64:96, :], in0=KT[64:96, :].bitcast(F32), scalar1=0.0, scalar2=None, op0=ALU.mult,
               reads=kt_t, writes=[kz_t])
            sca = 32 ** -0.5
            bg = []

            def step_bg():
                if bg:
                    try:
                        next(bg[0])
                    except StopIteration:
                        bg.pop(0)

            def drain_bg(keep=0):
                while len(bg) > keep:
                    step_bg()

            def main_block(hl, Qp, nums):
                for m in range(2):
                    r0 = 32 * (2 * hl + m)
                    num, numt = PSLA.next()
                    nums.append((num, numt))
                    last = 4 * Qp + 3
                    pend = []

                    def do_pv(item, num=num, numt=numt, last=last):
                        j, E, Et, c0, n = item
                        op("pe", "matmul", num[0:65, c0 - 512 * Qp:512], lhsT=VA[:, j, hl, :], rhs=E[:, 0:n],
                           start=(j == 0), stop=(j == last), reads=[va_t, Et], writes=[numt])

                    for j in range(last + 1):
                        c0 = max(128 * j, 512 * Qp)
                        n = 512 * Qp + 512 - c0
                        se, set_ = PSA.next()
                        if hl == 1 and m == 1:
                            op("pe", "matmul", se[:, 0:n], lhsT=KZ[64:128, 128 * j:128 * j + 128], rhs=QT[64:128, c0:c0 + n],
                               start=True, stop=True, reads=[kz_t] + qt_t[c0 // 512:Qp + 1], writes=[set_])
                        else:
                            op("pe", "matmul", se[:, 0:n], lhsT=KT[r0:r0 + 32, 128 * j:128 * j + 128], rhs=QT[r0:r0 + 32, c0:c0 + n],
                               start=True, stop=True, reads=[kt_t[j // 4]] + qt_t[c0 // 512:Qp + 1], writes=[set_])
                        E, Et = e512.next()
                        op("act", "activation", out=E[:, 0:n], in_=se[:, 0:n], func=AF.Exp, scale=sca, reads=[set_], writes=[Et])
                        if 128 * j >= 512 * Qp:
                            op("pool", "tensor_tensor", out=E[:, 0:128], in0=E[:, 0:128].bitcast(F32), in1=CAUS, op=ALU.mult,
                               reads=[Et, t_cst], writes=[Et])
                        pend.append((j, E, Et, c0, n))
                        if len(pend) > 2:
                            do_pv(pend.pop(0))
                        yield
                    while pend:
                        do_pv(pend.pop(0))
                        yield

            def post_block(hl, Qp, nums):
                (n1, n1t), (n2, n2t) = nums
                X0, X0t = scr.next()
                X1, X1t = scr.next()
                X2, X2t = scr.next()
                for (nn, nnt, prt) in ((n1, n1t, 64), (n2, n2t, 32)):
                    op("act", "activation", out=X0[prt:prt + 1, :], in_=nn[64:65, :], func=AF.Ln, reads=[nnt], writes=[X0t])
                    yield
                    op("act", "activation", out=X0[prt:prt + 1, :], in_=X0[prt:prt + 1, :], func=AF.Exp, scale=-1.0,
                       reads=[X0t], writes=[X0t])
                    yield
                for (prt, Xd, Xdt) in ((64, X1, X1t), (32, X2, X2t)):
                    pb, pbt = PSA.next()
                    op("pe", "matmul", pb[0:64, :], lhsT=ONE1[prt:prt + 1, 0:64], rhs=X0[prt:prt + 1, :], start=True, stop=True,
                       reads=[t_lam, X0t], writes=[pbt])
                    yield
                    op("act", "copy", out=Xd[0:64, :], in_=pb[0:64, :], reads=[pbt], writes=[Xdt])
                    yield
                op("dve", "tensor_tensor", out=X1[0:64, :], in0=n1[0:64, :], in1=X1[0:64, :], op=ALU.mult, reads=[n1t, X1t], writes=[X1t])
                yield
                op("dve", "tensor_tensor", out=X2[0:64, :], in0=n2[0:64, :], in1=X2[0:64, :], op=ALU.mult, reads=[n2t, X2t], writes=[X2t])
                yield
                op("dve", "scalar_tensor_tensor", out=X1[0:64, :], in0=X2[0:64, :], scalar=NEGLAM[0:64, l:l + 1], in1=X1[0:64, :],
                   op0=ALU.mult, op1=ALU.add, reads=[X1t, X2t, t_lam], writes=[X1t])
                yield
                op("act", "activation", out=X2[0:64, :], in_=X1[0:64, :], func=AF.Square, reads=[X1t], writes=[X2t])
                yield
                pm, pmt = PSA.next()
                op("pe", "matmul", pm[0:64, :], lhsT=ONES64[0:64, 0:64], rhs=X2[0:64, :], start=True, stop=True,
                   reads=[t_cst, X2t], writes=[pmt])
                yield
                op("dve", "tensor_scalar", out=X2[0:64, :], in0=pm[0:64, :], scalar1=RMS_EPS, scalar2=None, op0=ALU.add,
                   reads=[pmt], writes=[X2t])
                yield
                op("act", "activation", out=X2[0:64, :], in_=X2[0:64, :], func=AF.Ln, reads=[X2t], writes=[X2t])
                yield
                op("act", "activation", out=X2[0:64, :], in_=X2[0:64, :], func=AF.Exp, scale=-0.5, reads=[X2t], writes=[X2t])
                yield
                op("dve", "scalar_tensor_tensor", out=X1[0:64, :], in0=X1[0:64, :], scalar=ppc(("ga", l))[0:64, :], in1=X2[0:64, :],
                   op0=ALU.mult, op1=ALU.mult, reads=[X1t, X2t, t_pp], writes=[X1t])
                yield
                op("dve", "tensor_scalar", out=OT[64 * hl:64 * hl + 64, i, tok(Qp)], in0=X1[0:64, :], scalar1=1.0 - lam_init,
                   scalar2=None, op0=ALU.mult, reads=[X1t], writes=[ot_t[i][Qp]])
                yield

            for hl in range(2):
                for Qp in range(4):
                    nums = []
                    drain_bg(keep=1)
                    cnt = 0
                    for _ in main_block(hl, Qp, nums):
                        cnt += 1
                        if cnt % 2 == 0:
                            step_bg()
                    drain_bg(keep=0) if False else None
                    bg.append(post_block(hl, Qp, nums))
            drain_bg(0)
            P.end_phase(st)


        def phase_b(i):
            st = setup_ab(False)
            rs_t = [TT(), TT()]
            proj_rope(12 + 4 * i, QT, qt_t, "ropeB")
            proj_rope(12 + 4 * i + 2, KT, kt_t, "ropeB")
            scb = 64 ** -0.5
            oc = 3 + i
            proj_vfm(35 + i)
            for dil in (1, 4, 16):
                proj_v(35 + i, dil)
                L = S // dil
                nt = L // 128
                pieces = []
                for hl in range(2):
                    for r in range(dil):
                        Es = {}
                        for g in range((nt + 3) // 4):
                            pieces.append((hl, r, g, Es))

                def stage1(pc):
                    hl, r, g, Es = pc
                    rows = slice(64 * hl, 64 * hl + 64)
                    for a in range(max(4 * g - 1, 0), min(4 * g + 3, nt - 1) + 1):
                        if a in Es:
                            continue
                        nq = min(256, L - 128 * a)
                        st_ = r + dil * 128 * a
                        se, set_ = PS.next()
                        op("pe", "matmul", se[:, 0:nq], lhsT=KT[rows, ssl(st_, 128, dil)], rhs=QT[rows, ssl(st_, nq, dil)],
                           start=True, stop=True, reads=kt_t + qt_t, writes=[set_])
                        E, Et = e256.next()
                        op("act", "activation", out=E[:, 0:nq], in_=se[:, 0:nq], func=AF.Exp, scale=scb, reads=[set_], writes=[Et])
                        op("pool", "tensor_tensor", out=E[:, 0:nq], in0=E[:, 0:nq].bitcast(F32), in1=BAND[:, 0:nq], op=ALU.mult,
                           reads=[Et, t_cst], writes=[Et])
                        Es[a] = (E, Et, nq)

                def stage2(pc):
                    hl, r, g, Es = pc
                    rows = slice(64 * hl, 64 * hl + 64)
                    rowp = 64 if hl == 0 else 32
                    ncols = min(512, L - 512 * g)
                    alist = list(range(max(4 * g - 1, 0), min(4 * g + 3, nt - 1) + 1))
                    num, numt = PSL.next()
                    for ai, a in enumerate(alist):
                        E, Et, nq = Es[a]
                        lo = max(128 * a, 512 * g)
                        hi = min(128 * a + nq, 512 * g + ncols)
                        op("pe", "matmul", num[0:65, lo - 512 * g:hi - 512 * g], lhsT=VA[:, r * nt + a, hl, :],
                           rhs=E[:, lo - 128 * a:hi - 128 * a], start=(ai == 0), stop=(ai == len(alist) - 1),
                           reads=[va_t, Et], writes=[numt])
                    p0 = r + dil * 512 * g
                    ps_ = ssl(p0, ncols, dil)
                    if dil == 1:
                        op("dve", "tensor_copy", out=OT[rows, oc, ps_], in_=num[0:64, 0:ncols], reads=[numt], writes=ot_t[oc])
                        op("act", "copy", out=RS[rowp:rowp + 1, ps_], in_=num[64:65, 0:ncols], reads=[numt], writes=[rs_t[hl]])
                    else:
                        op("dve", "tensor_tensor", out=OT[rows, oc, ps_], in0=num[0:64, 0:ncols], in1=OTf[rows, oc, ps_], op=ALU.add,
                           reads=[numt] + ot_t[oc], writes=ot_t[oc])
                        op("dve", "tensor_tensor", out=RS[rowp:rowp + 1, ps_], in0=num[64:65, 0:ncols], in1=RS[rowp:rowp + 1, ps_],
                           op=ALU.add, reads=[numt, rs_t[hl]], writes=[rs_t[hl]])

                stage1(pieces[0])
                for k in range(len(pieces)):
                    if k + 1 < len(pieces):
                        stage1(pieces[k + 1])
                    stage2(pieces[k])
            P.barrier()
            chains = []

            def post_chain(hl, tp, X, Xt):
                rows = slice(64 * hl, 64 * hl + 64)
                rowp = 64 if hl == 0 else 32
                op("act", "activation", out=X[rows, :], in_=OTf[rows, oc, tok(tp)], func=AF.Square, reads=[ot_t[oc][tp]], writes=[Xt])
                yield
                op("act", "activation", out=X[rowp:rowp + 1, :], in_=RS[rowp:rowp + 1, tok(tp)], func=AF.Square,
                   scale=math.sqrt(RMS_EPS), reads=[rs_t[hl]], writes=[Xt])
                yield
                pm, pmt = PS.next()
                op("pe", "matmul", pm[0:64, :], lhsT=ONES64[rows, 0:64], rhs=X[rows, :], start=True, stop=False,
                   reads=[t_cst, Xt], writes=[pmt])
                op("pe", "matmul", pm[0:64, :], lhsT=ONE1[rowp:rowp + 1, 0:64], rhs=X[rowp:rowp + 1, :], start=False, stop=True,
                   reads=[t_lam, Xt], writes=[pmt])
                yield
                op("act", "activation", out=X[rows, :], in_=pm[0:64, :], func=AF.Ln, reads=[pmt], writes=[Xt])
                yield
                op("act", "activation", out=X[rows, :], in_=X[rows, :], func=AF.Exp, scale=-0.5, reads=[Xt], writes=[Xt])
                yield
                op("dve", "scalar_tensor_tensor", out=OT[rows, oc, tok(tp)], in0=OTf[rows, oc, tok(tp)], scalar=ppc(("gb", l))[rows, :],
                   in1=X[rows, :], op0=ALU.mult, op1=ALU.mult, reads=[ot_t[oc][tp], Xt, t_pp], writes=[ot_t[oc][tp]])
                yield

            xt4 = [(SCR0[:, 1024 + k * 512:1024 + (k + 1) * 512], TT()) for k in range(4)]
            for hl in range(2):
                gens = [post_chain(hl, tp, xt4[tp][0], xt4[tp][1]) for tp in range(4)]
                while gens:
                    for gsn in list(gens):
                        try:
                            next(gsn)
                        except StopIteration:
                            gens.remove(gsn)
            P.end_phase(st)


        def phase_c(hp):
            st = P.phase()
            AR = P.ph_sb(st, "CF", [128, CFW], F32)
            oc = 6 + hp
            base = 0
            F = [AR[:, base + k * 2048:base + (k + 1) * 2048] for k in range(6)]
            ft = [[TT() for _ in range(4)] for _ in range(6)]
            sm = base + 6 * 2048
            CW = P.ph_sb(st, "CW", [128, 1024], F32R)
            w1 = Rot([CW[:, 0:1024].rearrange("p (kc f) -> p kc f", kc=8)])
            RAWPS = [AR[:, sm:sm + 516], AR[:, sm + 1024:sm + 1024 + 516]]
            rawp_ts = [TT(), TT()]
            RAWP = RAWPS[1]
            rawp_t = rawp_ts[1]
            TA = AR[:, sm + 1540:sm + 2052]
            TB = AR[:, sm + 2052:sm + 2564]
            ta_t, tb_t = TT(), TT()
            GC = AR[:, sm + 2564:sm + 2580]
            gc_t = TT()
            assert sm + 2580 <= CFW
            dma("sp", "dma_start", out=CSM[:], in_=dr["csm"][:, l * 832:(l + 1) * 832], writes=[t_csm])

            def loadw1(ci):
                w, wt = w1.next()
                dma("pool", "dma_start", out=w, in_=dr["wch"][l * NCHUNK_W + ci].bitcast(F32R).rearrange("p (kc f) -> p kc f", kc=8),
                    writes=[wt])
                return w, wt

            def proj_lerp(cidx, dst, dst_t):
                w, wt = loadw1(24 + cidx)
                mu = ppc(("mu", l, cidx))
                op("pool", "memset", RAWPS[0][:, 0:1], 0.0, writes=[rawp_ts[0]])
                for tp in range(4):
                    RP, RPt = RAWPS[tp % 2], rawp_ts[tp % 2]
                    RN, RNt = RAWPS[(tp + 1) % 2], rawp_ts[(tp + 1) % 2]
                    pp_, ppt = PS.next()
                    for kc in range(8):
                        op("pe", "matmul", pp_, lhsT=w[:, kc, :], rhs=XT[:, kc, tok(tp)], start=(kc == 0), stop=(kc == 7),
                           reads=[wt, xt_t[kc][tp]], writes=[ppt])
                    op("act", "copy", out=RP[:, 1:513], in_=pp_, reads=[ppt], writes=[RPt])
                    if tp < 3:
                        op("act", "copy", out=RN[:, 0:1], in_=RP[:, 512:513], reads=[RPt], writes=[RNt])
                    op("dve", "tensor_tensor", out=TA, in0=RP[:, 0:512], in1=RP[:, 1:513], op=ALU.subtract,
                       reads=[RPt], writes=[ta_t])
                    op("dve", "scalar_tensor_tensor", out=dst[:, tok(tp)], in0=TA, scalar=mu, in1=RP[:, 1:513],
                       op0=ALU.mult, op1=ALU.add, reads=[ta_t, RPt, t_pp], writes=[dst_t[tp]])

            Kt, Bt, KKt, Rt, Vt = F[0], F[2], F[3], F[4], F[5]
            LW = F[1]
            proj_lerp(4 + hp, F[5], ft[5])
            if l == 0:
                dma("sp", "dma_start", out=vf_d[hp * 128:(hp + 1) * 128, :], in_=F[5], reads=ft[5], writes=[vf_t[hp]])
            else:
                proj_lerp(4 + (1 - hp), F[4], ft[4])
                vch = {hp: (F[5], ft[5]), 1 - hp: (F[4], ft[4])}
                for tp in range(4):
                    p32, p32t = PS.next()
                    for kc in range(2):
                        vv, vvt = vch[kc]
                        op("pe", "matmul", p32[0:32, :], lhsT=CSM[:, 512 + kc * 32:512 + kc * 32 + 32], rhs=vv[:, tok(tp)],
                           start=(kc == 0), stop=(kc == 1), reads=[t_csm, vvt[tp]], writes=[p32t])
                    op("act", "copy", out=TB[0:32, :], in_=p32[0:32, :], reads=[p32t], writes=[tb_t])
                    pg, pgt = PS.next()
                    op("pe", "matmul", pg, lhsT=CSM[0:32, 576 + hp * 128:576 + hp * 128 + 128], rhs=TB[0:32, :], start=True, stop=True,
                       reads=[t_csm, tb_t], writes=[pgt])
                    op("act", "activation", out=TB, in_=pg, func=AF.Sigmoid, bias=ppc(("v0", l, hp)), scale=1.0,
                       reads=[pgt, t_pp], writes=[tb_t])
                    dma("sp", "dma_start", out=TA, in_=vf_d[hp * 128:(hp + 1) * 128, tok(tp)], reads=[vf_t[hp]], writes=[ta_t])
                    op("dve", "tensor_tensor", out=TA, in0=TA, in1=F[5][:, tok(tp)], op=ALU.subtract, reads=[ta_t, ft[5][tp]], writes=[ta_t])
                    op("dve", "tensor_tensor", out=TA, in0=TA, in1=TB, op=ALU.mult, reads=[ta_t, tb_t], writes=[ta_t])
                    op("dve", "tensor_tensor", out=F[5][:, tok(tp)], in0=F[5][:, tok(tp)], in1=TA, op=ALU.add,
                       reads=[ta_t, ft[5][tp]], writes=[ft[5][tp]])
            proj_lerp(6, F[0], ft[0])
            for tp in range(4):
                op("act", "activation", out=F[0][0:64, tok(tp)], in_=F[0][0:64, tok(tp)], func=AF.Tanh, reads=[ft[0][tp]], writes=[ft[0][tp]])
                pw, pwt = PS.next()
                op("pe", "matmul", pw, lhsT=CSM[0:64, hp * 128:hp * 128 + 128], rhs=F[0][0:64, tok(tp)], start=True, stop=True,
                   reads=[t_csm, ft[0][tp]], writes=[pwt])
                op("act", "activation", out=F[1][:, tok(tp)], in_=pw, func=AF.Sigmoid, bias=ppc(("w0", l, hp)), scale=1.0,
                   reads=[pwt, t_pp], writes=[ft[1][tp]])
                op("pool", "tensor_scalar", out=F[1][:, tok(tp)], in0=F[1][:, tok(tp)], scalar1=-math.exp(-0.5), scalar2=None, op0=ALU.mult,
                   reads=[ft[1][tp]], writes=[ft[1][tp]])
                pa, pat = PS.next()
                op("pe", "matmul", pa, lhsT=CSM[64:128, hp * 128:hp * 128 + 128], rhs=F[0][64:128, tok(tp)], start=True, stop=True,
                   reads=[t_csm, ft[0][tp]], writes=[pat])
                op("act", "activation", out=F[2][:, tok(tp)], in_=pa, func=AF.Sigmoid, bias=ppc(("a0", l, hp)), scale=1.0,
                   reads=[pat, t_pp], writes=[ft[2][tp]])
            proj_lerp(2 + hp, F[0], ft[0])
            for tp in range(4):
                tk = tok(tp)
                op("dve", "tensor_scalar", out=F[3][:, tk], in0=F[0][:, tk], scalar1=ppc(("kk", l, hp)), scalar2=None, op0=ALU.mult,
                   reads=[ft[0][tp], t_pp], writes=[ft[3][tp]])
                op("act", "activation", out=TA, in_=F[3][:, tk], func=AF.Square, reads=[ft[3][tp]], writes=[ta_t])
                pq, pqt = PS.next()
                op("pe", "matmul", pq, lhsT=BLK, rhs=TA, start=True, stop=True, reads=[t_cst, ta_t], writes=[pqt])
                op("act", "activation", out=TB, in_=pq, func=AF.Sqrt, reads=[pqt], writes=[tb_t])
                op("dve", "tensor_scalar", out=TB, in0=TB, scalar1=1e-12, scalar2=None, op0=ALU.max, reads=[tb_t], writes=[tb_t])
                op("dve", "reciprocal", out=TB, in_=TB, reads=[tb_t], writes=[tb_t])
                op("dve", "tensor_tensor", out=F[3][:, tk], in0=F[3][:, tk], in1=TB, op=ALU.mult, reads=[ft[3][tp], tb_t], writes=[ft[3][tp]])
                op("dve", "tensor_scalar", out=TA, in0=F[2][:, tk], scalar1=-1.0, scalar2=ppc(("ka", l, hp)), op0=ALU.add, op1=ALU.mult,
                   reads=[ft[2][tp], t_pp], writes=[ta_t])
                op("dve", "scalar_tensor_tensor", out=F[0][:, tk], in0=TA, scalar=1.0, in1=F[0][:, tk], op0=ALU.add, op1=ALU.mult,
                   reads=[ta_t, ft[0][tp]], writes=[ft[0][tp]])
                op("pool", "tensor_tensor", out=F[2][:, tk], in0=F[2][:, tk], in1=F[3][:, tk], op=ALU.mult,
                   reads=[ft[2][tp], ft[3][tp]], writes=[ft[2][tp]])
            proj_lerp(hp, F[4], ft[4])
            for tp in range(4):
                tk = tok(tp)
                op("dve", "scalar_tensor_tensor", out=TA, in0=F[4][:, tk], scalar=ppc(("rk", l, hp)), in1=F[0][:, tk], op0=ALU.mult, op1=ALU.mult,
                   reads=[ft[4][tp], ft[0][tp], t_pp], writes=[ta_t])
                pq, pqt = PS.next()
                op("pe", "matmul", pq, lhsT=BLK, rhs=TA, start=True, stop=True, reads=[t_cst, ta_t], writes=[pqt])
                op("dve", "tensor_tensor", out=OT[:, oc, tk], in0=pq, in1=F[5][:, tk], op=ALU.mult, reads=[pqt, ft[5][tp]], writes=[ot_t[oc][tp]])
            for tp in range(4):
                tk = tok(tp)
                for cc in range(4):
                    cs = slice(tp * 512 + cc * 128, tp * 512 + cc * 128 + 128)
                    op("dve", "tensor_tensor_scan", out=TA[:, cc * 128:(cc + 1) * 128], data0=ONES, data1=F[1][:, cs], initial=0.0,
                       op0=ALU.mult, op1=ALU.add, reads=[t_cst, ft[1][tp]], writes=[ta_t])
                op("dve", "tensor_tensor", out=TB, in0=TA, in1=F[1][:, tk], op=ALU.subtract, reads=[ta_t, ft[1][tp]], writes=[tb_t])
                op("act", "activation", out=TB, in_=TB, func=AF.Exp, reads=[tb_t], writes=[tb_t])
                op("dve", "tensor_tensor", out=F[3][:, tk], in0=F[3][:, tk], in1=TB, op=ALU.mult, reads=[ft[3][tp], tb_t], writes=[ft[3][tp]])
                op("act", "activation", out=TB, in_=TA, func=AF.Exp, reads=[ta_t], writes=[tb_t])
                op("dve", "tensor_tensor", out=F[4][:, tk], in0=F[4][:, tk], in1=TB, op=ALU.mult, reads=[ft[4][tp], tb_t], writes=[ft[4][tp]])
                op("act", "copy", out=GC[:, tp * 4:tp * 4 + 4], in_=TB[:, 127:512:128], reads=[tb_t], writes=[gc_t])
                op("act", "activation", out=TB, in_=TA, func=AF.Exp, scale=-1.0, reads=[ta_t], writes=[tb_t])
                op("dve", "tensor_tensor", out=F[0][:, tk], in0=F[0][:, tk], in1=TB, op=ALU.mult, reads=[ft[0][tp], tb_t], writes=[ft[0][tp]])
                op("pool", "tensor_tensor", out=F[2][:, tk], in0=F[2][:, tk], in1=TB, op=ALU.mult, reads=[ft[2][tp], tb_t], writes=[ft[2][tp]])
            P.barrier()
            def cut(region, n, width):
                out = []
                for _ in range(n):
                    out.append(AR[:, region[0]:region[0] + width])
                    region[0] += width
                return out
            reg1 = [base + 2048]
            reg2 = [sm]
            reg3 = [sm + 2580]
            KhF, BhF = cut(reg1, 2, 128)
            MB = [cut(reg1, 2, 256) for _ in range(2)]
            TK = [cut(reg1, 3, 128) for _ in range(2)]
            assert reg1[0] <= base + 4096
            MK = [cut(reg3, 2, 256) for _ in range(2)]
            assert reg3[0] <= CFW, reg3[0]
            inv = [[cut(reg2, 3, 128) for _ in range(2)] for _ in range(2)]
            Wsb = cut(reg2, 2, 64)
            Usb = cut(reg2, 2, 64)
            ST = cut(reg2, 1, 64)[0]
            YC = cut(reg2, 1, 128)[0]
            G1 = cut(reg2, 3, 128)
            assert reg2[0] <= sm + 2564, reg2[0]
            t_kh, t_bh = TT(), TT()
            t_tk = [[TT(), TT(), TT()] for _ in range(2)]
            t_mb = [[TT(), TT()] for _ in range(2)]
            t_mk = [[TT(), TT()] for _ in range(2)]
            t_inv = [[[TT() for _ in range(3)] for _ in range(2)] for _ in range(2)]
            t_w, t_u = [TT(), TT()], [TT(), TT()]
            t_st = [TT(), TT()]
            t_yc = TT()
            t_g1 = [TT() for _ in range(3)]
            RW = [slice(0, 64), slice(64, 128)]
            op("pool", "memset", ST, 0.0, writes=t_st)

            def build(c):
                bf = c % 2
                cs = slice(c * 128, c * 128 + 128)
                tp = c // 4
                KhT, BhT, VT = TK[bf]
                t_kht, t_bht, t_vt = t_tk[bf]
                op("dve", "tensor_scalar", out=KhF, in0=F[0][:, cs], scalar1=GC[:, c:c + 1], scalar2=None, op0=ALU.mult,
                   reads=[ft[0][tp], gc_t], writes=[t_kh])
                op("pool", "tensor_scalar", out=BhF, in0=F[2][:, cs], scalar1=GC[:, c:c + 1], scalar2=None, op0=ALU.mult,
                   reads=[ft[2][tp], gc_t], writes=[t_bh])
                yield
                for (src, srct, dstT, dstt) in ((KhF, [t_kh], KhT, t_kht), (BhF, [t_bh], BhT, t_bht), (F[5][:, cs], [ft[5][tp]], VT, t_vt)):
                    ptr, ptrt = PS.next()
                    op("pe", "transpose", ptr[:, 0:128], src, IDENT, reads=srct + [t_cst], writes=[ptrt])
                    op("act", "copy", out=dstT, in_=ptr[:, 0:128], reads=[ptrt], writes=[dstt])
                    yield
                for hl in range(2):
                    rows = RW[hl]
                    for (lt, ltt, Mx, Mxt) in ((F[2], ft[2][tp], MB[bf][hl], t_mb[bf][hl]), (F[0], ft[0][tp], MK[bf][hl], t_mk[bf][hl])):
                        pmx, pmxt = PS.next()
                        op("pe", "matmul", pmx[:, 0:128], lhsT=lt[rows, cs], rhs=F[3][rows, cs], start=True, stop=True,
                           reads=[ltt, ft[3][tp]], writes=[pmxt])
                        op("pe", "matmul", pmx[:, 128:256], lhsT=lt[rows, cs], rhs=F[4][rows, cs], start=True, stop=True,
                           reads=[ltt, ft[4][tp]], writes=[pmxt])
                        op("dve", "tensor_tensor", out=Mx, in0=pmx[:, 0:256], in1=UPP, op=ALU.mult, reads=[pmxt, t_cst], writes=[Mxt])
                        yield
                    Pm, PTm, Zm = inv[bf][hl]
                    tPm, tPTm, tZm = t_inv[bf][hl]
                    op("act", "mul", out=Pm, in_=MB[bf][hl][:, 0:128], mul=-1.0, reads=[t_mb[bf][hl]], writes=[tPm])
                    pnt, pntt = PS.next()
                    op("pe", "matmul", pnt[:, 0:128], lhsT=F[3][rows, cs], rhs=F[2][rows, cs], start=True, stop=True,
                       reads=[ft[3][tp], ft[2][tp]], writes=[pntt])
                    op("dve", "scalar_tensor_tensor", out=PTm, in0=pnt[:, 0:128], scalar=-1.0, in1=LOWS, op0=ALU.mult, op1=ALU.mult,
                       reads=[pntt, t_cst], writes=[tPTm])
                    op("pool", "tensor_tensor", out=Zm, in0=Pm, in1=IDENT, op=ALU.add, reads=[tPm, t_cst], writes=[tZm])
                    yield
                for stg in range(6):
                    for hl in range(2):
                        Pm, PTm, Zm = inv[bf][hl]
                        tPm, tPTm, tZm = t_inv[bf][hl]
                        ppt2, ppt2t = PS.next()
                        op("pe", "matmul", ppt2[:, 0:128], lhsT=Pm, rhs=PTm, start=True, stop=True, reads=[tPm, tPTm], writes=[ppt2t])
                        if stg < 5:
                            pp2, pp2t = PS.next()
                            op("pe", "matmul", pp2[:, 0:128], lhsT=PTm, rhs=Pm, start=True, stop=True, reads=[tPm, tPTm], writes=[pp2t])
                        op("act", "copy", out=PTm, in_=ppt2[:, 0:128], reads=[ppt2t], writes=[tPTm])
                        if stg < 5:
                            op("act", "copy", out=Pm, in_=pp2[:, 0:128], reads=[pp2t], writes=[tPm])
                        yield
                    for hl in range(2):
                        Pm, PTm, Zm = inv[bf][hl]
                        tPm, tPTm, tZm = t_inv[bf][hl]
                        pz, pzt = PS.next()
                        op("pe", "matmul", pz[:, 0:128], lhsT=PTm, rhs=Zm, start=True, stop=True, reads=[tPTm, tZm], writes=[pzt])
                        op("dve", "tensor_tensor", out=Zm, in0=pz[:, 0:128], in1=Zm, op=ALU.add, reads=[pzt, tZm], writes=[tZm])
                        yield

            def seq(c):
                bf = c % 2
                cs = slice(c * 128, c * 128 + 128)
                tp = c // 4
                KhT, BhT, VT = TK[bf]
                t_kht, t_bht, t_vt = t_tk[bf]
                for hl in range(2):
                    rows, hc = RW[hl], RW[hl]
                    pw, pwt = PS.next()
                    op("pe", "matmul", pw[:, 0:64], lhsT=F[3][rows, cs], rhs=ST[rows, :], start=True, stop=False,
                       reads=[ft[3][tp], t_st[hl]], writes=[pwt])
                    op("pe", "matmul", pw[:, 0:64], lhsT=MK[bf][hl][:, 0:128], rhs=VT[:, hc], start=False, stop=True,
                       reads=[t_mk[bf][hl], t_vt], writes=[pwt])
                    op("act", "copy", out=Wsb[hl], in_=pw[:, 0:64], reads=[pwt], writes=[t_w[hl]])
                    yield
                for hl in range(2):
                    pu, put = PS.next()
                    op("pe", "matmul", pu[:, 0:64], lhsT=inv[bf][hl][2], rhs=Wsb[hl], start=True, stop=True,
                       reads=[t_inv[bf][hl][2], t_w[hl]], writes=[put])
                    op("act", "mul", out=Usb[hl], in_=pu[:, 0:64], mul=-1.0, reads=[put], writes=[t_u[hl]])
                    yield
                for hl in range(2):
                    rows, hc = RW[hl], RW[hl]
                    py, pyt = PS.next()
                    op("pe", "matmul", py[0:64, 0:128], lhsT=ST[rows, :], rhs=F[4][rows, cs], start=True, stop=False,
                       reads=[t_st[hl], ft[4][tp]], writes=[pyt])
                    op("pe", "matmul", py[0:64, 0:128], lhsT=Usb[hl], rhs=MB[bf][hl][:, 128:256], start=False, stop=False,
                       reads=[t_u[hl], t_mb[bf][hl]], writes=[pyt])
                    op("pe", "matmul", py[0:64, 0:128], lhsT=VT[:, hc], rhs=MK[bf][hl][:, 128:256], start=False, stop=True,
                       reads=[t_vt, t_mk[bf][hl]], writes=[pyt])
                    op("act", "copy", out=YC[rows, :], in_=py[0:64, 0:128], reads=[pyt], writes=[t_yc])
                    pn, pnt_ = PS.next()
                    op("pe", "matmul", pn[0:64, 0:64], lhsT=BhT[:, hc], rhs=Usb[hl], start=True, stop=False,
                       reads=[t_bht, t_u[hl]], writes=[pnt_])
                    op("pe", "matmul", pn[0:64, 0:64], lhsT=KhT[:, hc], rhs=VT[:, hc], start=False, stop=True,
                       reads=[t_kht, t_vt], writes=[pnt_])
                    op("dve", "scalar_tensor_tensor", out=ST[rows, :], in0=ST[rows, :], scalar=GC[rows, c:c + 1], in1=pn[0:64, 0:64],
                       op0=ALU.mult, op1=ALU.add, reads=[t_st[hl], gc_t, pnt_], writes=[t_st[hl]])
                    yield
                pmu, pmut = PS.next()
                op("pe", "matmul", pmu[:, 0:128], lhsT=BLK64, rhs=YC, start=True, stop=True, reads=[t_cst, t_yc], writes=[pmut])
                op("act", "activation", out=G1[0], in_=YC, func=AF.Square, reads=[t_yc], writes=[t_g1[0]])
                yield
                pe2, pe2t = PS.next()
                op("pe", "matmul", pe2[:, 0:128], lhsT=BLK64, rhs=G1[0], start=True, stop=True, reads=[t_cst, t_g1[0]], writes=[pe2t])
                op("act", "copy", out=G1[1], in_=pmu[:, 0:128], reads=[pmut], writes=[t_g1[1]])
                yield
                op("dve", "tensor_tensor", out=G1[0], in0=G1[1], in1=G1[1], op=ALU.mult, reads=[t_g1[1]], writes=[t_g1[0]])
                op("dve", "scalar_tensor_tensor", out=G1[2], in0=pe2[:, 0:128], scalar=GN_EPS, in1=G1[0], op0=ALU.add, op1=ALU.subtract,
                   reads=[pe2t, t_g1[0]], writes=[t_g1[2]])
                yield
                op("act", "activation", out=G1[2], in_=G1[2], func=AF.Ln, reads=[t_g1[2]], writes=[t_g1[2]])
                op("act", "activation", out=G1[2], in_=G1[2], func=AF.Exp, scale=-0.5, reads=[t_g1[2]], writes=[t_g1[2]])
                yield
                op("dve", "tensor_tensor", out=G1[0], in0=YC, in1=G1[1], op=ALU.subtract, reads=[t_yc, t_g1[1]], writes=[t_g1[0]])
                op("dve", "tensor_tensor", out=G1[0], in0=G1[0], in1=G1[2], op=ALU.mult, reads=[t_g1[0], t_g1[2]], writes=[t_g1[0]])
                yield
                op("act", "activation", out=G1[0], in_=G1[0], func=AF.Identity, bias=ppc(("gnb", l, hp)), scale=ppc(("gng", l, hp)),
                   reads=[t_g1[0], t_pp], writes=[t_g1[0]])
                op("pool", "tensor_tensor", out=OT[:, oc, cs], in0=OTf[:, oc, cs], in1=G1[0], op=ALU.add,
                   reads=[ot_t[oc][tp], t_g1[0]], writes=[ot_t[oc][tp]])
                yield

            for _ in build(0):
                pass
            NCH = int(_os.environ.get("CSCAN", "16"))
            for c in range(NCH):
                gb = build(c + 1) if c + 1 < NCH else iter(())
                gs = seq(c)
                alive = [gb, gs]
                while alive:
                    for gsn in list(alive):
                        try:
                            next(gsn)
                        except StopIteration:
                            alive.remove(gsn)
            P.barrier()
            ft0 = [TT() for _ in range(4)]
            rawp_t2 = TT()
            w, wt = w1.next()
            dma("pool", "dma_start", out=w, in_=dr["wch"][l * NCHUNK_W + 24 + 7].bitcast(F32R).rearrange("p (kc f) -> p kc f", kc=8),
                writes=[wt])
            op("pool", "memset", RAWP[:, 0:1], 0.0, writes=[rawp_t2])
            mu = ppc(("mu", l, 7))
            ta2, tb2 = TT(), TT()
            for tp in range(4):
                pp_, ppt = PS.next()
                for kc in range(8):
                    op("pe", "matmul", pp_, lhsT=w[:, kc, :], rhs=XT[:, kc, tok(tp)], start=(kc == 0), stop=(kc == 7),
                       reads=[wt, xt_t[kc][tp]], writes=[ppt])
                op("act", "copy", out=RAWP[:, 1:513], in_=pp_, reads=[ppt], writes=[rawp_t2])
                op("dve", "tensor_tensor", out=TA, in0=RAWP[:, 0:512], in1=RAWP[:, 1:513], op=ALU.subtract, reads=[rawp_t2], writes=[ta2])
                op("dve", "scalar_tensor_tensor", out=TB, in0=TA, scalar=mu, in1=RAWP[:, 1:513], op0=ALU.mult, op1=ALU.add,
                   reads=[ta2, rawp_t2, t_pp], writes=[tb2])
                op("act", "copy", out=RAWP[:, 0:1], in_=RAWP[:, 512:513], reads=[rawp_t2], writes=[rawp_t2])
                op("act", "activation", out=TB, in_=TB, func=AF.Sigmoid, reads=[tb2], writes=[tb2])
                pg, pgt = PS.next()
                op("pe", "matmul", pg, lhsT=CSM[:, 256 + hp * 128:256 + hp * 128 + 128], rhs=TB, start=True, stop=True,
                   reads=[t_csm, tb2], writes=[pgt])
                op("dve", "tensor_tensor", out=OT[:, oc, tok(tp)], in0=pg, in1=OTf[:, oc, tok(tp)], op=ALU.mult,
                   reads=[pgt, ot_t[oc][tp]], writes=[ot_t[oc][tp]])
            P.end_phase(st)

        if stage in ("A", "mix", "full"):
            for i in range(3):
                phase_a(i)
        else:
            for c in range(0, 3):
                op("pool", "tensor_scalar", out=OT[:, c, :], in0=XF[:, c, :], scalar1=0.0, scalar2=None, op0=ALU.mult, reads=xt_t[c], writes=ot_t[c])
        if stage in ("B", "mix", "full"):
            for i in range(3):
                phase_b(i)
        else:
            for c in range(3, 6):
                op("pool", "tensor_scalar", out=OT[:, c, :], in0=XF[:, c, :], scalar1=0.0, scalar2=None, op0=ALU.mult, reads=xt_t[c], writes=ot_t[c])
        if stage in ("C", "mix", "full"):
            for hp in range(2):
                phase_c(hp)
        else:
            for c in range(6, 8):
                op("pool", "tensor_scalar", out=OT[:, c, :], in0=XF[:, c, :], scalar1=0.0, scalar2=None, op0=ALU.mult, reads=xt_t[c], writes=ot_t[c])
        return OT, ot_t, OTf

    def wout_ln(l, OT, ot_t, w_r):
        st = P.phase()
        WL = P.ph_sb(st, "WL", [128, 4096 + 1536 + 2048], F32)
        WR = P.ph_sb(st, "WR", [128, 1024 + 128 + 2048], F32R)
        sqr = Rot([WR[:, k * 512:(k + 1) * 512] for k in range(2)])
        onesr = WR[:, 1024:1152]
        onest = TT()
        op("act", "copy", out=onesr, in_=ONESD, reads=[t_cst], writes=[onest])
        sc = Rot([WL[:, k * 512:(k + 1) * 512] for k in range(8)])
        lnk = Rot([WL[:, 4096 + k * 512:4096 + (k + 1) * 512] for k in range(3)])
        wo_r = Rot([WR[:, 1152 + k * 1024:1152 + (k + 1) * 1024].rearrange("p (kc f) -> p kc f", kc=8) for k in range(2)])
        for dc in range(8):
            w, wt = wo_r.next()
            dma("pool", "dma_start", out=w, in_=dr["wch"][l * NCHUNK_W + 38 + dc].bitcast(F32R).rearrange("p (kc f) -> p kc f", kc=8),
                writes=[wt])
            for tp in range(4):
                py, pyt = PS.next()
                for kc in range(8):
                    op("pe", "matmul", py, lhsT=w[:, kc, :], rhs=OT[:, kc, tok(tp)], start=(kc == 0), stop=(kc == 7),
                       reads=[wt, ot_t[kc][tp]], writes=[pyt])
                op("dve", "scalar_tensor_tensor", out=XT[:, dc, tok(tp)], in0=py, scalar=1.0 / ALPHA, in1=XF[:, dc, tok(tp)],
                   op0=ALU.mult, op1=ALU.add, reads=[pyt, xt_t[dc][tp]], writes=[xt_t[dc][tp]])
        for tp in range(4):
            layer_norm(l, 1, tp, sc, lnk, sqr, onesr, onest)
        return st

    def dump_o(OTf, ot_t):
        for c in range(8):
            dma("sp", "dma_start", out=out_d[c * 128:(c + 1) * 128, :], in_=OTf[:, c, :], reads=ot_t[c])

    def dump_x():
        for c in range(8):
            dma("sp", "dma_start", out=out_d[c * 128:(c + 1) * 128, :], in_=XF[:, c, :], reads=xt_t[c])

    dumped = False
    stage = dbg[1] if dbg else "full"
    for l in range(depth):
        st = ffn_phase(l, 0)
        if dbg == ("x", "ffn_a") and l == depth - 1:
            dump_x()
            dumped = True
            P.end_phase(st)
            break
        P.end_phase(st)
        stm = P.phase()
        OTt = P.ph_sb(stm, "OT", [128, 16384], F32R)
        OT, ot_t, w_r = mixer_phase(l, stage)
        if dbg is not None and dbg[0] == "o" and l == depth - 1:
            dump_o(w_r, ot_t)
            dumped = True
            P.end_phase(stm)
            break
        st = wout_ln(l, OT, ot_t, w_r)
        if dbg == ("x", "mixln") and l == depth - 1:
            dump_x()
            dumped = True
            P.end_phase(st)
            P.end_phase(stm)
            break
        P.end_phase(st)
        P.end_phase(stm)
        st = ffn_phase(l, 1)
        if l == depth - 1:
            dump_x()
            dumped = True
        P.end_phase(st)
    P.barrier()
    P.finish()
    return nc


_CACHE = {}


def kernel(**inputs):
    sh, xs, idx = prep_inputs(inputs)
    shapes = {k: v.shape for k, v in sh.items()}
    nc = build(shapes, idx)
    n = len(xs)
    in_maps = []
    for b in range(n):
        m = dict(sh)
        m["xT"] = xs[b]
        in_maps.append(m)
    res = run_bass_kernel_spmd(nc, in_maps, core_ids=list(range(n)))
    out = np.stack([np.ascontiguousarray(res.results[b]["out"].T) for b in range(n)], axis=0)
    return out.astype(np.float32)
```

```python
import contextlib
import math
import numpy as np
import concourse.bass as bass
import concourse.mybir as mybir
from concourse.bass_utils import run_bass_kernel_spmd

F32 = mybir.dt.float32
F32R = mybir.dt.float32r
AF = mybir.ActivationFunctionType
ALU = mybir.AluOpType
AX = mybir.AxisListType

DEPTH = 4
D = 1024
S = 2048
FF = 2816
NFC = 22
ALPHA = (2 * DEPTH) ** 0.25
LN_EPS = 1e-5
RMS_EPS = 1e-5
GN_EPS = 64e-5
THETA = 10000.0
NCHUNK_W = 38 + 8
ARC = 31300
CFW = 14916 + 1024


class TT:
    __slots__ = ("w", "r")

    def __init__(self):
        self.w = None
        self.r = []


class Prog:
    ENG = ("pe", "act", "dve", "pool", "sp")

    def __init__(self, nc, same_sync=True, n_dma_sems=40):
        self.nc = nc
        self.same_sync = same_sync
        self.q = {e: [] for e in self.ENG}
        self.cnt = {e: 0 for e in self.ENG}
        self.known = {e: {} for e in self.ENG}
        self.stack = contextlib.ExitStack()
        self.sem = {}
        for e in self.ENG:
            self.sem[e] = self.stack.enter_context(nc.semaphore("s_" + e))
        self.dsem = []
        self.dval = []
        for i in range(n_dma_sems):
            self.dsem.append(self.stack.enter_context(nc.semaphore("d%d" % i)))
            self.dval.append(0)
        self.ndma = 0

    def sb(self, name, shape, dt=F32):
        return self.stack.enter_context(self.nc.sbuf_tensor(name, shape, dt))

    def ps(self, name, shape, dt=F32):
        return self.stack.enter_context(self.nc.psum_tensor(name, shape, dt))

    def _waits(self, eng, reads, writes):
        deps = {}
        for t in reads:
            if t.w is not None:
                k, v = t.w
                if deps.get(k, 0) < v:
                    deps[k] = v
        for t in writes:
            if t.w is not None:
                k, v = t.w
                if deps.get(k, 0) < v:
                    deps[k] = v
            for (k, v) in t.r:
                if deps.get(k, 0) < v:
                    deps[k] = v
        out = []
        kn = self.known[eng]
        for k, v in deps.items():
            if k == eng and (eng == "pe" or not self.same_sync):
                continue
            if kn.get(k, 0) >= v:
                continue
            kn[k] = v
            out.append((k, v))
        return out

    def _semh(self, k):
        return self.sem[k] if isinstance(k, str) else self.dsem[k]

    def _mark(self, tok, reads, writes):
        for t in reads:
            t.r.append(tok)
            if len(t.r) > 24:
                best = {}
                for k, v in t.r:
                    if best.get(k, 0) < v:
                        best[k] = v
                t.r = list(best.items())
        for t in writes:
            t.w = tok
            t.r = []

    def op(self, eng, fn, *args, reads=(), writes=(), **kwargs):
        waits = self._waits(eng, reads, writes)
        self.cnt[eng] += 1
        self._mark((eng, self.cnt[eng]), reads, writes)
        sem = self.sem[eng]
        wl = [(self._semh(k), v) for k, v in waits]

        def emit(e, fn=fn, wl=wl, sem=sem, args=args, kwargs=kwargs):
            for s, v in wl:
                e.wait_ge(s, v)
            getattr(e, fn)(*args, **kwargs).then_inc(sem, 1)

        self.q[eng].append(emit)

    def dma(self, eng, fn, *args, reads=(), writes=(), di=None, **kwargs):
        if di is None:
            di = self.ndma % len(self.dsem)
            self.ndma += 1
        waits = self._waits(eng, reads, writes)
        self.dval[di] += 16
        self._mark((di, self.dval[di]), reads, writes)
        sem = self.dsem[di]
        wl = [(self._semh(k), v) for k, v in waits]

        def emit(e, fn=fn, wl=wl, sem=sem, args=args, kwargs=kwargs):
            for s, v in wl:
                e.wait_ge(s, v)
            getattr(e, fn)(*args, **kwargs).then_inc(sem, 16)

        self.q[eng].append(emit)

    def barrier(self):
        for e in self.ENG:
            wl = []
            kn = self.known[e]
            for k in self.ENG:
                if k != e and self.cnt[k] > kn.get(k, 0):
                    kn[k] = self.cnt[k]
                    wl.append((self.sem[k], self.cnt[k]))
            for i, v in enumerate(self.dval):
                if v > kn.get(i, 0):
                    kn[i] = v
                    wl.append((self.dsem[i], v))
            if self.same_sync and e != "pe" and self.cnt[e] > kn.get(e, 0):
                kn[e] = self.cnt[e]
                wl.append((self.sem[e], self.cnt[e]))

            def emit(en, wl=wl):
                for s, v in wl:
                    en.wait_ge(s, v)

            self.q[e].append(emit)

    def phase(self):
        return contextlib.ExitStack()

    def ph_sb(self, st, name, shape, dt=F32):
        self.uid = getattr(self, "uid", 0) + 1
        return st.enter_context(self.nc.sbuf_tensor("%s_%d" % (name, self.uid), shape, dt))

    def end_phase(self, st):
        self.barrier()
        self.flush()
        st.close()

    def finish(self):
        self.flush()
        self.stack.close()

    def flush(self):
        nc = self.nc
        q = self.q
        self.q = {e: [] for e in self.ENG}
        with nc.Block() as block:
            @block.tensor
            def _(e):
                for f in q["pe"]:
                    f(e)

            @block.scalar
            def _(e):
                for f in q["act"]:
                    f(e)

            @block.vector
            def _(e):
                for f in q["dve"]:
                    f(e)

            @block.gpsimd
            def _(e):
                for f in q["pool"]:
                    f(e)

            @block.sync
            def _(e):
                for f in q["sp"]:
                    f(e)


class Rot:
    def __init__(self, views, tts=None):
        self.v = views
        self.t = tts if tts is not None else [TT() for _ in views]
        self.i = 0

    def next(self):
        i = self.i % len(self.v)
        self.i += 1
        return self.v[i], self.t[i]


def _chunk(W, cols):
    return np.ascontiguousarray(W[:, cols].reshape(8, 128, len(cols)).transpose(1, 0, 2))


def _swap_idx(base, n, grp):
    idx = np.arange(n)
    half = grp // 2
    return base + (idx // grp) * grp + ((idx % grp) + half) % grp


def _rope_table(grp):
    half = grp // 2
    inv = (THETA ** (-np.arange(0, grp, 2, dtype=np.float32) / grp)).astype(np.float32)
    pos = np.arange(S, dtype=np.float32)
    ang = (pos[:, None] * inv[None, :]).astype(np.float32)
    cos = np.cos(ang).astype(np.float32).T
    sin = np.sin(ang).astype(np.float32).T
    r = np.arange(128)
    i = r % half
    sign = np.where((r % grp) < half, -1.0, 1.0).astype(np.float32)
    C = cos[i]
    Sg = sin[i] * sign[:, None]
    t = np.stack([C, Sg], axis=1)
    t = t.reshape(128, 2, 4, 512).transpose(0, 2, 1, 3)
    return np.ascontiguousarray(t.reshape(128, 4 * 2 * 512)).astype(np.float32)


def _consts():
    c = {}
    k = np.arange(128)[:, None]
    q = np.arange(256)[None, :]
    c["causal"] = (q[:, :128] >= k).astype(np.float32)
    c["band"] = ((q >= k) & (q <= k + 128)).astype(np.float32)
    su = (q[:, :128] > k).astype(np.float32)
    iu = (q[:, :128] >= k).astype(np.float32)
    c["uppers"] = np.concatenate([su, iu], axis=1)
    c["ident"] = np.eye(128, dtype=np.float32)
    blk = np.zeros((128, 128), np.float32)
    blk[:64, :64] = 1.0
    blk[64:, 64:] = 1.0
    c["blk"] = blk
    c["lowers"] = (q[:, :128] < k).astype(np.float32)
    return c


def prep_inputs(inp):
    L = DEPTH
    f = lambda a: np.asarray(a, dtype=np.float32)
    sh = {}
    wgu = np.empty((L, 2, 22, 128, 2, 8, 128), np.float32)
    wd = np.empty((L, 2, 8, 128, 22, 128), np.float32)
    ffw = {"a": (inp["ffn_a_gate"], inp["ffn_a_up"], inp["ffn_a_down"]),
           "b": (inp["ffn_b_gate"], inp["ffn_b_up"], inp["ffn_b_down"])}
    for l in range(L):
        for i, nm in enumerate("ab"):
            g = f(ffw[nm][0][l]).reshape(8, 128, 22, 128).transpose(2, 1, 0, 3)
            u = f(ffw[nm][1][l]).reshape(8, 128, 22, 128).transpose(2, 1, 0, 3)
            wgu[l, i, :, :, 0] = g
            wgu[l, i, :, :, 1] = u
            wd[l, i] = f(ffw[nm][2][l]).reshape(22, 128, 8, 128).transpose(2, 1, 0, 3)
    sh["wgu"] = wgu.reshape(L * 2 * 22, 128, 2048)
    sh["wd"] = wd.reshape(L * 2 * 8, 128, 2816)
    wch = np.empty((L, NCHUNK_W, 128, 8, 128), np.float32)
    for l in range(L):
        W = f(inp["w_in"][l])
        ci = 0
        for base, grp in ((0, 32), (1152, 64)):
            for i in range(3):
                qc = base + 128 * i + np.arange(128)
                kc_ = base + 384 + 128 * i + np.arange(128)
                wch[l, ci + 0] = _chunk(W, qc)
                wch[l, ci + 1] = _chunk(W, _swap_idx(base + 128 * i, 128, grp))
                wch[l, ci + 2] = _chunk(W, kc_)
                wch[l, ci + 3] = _chunk(W, _swap_idx(base + 384 + 128 * i, 128, grp))
                ci += 4
        for c in range(8):
            wch[l, 24 + c] = _chunk(W, 2304 + 128 * c + np.arange(128))
        for i in range(3):
            wch[l, 32 + i] = _chunk(W, 768 + 128 * i + np.arange(128))
            wch[l, 35 + i] = _chunk(W, 1920 + 128 * i + np.arange(128))
        Wo = f(inp["w_out"][l])
        for dc in range(8):
            wch[l, 38 + dc] = _chunk(Wo, 128 * dc + np.arange(128))
    sh["wch"] = wch.reshape(L * NCHUNK_W, 128, 1024)
    sh["ropeA"] = _rope_table(32)
    sh["ropeB"] = _rope_table(64)
    for k_, v_ in _consts().items():
        sh["c_" + k_] = v_
    cols = []

    def addcol(v):
        cols.append(np.asarray(v, np.float32).reshape(128, 1))
        return len(cols) - 1

    idx = {}
    for l in range(L):
        for i in range(3):
            for c in range(8):
                idx[("lng", l, i, c)] = addcol(f(inp["ln_g"][l, i, c * 128:(c + 1) * 128]))
                idx[("lnb", l, i, c)] = addcol(f(inp["ln_b"][l, i, c * 128:(c + 1) * 128]))
        idx[("ga", l)] = addcol(np.tile(f(inp["a_norm_g"][l]), 2))
        idx[("gb", l)] = addcol(np.tile(f(inp["b_norm_g"][l]), 2))
        for c in range(8):
            idx[("mu", l, c)] = addcol(f(inp["c_mu"][l, c * 128:(c + 1) * 128]))
        for hp in range(2):
            sl = slice(hp * 128, hp * 128 + 128)
            idx[("w0", l, hp)] = addcol(f(inp["c_w0"][l, sl]))
            idx[("a0", l, hp)] = addcol(f(inp["c_a0"][l, sl]))
            idx[("kk", l, hp)] = addcol(f(inp["c_k_k"][l, sl]))
            idx[("ka", l, hp)] = addcol(f(inp["c_k_a"][l, sl]))
            idx[("rk", l, hp)] = addcol(f(inp["c_r_k"][l].reshape(256)[sl]))
            idx[("gng", l, hp)] = addcol(f(inp["c_gn_g"][l, sl]))
            idx[("gnb", l, hp)] = addcol(f(inp["c_gn_b"][l, sl]))
            if l > 0:
                idx[("v0", l, hp)] = addcol(f(inp["c_v0"][l - 1, sl]))
    sh["pp"] = np.ascontiguousarray(np.concatenate(cols, axis=1))
    lamrow = np.stack([np.stack([f(inp["a_lam_q1"][l]), f(inp["a_lam_k1"][l]),
                                 f(inp["a_lam_q2"][l]), f(inp["a_lam_k2"][l])]) for l in range(L)])
    sh["lamrow"] = np.ascontiguousarray(lamrow.reshape(1, L * 4 * 32))
    sm = np.zeros((L, 128, 256 + 256 + 64 + 256), np.float32)
    for l in range(L):
        sm[l, 0:64, 0:256] = f(inp["c_w2"][l])
        sm[l, 64:128, 0:256] = f(inp["c_a2"][l])
        sm[l, :, 256:512] = f(inp["c_g2"][l])
        if l > 0:
            sm[l, :, 512:576] = f(inp["c_v1"][l - 1]).reshape(2, 128, 32).transpose(1, 0, 2).reshape(128, 64)
            sm[l, 0:32, 576:832] = f(inp["c_v2"][l - 1])
    sh["csm"] = np.ascontiguousarray(sm.transpose(1, 0, 2).reshape(128, L * 832))
    xs = [np.ascontiguousarray(f(inp["x"][b]).T) for b in range(inp["x"].shape[0])]
    return sh, xs, idx


def build(shapes, idx, depth=DEPTH, dbg=None):
    nc = bass.Bass("TRN2", target_bir_lowering=False)
    dr = {}
    for k, shp in shapes.items():
        dr[k] = nc.dram_tensor(k, list(shp), F32, kind="ExternalInput").ap()
    xT_d = nc.dram_tensor("xT", [D, S], F32, kind="ExternalInput").ap()
    out_d = nc.dram_tensor("out", [D, S], F32, kind="ExternalOutput").ap()
    vf_d = nc.dram_tensor("vf_scratch", [256, S], F32, kind="Internal").ap()
    npp = shapes["pp"][1]

    import os as _os
    P = Prog(nc, same_sync=(_os.environ.get('NOSAME') is None))
    op, dma = P.op, P.dma
    XT = P.sb("XT", [128, 8, S], F32R)
    XF = XT[:].bitcast(F32)
    AR = None
    OTt = None
    CST = P.sb("CST", [128, 1472])
    PP = P.sb("PP", [128, npp])
    CSM = P.sb("CSM", [128, 832])
    LAM = P.sb("LAM", [128, 64 + DEPTH * 128 + 64])
    banks = [P.ps("pb%d" % i, [128, 512]) for i in range(8)]
    bank_t = [TT() for _ in range(8)]
    PS = Rot([b[:] for b in banks[0:6]], bank_t[0:6])
    PSL = Rot([b[:] for b in banks[6:8]], bank_t[6:8])
    PSA = Rot([b[:] for b in banks[0:4]], bank_t[0:4])
    PSLA = Rot([b[:] for b in banks[4:8]], bank_t[4:8])

    xt_t = [[TT() for _ in range(4)] for _ in range(8)]
    vf_t = [TT(), TT()]
    t_cst, t_pp, t_csm, t_lam, t_rst = TT(), TT(), TT(), TT(), TT()
    CAUS = CST[:, 0:128]
    BAND = CST[:, 128:384]
    UPP = CST[:, 384:640]
    IDENT = CST[:, 640:768]
    BLK = CST[:, 768:896]
    ONESD = CST[:, 896:1024]
    ONES64 = CST[:, 1024:1088]
    ONES = CST[:, 1088:1216]
    LOWS = CST[:, 1216:1344]
    BLK64 = CST[:, 1344:1472]
    dma("sp", "dma_start", out=CAUS, in_=dr["c_causal"], writes=[t_cst])
    dma("sp", "dma_start", out=BAND, in_=dr["c_band"], writes=[t_cst])
    dma("sp", "dma_start", out=UPP, in_=dr["c_uppers"], writes=[t_cst])
    dma("sp", "dma_start", out=IDENT, in_=dr["c_ident"], writes=[t_cst])
    dma("sp", "dma_start", out=BLK, in_=dr["c_blk"], writes=[t_cst])
    dma("sp", "dma_start", out=PP[:], in_=dr["pp"], writes=[t_pp])
    op("dve", "memset", ONESD, 1.0 / 1024.0, writes=[t_cst])
    op("dve", "memset", ONES64, 1.0 / 64.0, writes=[t_cst])
    op("dve", "memset", ONES, 1.0, writes=[t_cst])
    dma("sp", "dma_start", out=LOWS, in_=dr["c_lowers"], writes=[t_cst])
    op("act", "mul", out=BLK64, in_=BLK, mul=1.0 / 64.0, reads=[t_cst], writes=[t_cst])
    for c in range(8):
        dma("pool", "dma_start", out=XT[:, c, :], in_=xT_d[c * 128:(c + 1) * 128, :].bitcast(F32R),
            writes=xt_t[c])
    ONE1 = LAM[:, 0:64]
    op("dve", "memset", LAM[:], 0.0, writes=[t_lam])
    op("dve", "memset", ONE1, 1.0, writes=[t_lam])
    LROW = LAM[0:1, 64:64 + DEPTH * 128]
    dma("sp", "dma_start", out=LROW, in_=dr["lamrow"], writes=[t_lam])
    NEGLAM = LAM[:, 64 + DEPTH * 128:64 + DEPTH * 128 + 8]
    LS = LAM[0:1, 64 + DEPTH * 128 + 8:64 + DEPTH * 128 + 64]
    for l in range(depth):
        b0 = 64 + l * 128
        lam_init = 0.8 - 0.6 * math.exp(-0.3 * l)
        for j in range(2):
            op("dve", "tensor_tensor", out=LS[:, 0:32], in0=LAM[0:1, b0 + 64 * j:b0 + 64 * j + 32],
                                                          in1=LAM[0:1, b0 + 64 * j + 32:b0 + 64 * j + 64], op=ALU.mult,
               reads=[t_lam], writes=[t_lam])
            op("dve", "reduce_sum", out=LS[:, 32 + j:33 + j], in_=LS[:, 0:32], axis=AX.X,
               reads=[t_lam], writes=[t_lam])
        op("act", "activation", out=LS[:, 34:36], in_=LS[:, 32:34], func=AF.Exp, reads=[t_lam], writes=[t_lam])
        op("dve", "scalar_tensor_tensor", out=LS[:, 36:37], in0=LS[:, 35:36], scalar=-lam_init, in1=LS[:, 34:35],
                                                                op0=ALU.add, op1=ALU.subtract, reads=[t_lam], writes=[t_lam])
        pb, pt = PS.next()
        op("pe", "matmul", pb[0:64, 0:1], lhsT=LAM[0:1, 0:64], rhs=LS[:, 36:37], start=True, stop=True,
           reads=[t_lam], writes=[pt])
        op("act", "copy", out=NEGLAM[0:64, l:l + 1], in_=pb[0:64, 0:1], reads=[pt], writes=[t_lam])

    def ppc(key):
        j = idx[key]
        return PP[:, j:j + 1]

    def tok(tp):
        return slice(tp * 512, (tp + 1) * 512)

    def ssl(st, n, step):
        return slice(st, st + step * (n - 1) + 1, step) if step > 1 else slice(st, st + n)

    def layer_norm(l, i, tp, sc, lnk, sqr=None, onesr=None, onest=None):
        eps = LN_EPS / (ALPHA * ALPHA)
        pm, pmt = PS.next()
        pe2, pe2t = PS.next()
        for c in range(8):
            if onesr is not None:
                op("pe", "matmul", pm, lhsT=onesr, rhs=XT[:, c, tok(tp)], start=(c == 0), stop=(c == 7),
                   reads=[onest, xt_t[c][tp]], writes=[pmt])
            else:
                op("pe", "matmul", pm, lhsT=ONESD, rhs=XF[:, c, tok(tp)], start=(c == 0), stop=(c == 7),
                   reads=[t_cst, xt_t[c][tp]], writes=[pmt])
        for c in range(8):
            sq, sqt = (sqr if sqr is not None else sc).next()
            op("act", "activation", out=sq, in_=XF[:, c, tok(tp)], func=AF.Square,
               reads=[xt_t[c][tp]], writes=[sqt])
            if onesr is not None:
                op("pe", "matmul", pe2, lhsT=onesr, rhs=sq, start=(c == 0), stop=(c == 7),
                   reads=[onest, sqt], writes=[pe2t])
            else:
                op("pe", "matmul", pe2, lhsT=ONESD, rhs=sq, start=(c == 0), stop=(c == 7),
                   reads=[t_cst, sqt], writes=[pe2t])
        mean, meant = lnk.next()
        op("act", "copy", out=mean, in_=pm, reads=[pmt], writes=[meant])
        msq, msqt = lnk.next()
        op("dve", "tensor_tensor", out=msq, in0=mean, in1=mean, op=ALU.mult, reads=[meant], writes=[msqt])
        var, vart = lnk.next()
        op("dve", "scalar_tensor_tensor", out=var, in0=pe2, scalar=eps, in1=msq, op0=ALU.add, op1=ALU.subtract,
           reads=[pe2t, msqt], writes=[vart])
        op("act", "activation", out=var, in_=var, func=AF.Ln, reads=[vart], writes=[vart])
        op("act", "activation", out=var, in_=var, func=AF.Exp, scale=-0.5, reads=[vart], writes=[vart])
        for c in range(8):
            t1, t1t = sc.next()
            op("dve", "tensor_tensor", out=t1, in0=XF[:, c, tok(tp)], in1=mean, op=ALU.subtract,
               reads=[xt_t[c][tp], meant], writes=[t1t])
            op("dve", "tensor_tensor", out=t1, in0=t1, in1=var, op=ALU.mult,
               reads=[t1t, vart], writes=[t1t])
            op("act", "activation", out=XT[:, c, tok(tp)], in_=t1, func=AF.Identity,
                                                        bias=ppc(("lnb", l, i, c)), scale=ppc(("lng", l, i, c)),
               reads=[t1t, t_pp], writes=[xt_t[c][tp]])

    def ffn_phase(l, i):
        st = P.phase()
        FR = P.ph_sb(st, "FR", [128, 11264 + 4 * 2048 + 2 * 2816 + 1024 + 128], F32R)
        FT = P.ph_sb(st, "FT", [128, 11 * 512], F32)
        o = 0
        Hv = FR[:, o:o + 22 * 512].rearrange("p (f t) -> p f t", t=512)
        o += 22 * 512
        h_t = [TT() for _ in range(22)]
        wgu_r = Rot([FR[:, o + k * 2048:o + (k + 1) * 2048].rearrange("p (g kc f) -> p g kc f", g=2, kc=8) for k in range(4)])
        o += 4 * 2048
        wd_r = Rot([FR[:, o + k * 2816:o + (k + 1) * 2816].rearrange("p (f d) -> p f d", d=128) for k in range(2)])
        o += 2 * 2816
        sqr = Rot([FR[:, o + k * 512:o + (k + 1) * 512] for k in range(2)])
        o += 1024
        onesr = FR[:, o:o + 128]
        onest = TT()
        op("act", "copy", out=onesr, in_=ONESD, reads=[t_cst], writes=[onest])
        sc = Rot([FT[:, k * 512:(k + 1) * 512] for k in range(8)])
        lnk = Rot([FT[:, (8 + k) * 512:(9 + k) * 512] for k in range(3)])
        lnidx = 0 if i == 0 else 2
        def p1(tp):
            for f in range(22):
                wb, wbt = wgu_r.next()
                dma("pool", "dma_start", out=wb,
                    in_=dr["wgu"][(l * 2 + i) * 22 + f].bitcast(F32R).rearrange("p (g kc f) -> p g kc f", g=2, kc=8), writes=[wbt])
                pg, pgt = PS.next()
                pu, put = PS.next()
                for g, pp_, ppt in ((0, pg, pgt), (1, pu, put)):
                    for kc in range(8):
                        op("pe", "matmul", pp_, lhsT=wb[:, g, kc, :], rhs=XT[:, kc, tok(tp)], start=(kc == 0), stop=(kc == 7),
                           reads=[wbt, xt_t[kc][tp]], writes=[ppt])
                sg, sgt = sc.next()
                op("act", "activation", out=sg, in_=pg, func=AF.Silu, reads=[pgt], writes=[sgt])
                op("dve", "tensor_tensor", out=Hv[:, f, :], in0=sg, in1=pu, op=ALU.mult, reads=[sgt, put], writes=[h_t[f]])

        def p2(tp):
            for dc in range(8):
                wb, wbt = wd_r.next()
                dma("pool", "dma_start", out=wb,
                    in_=dr["wd"][(l * 2 + i) * 8 + dc].bitcast(F32R).rearrange("p (f d) -> p f d", d=128), writes=[wbt])
                py, pyt = PS.next()
                for f in range(22):
                    op("pe", "matmul", py, lhsT=wb[:, f, :], rhs=Hv[:, f, :], start=(f == 0), stop=(f == 21),
                       reads=[wbt, h_t[f]], writes=[pyt])
                op("dve", "scalar_tensor_tensor", out=XT[:, dc, tok(tp)], in0=py, scalar=0.5 / ALPHA, in1=XF[:, dc, tok(tp)],
                   op0=ALU.mult, op1=ALU.add, reads=[pyt, xt_t[dc][tp]], writes=[xt_t[dc][tp]])

        for tp in range(4):
            p1(tp)
            if tp > 0:
                layer_norm(l, lnidx, tp - 1, sc, lnk, sqr, onesr, onest)
            p2(tp)
        layer_norm(l, lnidx, 3, sc, lnk, sqr, onesr, onest)
        return st

    def mixer_phase(l, stage):
        OT = OTt[:].rearrange("p (c t) -> p c t", t=S)
        OTf = OTt[:].bitcast(F32).rearrange("p (c t) -> p c t", t=S)
        ot_t = [[TT() for _ in range(4)] for _ in range(8)]
        QT = KT = KZ = VA = RS = VTF = vtf_t = EREG = PX0 = PX2 = ONER = ONE64R = pxt = ETT = None
        w_r = rope_r = e512 = e256 = scr = None
        SCR0 = None
        qt_t = kt_t = None
        va_t = kz_t = None

        def setup_ab(is_a):
            nonlocal QT, KT, KZ, VA, RS, w_r, rope_r, e512, e256, scr, SCR0, qt_t, kt_t, va_t, kz_t, VTF, vtf_t, EREG, PX0, PX2, ONER, ONE64R, pxt, ETT
            st = P.phase()
            AQ = P.ph_sb(st, "AQ", [128, 11808 + 1280 if is_a else 9760 + 768], F32R)
            ABp = P.ph_sb(st, "ABp", [128, 3072 if is_a else 5120], F32)
            o = 0
            QT = AQ[:, o:o + 2048]
            KT = AQ[:, o + 2048:o + 4096]
            o += 4096
            qt_t = [TT() for _ in range(4)]
            kt_t = [TT() for _ in range(4)]
            VA = AQ[:, o:o + 2080].rearrange("p (j h d) -> p j h d", h=2, d=65)
            va_t = TT()
            o += 2080
            w_r = Rot([AQ[:, o + k * 1024:o + (k + 1) * 1024].rearrange("p (kc f) -> p kc f", kc=8) for k in range(2)])
            VTF = AQ[:, o:o + 2048]
            vtf_t = TT()
            o += 2048
            EREG = AQ[:, o:o + 1024]
            e512 = Rot([AQ[:, o + k * 512:o + (k + 1) * 512] for k in range(3)])
            e256 = Rot([AQ[:, o + k * 256:o + (k + 1) * 256] for k in range(6 if is_a else 9)])
            o += 1536 if is_a else 2304
            ETT = e512.t[0:2] if is_a else e256.t[0:4]
            if is_a:
                KZ = AQ[:, o:o + 2048]
                kz_t = TT()
                o += 2048
                PX0 = AQ[:, o:o + 512]
                PX2 = AQ[:, o + 512:o + 1024]
                ONER = AQ[:, o + 1024:o + 1152]
                ONE64R = AQ[:, o + 1152:o + 1280]
                pxt = [TT(), TT(), TT()]
                op("act", "copy", out=ONER, in_=ONES, reads=[t_cst], writes=[pxt[2]])
                op("act", "mul", out=ONE64R, in_=ONES, mul=1.0 / 64.0, reads=[t_cst], writes=[pxt[2]])
                o += 1280
            rope_r = Rot([ABp[:, 0:1024].rearrange("p (c t) -> p c t", c=2)])
            scr = Rot([ABp[:, 1024 + k * 512:1024 + (k + 1) * 512] for k in range(4)])
            SCR0 = ABp
            if not is_a:
                RS = ABp[:, 3072:5120]
            return st

        lam_init = 0.8 - 0.6 * math.exp(-0.3 * l)

        def load_w(ci):
            w, wt = w_r.next()
            dma("pool", "dma_start", out=w, in_=dr["wch"][l * NCHUNK_W + ci].bitcast(F32R).rearrange("p (kc f) -> p kc f", kc=8),
                writes=[wt])
            return w, wt

        def proj_rope(ci, dst, dst_t, rname):
            w1, w1t = load_w(ci)
            w2, w2t = load_w(ci + 1)
            for tp in range(4):
                rp, rpt = rope_r.next()
                dma("sp", "dma_start", out=rp, in_=dr[rname][:, tp * 1024:(tp + 1) * 1024].rearrange("p (c t) -> p c t", c=2),
                    writes=[rpt])
                p1, p1t = PS.next()
                p2, p2t = PS.next()
                for w, wt, pp_, ppt in ((w1, w1t, p1, p1t), (w2, w2t, p2, p2t)):
                    for kc in range(8):
                        op("pe", "matmul", pp_, lhsT=w[:, kc, :], rhs=XT[:, kc, tok(tp)], start=(kc == 0), stop=(kc == 7),
                           reads=[wt, xt_t[kc][tp]], writes=[ppt])
                a, at = scr.next()
                b, bt = scr.next()
                op("dve", "tensor_tensor", out=a, in0=p1, in1=rp[:, 0, :], op=ALU.mult, reads=[p1t, rpt], writes=[at])
                op("dve", "tensor_tensor", out=b, in0=p2, in1=rp[:, 1, :], op=ALU.mult, reads=[p2t, rpt], writes=[bt])
                op("pool", "tensor_tensor", out=dst[:, tok(tp)], in0=a, in1=b, op=ALU.add, reads=[at, bt], writes=[dst_t[tp]])

        def proj_vfm(ci):
            wv = EREG.rearrange("p (kc f) -> p kc f", kc=8)
            dma("pool", "dma_start", out=wv, in_=dr["wch"][l * NCHUNK_W + ci].bitcast(F32R).rearrange("p (kc f) -> p kc f", kc=8),
                writes=list(ETT))
            for tp in range(4):
                pv, pvt = PS.next()
                for kc in range(8):
                    op("pe", "matmul", pv, lhsT=wv[:, kc, :], rhs=XT[:, kc, tok(tp)], start=(kc == 0), stop=(kc == 7),
                       reads=list(ETT) + [xt_t[kc][tp]], writes=[pvt])
                if tp % 2 == 0:
                    op("act", "copy", out=VTF[:, tok(tp)], in_=pv, reads=[pvt], writes=[vtf_t] + list(w_r.t))
                else:
                    op("dve", "tensor_copy", out=VTF[:, tok(tp)], in_=pv, reads=[pvt], writes=[vtf_t] + list(w_r.t))

        def proj_v(ci, dil):
            op("act", "copy", out=VA[:, :, :, 64:65], in_=ONES[:, 0:32].rearrange("p (j h d) -> p j h d", h=2, d=1),
               reads=[t_cst], writes=[va_t])
            nt = 16 // dil
            VTFf = VTF.bitcast(F32)
            k = 0
            for r in range(dil):
                for a in range(nt):
                    st = r + dil * 128 * a
                    pv, pvt = PS.next()
                    op("pe", "transpose", pv[:, 0:128], VTFf[:, ssl(st, 128, dil)], IDENT, reads=[vtf_t, t_cst], writes=[pvt])
                    if k % 2 == 0:
                        op("act", "copy", out=VA[:, r * nt + a, :, 0:64], in_=pv[:, 0:128].rearrange("p (h d) -> p h d", h=2),
                           reads=[pvt], writes=[va_t])
                    else:
                        op("dve", "tensor_copy", out=VA[:, r * nt + a, :, 0:64], in_=pv[:, 0:128].rearrange("p (h d) -> p h d", h=2),
                           reads=[pvt], writes=[va_t])
                    k += 1

        def phase_a(i):
            st = setup_ab(True)
            proj_rope(4 * i, QT, qt_t, "ropeA")
            proj_rope(4 * i + 2, KT, kt_t, "ropeA")
            proj_vfm(32 + i)
            proj_v(32 + i, 1)
            op("pool", "tensor_copy", out=KZ[96:128, :], in_=KT[96:128, :].bitcast(F32), reads=kt_t, writes=[kz_t])
            op("pool", "tensor_scalar", out=KZ[64:96, :], in0=KT[64:96, :].bitcast(F32), scalar1=0.0, scalar2=None, op0=ALU.mult,
               reads=kt_t, writes=[kz_t])
            sca = 32 ** -0.5
            bg = []

            def step_bg():
                if bg:
                    try:
                        next(bg[0])
                    except StopIteration:
                        bg.pop(0)

            def drain_bg(keep=0):
                while len(bg) > keep:
                    step_bg()

            def main_block(hl, Qp, nums):
                for m in range(2):
                    r0 = 32 * (2 * hl + m)
                    num, numt = PSLA.next()
                    nums.append((num, numt))
                    last = 4 * Qp + 3
                    pend = []

                    def do_pv(item, num=num, numt=numt, last=last):
                        j, E, Et, c0, n = item
                        op("pe", "matmul", num[0:65, c0 - 512 * Qp:512], lhsT=VA[:, j, hl, :], rhs=E[:, 0:n],
                           start=(j == 0), stop=(j == last), reads=[va_t, Et], writes=[numt])

                    for j in range(last + 1):
                        c0 = max(128 * j, 512 * Qp)
                        n = 512 * Qp + 512 - c0
                        se, set_ = PSA.next()
                        if hl == 1 and m == 1:
                            op("pe", "matmul", se[:, 0:n], lhsT=KZ[64:128, 128 * j:128 * j + 128], rhs=QT[64:128, c0:c0 + n],
                               start=True, stop=True, reads=[kz_t] + qt_t[c0 // 512:Qp + 1], writes=[set_])
                        else:
                            op("pe", "matmul", se[:, 0:n], lhsT=KT[r0:r0 + 32, 128 * j:128 * j + 128], rhs=QT[r0:r0 + 32, c0:c0 + n],
                               start=True, stop=True, reads=[kt_t[j // 4]] + qt_t[c0 // 512:Qp + 1], writes=[set_])
                        E, Et = e512.next()
                        op("act", "activation", out=E[:, 0:n], in_=se[:, 0:n], func=AF.Exp, scale=sca, reads=[set_], writes=[Et])
                        if 128 * j >= 512 * Qp:
                            op("pool", "tensor_tensor", out=E[:, 0:128], in0=E[:, 0:128].bitcast(F32), in1=CAUS, op=ALU.mult,
                               reads=[Et, t_cst], writes=[Et])
                        pend.append((j, E, Et, c0, n))
                        if len(pend) > 2:
                            do_pv(pend.pop(0))
                        yield
                    while pend:
                        do_pv(pend.pop(0))
                        yield

            def post_block(hl, Qp, nums):
                (n1, n1t), (n2, n2t) = nums
                X1, X1t = scr.next()
                X2, X2t = scr.next()
                X0t = pxt[0]
                for (nn, nnt, prt) in ((n1, n1t, 64), (n2, n2t, 32)):
                    op("act", "activation", out=PX0[prt:prt + 1, :], in_=nn[64:65, :], func=AF.Ln, reads=[nnt], writes=[X0t])
                    yield
                    op("act", "activation", out=PX0[prt:prt + 1, :], in_=PX0[prt:prt + 1, :].bitcast(F32), func=AF.Exp, scale=-1.0,
                       reads=[X0t], writes=[X0t])
                    yield
                for (prt, Xd, Xdt) in ((64, X1, X1t), (32, X2, X2t)):
                    pb, pbt = PSA.next()
                    op("pe", "matmul", pb, lhsT=ONER[prt:prt + 1, :], rhs=PX0[prt:prt + 1, :], start=True, stop=True,
                       reads=[pxt[2], X0t], writes=[pbt])
                    yield
                    op("act", "copy", out=Xd[0:64, :], in_=pb[0:64, :], reads=[pbt], writes=[Xdt])
                    yield
                op("dve", "tensor_tensor", out=X1[0:64, :], in0=n1[0:64, :], in1=X1[0:64, :], op=ALU.mult, reads=[n1t, X1t], writes=[X1t])
                yield
                op("dve", "tensor_tensor", out=X2[0:64, :], in0=n2[0:64, :], in1=X2[0:64, :], op=ALU.mult, reads=[n2t, X2t], writes=[X2t])
                yield
                op("dve", "scalar_tensor_tensor", out=X1[0:64, :], in0=X2[0:64, :], scalar=NEGLAM[0:64, l:l + 1], in1=X1[0:64, :],
                   op0=ALU.mult, op1=ALU.add, reads=[X1t, X2t, t_lam], writes=[X1t])
                yield
                op("act", "activation", out=PX2[0:64, :], in_=X1[0:64, :], func=AF.Square, reads=[X1t], writes=[pxt[1]])
                yield
                pm, pmt = PSA.next()
                op("pe", "matmul", pm, lhsT=ONE64R[0:64, :], rhs=PX2[0:64, :], start=True, stop=True,
                   reads=[pxt[2], pxt[1]], writes=[pmt])
                yield
                op("dve", "tensor_scalar", out=X2[0:64, :], in0=pm[0:64, :], scalar1=RMS_EPS, scalar2=None, op0=ALU.add,
                   reads=[pmt], writes=[X2t])
                yield
                op("act", "activation", out=X2[0:64, :], in_=X2[0:64, :], func=AF.Ln, reads=[X2t], writes=[X2t])
                yield
                op("act", "activation", out=X2[0:64, :], in_=X2[0:64, :], func=AF.Exp, scale=-0.5, reads=[X2t], writes=[X2t])
                yield
                op("dve", "scalar_tensor_tensor", out=X1[0:64, :], in0=X1[0:64, :], scalar=ppc(("ga", l))[0:64, :], in1=X2[0:64, :],
                   op0=ALU.mult, op1=ALU.mult, reads=[X1t, X2t, t_pp], writes=[X1t])
                yield
                op("dve", "tensor_scalar", out=OT[64 * hl:64 * hl + 64, i, tok(Qp)], in0=X1[0:64, :], scalar1=1.0 - lam_init,
                   scalar2=None, op0=ALU.mult, reads=[X1t], writes=[ot_t[i][Qp]])
                yield

            for hl in range(2):
                for Qp in range(4):
                    nums = []
                    drain_bg(keep=1)
                    cnt = 0
                    for _ in main_block(hl, Qp, nums):
                        cnt += 1
                        if cnt % 2 == 0:
                            step_bg()
                    drain_bg(keep=0) if False else None
                    bg.append(post_block(hl, Qp, nums))
            drain_bg(0)
            P.end_phase(st)


        def phase_b(i):
            st = setup_ab(False)
            rs_t = [TT(), TT()]
            proj_rope(12 + 4 * i, QT, qt_t, "ropeB")
            proj_rope(12 + 4 * i + 2, KT, kt_t, "ropeB")
            scb = 64 ** -0.5
            oc = 3 + i
            proj_vfm(35 + i)
            for dil in (1, 4, 16):
                proj_v(35 + i, dil)
                L = S // dil
                nt = L // 128
                pieces = []
                for hl in range(2):
                    for r in range(dil):
                        Es = {}
                        for g in range((nt + 3) // 4):
                            pieces.append((hl, r, g, Es))

                def stage1(pc):
                    hl, r, g, Es = pc
                    rows = slice(64 * hl, 64 * hl + 64)
                    for a in range(max(4 * g - 1, 0), min(4 * g + 3, nt - 1) + 1):
                        if a in Es:
                            continue
                        nq = min(256, L - 128 * a)
                        st_ = r + dil * 128 * a
                        se, set_ = PS.next()
                        op("pe", "matmul", se[:, 0:nq], lhsT=KT[rows, ssl(st_, 128, dil)], rhs=QT[rows, ssl(st_, nq, dil)],
                           start=True, stop=True, reads=kt_t + qt_t, writes=[set_])
                        E, Et = e256.next()
                        op("act", "activation", out=E[:, 0:nq], in_=se[:, 0:nq], func=AF.Exp, scale=scb, reads=[set_], writes=[Et])
                        op("pool", "tensor_tensor", out=E[:, 0:nq], in0=E[:, 0:nq].bitcast(F32), in1=BAND[:, 0:nq], op=ALU.mult,
                           reads=[Et, t_cst], writes=[Et])
                        Es[a] = (E, Et, nq)

                def stage2(pc):
                    hl, r, g, Es = pc
                    rows = slice(64 * hl, 64 * hl + 64)
                    rowp = 64 if hl == 0 else 32
                    ncols = min(512, L - 512 * g)
                    alist = list(range(max(4 * g - 1, 0), min(4 * g + 3, nt - 1) + 1))
                    num, numt = PSL.next()
                    for ai, a in enumerate(alist):
                        E, Et, nq = Es[a]
                        lo = max(128 * a, 512 * g)
                        hi = min(128 * a + nq, 512 * g + ncols)
                        op("pe", "matmul", num[0:65, lo - 512 * g:hi - 512 * g], lhsT=VA[:, r * nt + a, hl, :],
                           rhs=E[:, lo - 128 * a:hi - 128 * a], start=(ai == 0), stop=(ai == len(alist) - 1),
                           reads=[va_t, Et], writes=[numt])
                    p0 = r + dil * 512 * g
                    ps_ = ssl(p0, ncols, dil)
                    if dil == 1:
                        op("dve", "tensor_copy", out=OT[rows, oc, ps_], in_=num[0:64, 0:ncols], reads=[numt], writes=ot_t[oc])
                        op("act", "copy", out=RS[rowp:rowp + 1, ps_], in_=num[64:65, 0:ncols], reads=[numt], writes=[rs_t[hl]])
                    else:
                        op("dve", "tensor_tensor", out=OT[rows, oc, ps_], in0=num[0:64, 0:ncols], in1=OTf[rows, oc, ps_], op=ALU.add,
                           reads=[numt] + ot_t[oc], writes=ot_t[oc])
                        op("dve", "tensor_tensor", out=RS[rowp:rowp + 1, ps_], in0=num[64:65, 0:ncols], in1=RS[rowp:rowp + 1, ps_],
                           op=ALU.add, reads=[numt, rs_t[hl]], writes=[rs_t[hl]])

                stage1(pieces[0])
                for k in range(len(pieces)):
                    if k + 1 < len(pieces):
                        stage1(pieces[k + 1])
                    stage2(pieces[k])
            P.barrier()
            chains = []

            def post_chain(hl, tp, X, Xt):
                rows = slice(64 * hl, 64 * hl + 64)
                rowp = 64 if hl == 0 else 32
                op("act", "activation", out=X[rows, :], in_=OTf[rows, oc, tok(tp)], func=AF.Square, reads=[ot_t[oc][tp]], writes=[Xt])
                yield
                op("act", "activation", out=X[rowp:rowp + 1, :], in_=RS[rowp:rowp + 1, tok(tp)], func=AF.Square,
                   scale=math.sqrt(RMS_EPS), reads=[rs_t[hl]], writes=[Xt])
                yield
                pm, pmt = PS.next()
                op("pe", "matmul", pm[0:64, :], lhsT=ONES64[rows, 0:64], rhs=X[rows, :], start=True, stop=False,
                   reads=[t_cst, Xt], writes=[pmt])
                op("pe", "matmul", pm[0:64, :], lhsT=ONE1[rowp:rowp + 1, 0:64], rhs=X[rowp:rowp + 1, :], start=False, stop=True,
                   reads=[t_lam, Xt], writes=[pmt])
                yield
                op("act", "activation", out=X[rows, :], in_=pm[0:64, :], func=AF.Ln, reads=[pmt], writes=[Xt])
                yield
                op("act", "activation", out=X[rows, :], in_=X[rows, :], func=AF.Exp, scale=-0.5, reads=[Xt], writes=[Xt])
                yield
                op("dve", "scalar_tensor_tensor", out=OT[rows, oc, tok(tp)], in0=OTf[rows, oc, tok(tp)], scalar=ppc(("gb", l))[rows, :],
                   in1=X[rows, :], op0=ALU.mult, op1=ALU.mult, reads=[ot_t[oc][tp], Xt, t_pp], writes=[ot_t[oc][tp]])
                yield

            xt4 = [(SCR0[:, 1024 + k * 512:1024 + (k + 1) * 512], TT()) for k in range(4)]
            for hl in range(2):
                gens = [post_chain(hl, tp, xt4[tp][0], xt4[tp][1]) for tp in range(4)]
                while gens:
                    for gsn in list(gens):
                        try:
                            next(gsn)
                        except StopIteration:
                            gens.remove(gsn)
            P.end_phase(st)


        def phase_c(hp):
            st = P.phase()
            AR = P.ph_sb(st, "CF", [128, CFW], F32)
            oc = 6 + hp
            base = 0
            F = [AR[:, base + k * 2048:base + (k + 1) * 2048] for k in range(6)]
            ft = [[TT() for _ in range(4)] for _ in range(6)]
            sm = base + 6 * 2048
            CW = P.ph_sb(st, "CW", [128, 1024], F32R)
            w1 = Rot([CW[:, 0:1024].rearrange("p (kc f) -> p kc f", kc=8)])
            RAWPS = [AR[:, sm:sm + 516], AR[:, sm + 1024:sm + 1024 + 516]]
            rawp_ts = [TT(), TT()]
            RAWP = RAWPS[1]
            rawp_t = rawp_ts[1]
            TA = AR[:, sm + 1540:sm + 2052]
            TB = AR[:, sm + 2052:sm + 2564]
            ta_t, tb_t = TT(), TT()
            GC = AR[:, sm + 2564:sm + 2580]
            gc_t = TT()
            assert sm + 2580 <= CFW
            dma("sp", "dma_start", out=CSM[:], in_=dr["csm"][:, l * 832:(l + 1) * 832], writes=[t_csm])

            def loadw1(ci):
                w, wt = w1.next()
                dma("pool", "dma_start", out=w, in_=dr["wch"][l * NCHUNK_W + ci].bitcast(F32R).rearrange("p (kc f) -> p kc f", kc=8),
                    writes=[wt])
                return w, wt

            def proj_lerp(cidx, dst, dst_t):
                w, wt = loadw1(24 + cidx)
                mu = ppc(("mu", l, cidx))
                op("pool", "memset", RAWPS[0][:, 0:1], 0.0, writes=[rawp_ts[0]])
                for tp in range(4):
                    RP, RPt = RAWPS[tp % 2], rawp_ts[tp % 2]
                    RN, RNt = RAWPS[(tp + 1) % 2], rawp_ts[(tp + 1) % 2]
                    pp_, ppt = PS.next()
                    for kc in range(8):
                        op("pe", "matmul", pp_, lhsT=w[:, kc, :], rhs=XT[:, kc, tok(tp)], start=(kc == 0), stop=(kc == 7),
                           reads=[wt, xt_t[kc][tp]], writes=[ppt])
                    op("act", "copy", out=RP[:, 1:513], in_=pp_, reads=[ppt], writes=[RPt])
                    if tp < 3:
                        op("act", "copy", out=RN[:, 0:1], in_=RP[:, 512:513], reads=[RPt], writes=[RNt])
                    op("dve", "tensor_tensor", out=TA, in0=RP[:, 0:512], in1=RP[:, 1:513], op=ALU.subtract,
                       reads=[RPt], writes=[ta_t])
                    op("dve", "scalar_tensor_tensor", out=dst[:, tok(tp)], in0=TA, scalar=mu, in1=RP[:, 1:513],
                       op0=ALU.mult, op1=ALU.add, reads=[ta_t, RPt, t_pp], writes=[dst_t[tp]])

            Kt, Bt, KKt, Rt, Vt = F[0], F[2], F[3], F[4], F[5]
            LW = F[1]
            proj_lerp(4 + hp, F[5], ft[5])
            if l == 0:
                dma("sp", "dma_start", out=vf_d[hp * 128:(hp + 1) * 128, :], in_=F[5], reads=ft[5], writes=[vf_t[hp]])
            else:
                proj_lerp(4 + (1 - hp), F[4], ft[4])
                vch = {hp: (F[5], ft[5]), 1 - hp: (F[4], ft[4])}
                for tp in range(4):
                    p32, p32t = PS.next()
                    for kc in range(2):
                        vv, vvt = vch[kc]
                        op("pe", "matmul", p32[0:32, :], lhsT=CSM[:, 512 + kc * 32:512 + kc * 32 + 32], rhs=vv[:, tok(tp)],
                           start=(kc == 0), stop=(kc == 1), reads=[t_csm, vvt[tp]], writes=[p32t])
                    op("act", "copy", out=TB[0:32, :], in_=p32[0:32, :], reads=[p32t], writes=[tb_t])
                    pg, pgt = PS.next()
                    op("pe", "matmul", pg, lhsT=CSM[0:32, 576 + hp * 128:576 + hp * 128 + 128], rhs=TB[0:32, :], start=True, stop=True,
                       reads=[t_csm, tb_t], writes=[pgt])
                    op("act", "activation", out=TB, in_=pg, func=AF.Sigmoid, bias=ppc(("v0", l, hp)), scale=1.0,
                       reads=[pgt, t_pp], writes=[tb_t])
                    dma("sp", "dma_start", out=TA, in_=vf_d[hp * 128:(hp + 1) * 128, tok(tp)], reads=[vf_t[hp]], writes=[ta_t])
                    op("dve", "tensor_tensor", out=TA, in0=TA, in1=F[5][:, tok(tp)], op=ALU.subtract, reads=[ta_t, ft[5][tp]], writes=[ta_t])
                    op("dve", "tensor_tensor", out=TA, in0=TA, in1=TB, op=ALU.mult, reads=[ta_t, tb_t], writes=[ta_t])
                    op("dve", "tensor_tensor", out=F[5][:, tok(tp)], in0=F[5][:, tok(tp)], in1=TA, op=ALU.add,
                       reads=[ta_t, ft[5][tp]], writes=[ft[5][tp]])
            proj_lerp(6, F[0], ft[0])
            for tp in range(4):
                op("act", "activation", out=F[0][0:64, tok(tp)], in_=F[0][0:64, tok(tp)], func=AF.Tanh, reads=[ft[0][tp]], writes=[ft[0][tp]])
                pw, pwt = PS.next()
                op("pe", "matmul", pw, lhsT=CSM[0:64, hp * 128:hp * 128 + 128], rhs=F[0][0:64, tok(tp)], start=True, stop=True,
                   reads=[t_csm, ft[0][tp]], writes=[pwt])
                op("act", "activation", out=F[1][:, tok(tp)], in_=pw, func=AF.Sigmoid, bias=ppc(("w0", l, hp)), scale=1.0,
                   reads=[pwt, t_pp], writes=[ft[1][tp]])
                op("pool", "tensor_scalar", out=F[1][:, tok(tp)], in0=F[1][:, tok(tp)], scalar1=-math.exp(-0.5), scalar2=None, op0=ALU.mult,
                   reads=[ft[1][tp]], writes=[ft[1][tp]])
                pa, pat = PS.next()
                op("pe", "matmul", pa, lhsT=CSM[64:128, hp * 128:hp * 128 + 128], rhs=F[0][64:128, tok(tp)], start=True, stop=True,
                   reads=[t_csm, ft[0][tp]], writes=[pat])
                op("act", "activation", out=F[2][:, tok(tp)], in_=pa, func=AF.Sigmoid, bias=ppc(("a0", l, hp)), scale=1.0,
                   reads=[pat, t_pp], writes=[ft[2][tp]])
            proj_lerp(2 + hp, F[0], ft[0])
            for tp in range(4):
                tk = tok(tp)
                op("dve", "tensor_scalar", out=F[3][:, tk], in0=F[0][:, tk], scalar1=ppc(("kk", l, hp)), scalar2=None, op0=ALU.mult,
                   reads=[ft[0][tp], t_pp], writes=[ft[3][tp]])
                op("act", "activation", out=TA, in_=F[3][:, tk], func=AF.Square, reads=[ft[3][tp]], writes=[ta_t])
                pq, pqt = PS.next()
                op("pe", "matmul", pq, lhsT=BLK, rhs=TA, start=True, stop=True, reads=[t_cst, ta_t], writes=[pqt])
                op("act", "activation", out=TB, in_=pq, func=AF.Sqrt, reads=[pqt], writes=[tb_t])
                op("dve", "tensor_scalar", out=TB, in0=TB, scalar1=1e-12, scalar2=None, op0=ALU.max, reads=[tb_t], writes=[tb_t])
                op("dve", "reciprocal", out=TB, in_=TB, reads=[tb_t], writes=[tb_t])
                op("dve", "tensor_tensor", out=F[3][:, tk], in0=F[3][:, tk], in1=TB, op=ALU.mult, reads=[ft[3][tp], tb_t], writes=[ft[3][tp]])
                op("dve", "tensor_scalar", out=TA, in0=F[2][:, tk], scalar1=-1.0, scalar2=ppc(("ka", l, hp)), op0=ALU.add, op1=ALU.mult,
                   reads=[ft[2][tp], t_pp], writes=[ta_t])
                op("dve", "scalar_tensor_tensor", out=F[0][:, tk], in0=TA, scalar=1.0, in1=F[0][:, tk], op0=ALU.add, op1=ALU.mult,
                   reads=[ta_t, ft[0][tp]], writes=[ft[0][tp]])
                op("pool", "tensor_tensor", out=F[2][:, tk], in0=F[2][:, tk], in1=F[3][:, tk], op=ALU.mult,
                   reads=[ft[2][tp], ft[3][tp]], writes=[ft[2][tp]])
            proj_lerp(hp, F[4], ft[4])
            for tp in range(4):
                tk = tok(tp)
                op("dve", "scalar_tensor_tensor", out=TA, in0=F[4][:, tk], scalar=ppc(("rk", l, hp)), in1=F[0][:, tk], op0=ALU.mult, op1=ALU.mult,
                   reads=[ft[4][tp], ft[0][tp], t_pp], writes=[ta_t])
                pq, pqt = PS.next()
                op("pe", "matmul", pq, lhsT=BLK, rhs=TA, start=True, stop=True, reads=[t_cst, ta_t], writes=[pqt])
                op("dve", "tensor_tensor", out=OT[:, oc, tk], in0=pq, in1=F[5][:, tk], op=ALU.mult, reads=[pqt, ft[5][tp]], writes=[ot_t[oc][tp]])
            for tp in range(4):
                tk = tok(tp)
                for cc in range(4):
                    cs = slice(tp * 512 + cc * 128, tp * 512 + cc * 128 + 128)
                    op("dve", "tensor_tensor_scan", out=TA[:, cc * 128:(cc + 1) * 128], data0=ONES, data1=F[1][:, cs], initial=0.0,
                       op0=ALU.mult, op1=ALU.add, reads=[t_cst, ft[1][tp]], writes=[ta_t])
                op("dve", "tensor_tensor", out=TB, in0=TA, in1=F[1][:, tk], op=ALU.subtract, reads=[ta_t, ft[1][tp]], writes=[tb_t])
                op("act", "activation", out=TB, in_=TB, func=AF.Exp, reads=[tb_t], writes=[tb_t])
                op("dve", "tensor_tensor", out=F[3][:, tk], in0=F[3][:, tk], in1=TB, op=ALU.mult, reads=[ft[3][tp], tb_t], writes=[ft[3][tp]])
                op("act", "activation", out=TB, in_=TA, func=AF.Exp, reads=[ta_t], writes=[tb_t])
                op("dve", "tensor_tensor", out=F[4][:, tk], in0=F[4][:, tk], in1=TB, op=ALU.mult, reads=[ft[4][tp], tb_t], writes=[ft[4][tp]])
                op("act", "copy", out=GC[:, tp * 4:tp * 4 + 4], in_=TB[:, 127:512:128], reads=[tb_t], writes=[gc_t])
                op("act", "activation", out=TB, in_=TA, func=AF.Exp, scale=-1.0, reads=[ta_t], writes=[tb_t])
                op("dve", "tensor_tensor", out=F[0][:, tk], in0=F[0][:, tk], in1=TB, op=ALU.mult, reads=[ft[0][tp], tb_t], writes=[ft[0][tp]])
                op("pool", "tensor_tensor", out=F[2][:, tk], in0=F[2][:, tk], in1=TB, op=ALU.mult, reads=[ft[2][tp], tb_t], writes=[ft[2][tp]])
            P.barrier()
            def cut(region, n, width):
                out = []
                for _ in range(n):
                    out.append(AR[:, region[0]:region[0] + width])
                    region[0] += width
                return out
            reg1 = [base + 2048]
            reg2 = [sm]
            reg3 = [sm + 2580]
            KhF, BhF = cut(reg1, 2, 128)
            MB = [cut(reg1, 2, 256) for _ in range(2)]
            TK = [cut(reg1, 3, 128) for _ in range(2)]
            assert reg1[0] <= base + 4096
            MK = [cut(reg3, 2, 256) for _ in range(2)]
            assert reg3[0] <= CFW, reg3[0]
            inv = [[cut(reg2, 3, 128) for _ in range(2)] for _ in range(2)]
            Wsb = cut(reg2, 2, 64)
            Usb = cut(reg2, 2, 64)
            ST = cut(reg2, 1, 64)[0]
            YC = cut(reg2, 1, 128)[0]
            G1 = cut(reg2, 3, 128)
            assert reg2[0] <= sm + 2564, reg2[0]
            t_kh, t_bh = TT(), TT()
            t_tk = [[TT(), TT(), TT()] for _ in range(2)]
            t_mb = [[TT(), TT()] for _ in range(2)]
            t_mk = [[TT(), TT()] for _ in range(2)]
            t_inv = [[[TT() for _ in range(3)] for _ in range(2)] for _ in range(2)]
            t_w, t_u = [TT(), TT()], [TT(), TT()]
            t_st = [TT(), TT()]
            t_yc = TT()
            t_g1 = [TT() for _ in range(3)]
            RW = [slice(0, 64), slice(64, 128)]
            op("pool", "memset", ST, 0.0, writes=t_st)

            def build(c):
                bf = c % 2
                cs = slice(c * 128, c * 128 + 128)
                tp = c // 4
                KhT, BhT, VT = TK[bf]
                t_kht, t_bht, t_vt = t_tk[bf]
                op("dve", "tensor_scalar", out=KhF, in0=F[0][:, cs], scalar1=GC[:, c:c + 1], scalar2=None, op0=ALU.mult,
                   reads=[ft[0][tp], gc_t], writes=[t_kh])
                op("pool", "tensor_scalar", out=BhF, in0=F[2][:, cs], scalar1=GC[:, c:c + 1], scalar2=None, op0=ALU.mult,
                   reads=[ft[2][tp], gc_t], writes=[t_bh])
                yield
                for (src, srct, dstT, dstt) in ((KhF, [t_kh], KhT, t_kht), (BhF, [t_bh], BhT, t_bht), (F[5][:, cs], [ft[5][tp]], VT, t_vt)):
                    ptr, ptrt = PS.next()
                    op("pe", "transpose", ptr[:, 0:128], src, IDENT, reads=srct + [t_cst], writes=[ptrt])
                    op("act", "copy", out=dstT, in_=ptr[:, 0:128], reads=[ptrt], writes=[dstt])
                    yield
                for hl in range(2):
                    rows = RW[hl]
                    for (lt, ltt, Mx, Mxt) in ((F[2], ft[2][tp], MB[bf][hl], t_mb[bf][hl]), (F[0], ft[0][tp], MK[bf][hl], t_mk[bf][hl])):
                        pmx, pmxt = PS.next()
                        op("pe", "matmul", pmx[:, 0:128], lhsT=lt[rows, cs], rhs=F[3][rows, cs], start=True, stop=True,
                           reads=[ltt, ft[3][tp]], writes=[pmxt])
                        op("pe", "matmul", pmx[:, 128:256], lhsT=lt[rows, cs], rhs=F[4][rows, cs], start=True, stop=True,
                           reads=[ltt, ft[4][tp]], writes=[pmxt])
                        op("dve", "tensor_tensor", out=Mx, in0=pmx[:, 0:256], in1=UPP, op=ALU.mult, reads=[pmxt, t_cst], writes=[Mxt])
                        yield
                    Pm, PTm, Zm = inv[bf][hl]
                    tPm, tPTm, tZm = t_inv[bf][hl]
                    op("act", "mul", out=Pm, in_=MB[bf][hl][:, 0:128], mul=-1.0, reads=[t_mb[bf][hl]], writes=[tPm])
                    pnt, pntt = PS.next()
                    op("pe", "matmul", pnt[:, 0:128], lhsT=F[3][rows, cs], rhs=F[2][rows, cs], start=True, stop=True,
                       reads=[ft[3][tp], ft[2][tp]], writes=[pntt])
                    op("dve", "scalar_tensor_tensor", out=PTm, in0=pnt[:, 0:128], scalar=-1.0, in1=LOWS, op0=ALU.mult, op1=ALU.mult,
                       reads=[pntt, t_cst], writes=[tPTm])
                    op("pool", "tensor_tensor", out=Zm, in0=Pm, in1=IDENT, op=ALU.add, reads=[tPm, t_cst], writes=[tZm])
                    yield
                for stg in range(6):
                    for hl in range(2):
                        Pm, PTm, Zm = inv[bf][hl]
                        tPm, tPTm, tZm = t_inv[bf][hl]
                        ppt2, ppt2t = PS.next()
                        op("pe", "matmul", ppt2[:, 0:128], lhsT=Pm, rhs=PTm, start=True, stop=True, reads=[tPm, tPTm], writes=[ppt2t])
                        if stg < 5:
                            pp2, pp2t = PS.next()
                            op("pe", "matmul", pp2[:, 0:128], lhsT=PTm, rhs=Pm, start=True, stop=True, reads=[tPm, tPTm], writes=[pp2t])
                        op("act", "copy", out=PTm, in_=ppt2[:, 0:128], reads=[ppt2t], writes=[tPTm])
                        if stg < 5:
                            op("act", "copy", out=Pm, in_=pp2[:, 0:128], reads=[pp2t], writes=[tPm])
                        yield
                    for hl in range(2):
                        Pm, PTm, Zm = inv[bf][hl]
                        tPm, tPTm, tZm = t_inv[bf][hl]
                        pz, pzt = PS.next()
                        op("pe", "matmul", pz[:, 0:128], lhsT=PTm, rhs=Zm, start=True, stop=True, reads=[tPTm, tZm], writes=[pzt])
                        op("dve", "tensor_tensor", out=Zm, in0=pz[:, 0:128], in1=Zm, op=ALU.add, reads=[pzt, tZm], writes=[tZm])
                        yield

            def seq(c):
                bf = c % 2
                cs = slice(c * 128, c * 128 + 128)
                tp = c // 4
                KhT, BhT, VT = TK[bf]
                t_kht, t_bht, t_vt = t_tk[bf]
                for hl in range(2):
                    rows, hc = RW[hl], RW[hl]
                    pw, pwt = PS.next()
                    op("pe", "matmul", pw[:, 0:64], lhsT=F[3][rows, cs], rhs=ST[rows, :], start=True, stop=False,
                       reads=[ft[3][tp], t_st[hl]], writes=[pwt])
                    op("pe", "matmul", pw[:, 0:64], lhsT=MK[bf][hl][:, 0:128], rhs=VT[:, hc], start=False, stop=True,
                       reads=[t_mk[bf][hl], t_vt], writes=[pwt])
                    op("act", "copy", out=Wsb[hl], in_=pw[:, 0:64], reads=[pwt], writes=[t_w[hl]])
                    yield
                for hl in range(2):
                    pu, put = PS.next()
                    op("pe", "matmul", pu[:, 0:64], lhsT=inv[bf][hl][2], rhs=Wsb[hl], start=True, stop=True,
                       reads=[t_inv[bf][hl][2], t_w[hl]], writes=[put])
                    op("act", "mul", out=Usb[hl], in_=pu[:, 0:64], mul=-1.0, reads=[put], writes=[t_u[hl]])
                    yield
                for hl in range(2):
                    rows, hc = RW[hl], RW[hl]
                    py, pyt = PS.next()
                    op("pe", "matmul", py[0:64, 0:128], lhsT=ST[rows, :], rhs=F[4][rows, cs], start=True, stop=False,
                       reads=[t_st[hl], ft[4][tp]], writes=[pyt])
                    op("pe", "matmul", py[0:64, 0:128], lhsT=Usb[hl], rhs=MB[bf][hl][:, 128:256], start=False, stop=False,
                       reads=[t_u[hl], t_mb[bf][hl]], writes=[pyt])
                    op("pe", "matmul", py[0:64, 0:128], lhsT=VT[:, hc], rhs=MK[bf][hl][:, 128:256], start=False, stop=True,
                       reads=[t_vt, t_mk[bf][hl]], writes=[pyt])
                    op("act", "copy", out=YC[rows, :], in_=py[0:64, 0:128], reads=[pyt], writes=[t_yc])
                    pn, pnt_ = PS.next()
                    op("pe", "matmul", pn[0:64, 0:64], lhsT=BhT[:, hc], rhs=Usb[hl], start=True, stop=False,
                       reads=[t_bht, t_u[hl]], writes=[pnt_])
                    op("pe", "matmul", pn[0:64, 0:64], lhsT=KhT[:, hc], rhs=VT[:, hc], start=False, stop=True,
                       reads=[t_kht, t_vt], writes=[pnt_])
                    op("dve", "scalar_tensor_tensor", out=ST[rows, :], in0=ST[rows, :], scalar=GC[rows, c:c + 1], in1=pn[0:64, 0:64],
                       op0=ALU.mult, op1=ALU.add, reads=[t_st[hl], gc_t, pnt_], writes=[t_st[hl]])
                    yield
                pmu, pmut = PS.next()
                op("pe", "matmul", pmu[:, 0:128], lhsT=BLK64, rhs=YC, start=True, stop=True, reads=[t_cst, t_yc], writes=[pmut])
                op("act", "activation", out=G1[0], in_=YC, func=AF.Square, reads=[t_yc], writes=[t_g1[0]])
                yield
                pe2, pe2t = PS.next()
                op("pe", "matmul", pe2[:, 0:128], lhsT=BLK64, rhs=G1[0], start=True, stop=True, reads=[t_cst, t_g1[0]], writes=[pe2t])
                op("act", "copy", out=G1[1], in_=pmu[:, 0:128], reads=[pmut], writes=[t_g1[1]])
                yield
                op("dve", "tensor_tensor", out=G1[0], in0=G1[1], in1=G1[1], op=ALU.mult, reads=[t_g1[1]], writes=[t_g1[0]])
                op("dve", "scalar_tensor_tensor", out=G1[2], in0=pe2[:, 0:128], scalar=GN_EPS, in1=G1[0], op0=ALU.add, op1=ALU.subtract,
                   reads=[pe2t, t_g1[0]], writes=[t_g1[2]])
                yield
                op("act", "activation", out=G1[2], in_=G1[2], func=AF.Ln, reads=[t_g1[2]], writes=[t_g1[2]])
                op("act", "activation", out=G1[2], in_=G1[2], func=AF.Exp, scale=-0.5, reads=[t_g1[2]], writes=[t_g1[2]])
                yield
                op("dve", "tensor_tensor", out=G1[0], in0=YC, in1=G1[1], op=ALU.subtract, reads=[t_yc, t_g1[1]], writes=[t_g1[0]])
                op("dve", "tensor_tensor", out=G1[0], in0=G1[0], in1=G1[2], op=ALU.mult, reads=[t_g1[0], t_g1[2]], writes=[t_g1[0]])
                yield
                op("act", "activation", out=G1[0], in_=G1[0], func=AF.Identity, bias=ppc(("gnb", l, hp)), scale=ppc(("gng", l, hp)),
                   reads=[t_g1[0], t_pp], writes=[t_g1[0]])
                op("pool", "tensor_tensor", out=OT[:, oc, cs], in0=OTf[:, oc, cs], in1=G1[0], op=ALU.add,
                   reads=[ot_t[oc][tp], t_g1[0]], writes=[ot_t[oc][tp]])
                yield

            for _ in build(0):
                pass
            NCH = int(_os.environ.get("CSCAN", "16"))
            for c in range(NCH):
                gb = build(c + 1) if c + 1 < NCH else iter(())
                gs = seq(c)
                alive = [gb, gs]
                while alive:
                    for gsn in list(alive):
                        try:
                            next(gsn)
                        except StopIteration:
                            alive.remove(gsn)
            P.barrier()
            ft0 = [TT() for _ in range(4)]
            rawp_t2 = TT()
            w, wt = w1.next()
            dma("pool", "dma_start", out=w, in_=dr["wch"][l * NCHUNK_W + 24 + 7].bitcast(F32R).rearrange("p (kc f) -> p kc f", kc=8),
                writes=[wt])
            op("pool", "memset", RAWP[:, 0:1], 0.0, writes=[rawp_t2])
            mu = ppc(("mu", l, 7))
            ta2, tb2 = TT(), TT()
            for tp in range(4):
                pp_, ppt = PS.next()
                for kc in range(8):
                    op("pe", "matmul", pp_, lhsT=w[:, kc, :], rhs=XT[:, kc, tok(tp)], start=(kc == 0), stop=(kc == 7),
                       reads=[wt, xt_t[kc][tp]], writes=[ppt])
                op("act", "copy", out=RAWP[:, 1:513], in_=pp_, reads=[ppt], writes=[rawp_t2])
                op("dve", "tensor_tensor", out=TA, in0=RAWP[:, 0:512], in1=RAWP[:, 1:513], op=ALU.subtract, reads=[rawp_t2], writes=[ta2])
                op("dve", "scalar_tensor_tensor", out=TB, in0=TA, scalar=mu, in1=RAWP[:, 1:513], op0=ALU.mult, op1=ALU.add,
                   reads=[ta2, rawp_t2, t_pp], writes=[tb2])
                op("act", "copy", out=RAWP[:, 0:1], in_=RAWP[:, 512:513], reads=[rawp_t2], writes=[rawp_t2])
                op("act", "activation", out=TB, in_=TB, func=AF.Sigmoid, reads=[tb2], writes=[tb2])
                pg, pgt = PS.next()
                op("pe", "matmul", pg, lhsT=CSM[:, 256 + hp * 128:256 + hp * 128 + 128], rhs=TB, start=True, stop=True,
                   reads=[t_csm, tb2], writes=[pgt])
                op("dve", "tensor_tensor", out=OT[:, oc, tok(tp)], in0=pg, in1=OTf[:, oc, tok(tp)], op=ALU.mult,
                   reads=[pgt, ot_t[oc][tp]], writes=[ot_t[oc][tp]])
            P.end_phase(st)

        if stage in ("A", "mix", "full"):
            for i in range(3):
                phase_a(i)
        else:
            for c in range(0, 3):
                op("pool", "tensor_scalar", out=OT[:, c, :], in0=XF[:, c, :], scalar1=0.0, scalar2=None, op0=ALU.mult, reads=xt_t[c], writes=ot_t[c])
        if stage in ("B", "mix", "full"):
            for i in range(3):
                phase_b(i)
        else:
            for c in range(3, 6):
                op("pool", "tensor_scalar", out=OT[:, c, :], in0=XF[:, c, :], scalar1=0.0, scalar2=None, op0=ALU.mult, reads=xt_t[c], writes=ot_t[c])
        if stage in ("C", "mix", "full"):
            for hp in range(2):
                phase_c(hp)
        else:
            for c in range(6, 8):
                op("pool", "tensor_scalar", out=OT[:, c, :], in0=XF[:, c, :], scalar1=0.0, scalar2=None, op0=ALU.mult, reads=xt_t[c], writes=ot_t[c])
        return OT, ot_t, OTf

    def wout_ln(l, OT, ot_t, w_r):
        st = P.phase()
        WL = P.ph_sb(st, "WL", [128, 4096 + 1536 + 2048], F32)
        WR = P.ph_sb(st, "WR", [128, 1024 + 128 + 2048], F32R)
        sqr = Rot([WR[:, k * 512:(k + 1) * 512] for k in range(2)])
        onesr = WR[:, 1024:1152]
        onest = TT()
        op("act", "copy", out=onesr, in_=ONESD, reads=[t_cst], writes=[onest])
        sc = Rot([WL[:, k * 512:(k + 1) * 512] for k in range(8)])
        lnk = Rot([WL[:, 4096 + k * 512:4096 + (k + 1) * 512] for k in range(3)])
        wo_r = Rot([WR[:, 1152 + k * 1024:1152 + (k + 1) * 1024].rearrange("p (kc f) -> p kc f", kc=8) for k in range(2)])
        for dc in range(8):
            w, wt = wo_r.next()
            dma("pool", "dma_start", out=w, in_=dr["wch"][l * NCHUNK_W + 38 + dc].bitcast(F32R).rearrange("p (kc f) -> p kc f", kc=8),
                writes=[wt])
            for tp in range(4):
                py, pyt = PS.next()
                for kc in range(8):
                    op("pe", "matmul", py, lhsT=w[:, kc, :], rhs=OT[:, kc, tok(tp)], start=(kc == 0), stop=(kc == 7),
                       reads=[wt, ot_t[kc][tp]], writes=[pyt])
                op("dve", "scalar_tensor_tensor", out=XT[:, dc, tok(tp)], in0=py, scalar=1.0 / ALPHA, in1=XF[:, dc, tok(tp)],
                   op0=ALU.mult, op1=ALU.add, reads=[pyt, xt_t[dc][tp]], writes=[xt_t[dc][tp]])
        for tp in range(4):
            layer_norm(l, 1, tp, sc, lnk, sqr, onesr, onest)
        return st

    def dump_o(OTf, ot_t):
        for c in range(8):
            dma("sp", "dma_start", out=out_d[c * 128:(c + 1) * 128, :], in_=OTf[:, c, :], reads=ot_t[c])

    def dump_x():
        for c in range(8):
            dma("sp", "dma_start", out=out_d[c * 128:(c + 1) * 128, :], in_=XF[:, c, :], reads=xt_t[c])

    dumped = False
    stage = dbg[1] if dbg else "full"
    for l in range(depth):
        st = ffn_phase(l, 0)
        if dbg == ("x", "ffn_a") and l == depth - 1:
            dump_x()
            dumped = True
            P.end_phase(st)
            break
        P.end_phase(st)
        stm = P.phase()
        OTt = P.ph_sb(stm, "OT", [128, 16384], F32R)
        OT, ot_t, w_r = mixer_phase(l, stage)
        if dbg is not None and dbg[0] == "o" and l == depth - 1:
            dump_o(w_r, ot_t)
            dumped = True
            P.end_phase(stm)
            break
        st = wout_ln(l, OT, ot_t, w_r)
        if dbg == ("x", "mixln") and l == depth - 1:
            dump_x()
            dumped = True
            P.end_phase(st)
            P.end_phase(stm)
            break
        P.end_phase(st)
        P.end_phase(stm)
        st = ffn_phase(l, 1)
        if l == depth - 1:
            dump_x()
            dumped = True
        P.end_phase(st)
    P.barrier()
    P.finish()
    return nc


_CACHE = {}


def kernel(**inputs):
    sh, xs, idx = prep_inputs(inputs)
    shapes = {k: v.shape for k, v in sh.items()}
    nc = build(shapes, idx)
    n = len(xs)
    in_maps = []
    for b in range(n):
        m = dict(sh)
        m["xT"] = xs[b]
        in_maps.append(m)
    res = run_bass_kernel_spmd(nc, in_maps, core_ids=list(range(n)))
    out = np.stack([np.ascontiguousarray(res.results[b]["out"].T) for b in range(n)], axis=0)
    return out.astype(np.float32)
```
